# Optimizing a Trainium2 kernel written in Bass

```python
import math
import jax, jax.numpy as jnp
from jax import lax
import numpy as np

D_MODEL = 1024
BATCH = 8
SEQ = 4096
DEPTH = 2

CTX_LEN = 256
GRID_W = 64
N_EVEN = (DEPTH + 1) // 2
N_ODD = DEPTH // 2
A_WIDTH = D_MODEL // 2
A_DK = 128
A_DV = 128
A_HEADS = A_WIDTH // A_DV
A_QKV = A_HEADS * (2 * A_DK + A_DV)
SHORT_CONV = 5
GDN_CHUNK = 64
B_WIDTH = D_MODEL - A_WIDTH
B_HD = 128
B_HEADS = B_WIDTH // B_HD
B_KV = B_HEADS // 2
Q_BLOCK = 128
ROPE_THETA = 10000.0
CONV_WIDTH = D_MODEL
CONV_K = 31
NORM_EPS = 1e-6
EVEN_SPLITS = (A_QKV, A_WIDTH, 2 * A_HEADS, 2 * A_HEADS, B_HEADS * B_HD, B_KV * B_HD, B_KV * B_HD, B_WIDTH)
EVEN_IN = sum(EVEN_SPLITS)

kernel_name = "hybrid_gdn_gqa_conformer_prefix_dit"

F32 = jnp.float32


def split_cols(a, sizes):
    return jnp.split(a, np.cumsum(sizes)[:-1].tolist(), axis=-1)


def rms_norm(x, g):
    xf = x.astype(F32)
    y = xf * lax.rsqrt(jnp.mean(xf * xf, axis=-1, keepdims=True) + NORM_EPS)
    return (y * g.astype(F32)).astype(x.dtype)


def layer_norm(x, g, b):
    xf = x.astype(F32)
    mu = jnp.mean(xf, axis=-1, keepdims=True)
    var = jnp.mean(jnp.square(xf - mu), axis=-1, keepdims=True)
    return ((xf - mu) * lax.rsqrt(var + 1e-5) * g.astype(F32) + b.astype(F32)).astype(x.dtype)


def l2_normalize(x):
    return x * lax.rsqrt(jnp.sum(x * x, axis=-1, keepdims=True) + NORM_EPS)


def depthwise_conv(x, w):
    k = w.shape[0]
    return lax.conv_general_dilated(x, w[:, None, :].astype(x.dtype), window_strides=(1,),
                                    padding=[(k // 2, k // 2)], dimension_numbers=('NWC', 'WIO', 'NWC'),
                                    feature_group_count=x.shape[-1])


def adaln(cond, w, b):
    m = (jax.nn.silu(cond) @ w + b)[:, None, :]
    return jnp.split(m, 3, axis=-1)


def axial_rope_tables(n_tokens, head_dim):
    rows = n_tokens // GRID_W
    row = jnp.repeat(jnp.arange(rows, dtype=F32), GRID_W)
    col = jnp.tile(jnp.arange(GRID_W, dtype=F32), rows)
    axis_dim = head_dim // 2
    inv_freq = ROPE_THETA ** (-jnp.arange(0, axis_dim, 2, dtype=F32) / axis_dim)
    ang_r = row[:, None] * inv_freq
    ang_c = col[:, None] * inv_freq
    return (jnp.cos(ang_r), jnp.sin(ang_r), jnp.cos(ang_c), jnp.sin(ang_c))


def rope_rotate(x, cos, sin):
    x1, x2 = jnp.split(x, 2, axis=-1)
    cs, sn = cos[None, :, None, :], sin[None, :, None, :]
    return jnp.concatenate([x1 * cs - x2 * sn, x2 * cs + x1 * sn], axis=-1)


def apply_axial_rope(x, tabs):
    cr, sr, cc, sc = tabs
    half = x.shape[-1] // 2
    xf = x.astype(F32)
    y = jnp.concatenate([rope_rotate(xf[..., :half], cr, sr), rope_rotate(xf[..., half:], cc, sc)], axis=-1)
    return y.astype(x.dtype)


def chunk_gated_delta(q, k, v, g, beta, s0, with_output):
    bsz, nh, t, dk = q.shape
    dv = v.shape[-1]
    n = t // GDN_CHUNK

    def blk(a):
        return a.reshape(bsz, nh, n, GDN_CHUNK, *a.shape[3:])

    q, k, v, g, beta = blk(q), blk(k), blk(v), blk(g), blk(beta)
    g = jnp.cumsum(g, axis=-1)
    idx = jnp.arange(GDN_CHUNK)
    incl = idx[:, None] >= idx[None, :]
    strict = idx[:, None] > idx[None, :]
    decay = jnp.exp(jnp.where(incl, g[..., :, None] - g[..., None, :], -jnp.inf))
    kb = k * beta[..., None]
    lower = jnp.where(strict, jnp.einsum('bhnid,bhnjd->bhnij', kb, k) * decay, 0.0)
    tmat = jnp.eye(GDN_CHUNK, dtype=q.dtype) + lower
    u = lax.linalg.triangular_solve(tmat, v * beta[..., None], left_side=True, lower=True)
    w = lax.linalg.triangular_solve(tmat, kb * jnp.exp(g)[..., None], left_side=True, lower=True)
    k_dec = k * jnp.exp(g[..., -1:] - g)[..., None]
    g_tot = jnp.exp(g[..., -1])
    if with_output:
        a_intra = jnp.einsum('bhnid,bhnjd->bhnij', q, k) * decay
        q_dec = q * jnp.exp(g)[..., None]
        xs = tuple(jnp.moveaxis(a, 2, 0) for a in (w, u, k_dec, g_tot, q_dec, a_intra))
    else:
        xs = tuple(jnp.moveaxis(a, 2, 0) for a in (w, u, k_dec, g_tot))

    def step(state, inp):
        w_c, u_c, kd_c, gt_c = inp[:4]
        v_new = u_c - jnp.einsum('bhid,bhde->bhie', w_c, state)
        new_state = state * gt_c[..., None, None] + jnp.einsum('bhid,bhie->bhde', kd_c, v_new)
        if with_output:
            qd_c, a_c = inp[4:]
            o_c = jnp.einsum('bhid,bhde->bhie', qd_c, state) + jnp.einsum('bhij,bhje->bhie', a_c, v_new)
            return new_state, o_c
        return new_state, None

    s_final, o = lax.scan(step, s0, xs)
    if with_output:
        o = jnp.moveaxis(o, 0, 2).reshape(bsz, nh, t, dv)
    return o, s_final


def bidir_gdn(q, k, v, g, beta, s0_f, s0_b, with_output):
    o_f, s_f = chunk_gated_delta(q, k, v, g[0], beta[0], s0_f, with_output)
    o_b, s_b = chunk_gated_delta(jnp.flip(q, 2), jnp.flip(k, 2), jnp.flip(v, 2), jnp.flip(g[1], -1),
                                 jnp.flip(beta[1], -1), s0_b, with_output)
    o = o_f + jnp.flip(o_b, 2) if with_output else None
    return o, s_f, s_b


def gdn_prep(qkv, a, b, conv_w, a_log, dt_bias):
    bsz, t, _ = qkv.shape
    u = jax.nn.silu(depthwise_conv(qkv, conv_w)).astype(F32)
    q, k, v = split_cols(u, [A_HEADS * A_DK, A_HEADS * A_DK, A_HEADS * A_DV])

    def heads(y, d):
        return y.reshape(bsz, t, A_HEADS, d).transpose(0, 2, 1, 3)

    q = l2_normalize(heads(q, A_DK)) * (A_DK ** -0.5)
    k = l2_normalize(heads(k, A_DK))
    v = heads(v, A_DV)

    def dirs(y):
        return y.astype(F32).reshape(bsz, t, 2, A_HEADS).transpose(2, 0, 3, 1)

    beta = jax.nn.sigmoid(dirs(b))
    g = -jnp.exp(a_log.astype(F32))[:, None, :, None] * jax.nn.softplus(
        dirs(a) + dt_bias.astype(F32)[:, None, :, None])
    return q, k, v, g, beta


def gdn_out(o, z, norm_g):
    bsz, nh, t, dv = o.shape
    o = rms_norm(o.transpose(0, 2, 1, 3), norm_g)
    y = o * jax.nn.silu(z.astype(F32).reshape(bsz, t, nh, dv))
    return y.reshape(bsz, t, nh * dv).astype(z.dtype)


def gqa_prep(q, k, v, qn_g, kn_g):
    bsz, t, _ = q.shape
    q = rms_norm(q.reshape(bsz, t, B_HEADS, B_HD), qn_g)
    k = rms_norm(k.reshape(bsz, t, B_KV, B_HD), kn_g)
    v = v.reshape(bsz, t, B_KV, B_HD)
    return q, k, v


def grouped_attend(q, k, v):
    bsz, nq = q.shape[:2]
    qg = q.reshape(bsz, nq, B_KV, B_HEADS // B_KV, B_HD)
    s = jnp.einsum('bqhgd,bnhd->bhgqn', qg, k).astype(F32) * (B_HD ** -0.5)
    p = jax.nn.softmax(s, axis=-1).astype(v.dtype)
    o = jnp.einsum('bhgqn,bnhd->bqhgd', p, v)
    return o.reshape(bsz, nq, B_HEADS * B_HD)


def gdn_gqa_mixer(hl, hc, w_in, conv_w, a_log, dt_bias, gdn_g, qn_g, kn_g, w_out, rope, ctx_out):
    bsz, t, _ = hl.shape
    pl = hl @ w_in
    pc = hc @ w_in
    qkv_l, za_l, a_l, b_l, qb_l, kb_l, vb_l, zb_l = split_cols(pl, EVEN_SPLITS)
    qkv_c, za_c, a_c, b_c, qb_c, kb_c, vb_c, zb_c = split_cols(pc, EVEN_SPLITS)
    qa_c, ka_c, va_c, ga_c, ba_c = gdn_prep(qkv_c, a_c, b_c, conv_w, a_log, dt_bias)
    qa_l, ka_l, va_l, ga_l, ba_l = gdn_prep(qkv_l, a_l, b_l, conv_w, a_log, dt_bias)
    zero = jnp.zeros((bsz, A_HEADS, A_DK, A_DV), F32)
    oa_c, s_f, s_b = bidir_gdn(qa_c, ka_c, va_c, ga_c, ba_c, zero, zero, ctx_out)
    oa_l, _, _ = bidir_gdn(qa_l, ka_l, va_l, ga_l, ba_l, s_f, s_b, True)
    ya_l = gdn_out(oa_l, za_l, gdn_g)
    q_l, k_l, v_l = gqa_prep(qb_l, kb_l, vb_l, qn_g, kn_g)
    q_l, k_l = apply_axial_rope(q_l, rope), apply_axial_rope(k_l, rope)
    q_c, k_c, v_c = gqa_prep(qb_c, kb_c, vb_c, qn_g, kn_g)
    k_all = jnp.concatenate([k_l, k_c], axis=1)
    v_all = jnp.concatenate([v_l, v_c], axis=1)
    nblk = t // Q_BLOCK
    q_blocks = q_l.reshape(bsz, nblk, Q_BLOCK, B_HEADS, B_HD).swapaxes(0, 1)
    ob = lax.map(lambda qq: grouped_attend(qq, k_all, v_all), q_blocks)
    ob = ob.swapaxes(0, 1).reshape(bsz, t, B_WIDTH)
    yb_l = ob * jax.nn.silu(zb_l)
    out_l = jnp.concatenate([ya_l, yb_l.astype(ya_l.dtype)], axis=-1) @ w_out
    if not ctx_out:
        return out_l, None
    ya_c = gdn_out(oa_c, za_c, gdn_g)
    yb_c = grouped_attend(q_c, k_c, v_c) * jax.nn.silu(zb_c)
    out_c = jnp.concatenate([ya_c, yb_c.astype(ya_c.dtype)], axis=-1) @ w_out
    return out_l, out_c


def conformer_mixer(h, w_in, b_in, dw_w, dw_b, ln_g, ln_b, w_out, b_out):
    p = h @ w_in + b_in
    a, glu_g, z = jnp.split(p, 3, axis=-1)
    u = a * jax.nn.sigmoid(glu_g)
    u = depthwise_conv(u, dw_w) + dw_b
    u = layer_norm(u, ln_g, ln_b)
    u = jax.nn.silu(u) * jax.nn.silu(z)
    return u @ w_out + b_out


def setup_inputs(seed: int = 0) -> dict:
    key = jax.random.key(seed)
    ks = jax.random.split(key, 24)

    def nrm(k, shape, scale):
        return jax.random.normal(k, shape, F32) * scale

    dt = jnp.exp(jax.random.uniform(ks[11], (N_EVEN, 2, A_HEADS), F32, math.log(1e-3), math.log(1e-1)))
    return {
        "x": nrm(ks[0], (BATCH, SEQ, D_MODEL), 1.0),
        "c": nrm(ks[1], (BATCH, D_MODEL), 1.0),
        "ctx": nrm(ks[2], (BATCH, CTX_LEN, D_MODEL), 1.0),
        "c_ctx": nrm(ks[3], (D_MODEL,), 1.0),
        "ada_w": nrm(ks[4], (DEPTH, D_MODEL, 3 * D_MODEL), 0.5 * D_MODEL ** -0.5),
        "ada_b": nrm(ks[5], (DEPTH, 3 * D_MODEL), 0.02),
        "pre_norm_g": 1.0 + nrm(ks[6], (DEPTH, D_MODEL), 0.02),
        "post_norm_g": 1.0 + nrm(ks[7], (DEPTH, D_MODEL), 0.02),
        "ev_w_in": nrm(ks[8], (N_EVEN, D_MODEL, EVEN_IN), D_MODEL ** -0.5),
        "ev_short_conv_w": nrm(ks[9], (N_EVEN, SHORT_CONV, A_QKV), SHORT_CONV ** -0.5),
        "ev_a_log": jnp.log(jax.random.uniform(ks[10], (N_EVEN, 2, A_HEADS), F32, 1.0, 16.0)),
        "ev_dt_bias": dt + jnp.log(-jnp.expm1(-dt)),
        "ev_gdn_norm_g": 1.0 + nrm(ks[12], (N_EVEN, A_DV), 0.02),
        "ev_q_norm_g": 1.0 + nrm(ks[13], (N_EVEN, B_HD), 0.02),
        "ev_k_norm_g": 1.0 + nrm(ks[14], (N_EVEN, B_HD), 0.02),
        "ev_w_out": nrm(ks[15], (N_EVEN, A_WIDTH + B_WIDTH, D_MODEL), (A_WIDTH + B_WIDTH) ** -0.5),
        "od_w_in": nrm(ks[16], (N_ODD, D_MODEL, 3 * CONV_WIDTH), D_MODEL ** -0.5),
        "od_b_in": nrm(ks[17], (N_ODD, 3 * CONV_WIDTH), 0.02),
        "od_dw_w": nrm(ks[18], (N_ODD, CONV_K, CONV_WIDTH), CONV_K ** -0.5),
        "od_dw_b": nrm(ks[19], (N_ODD, CONV_WIDTH), 0.02),
        "od_ln_g": 1.0 + nrm(ks[20], (N_ODD, CONV_WIDTH), 0.02),
        "od_ln_b": nrm(ks[21], (N_ODD, CONV_WIDTH), 0.02),
        "od_w_out": nrm(ks[22], (N_ODD, CONV_WIDTH, D_MODEL), CONV_WIDTH ** -0.5),
        "od_b_out": nrm(ks[23], (N_ODD, D_MODEL), 0.02),
    }


def reference(x, c, ctx, c_ctx, ada_w, ada_b, pre_norm_g, post_norm_g, ev_w_in, ev_short_conv_w, ev_a_log,
              ev_dt_bias, ev_gdn_norm_g, ev_q_norm_g, ev_k_norm_g, ev_w_out, od_w_in, od_b_in, od_dw_w, od_dw_b,
              od_ln_g, od_ln_b, od_w_out, od_b_out):
    rope = axial_rope_tables(x.shape[1], B_HD)
    xl, xc = x, ctx
    for layer in range(DEPTH):
        even = layer % 2 == 0
        i = layer // 2
        ctx_out = any(j % 2 == 0 for j in range(layer + 1, DEPTH))
        shift_l, scale_l, gate_l = adaln(c, ada_w[layer], ada_b[layer])
        hl = rms_norm(xl, pre_norm_g[layer]) * (1.0 + scale_l) + shift_l
        if even or ctx_out:
            shift_c, scale_c, gate_c = adaln(c_ctx[None, :], ada_w[layer], ada_b[layer])
            hc = rms_norm(xc, pre_norm_g[layer]) * (1.0 + scale_c) + shift_c
        if even:
            out_l, out_c = gdn_gqa_mixer(hl, hc, ev_w_in[i], ev_short_conv_w[i], ev_a_log[i], ev_dt_bias[i],
                                         ev_gdn_norm_g[i], ev_q_norm_g[i], ev_k_norm_g[i], ev_w_out[i],
                                         rope, ctx_out)
        else:
            out_l = conformer_mixer(hl, od_w_in[i], od_b_in[i], od_dw_w[i], od_dw_b[i], od_ln_g[i], od_ln_b[i],
                                    od_w_out[i], od_b_out[i])
            out_c = conformer_mixer(hc, od_w_in[i], od_b_in[i], od_dw_w[i], od_dw_b[i], od_ln_g[i], od_ln_b[i],
                                    od_w_out[i], od_b_out[i]) if ctx_out else None
        xl = xl + gate_l * rms_norm(out_l, post_norm_g[layer])
        if ctx_out:
            xc = xc + gate_c * rms_norm(out_c, post_norm_g[layer])
    return xl
```

```python
import contextlib
import functools
import numpy as np
import concourse.bass as bass
import concourse.mybir as mybir
from concourse.bass_utils import run_bass_kernel_spmd

F32 = mybir.dt.float32
BF16 = mybir.dt.bfloat16
AF = mybir.ActivationFunctionType
ALU = mybir.AluOpType
AX = mybir.AxisListType

ENGS = ["pe", "act", "dve", "pool", "sp"]
NDMA_Q = {"sp": 20, "pool": 40}
EPOCH = 20000

D = 1024
T = 4096
CTX = 256
NORM_EPS = 1e-6


class Sched:
    def __init__(self, nc):
        self.nc = nc
        self.ops = []
        self.last_w = {}
        self.readers = {}
        self.cur = {e: {} for e in ENGS}
        self.pos = {e: 0 for e in ENGS}
        self.ndma = {"sp": 0, "pool": 0}
        self.dma_ops = {"sp": [], "pool": []}
        self.seen = set()
        self.inherit = {}

    def retire(self, names):
        names = set(names)
        for key in list(self.seen):
            if key[0] in names:
                cand = list(self.readers.get(key, ()))
                w = self.last_w.get(key)
                if w is not None:
                    cand.append(w)
                for c in cand:
                    o = self.ops[c]
                    old = self.inherit.get(o["src"])
                    if old is None or self.ops[old]["p"] < o["p"]:
                        self.inherit[o["src"]] = c
                self.seen.discard(key)
                self.readers.pop(key, None)
                self.last_w.pop(key, None)

    def _touch(self, key):
        if key not in self.seen:
            self.seen.add(key)
            if self.inherit:
                self.readers[key] = list(self.inherit.values())

    def add(self, eng, fn, reads=(), writes=(), dma=False):
        idx = len(self.ops)
        deps = []
        for r in reads:
            self._touch(r)
            w = self.last_w.get(r)
            if w is not None:
                deps.append((w, True))
        for k in writes:
            self._touch(k)
            w = self.last_w.get(k)
            if w is not None:
                deps.append((w, False))
            for rd in self.readers.get(k, ()):
                deps.append((rd, False))
        if dma:
            nd = self.ndma[eng]
            ns = NDMA_Q[eng]
            slot = nd % ns
            cnt = nd // ns + 1
            if nd >= ns:
                deps.append((self.dma_ops[eng][nd - ns], True))
            src = ("d", eng, slot)
            p = cnt
            self.ndma[eng] += 1
        else:
            self.pos[eng] += 1
            src = eng
            p = self.pos[eng]
        cur = self.cur[eng]
        waits = []
        for d, raw in deps:
            o = self.ops[d]
            s, v = o["src"], o["p"]
            if s == eng and not dma and not o["dma"]:
                if eng == "pe":
                    continue
            if cur.get(s, 0) >= v:
                continue
            waits.append((s, v))
            o["needed"] = True
            for ks, kv in o["vc"].items():
                if cur.get(ks, 0) < kv:
                    cur[ks] = kv
            cur[s] = v
        vc = dict(cur)
        vc[src] = p
        op = dict(eng=eng, fn=fn, waits=waits, src=src, p=p, dma=dma, vc=vc, needed=False, deps=deps)
        self.ops.append(op)
        if dma:
            self.dma_ops[eng].append(idx)
        for r in reads:
            self.readers.setdefault(r, []).append(idx)
        for k in writes:
            self.last_w[k] = idx
            self.readers[k] = []
        return idx

    def emit(self):
        nc = self.nc
        rank = {e: {} for e in ENGS}
        cnt = {e: 0 for e in ENGS}
        for o in self.ops:
            if not o["dma"] and o["needed"]:
                cnt[o["eng"]] += 1
                rank[o["eng"]][o["p"]] = cnt[o["eng"]]
        nsem = {e: max(1, (cnt[e] + EPOCH - 1) // EPOCH) for e in ENGS}
        with contextlib.ExitStack() as st:
            sems = {e: [st.enter_context(nc.semaphore(f"s_{e}{i}")) for i in range(nsem[e])] for e in ENGS}
            dsem = {q: [st.enter_context(nc.semaphore(f"s_d{q}{i}")) for i in range(NDMA_Q[q])] for q in NDMA_Q}
            block = st.enter_context(nc.Block())
            engobj = {"pe": nc.tensor, "act": nc.scalar, "dve": nc.vector, "pool": nc.gpsimd, "sp": nc.sync}
            per = {e: [o for o in self.ops if o["eng"] == e] for e in ENGS}

            def run(e):
                eo = engobj[e]
                for o in per[e]:
                    for s, v in o["waits"]:
                        if isinstance(s, tuple):
                            eo.wait_ge(dsem[s[1]][s[2]], 16 * v)
                        else:
                            r = rank[s][v] - 1
                            eo.wait_ge(sems[s][r // EPOCH], r % EPOCH + 1)
                    ins = o["fn"]()
                    if o["dma"]:
                        ins.then_inc(dsem[o["src"][1]][o["src"][2]], 16)
                    elif o["needed"]:
                        r = rank[e][o["p"]] - 1
                        ins.then_inc(sems[e][r // EPOCH], 1)
                if e == "sp":
                    for q in NDMA_Q:
                        for sl in range(min(self.ndma[q], NDMA_Q[q])):
                            last = (self.ndma[q] - 1 - sl) // NDMA_Q[q] + 1
                            eo.wait_ge(dsem[q][sl], 16 * last)

            @block.tensor
            def _(x):
                run("pe")

            @block.scalar
            def _(x):
                run("act")

            @block.vector
            def _(x):
                run("dve")

            @block.gpsimd
            def _(x):
                run("pool")

            @block.sync
            def _(x):
                run("sp")
        return cnt


class Interleaver:
    def __init__(self, gens, width, slotted=False):
        self.it = iter(gens)
        self.width = width
        self.slotted = slotted
        self.active = []
        self.free = list(range(width))

    def step(self):
        while len(self.active) < self.width:
            try:
                g = next(self.it)
            except StopIteration:
                break
            if self.slotted:
                sl = self.free.pop(0)
                self.active.append((g(sl), sl))
            else:
                self.active.append((g, None))
        if not self.active:
            return False
        for item in list(self.active):
            try:
                next(item[0])
            except StopIteration:
                self.active.remove(item)
                if self.slotted:
                    self.free.append(item[1])
        return True


def run_interleaved(gens, width, slotted=False):
    il = Interleaver(gens, width, slotted)
    while il.step():
        pass


class Tl:
    def __init__(self, name, ap):
        self.name = name
        self.ap = ap

    def k(self, sub=None):
        return (self.name, sub)

    def __getitem__(self, idx):
        return self.ap[idx]


ARENA_COLS = 52000


class K:
    def __init__(self, nc, ext_in=(), ext_out=()):
        self.nc = nc
        self.S = Sched(nc)
        self.st = contextlib.ExitStack()
        self.big = self.st.enter_context(nc.sbuf_tensor("arena", [128, ARENA_COLS], F32))
        self.off = 0
        self.live = []
        self.ext_in = set(ext_in)
        self.ext_out = set(ext_out)
        self.drams = {}
        self.uid = 0
        self.PS = [Tl(f"ps{i}", self.st.enter_context(nc.psum_tensor(f"ps{i}", [128, 512], F32))[:, :]) for i in range(8)]

    def alloc(self, name, cols, dt=F32):
        size = 4 if dt == F32 else 2
        n32 = (cols * size + 3) // 4
        n32 = (n32 + 7) // 8 * 8
        assert self.off + n32 <= ARENA_COLS, f"SBUF arena overflow at {name}: {self.off}+{n32}"
        ap = self.big[:, self.off:self.off + n32]
        if dt != F32:
            ap = ap.bitcast(dt)[:, :cols]
        else:
            ap = ap[:, :cols]
        self.off += n32
        self.uid += 1
        t = Tl(f"{name}#{self.uid}", ap)
        self.live.append(t.name)
        return t

    def mark(self):
        return (self.off, len(self.live))

    def release(self, mark):
        off, n = mark
        self.S.retire(self.live[n:])
        del self.live[n:]
        self.off = off

    def dram(self, name, shape, dt):
        if name in self.drams:
            return self.drams[name]
        if name in self.ext_in:
            t = self.nc.dram_tensor(name, list(shape), dt, kind="ExternalInput")
        elif name in self.ext_out:
            t = self.nc.dram_tensor(name, list(shape), dt, kind="ExternalOutput")
        else:
            t = self.nc.dram_tensor(name, list(shape), dt)
        self.drams[name] = t.ap()
        return self.drams[name]

    def _op(self, eng, name, a, kw):
        r = kw.pop("r", ())
        w = kw.pop("w", ())
        w = list(w) + [key for key in r if key[0].startswith("ps") and key not in w]
        obj = {"pe": self.nc.tensor, "act": self.nc.scalar, "dve": self.nc.vector, "pool": self.nc.gpsimd}[eng]
        return self.S.add(eng, functools.partial(getattr(obj, name), *a, **kw), r, w)

    def pe(self, name, *a, **kw):
        return self._op("pe", name, a, kw)

    def act(self, name, *a, **kw):
        return self._op("act", name, a, kw)

    def dve(self, name, *a, **kw):
        return self._op("dve", name, a, kw)

    def pool(self, name, *a, **kw):
        return self._op("pool", name, a, kw)

    def any(self, eng, name, *a, **kw):
        return self._op(eng, name, a, kw)

    def dma(self, out, in_, r=(), w=(), q="sp"):
        nc = self.nc
        if q == "sp":
            return self.S.add("sp", functools.partial(nc.sync.dma_start, out=out, in_=in_), r, w, dma=True)
        return self.S.add("pool", functools.partial(nc.gpsimd.dma_start, out=out, in_=in_), r, w, dma=True)

    def veng(self, eng):
        return {"dve": self.nc.vector, "pool": self.nc.gpsimd}[eng]


def load_consts(k):
    nc = k.nc
    identd = k.dram("c_ident", [128, 128], F32)
    c = {}
    c["identf"] = k.alloc("identf", 128, F32)
    c["identb"] = k.alloc("identb", 128, BF16)
    k.dma(c["identf"][:, :], identd, r=[("dram", "c_ident")], w=[c["identf"].k()])
    k.dma(c["identb"][:, :], identd, r=[("dram", "c_ident")], w=[c["identb"].k()], q="pool")
    return c


def prep_rows(k, c, xrows_ap, xkey, Amod, Bmod, hT, col0, nrows, tmp, ps_t, idx):
    nc = k.nc
    xt, junk, ss, t1, hb = tmp
    n = nrows
    k.dma(xt[:n, :], xrows_ap, r=[xkey], w=[xt.k()])
    k.act("activation", out=junk[:n, :], in_=xt[:n, :], func=AF.Square, accum_out=ss[:n, 0:1],
          r=[xt.k()], w=[junk.k(), ss.k()])
    k.act("activation", out=ss[:n, 1:2], in_=ss[:n, 0:1], func=AF.Sqrt, scale=1.0 / D, bias=NORM_EPS,
          r=[ss.k()], w=[ss.k()])
    k.dve("reciprocal", out=ss[:n, 2:3], in_=ss[:n, 1:2], r=[ss.k()], w=[ss.k()])
    k.dve("scalar_tensor_tensor", out=t1[:n, :], in0=xt[:n, :], scalar=ss[:n, 2:3], in1=Amod[:n, :],
                                                 op0=ALU.mult, op1=ALU.mult,
          r=[xt.k(), ss.k(), Amod.k()], w=[t1.k()])
    k.pool("tensor_tensor", out=hb[:n, :], in0=t1[:n, :], in1=Bmod[:n, :], op=ALU.add,
           r=[t1.k(), Bmod.k()], w=[hb.k()])
    pT = ps_t.ap.bitcast(BF16)
    for kc in range(8):
        k.pe("transpose", pT[:, kc * 128:kc * 128 + n], hb[:n, kc * 128:(kc + 1) * 128], c["identb"][:n, :n],
             r=[hb.k(), c["identb"].k()], w=[ps_t.k()])
    src = pT.rearrange("p (a b) -> p a b", a=8)[:, :, :n]
    dst = hT[:, :].rearrange("p (a b) -> p a b", a=8)[:, :, col0:col0 + n]
    if idx % 2 == 0:
        k.act("copy", out=dst, in_=src, r=[ps_t.k()], w=[hT.k()])
    else:
        k.dve("tensor_copy", out=dst, in_=src, r=[ps_t.k()], w=[hT.k()])


def prep_rows_gen(k, c, xrows_ap, xkey, Amod, Bmod, hT, col0, nrows, tmp, ps_t, idx):
    nc = k.nc
    xt, junk, ss, t1, hb = tmp
    n = nrows
    k.dma(xt[:n, :], xrows_ap, r=[xkey], w=[xt.k()])
    yield
    k.act("activation", out=junk[:n, :], in_=xt[:n, :], func=AF.Square, accum_out=ss[:n, 0:1],
          r=[xt.k()], w=[junk.k(), ss.k()])
    yield
    k.act("activation", out=ss[:n, 1:2], in_=ss[:n, 0:1], func=AF.Sqrt, scale=1.0 / D, bias=NORM_EPS,
          r=[ss.k()], w=[ss.k()])
    yield
    k.dve("reciprocal", out=ss[:n, 2:3], in_=ss[:n, 1:2], r=[ss.k()], w=[ss.k()])
    yield
    k.dve("scalar_tensor_tensor", out=t1[:n, :], in0=xt[:n, :], scalar=ss[:n, 2:3], in1=Amod[:n, :],
                                                 op0=ALU.mult, op1=ALU.mult,
          r=[xt.k(), ss.k(), Amod.k()], w=[t1.k()])
    yield
    k.pool("tensor_tensor", out=hb[:n, :], in0=t1[:n, :], in1=Bmod[:n, :], op=ALU.add,
           r=[t1.k(), Bmod.k()], w=[hb.k()])
    yield
    pT = ps_t.ap.bitcast(BF16)
    for kc in range(8):
        k.pe("transpose", pT[:, kc * 128:kc * 128 + n], hb[:n, kc * 128:(kc + 1) * 128], c["identb"][:n, :n],
             r=[hb.k(), c["identb"].k()], w=[ps_t.k()])
    yield
    src = pT.rearrange("p (a b) -> p a b", a=8)[:, :, :n]
    dst = hT[:, :].rearrange("p (a b) -> p a b", a=8)[:, :, col0:col0 + n]
    if idx % 2 == 0:
        k.act("copy", out=dst, in_=src, r=[ps_t.k()], w=[hT.k()])
        yield
    else:
        k.dve("tensor_copy", out=dst, in_=src, r=[ps_t.k()], w=[hT.k()])
        yield


def alloc_prep_tmp(k, tag):
    xt = k.alloc(f"xt{tag}", 1024, F32)
    junk = k.alloc(f"junk{tag}", 1024, BF16)
    ss = k.alloc(f"ss{tag}", 8, F32)
    t1 = k.alloc(f"t1{tag}", 1024, F32)
    hb = k.alloc(f"hb{tag}", 1024, BF16)
    return (xt, junk, ss, t1, hb)


PV_BIN = 0
PV_DWB = 24
PV_LNG = 32
PV_LNB = 40
PV_DWW = 48
PV_C5W = 48 + 248
PV_N = PV_C5W + 60
U1PAD = 15
U1W = T + 2 * U1PAD


def l1_pass_a(k, c):
    nc = k.nc
    m0 = k.mark()
    x1 = k.dram("x1", [T, D], F32)
    modd = k.dram("mod", [8, 128, D], F32)
    w_in_d = k.dram("od_w_in_r", [128, 8 * 3072], F32)
    pvd = k.dram("pv", [128, PV_N], F32)
    U1 = k.dram("U1", [8, 128, U1W], BF16)
    ZG1 = k.dram("ZG1", [8, 128, T], BF16)

    w_in = k.alloc("w_in1", 8 * 3072, BF16)
    for kc in range(8):
        k.dma(w_in[:, kc * 3072:(kc + 1) * 3072], w_in_d[:, kc * 3072:(kc + 1) * 3072], r=[("dram", "od_w_in_r")], w=[w_in.k(kc)], q="pool")
    pv = k.alloc("pv", PV_N, F32)
    k.dma(pv[:, :], pvd, r=[("dram", "pv")], w=[pv.k()])
    A1 = k.alloc("A1", D, F32)
    B1 = k.alloc("B1", D, F32)
    k.dma(A1[:, :], modd[5], r=[("dram", "mod")], w=[A1.k()])
    k.dma(B1[:, :], modd[6], r=[("dram", "mod")], w=[B1.k()])
    zt = k.alloc("zt", 16, BF16)
    k.dve("memset", zt[:, :], 0.0, w=[zt.k()])
    for j in range(8):
        k.dma(U1[j][:, 0:U1PAD], zt[:, 0:U1PAD], r=[zt.k()], w=[("dram", "U1", j, "padl")])
        k.dma(U1[j][:, U1PAD + T:U1W], zt[:, 0:U1PAD], r=[zt.k()], w=[("dram", "U1", j, "padr")])
    tmps = [alloc_prep_tmp(k, i) for i in range(2)]
    hTs = [k.alloc(f"hT{i}", 8 * 512, BF16) for i in range(2)]
    sg = [k.alloc(f"sg{i}", 512, F32) for i in range(2)]
    ub = [k.alloc(f"ub{i}", 512, BF16) for i in range(3)]
    zb = [k.alloc(f"zb{i}", 512, BF16) for i in range(3)]
    ps_t = [k.PS[0], k.PS[1]]
    ps_mm = [k.PS[2], k.PS[3], k.PS[4], k.PS[5], k.PS[6], k.PS[7]]
    nmt = T // 512

    def prep_m(m):
        hT = hTs[m % 2]

        def sub(s, slot):
            r0 = m * 512 + s * 128
            return prep_rows_gen(k, c, x1[r0:r0 + 128, :], ("dram", "x1", r0 // 128), A1, B1, hT, s * 128, 128, tmps[slot], ps_t[slot], 4 * m + s)
        il = Interleaver([functools.partial(sub, s) for s in range(4)], 2, slotted=True)
        while il.step():
            yield

    def comp_m(m):
        hT = hTs[m % 2]
        hT3 = hT[:, :].rearrange("p (a b) -> p a b", a=8)

        def mmgroup(pst, col0):
            for kc in range(8):
                k.pe("matmul", pst[:, :], lhsT=w_in[:, kc * 3072 + col0:kc * 3072 + col0 + 128], rhs=hT3[:, kc, :],
                     start=(kc == 0), stop=(kc == 7), r=[w_in.k(kc), hT.k()], w=[pst.k()])

        for j in range(8):
            pa = ps_mm[(2 * j) % 4]
            pg = ps_mm[(2 * j + 1) % 4]
            pz = ps_mm[4 + j % 2]
            mmgroup(pa, j * 128)
            yield
            mmgroup(pg, 1024 + j * 128)
            yield
            mmgroup(pz, 2048 + j * 128)
            yield
            sgt = sg[j % 2]
            ubt = ub[j % 3]
            zbt = zb[j % 3]
            k.act("activation", out=sgt[:, :], in_=pg[:, :], func=AF.Sigmoid, bias=pv[:, PV_BIN + 8 + j:PV_BIN + 9 + j], scale=1.0,
                  r=[pg.k(), pv.k()], w=[sgt.k()])
            yield
            k.dve("scalar_tensor_tensor", out=ubt[:, :], in0=pa[:, :], scalar=pv[:, PV_BIN + j:PV_BIN + j + 1], in1=sgt[:, :],
                  op0=ALU.add, op1=ALU.mult, r=[pa.k(), sgt.k(), pv.k()], w=[ubt.k()])
            k.dma(U1[j][:, U1PAD + m * 512:U1PAD + (m + 1) * 512], ubt[:, :], r=[ubt.k()], w=[("dram", "U1", j, m)])
            yield
            k.act("activation", out=zbt[:, :], in_=pz[:, :], func=AF.Silu, bias=pv[:, PV_BIN + 16 + j:PV_BIN + 17 + j], scale=1.0,
                  r=[pz.k(), pv.k()], w=[zbt.k()])
            k.dma(ZG1[j][:, m * 512:(m + 1) * 512], zbt[:, :], r=[zbt.k()], w=[("dram", "ZG1", j, m)])
            yield

    for _ in prep_m(0):
        pass
    for m in range(nmt):
        gl = [comp_m(m)] + ([prep_m(m + 1)] if m + 1 < nmt else [])
        run_interleaved(gl, 2)
    k.release(m0)


def l1_pass_b(k, c):
    nc = k.nc
    m0 = k.mark()
    x1 = k.dram("x1", [T, D], F32)
    y = k.dram("y", [T, D], F32)
    modd = k.dram("mod", [8, 128, D], F32)
    w_out_d = k.dram("od_w_out_r", [128, 8 * 1024], F32)
    pvd = k.dram("pv", [128, PV_N], F32)
    U1 = k.dram("U1", [8, 128, U1W], BF16)
    ZG1 = k.dram("ZG1", [8, 128, T], BF16)

    w_out = k.alloc("w_out1", 8 * 1024, BF16)
    k.dma(w_out[:, :], w_out_d, r=[("dram", "od_w_out_r")], w=[w_out.k()], q="pool")
    pv = k.alloc("pv", PV_N, F32)
    k.dma(pv[:, :], pvd, r=[("dram", "pv")], w=[pv.k()])
    G1 = k.alloc("G1", D, F32)
    k.dma(G1[:, :], modd[7], r=[("dram", "mod")], w=[G1.k()])
    BOUT = k.alloc("BOUT", D, F32)
    k.dma(BOUT[:, :], row_bc(k, "b_out"), r=[("dram", "rows")], w=[BOUT.k()])
    onesm = k.alloc("onesm", 128, BF16)
    k.dve("memset", onesm[:, :], 1.0 / 1024.0, w=[onesm.k()])
    DG = k.alloc("DG", 8 * 31 * 128, BF16)
    for j in range(8):
        for t in range(31):
            i = j * 31 + t
            eng = "dve" if i % 2 == 0 else "pool"
            ve = k.veng(eng)
            k.any(eng, "tensor_scalar", out=DG[:, i * 128:(i + 1) * 128], in0=c["identf"][:, :],
                                                           scalar1=pv[:, PV_DWW + i:PV_DWW + i + 1], scalar2=None, op0=ALU.mult,
                  r=[c["identf"].k(), pv.k()], w=[DG.k(j)])
    PW = 512 + 2 * U1PAD
    PRE = [k.alloc(f"PRE{i}", 8 * PW, BF16) for i in range(2)]
    ZGt = [k.alloc(f"ZGt{i}", 8 * 512, BF16) for i in range(2)]
    UC = k.alloc("UC", 8 * 512, F32)
    UCb = k.alloc("UCb", 8 * 512, BF16)
    SQ = k.alloc("SQ", 8 * 512, BF16)
    MEAN = k.alloc("MEAN", 512, F32)
    M2 = k.alloc("M2", 512, F32)
    RSTD = k.alloc("RSTD", 512, F32)
    TA = [k.alloc(f"TA{i}", 512, F32) for i in range(4)]
    TB = TA
    YS = [k.alloc(f"YS{i}", 512, BF16) for i in range(4)]
    YT = k.alloc("YT", 8 * 512, BF16)
    O = [k.alloc(f"O{i}", D, F32) for i in range(2)]
    junk = k.alloc("junkb", D, BF16)
    ss = [k.alloc(f"ssb{i}", 8, F32) for i in range(2)]
    XR = [k.alloc(f"XR{i}", D, F32) for i in range(2)]
    T2 = O
    OUT = XR
    ps_cv = [k.PS[0], k.PS[1]]
    ps_mean, ps_msq = k.PS[2], k.PS[3]
    ps_o = [(k.PS[4], k.PS[5]), (k.PS[6], k.PS[7])]
    nmt = T // 512
    oc = 0
    for m in range(nmt):
        pre = PRE[m % 2]
        zgt = ZGt[m % 2]
        for j in range(8):
            k.dma(pre[:, j * PW:(j + 1) * PW], U1[j][:, m * 512:m * 512 + PW],
                  r=[("dram", "U1", j, mm) for mm in range(max(0, m - 1), min(nmt, m + 2))] + [("dram", "U1", j, "padl"), ("dram", "U1", j, "padr")],
                  w=[pre.k(j)])
            k.dma(zgt[:, j * 512:(j + 1) * 512], ZG1[j][:, m * 512:(m + 1) * 512], r=[("dram", "ZG1", j, m)], w=[zgt.k(j)])
        for j in range(8):
            pcv = ps_cv[j % 2]
            for t in range(31):
                i = j * 31 + t
                k.pe("matmul", pcv[:, :], lhsT=DG[:, i * 128:(i + 1) * 128],
                                                                    rhs=pre[:, j * PW + t:j * PW + t + 512], start=(t == 0), stop=(t == 30),
                     r=[DG.k(j), pre.k(j)], w=[pcv.k()])
            k.act("activation", out=UC[:, j * 512:(j + 1) * 512], in_=pcv[:, :], func=AF.Identity,
                                                           bias=pv[:, PV_DWB + j:PV_DWB + j + 1], scale=1.0,
                  r=[pcv.k(), pv.k()], w=[UC.k(j)])
            k.act("activation", out=SQ[:, j * 512:(j + 1) * 512], in_=pcv[:, :], func=AF.Square,
                                                           bias=pv[:, PV_DWB + j:PV_DWB + j + 1], scale=1.0,
                  r=[pcv.k(), pv.k()], w=[SQ.k(j)])
            k.pool("tensor_copy", out=UCb[:, j * 512:(j + 1) * 512], in_=UC[:, j * 512:(j + 1) * 512],
                   r=[UC.k(j)], w=[UCb.k(j)])
        for j in range(8):
            k.pe("matmul", ps_mean[:, :], lhsT=onesm[:, :], rhs=UCb[:, j * 512:(j + 1) * 512], start=(j == 0), stop=(j == 7),
                 r=[onesm.k(), UCb.k(j)], w=[ps_mean.k()])
        for j in range(8):
            k.pe("matmul", ps_msq[:, :], lhsT=onesm[:, :], rhs=SQ[:, j * 512:(j + 1) * 512], start=(j == 0), stop=(j == 7),
                 r=[onesm.k(), SQ.k(j)], w=[ps_msq.k()])
        k.act("copy", out=MEAN[:, :], in_=ps_mean[:, :], r=[ps_mean.k()], w=[MEAN.k()])
        k.dve("tensor_tensor", out=M2[:, :], in0=MEAN[:, :], in1=MEAN[:, :], op=ALU.mult, r=[MEAN.k()], w=[M2.k()])
        k.dve("tensor_tensor", out=M2[:, :], in0=ps_msq[:, :], in1=M2[:, :], op=ALU.subtract, r=[ps_msq.k(), M2.k()], w=[M2.k()])
        k.act("activation", out=RSTD[:, :], in_=M2[:, :], func=AF.Sqrt, bias=1e-5, scale=1.0, r=[M2.k()], w=[RSTD.k()])
        k.dve("reciprocal", out=RSTD[:, :], in_=RSTD[:, :], r=[RSTD.k()], w=[RSTD.k()])
        def ln_chunk(j, sl):
            ta, tb, ys = TA[sl], TB[sl], YS[sl]
            k.dve("tensor_tensor", out=ta[:, :], in0=UC[:, j * 512:(j + 1) * 512], in1=MEAN[:, :], op=ALU.subtract,
                  r=[UC.k(j), MEAN.k()], w=[ta.k()])
            yield
            k.pool("tensor_tensor", out=tb[:, :], in0=ta[:, :], in1=RSTD[:, :], op=ALU.mult,
                   r=[ta.k(), RSTD.k()], w=[tb.k()])
            yield
            k.act("activation", out=ys[:, :], in_=tb[:, :], func=AF.Silu,
                  scale=pv[:, PV_LNG + j:PV_LNG + j + 1], bias=pv[:, PV_LNB + j:PV_LNB + j + 1],
                  r=[tb.k(), pv.k()], w=[ys.k()])
            yield
            k.dve("tensor_tensor", out=YT[:, j * 512:(j + 1) * 512], in0=ys[:, :], in1=zgt[:, j * 512:(j + 1) * 512], op=ALU.mult,
                  r=[ys.k(), zgt.k(j)], w=[YT.k(j)])
            yield
        run_interleaved([functools.partial(ln_chunk, j) for j in range(8)], 4, slotted=True)
        for s in range(4):
            r0 = m * 512 + s * 128
            po = ps_o[oc % 2]
            o, sst, xr, t2, out = O[oc % 2], ss[oc % 2], XR[oc % 2], T2[oc % 2], OUT[oc % 2]
            oc += 1
            k.dma(xr[:, :], x1[r0:r0 + 128, :], r=[("dram", "x1", r0 // 128)], w=[xr.k()])
            for nh in range(2):
                for j in range(8):
                    k.pe("matmul", po[nh][:, :], lhsT=YT[:, j * 512 + s * 128:j * 512 + (s + 1) * 128],
                                                                        rhs=w_out[:, j * 1024 + nh * 512:j * 1024 + (nh + 1) * 512],
                                                                        start=(j == 0), stop=(j == 7),
                         r=[YT.k(j), w_out.k()], w=[po[nh].k()])
            for nh in range(2):
                k.dve("tensor_tensor", out=o[:, nh * 512:(nh + 1) * 512], in0=po[nh][:, :],
                                                                       in1=BOUT[:, nh * 512:(nh + 1) * 512], op=ALU.add,
                      r=[po[nh].k(), BOUT.k()], w=[o.k()])
            post_res(k, o, sst, junk, G1, xr, t2, out, y[r0:r0 + 128, :], ("dram", "y", r0 // 128))
    k.release(m0)


def post_res(k, o, sst, junk, G, xr, t2, out, ydst, ykey):
    nc = k.nc
    k.act("activation", out=junk[:, :], in_=o[:, :], func=AF.Square, accum_out=sst[:, 0:1],
          r=[o.k()], w=[junk.k(), sst.k()])
    k.act("activation", out=sst[:, 1:2], in_=sst[:, 0:1], func=AF.Sqrt, scale=1.0 / D, bias=NORM_EPS,
          r=[sst.k()], w=[sst.k()])
    k.dve("reciprocal", out=sst[:, 2:3], in_=sst[:, 1:2], r=[sst.k()], w=[sst.k()])
    k.dve("scalar_tensor_tensor", out=t2[:, :], in0=o[:, :], scalar=sst[:, 2:3], in1=G[:, :], op0=ALU.mult, op1=ALU.mult,
          r=[o.k(), sst.k(), G.k()], w=[t2.k()])
    k.pool("tensor_tensor", out=out[:, :], in0=t2[:, :], in1=xr[:, :], op=ALU.add, r=[t2.k(), xr.k()], w=[out.k()])
    k.dma(ydst, out[:, :], r=[out.k()], w=[ykey])


def post_res_gen(k, o, sst, junk, G, xr, t2, out, ydst, ykey):
    nc = k.nc
    k.act("activation", out=junk[:, :], in_=o[:, :], func=AF.Square, accum_out=sst[:, 0:1],
          r=[o.k()], w=[junk.k(), sst.k()])
    yield
    k.act("activation", out=sst[:, 1:2], in_=sst[:, 0:1], func=AF.Sqrt, scale=1.0 / D, bias=NORM_EPS,
          r=[sst.k()], w=[sst.k()])
    yield
    k.dve("reciprocal", out=sst[:, 2:3], in_=sst[:, 1:2], r=[sst.k()], w=[sst.k()])
    yield
    k.dve("scalar_tensor_tensor", out=t2[:, :], in0=o[:, :], scalar=sst[:, 2:3], in1=G[:, :], op0=ALU.mult, op1=ALU.mult,
          r=[o.k(), sst.k(), G.k()], w=[t2.k()])
    yield
    k.pool("tensor_tensor", out=out[:, :], in0=t2[:, :], in1=xr[:, :], op=ALU.add, r=[t2.k(), xr.k()], w=[out.k()])
    yield
    k.dma(ydst, out[:, :], r=[out.k()], w=[ykey])
    yield


def build_program(passes, ext_in, ext_out):
    nc = bass.Bass("TRN2", target_bir_lowering=False)
    k = K(nc, ext_in, ext_out)
    c = load_consts(k)
    for p in passes:
        p(k, c)
    cnt = k.S.emit()
    k.st.close()
    return nc, cnt


ROW_OFF = {}
_o = 0
for _n, _l in [("ada_b0", 3072), ("ada_b1", 3072), ("pre_g0", 1024), ("pre_g1", 1024), ("post_g0", 1024), ("post_g1", 1024),
               ("b_out", 1024), ("gdn_g", 128), ("qn_g", 128), ("kn_g", 128), ("a_log", 8), ("dt_bias", 8)]:
    ROW_OFF[_n] = (_o, _l)
    _o += _l
ROWS_N = _o


def row_bc(k, name, off=0, n=None):
    rows = k.dram("rows", [1, ROWS_N], F32)
    o, l = ROW_OFF[name]
    if n is None:
        n = l
    return rows[0, o + off:o + off + n].partition_broadcast(128)


def p0_mod(k, c):
    nc = k.nc
    m0 = k.mark()
    modd = k.dram("mod", [8, 128, D], F32)
    cvec = k.dram("cvec", [128, 16], F32)
    seld = k.dram("c_sel", [2, 256], F32)
    adaw = k.dram("ada_w_r", [2, 128, 8 * 3072], F32)
    cv = k.alloc("cv", 16, F32)
    k.dma(cv[:, :], cvec, r=[("dram", "cvec")], w=[cv.k()])
    scb = k.alloc("scb", 16, BF16)
    k.act("activation", out=scb[:, :], in_=cv[:, :], func=AF.Silu, r=[cv.k()], w=[scb.k()])
    sel = k.alloc("sel", 256, F32)
    k.dma(sel[0:2, :], seld, r=[("dram", "c_sel")], w=[sel.k()])
    mrow = k.alloc("mrow", 6144, F32)
    aw = [k.alloc(f"aw{l}", 8 * 3072, BF16) for l in range(2)]
    for l in range(2):
        for kc in range(8):
            k.dma(aw[l][:, kc * 3072:(kc + 1) * 3072], adaw[l][:, kc * 3072:(kc + 1) * 3072], r=[("dram", "ada_w_r")], w=[aw[l].k(kc)], q="pool")
    adab = [k.alloc(f"adab{l}", 3072, F32) for l in range(2)]
    preg = [k.alloc(f"preg{l}", 1024, F32) for l in range(2)]
    postg = [k.alloc(f"postg{l}", 1024, F32) for l in range(2)]
    for l in range(2):
        k.dma(adab[l][:, :], row_bc(k, f"ada_b{l}"), r=[("dram", "rows")], w=[adab[l].k()])
        k.dma(preg[l][:, :], row_bc(k, f"pre_g{l}"), r=[("dram", "rows")], w=[preg[l].k()])
        k.dma(postg[l][:, :], row_bc(k, f"post_g{l}"), r=[("dram", "rows")], w=[postg[l].k()])
    for l in range(2):
        for nt in range(6):
            ps = k.PS[nt % 2]
            for kc in range(8):
                k.pe("matmul", ps[0:2, :], lhsT=scb[:, 2 * kc:2 * kc + 2], rhs=aw[l][:, kc * 3072 + nt * 512:kc * 3072 + (nt + 1) * 512],
                     start=(kc == 0), stop=(kc == 7), r=[scb.k(), aw[l].k(kc)], w=[ps.k()])
            k.act("copy", out=mrow[0:2, l * 3072 + nt * 512:l * 3072 + (nt + 1) * 512], in_=ps[0:2, :], r=[ps.k()], w=[mrow.k((l, nt))])
    tmp = [k.alloc(f"mt{i}", 512, F32) for i in range(2)]
    outt = [k.alloc(f"mo{i}", 1024, F32) for i in range(2)]
    plan = [(0, 0, 1, 0), (1, 0, 0, 0), (2, 0, 2, 0), (3, 0, 1, 1), (4, 0, 0, 1), (5, 1, 1, 0), (6, 1, 0, 0), (7, 1, 2, 0)]
    cnt = 0
    for (mi, l, part, si) in plan:
        ot = outt[mi % 2]
        for nh in range(2):
            ps = k.PS[2 + cnt % 2]
            tt = tmp[cnt % 2]
            cnt += 1
            seg = l * 3072 + part * 1024 + nh * 512
            nt = (part * 1024 + nh * 512) // 512
            k.pe("matmul", ps[:, :], lhsT=sel[0:2, si * 128:(si + 1) * 128], rhs=mrow[0:2, seg:seg + 512], start=True, stop=True,
                 r=[sel.k(), mrow.k((l, nt))], w=[ps.k()])
            ab_ = adab[l][:, part * 1024 + nh * 512:part * 1024 + (nh + 1) * 512]
            osl = ot[:, nh * 512:(nh + 1) * 512]
            if part == 0:
                k.dve("tensor_tensor", out=osl, in0=ps[:, :], in1=ab_, op=ALU.add, r=[ps.k(), adab[l].k()], w=[ot.k()])
            elif part == 1:
                k.dve("scalar_tensor_tensor", out=tt[:, :], in0=ps[:, :], scalar=1.0, in1=ab_, op0=ALU.add, op1=ALU.add,
                      r=[ps.k(), adab[l].k()], w=[tt.k()])
                k.pool("tensor_tensor", out=osl, in0=tt[:, :], in1=preg[l][:, nh * 512:(nh + 1) * 512], op=ALU.mult,
                       r=[tt.k(), preg[l].k()], w=[ot.k()])
            else:
                k.dve("tensor_tensor", out=tt[:, :], in0=ps[:, :], in1=ab_, op=ALU.add, r=[ps.k(), adab[l].k()], w=[tt.k()])
                k.pool("tensor_tensor", out=osl, in0=tt[:, :], in1=postg[l][:, nh * 512:(nh + 1) * 512], op=ALU.mult,
                       r=[tt.k(), postg[l].k()], w=[ot.k()])
        k.dma(modd[mi], ot[:, :], r=[ot.k()], w=[("dram", "mod")])
    k.release(m0)


NTILE = 34
TK = T + CTX
GP_CTX0 = 2
GP_LAT0 = 2 + CTX + 2 + 2
GP_W = GP_LAT0 + T + 2
C_QKV, C_ZA, C_AB, C_QB, C_KB, C_VB, C_ZB, EVEN_IN = 0, 1536, 2048, 2064, 2576, 2832, 3088, 3600


P1_FLAGS = {"pads": True, "tm": True, "fm": True, "qk": True, "ab": True, "norm": True, "tr": True}


def p1_inproj(k, c):
    nc = k.nc
    m0 = k.mark()
    x = k.dram("x", [T, D], F32)
    ctxd = k.dram("ctx", [CTX, D], F32)
    modd = k.dram("mod", [8, 128, D], F32)
    w_in_d = k.dram("ev_w_in_r", [128, 8 * EVEN_IN], F32)
    ropecs = k.dram("rope_cs", [32, 128, 128], F32)
    ropesn = k.dram("rope_sn", [32, 128, 128], F32)
    QT = k.dram("QT", [4, 128, T], BF16)
    KT = k.dram("KT", [2, 128, TK], BF16)
    V = k.dram("V", [NTILE, 128, 256], BF16)
    ZBT = k.dram("ZBT", [4, 128, T], BF16)
    ZA = k.dram("ZA", [T, 512], BF16)
    GPRE = k.dram("GPRE", [12, 128, GP_W], BF16)
    AB = k.dram("AB", [NTILE, 128, 16], F32)

    w_in = k.alloc("w_in0", 8 * EVEN_IN, BF16)
    for kc in range(8):
        k.dma(w_in[:, kc * EVEN_IN:(kc + 1) * EVEN_IN], w_in_d[:, kc * EVEN_IN:(kc + 1) * EVEN_IN], r=[("dram", "ev_w_in_r")], w=[w_in.k(kc)], q="pool")
    Am = k.alloc("Am", D, F32)
    Bm = k.alloc("Bm", D, F32)
    G6 = k.alloc("G6", 768, F32)
    for h in range(4):
        k.dma(G6[:, h * 128:(h + 1) * 128], row_bc(k, "qn_g"), r=[("dram", "rows")], w=[G6.k()])
    for h in range(2):
        k.dma(G6[:, 512 + h * 128:512 + (h + 1) * 128], row_bc(k, "kn_g"), r=[("dram", "rows")], w=[G6.k()])
    zt = k.alloc("zt", 16, BF16)
    k.dve("memset", zt[:, :], 0.0, w=[zt.k()])
    for j in range(12 if P1_FLAGS["pads"] else 0):
        k.dma(GPRE[j][:, 0:2], zt[:, 0:2], r=[zt.k()], w=[("dram", "GPRE", j, "p0")])
        k.dma(GPRE[j][:, 2 + CTX:GP_LAT0], zt[:, 0:4], r=[zt.k()], w=[("dram", "GPRE", j, "p1")])
        k.dma(GPRE[j][:, GP_LAT0 + T:GP_W], zt[:, 0:2], r=[zt.k()], w=[("dram", "GPRE", j, "p2")])
    tmps = [alloc_prep_tmp(k, i) for i in range(2)]
    hTs = [k.alloc(f"hT{i}", 8 * 512, BF16) for i in range(2)]
    CS = [k.alloc(f"CS{i}", 128, F32) for i in range(2)]
    SN = [k.alloc(f"SN{i}", 128, F32) for i in range(2)]
    QK = [k.alloc(f"QK{i}", 768, F32) for i in range(2)]
    SQ = [k.alloc(f"SQ{i}", 768, F32) for i in range(2)]
    ssq = [k.alloc(f"ssq{i}", 24, F32) for i in range(2)]
    QN = [k.alloc(f"QN{i}", 768, F32) for i in range(2)]
    R1 = [k.alloc(f"R1{i}", 768, F32) for i in range(2)]
    R2 = [k.alloc(f"R2{i}", 768, F32) for i in range(2)]
    QKb = [k.alloc(f"QKb{i}", 768, BF16) for i in range(2)]
    zas = [k.alloc(f"zas{i}", 512, BF16) for i in range(2)]
    vs = [k.alloc(f"vs{i}", 256, BF16) for i in range(2)]
    abs_ = [k.alloc(f"abs{i}", 16, F32) for i in range(2)]
    QTs = [k.alloc(f"QTs{i}", 4 * 512, BF16) for i in range(2)]
    KTs = [k.alloc(f"KTs{i}", 2 * 512, BF16) for i in range(2)]
    gps = [k.alloc(f"gps{i}", 512, BF16) for i in range(3)]
    zbs = [k.alloc(f"zbs{i}", 512, BF16) for i in range(3)]
    ps_t = [k.PS[0], k.PS[0]]
    ps_za, ps_q, ps_kv = k.PS[2], k.PS[3], k.PS[4]
    PSX = [k.PS[5], k.PS[1]]
    ps_f = [k.PS[6], k.PS[7]]
    cnt = 0
    fcnt = 0
    mts = [("ctx", 0, 256)] + [("lat", m * 512, 512) for m in range(T // 512)]
    mt_list = mts[:P1_FLAGS.get("nmt", 9)]

    def prep_m(mi):
        kind, t0, W = mt_list[mi]
        lat = kind == "lat"
        if mi == 0:
            k.dma(Am[:, :], modd[3], r=[("dram", "mod")], w=[Am.k()])
            k.dma(Bm[:, :], modd[4], r=[("dram", "mod")], w=[Bm.k()])
        elif mi == 1:
            k.dma(Am[:, :], modd[0], r=[("dram", "mod")], w=[Am.k()])
            k.dma(Bm[:, :], modd[1], r=[("dram", "mod")], w=[Bm.k()])
        hT = hTs[mi % 2]
        cnt0 = 0 if mi == 0 else 2 + 4 * (mi - 1)

        def sub(s, slot):
            r0 = t0 + s * 128
            src = x[r0:r0 + 128, :] if lat else ctxd[r0:r0 + 128, :]
            skey = ("dram", "x" if lat else "ctx", r0 // 128)
            return prep_rows_gen(k, c, src, skey, Am, Bm, hT, s * 128, 128, tmps[slot], ps_t[slot], cnt0 + s)
        il = Interleaver([functools.partial(sub, s) for s in range(W // 128)], 1, slotted=True)
        while il.step():
            yield

    def comp_m(mi):
        kind, t0, W = mt_list[mi]
        lat = kind == "lat"
        hT = hTs[mi % 2]
        hT3 = hT[:, :].rearrange("p (a b) -> p a b", a=8)
        nsub = W // 128
        qts, kts = QTs[mi % 2], KTs[mi % 2]
        fcnt = 0 if mi == 0 else 12 + 16 * (mi - 1)
        def tm_group(s, ps_ap, pskey, col0, n):
            for kc in range(8):
                k.pe("matmul", ps_ap, lhsT=hT3[:, kc, s * 128:(s + 1) * 128], rhs=w_in[:, kc * EVEN_IN + col0:kc * EVEN_IN + col0 + n],
                     start=(kc == 0), stop=(kc == 7), r=[hT.k(), w_in.k(kc)], w=[pskey])

        def stage_a(s):
            sl = s % 2
            r0 = t0 + s * 128
            tile_id = (r0 // 128) if lat else (32 + r0 // 128)
            psx = PSX[sl]
            qk = QK[sl]
            if lat:
                tm_group(s, ps_za[:, :], ps_za.k(), C_ZA, 512)
                zst = zas[sl]
                k.act("activation", out=zst[:, :], in_=ps_za[:, :], func=AF.Silu, r=[ps_za.k()], w=[zst.k()])
                k.dma(ZA[r0:r0 + 128, :], zst[:, :], r=[zst.k()], w=[("dram", "ZA", r0 // 128)])
                yield
                tm_group(s, ps_q[:, :], ps_q.k(), C_QB, 512)
                k.act("copy", out=qk[:, 0:512], in_=ps_q[:, :], r=[ps_q.k()], w=[qk.k()])
                yield
            tm_group(s, psx[:, 496:512], psx.k(), C_AB, 16)
            abst = abs_[sl]
            k.dve("tensor_copy", out=abst[:, :], in_=psx[:, 496:512], r=[psx.k()], w=[abst.k()])
            k.dma(AB[tile_id], abst[:, :], r=[abst.k()], w=[("dram", "AB", tile_id)])
            yield
            tm_group(s, ps_kv[:, :], ps_kv.k(), C_KB, 512)
            vst = vs[sl]
            k.act("copy", out=vst[:, :], in_=ps_kv[:, 256:512], r=[ps_kv.k()], w=[vst.k()])
            k.dma(V[tile_id], vst[:, :], r=[vst.k()], w=[("dram", "V", tile_id)])
            yield
            k.dve("tensor_copy", out=qk[:, 512:768], in_=ps_kv[:, 0:256], r=[ps_kv.k()], w=[qk.k()])
            yield

        def stage_b(s):
            sl = s % 2
            r0 = t0 + s * 128
            psx = PSX[sl]
            pxT = psx.ap.bitcast(BF16)
            qk, ssq_t, qkb, sq_, qn_, r1_, r2_ = QK[sl], ssq[sl], QKb[sl], SQ[sl], QN[sl], R1[sl], R2[sl]
            c0 = 0 if lat else 512
            nh = 6 if lat else 2
            h0 = 0 if lat else 4
            k.pool("tensor_tensor", out=sq_[:, c0:768], in0=qk[:, c0:768], in1=qk[:, c0:768], op=ALU.mult, r=[qk.k()], w=[sq_.k()])
            yield
            k.dve("tensor_reduce", out=ssq_t[:, h0:6], in_=sq_[:, c0:768].rearrange("p (h d) -> p h d", d=128), axis=AX.X, op=ALU.add,
                  r=[sq_.k()], w=[ssq_t.k()])
            yield
            k.act("activation", out=ssq_t[:, 8 + h0:14], in_=ssq_t[:, h0:6], func=AF.Sqrt, scale=1.0 / 128, bias=NORM_EPS,
                  r=[ssq_t.k()], w=[ssq_t.k()])
            yield
            k.dve("reciprocal", out=ssq_t[:, 16 + h0:22], in_=ssq_t[:, 8 + h0:14], r=[ssq_t.k()], w=[ssq_t.k()])
            yield
            k.dve("tensor_tensor", out=qn_[:, c0:768].rearrange("p (h d) -> p h d", d=128), in0=qk[:, c0:768].rearrange("p (h d) -> p h d", d=128),
                  in1=ssq_t[:, 16 + h0:22].unsqueeze(2).to_broadcast([128, nh, 128]), op=ALU.mult, r=[qk.k(), ssq_t.k()], w=[qn_.k()])
            yield
            if not lat:
                k.pool("tensor_tensor", out=qkb[:, c0:768], in0=qn_[:, c0:768], in1=G6[:, c0:768], op=ALU.mult, r=[qn_.k(), G6.k()], w=[qkb.k()])
                yield
            else:
                cs, sn = CS[sl], SN[sl]
                k.dma(cs[:, :], ropecs[r0 // 128], r=[("dram", "rope_cs")], w=[cs.k()])
                k.dma(sn[:, :], ropesn[r0 // 128], r=[("dram", "rope_sn")], w=[sn.k()])
                k.pool("tensor_tensor", out=qn_[:, :], in0=qn_[:, :], in1=G6[:, :], op=ALU.mult, r=[qn_.k(), G6.k()], w=[qn_.k()])
                yield
                k.dve("tensor_tensor", out=r1_[:, :].rearrange("p (h d) -> p h d", d=128), in0=qn_[:, :].rearrange("p (h d) -> p h d", d=128),
                      in1=cs[:, :].unsqueeze(1).to_broadcast([128, 6, 128]), op=ALU.mult, r=[qn_.k(), cs.k()], w=[r1_.k()])
                yield
                qn5 = qn_[:, :].rearrange("p (h a b e) -> p h a b e", h=6, a=2, b=2)
                r25 = r2_[:, :].rearrange("p (h a b e) -> p h a b e", h=6, a=2, b=2)
                sn4 = sn[:, :].rearrange("p (a b e) -> p a b e", a=2, b=2)
                for bsel in range(2):
                    k.pool("tensor_tensor", out=r25[:, :, :, bsel, :], in0=qn5[:, :, :, 1 - bsel, :],
                           in1=sn4[:, :, bsel, :].unsqueeze(1).to_broadcast([128, 6, 2, 32]), op=ALU.mult, r=[qn_.k(), sn.k()], w=[r2_.k()])
                    yield
                k.dve("tensor_tensor", out=qkb[:, :], in0=r1_[:, :], in1=r2_[:, :], op=ALU.add, r=[r1_.k(), r2_.k()], w=[qkb.k()])
                yield
            for h in range(h0, 6):
                k.pe("transpose", pxT[:, h * 128:(h + 1) * 128], qkb[:, h * 128:(h + 1) * 128], c["identb"][:, :],
                     r=[qkb.k(), c["identb"].k()], w=[psx.k()])
            yield
            if lat:
                k.act("copy", out=qts[:, :].rearrange("p (h t) -> p h t", h=4)[:, :, s * 128:(s + 1) * 128],
                      in_=pxT[:, 0:512].rearrange("p (h t) -> p h t", h=4), r=[psx.k()], w=[qts.k()])
                yield
            k.dve("tensor_copy", out=kts[:, :].rearrange("p (h t) -> p h t", h=2)[:, :, s * 128:(s + 1) * 128],
                  in_=pxT[:, 512:768].rearrange("p (h t) -> p h t", h=2), r=[psx.k()], w=[kts.k()])
            yield

        for _ in stage_a(0):
            yield
        for s in range(nsub):
            gl = [stage_b(s)] + ([stage_a(s + 1)] if s + 1 < nsub else [])
            il = Interleaver(gl, 2)
            while il.step():
                yield
        kcol0 = t0 if lat else T + t0
        for h in range(2):
            k.dma(KT[h][:, kcol0:kcol0 + W], kts[:, h * 512:h * 512 + W], r=[kts.k()], w=[("dram", "KT", h, mi)])
        if lat:
            for h in range(4):
                k.dma(QT[h][:, t0:t0 + W], qts[:, h * 512:h * 512 + W], r=[qts.k()], w=[("dram", "QT", h, mi)])
        gcol0 = (GP_LAT0 + t0) if lat else (GP_CTX0 + t0)
        for j in range((12 + (4 if lat else 0)) if P1_FLAGS["fm"] else 0):
            yield
            pf = ps_f[fcnt % 2]
            col0 = j * 128 if j < 12 else C_ZB + (j - 12) * 128
            for kc in range(8):
                k.pe("matmul", pf[:, 0:W], lhsT=w_in[:, kc * EVEN_IN + col0:kc * EVEN_IN + col0 + 128], rhs=hT3[:, kc, 0:W],
                     start=(kc == 0), stop=(kc == 7), r=[hT.k(), w_in.k(kc)], w=[pf.k()])
            yield
            if j < 12:
                g = gps[fcnt % 3]
                if fcnt % 2 == 0:
                    k.dve("tensor_copy", out=g[:, 0:W], in_=pf[:, 0:W], r=[pf.k()], w=[g.k()])
                else:
                    k.act("copy", out=g[:, 0:W], in_=pf[:, 0:W], r=[pf.k()], w=[g.k()])
                k.dma(GPRE[j][:, gcol0:gcol0 + W], g[:, 0:W], r=[g.k()], w=[("dram", "GPRE", j, mi)])
            else:
                g = zbs[fcnt % 3]
                k.act("activation", out=g[:, 0:W], in_=pf[:, 0:W], func=AF.Silu, r=[pf.k()], w=[g.k()])
                k.dma(ZBT[j - 12][:, t0:t0 + W], g[:, 0:W], r=[g.k()], w=[("dram", "ZBT", j - 12, mi)])
            fcnt += 1

    for _ in prep_m(0):
        pass
    for mi in range(len(mt_list)):
        gl = [comp_m(mi)] + ([prep_m(mi + 1)] if mi + 1 < len(mt_list) else [])
        run_interleaved(gl, 2)
    k.release(m0)


def _rearr_w(w):
    n = w.shape[1]
    return np.ascontiguousarray(w.reshape(8, 128, n).transpose(1, 0, 2).reshape(128, 8 * n))


def _fm(v):
    return np.ascontiguousarray(v.reshape(-1, 128).T)


def _rope_tables():
    t = np.arange(T)
    row = (t // 64).astype(np.float32)
    col = (t % 64).astype(np.float32)
    inv = (10000.0 ** (-np.arange(0, 64, 2, dtype=np.float32) / 64)).astype(np.float32)
    ar = row[:, None] * inv
    ac = col[:, None] * inv
    cs = np.concatenate([np.cos(ar), np.cos(ar), np.cos(ac), np.cos(ac)], 1).astype(np.float32)
    sn = np.concatenate([-np.sin(ar), np.sin(ar), -np.sin(ac), np.sin(ac)], 1).astype(np.float32)
    return cs.reshape(32, 128, 128), sn.reshape(32, 128, 128)


def host_prep(inp):
    f = lambda a: np.asarray(a, dtype=np.float32)
    shared = {}
    shared["c_ident"] = np.eye(128, dtype=np.float32)
    sel = np.zeros((2, 256), np.float32)
    sel[0, :128] = 1.0
    sel[1, 128:] = 1.0
    shared["c_sel"] = sel
    shared["ada_w_r"] = np.stack([_rearr_w(f(inp["ada_w"][l])) for l in range(2)])
    rows = np.zeros((1, ROWS_N), np.float32)
    vals = {"ada_b0": inp["ada_b"][0], "ada_b1": inp["ada_b"][1], "pre_g0": inp["pre_norm_g"][0], "pre_g1": inp["pre_norm_g"][1],
            "post_g0": inp["post_norm_g"][0], "post_g1": inp["post_norm_g"][1], "b_out": inp["od_b_out"][0], "gdn_g": inp["ev_gdn_norm_g"][0],
            "qn_g": inp["ev_q_norm_g"][0], "kn_g": inp["ev_k_norm_g"][0], "a_log": f(inp["ev_a_log"][0]).reshape(-1), "dt_bias": f(inp["ev_dt_bias"][0]).reshape(-1)}
    for n_, v in vals.items():
        o, l = ROW_OFF[n_]
        rows[0, o:o + l] = f(v).reshape(-1)
    shared["rows"] = rows
    shared["ev_w_in_r"] = _rearr_w(f(inp["ev_w_in"][0]))
    shared["ev_w_out_r"] = _rearr_w(f(inp["ev_w_out"][0]))
    shared["od_w_in_r"] = _rearr_w(f(inp["od_w_in"][0]))
    shared["od_w_out_r"] = _rearr_w(f(inp["od_w_out"][0]))
    cs, sn = _rope_tables()
    shared["rope_cs"] = cs
    shared["rope_sn"] = sn
    pv = np.zeros((128, PV_N), np.float32)
    pv[:, PV_BIN:PV_BIN + 24] = _fm(f(inp["od_b_in"][0]))
    pv[:, PV_DWB:PV_DWB + 8] = _fm(f(inp["od_dw_b"][0]))
    pv[:, PV_LNG:PV_LNG + 8] = _fm(f(inp["od_ln_g"][0]))
    pv[:, PV_LNB:PV_LNB + 8] = _fm(f(inp["od_ln_b"][0]))
    pv[:, PV_DWW:PV_DWW + 248] = f(inp["od_dw_w"][0]).T.reshape(8, 128, 31).transpose(1, 0, 2).reshape(128, 248)
    pv[:, PV_C5W:PV_C5W + 60] = f(inp["ev_short_conv_w"][0]).T.reshape(12, 128, 5).transpose(1, 0, 2).reshape(128, 60)
    shared["pv"] = pv
    maps = []
    cctx = _fm(f(inp["c_ctx"]))
    for b in range(8):
        m = dict(shared)
        m["x"] = np.ascontiguousarray(f(inp["x"][b]))
        m["ctx"] = np.ascontiguousarray(f(inp["ctx"][b]))
        cv = np.zeros((128, 16), np.float32)
        cv[:, 0::2] = _fm(f(inp["c"][b]))
        cv[:, 1::2] = cctx
        m["cvec"] = cv
        maps.append(m)
    return maps


NKC = TK // 128


def p2b_attn(k, c, banks=None, as_gen=False):
    nc = k.nc
    m0 = None if as_gen else k.mark()
    QTd = k.dram("QT", [4, 128, T], BF16)
    KTd = k.dram("KT", [2, 128, TK], BF16)
    Vd = k.dram("V", [NTILE, 128, 256], BF16)
    ZBT = k.dram("ZBT", [4, 128, T], BF16)
    YT = k.dram("YT", [8, 128, T], BF16)
    QTs = k.alloc("QTa", 4 * T, BF16)
    KTs = k.alloc("KTa", 2 * TK, BF16)
    Vs = k.alloc("Va", NTILE * 256, BF16)
    for h in range(4):
        for hf in range(2):
            k.dma(QTs[:, h * T + hf * 2048:h * T + (hf + 1) * 2048], QTd[h][:, hf * 2048:(hf + 1) * 2048],
                  r=[("dram", "QT", h, m) for m in range(1, 9)], w=[QTs.k(h)])
    for h in range(2):
        k.dma(KTs[:, h * TK:(h + 1) * TK], KTd[h], r=[("dram", "KT", h, m) for m in range(9)], w=[KTs.k(h)])
    for t in range(NTILE):
        k.dma(Vs[:, t * 256:(t + 1) * 256], Vd[t], r=[("dram", "V", t)], w=[Vs.k(t)])
    onesb = k.alloc("onesb", 128, BF16)
    k.dve("memset", onesb[:, :], 1.0, w=[onesb.k()])
    PT = [k.alloc(f"PT{i}", 512, BF16) for i in range(4)]
    RD = [k.alloc(f"RD{i}", 512, F32) for i in range(2)]
    OO = [k.alloc(f"OO{i}", 512, F32) for i in range(2)]
    ZG = [k.alloc(f"ZGa{i}", 512, BF16) for i in range(2)]
    YB = [k.alloc(f"YB{i}", 512, BF16) for i in range(2)]
    if banks is None:
        ps_s = [k.PS[0], k.PS[1], k.PS[2], k.PS[3]]
        ps_o = [k.PS[4], k.PS[5]]
        ps_d = [k.PS[6], k.PS[7]]
    else:
        ps_s = [banks[0], banks[1]]
        ps_o = [banks[2], banks[2]]
        ps_d = [banks[3], banks[3]]
    NPS = len(ps_s)
    scale = 128.0 ** -0.5

    def body():
      it = 0
      sc = 0
      for qi in range(T // 512):
        for h in range(4):
            kv = h // 2
            po, pd = ps_o[it % 2], ps_d[it % 2]
            rd, oo, zg, yb = RD[it % 2], OO[it % 2], ZG[it % 2], YB[it % 2]
            k.dma(zg[:, :], ZBT[h][:, qi * 512:(qi + 1) * 512], r=[("dram", "ZBT", h, qi + 1)], w=[zg.k()])
            qsl = QTs[:, h * T + qi * 512:h * T + (qi + 1) * 512]

            def score(kc):
                ps = ps_s[(sc + kc) % NPS]
                k.pe("matmul", ps[:, :], lhsT=KTs[:, kv * TK + kc * 128:kv * TK + (kc + 1) * 128], rhs=qsl, start=True, stop=True,
                     r=[KTs.k(kv), QTs.k(h)], w=[ps.k()])
                pt = PT[(sc + kc) % 4]
                k.act("activation", out=pt[:, :], in_=ps[:, :], func=AF.Exp, scale=scale, r=[ps.k()], w=[pt.k()])

            score(0)
            score(1)
            for kc in range(NKC):
                if kc + 2 < NKC:
                    score(kc + 2)
                pt = PT[(sc + kc) % 4]
                tile_id = kc if kc < 32 else kc
                k.pe("matmul", po[:, :], lhsT=Vs[:, kc * 256 + kv * 128:kc * 256 + (kv + 1) * 128], rhs=pt[:, :], start=(kc == 0), stop=(kc == NKC - 1),
                     r=[Vs.k(kc), pt.k()], w=[po.k()])
                k.pe("matmul", pd[:, :], lhsT=onesb[:, :], rhs=pt[:, :], start=(kc == 0), stop=(kc == NKC - 1),
                     r=[onesb.k(), pt.k()], w=[pd.k()])
                yield
            sc += NKC
            k.dve("reciprocal", out=rd[:, :], in_=pd[:, :], r=[pd.k()], w=[rd.k()])
            k.dve("tensor_tensor", out=oo[:, :], in0=po[:, :], in1=rd[:, :], op=ALU.mult, r=[po.k(), rd.k()], w=[oo.k()])
            k.pool("tensor_tensor", out=yb[:, :], in0=oo[:, :], in1=zg[:, :], op=ALU.mult, r=[oo.k(), zg.k()], w=[yb.k()])
            k.dma(YT[4 + h][:, qi * 512:(qi + 1) * 512], yb[:, :], r=[yb.k()], w=[("dram", "YT", 4 + h, qi)])
            it += 1
            yield

    if as_gen:
        return body()
    for _ in body():
        pass
    k.release(m0)


def p3_outproj(k, c):
    nc = k.nc
    m0 = k.mark()
    x = k.dram("x", [T, D], F32)
    x1 = k.dram("x1", [T, D], F32)
    modd = k.dram("mod", [8, 128, D], F32)
    YT = k.dram("YT", [8, 128, T], BF16)
    w_out_d = k.dram("ev_w_out_r", [128, 8 * 1024], F32)
    w_out = k.alloc("w_out0", 8 * 1024, BF16)
    k.dma(w_out[:, :], w_out_d, r=[("dram", "ev_w_out_r")], w=[w_out.k()], q="pool")
    G0 = k.alloc("G0", D, F32)
    k.dma(G0[:, :], modd[2], r=[("dram", "mod")], w=[G0.k()])
    YTt = [k.alloc(f"YTt{i}", 8 * 512, BF16) for i in range(2)]
    O = [k.alloc(f"O{i}", D, F32) for i in range(2)]
    junks = [k.alloc(f"junkp3{i}", D, BF16) for i in range(2)]
    ss = [k.alloc(f"ssp{i}", 8, F32) for i in range(2)]
    XR = [k.alloc(f"XR{i}", D, F32) for i in range(2)]
    ps_o = [(k.PS[0], k.PS[1]), (k.PS[2], k.PS[3])]
    oc = 0
    for m in range(T // 512):
        yt = YTt[m % 2]
        for j in range(8):
            k.dma(yt[:, j * 512:(j + 1) * 512], YT[j][:, m * 512:(m + 1) * 512], r=[("dram", "YT", j, m)], w=[yt.k(j)])
        def subtile(s, slot):
            r0 = m * 512 + s * 128
            po = ps_o[slot]
            o, sst, xr, jk = O[slot], ss[slot], XR[slot], junks[slot]
            k.dma(xr[:, :], x[r0:r0 + 128, :], r=[("dram", "x", r0 // 128)], w=[xr.k()])
            for nh in range(2):
                for j in range(8):
                    k.pe("matmul", po[nh][:, :], lhsT=yt[:, j * 512 + s * 128:j * 512 + (s + 1) * 128], rhs=w_out[:, j * 1024 + nh * 512:j * 1024 + (nh + 1) * 512],
                         start=(j == 0), stop=(j == 7), r=[yt.k(j), w_out.k()], w=[po[nh].k()])
                yield
            k.act("copy", out=o[:, 0:512], in_=po[0][:, :], r=[po[0].k()], w=[o.k()])
            yield
            k.dve("tensor_copy", out=o[:, 512:1024], in_=po[1][:, :], r=[po[1].k()], w=[o.k()])
            yield
            yield from post_res_gen(k, o, sst, jk, G0, xr, o, xr, x1[r0:r0 + 128, :], ("dram", "x1", r0 // 128))
        run_interleaved([functools.partial(subtile, s) for s in range(4)], 2, slotted=True)
    k.release(m0)


def p1b_gdnprep(k, c):
    nc = k.nc
    m0 = k.mark()
    GPRE = k.dram("GPRE", [12, 128, GP_W], BF16)
    pvd = k.dram("pv", [128, PV_N], F32)
    GQT = k.dram("GQT", [4, 128, TK], BF16)
    GKT = k.dram("GKT", [4, 128, TK], BF16)
    GK = k.dram("GK", [NTILE, 128, 512], BF16)
    GV = k.dram("GV", [NTILE, 128, 512], BF16)
    pv = k.alloc("pv", PV_N, F32)
    k.dma(pv[:, :], pvd, r=[("dram", "pv")], w=[pv.k()])
    DG = k.alloc("DG5", 60 * 128, BF16)
    for i in range(60):
        eng = "dve" if i % 2 == 0 else "pool"
        k.any(eng, "tensor_scalar", out=DG[:, i * 128:(i + 1) * 128], in0=c["identf"][:, :], scalar1=pv[:, PV_C5W + i:PV_C5W + i + 1], scalar2=None,
              op0=ALU.mult, r=[c["identf"].k(), pv.k()], w=[DG.k(i // 5)])
    onesb = k.alloc("onesb", 128, BF16)
    k.dve("memset", onesb[:, :], 1.0, w=[onesb.k()])
    PRE = [k.alloc(f"PRE5{i}", 516, BF16) for i in range(8)]
    U = [k.alloc(f"U5{i}", 512, F32) for i in range(8)]
    SQ = [k.alloc(f"SQ5{i}", 512, BF16) for i in range(8)]
    RS = [k.alloc(f"RS5{i}", 512, F32) for i in range(8)]
    UN = [k.alloc(f"UN5{i}", 512, BF16) for i in range(8)]
    GKs = [k.alloc(f"GKs{i}", 4 * 512, BF16) for i in range(2)]
    GVs = [k.alloc(f"GVs{i}", 4 * 512, BF16) for i in range(2)]
    ps_cv = list(k.PS)
    ps_ss = ps_cv
    ps_tr = ps_cv
    mts = [("ctx", 0, 256)] + [("lat", m * 512, 512) for m in range(T // 512)]
    cc = 0
    for mi, (kind, t0, W) in enumerate(mts):
        lat = kind == "lat"
        gcol0 = (GP_LAT0 + t0) if lat else (GP_CTX0 + t0)
        tile0 = (t0 // 128) if lat else 32
        col0 = tile0 * 128
        nsub = W // 128
        gks, gvs = GKs[mi % 2], GVs[mi % 2]
        def chunk(j, cc, sl):
            pre = PRE[sl]
            pcv = ps_cv[sl]
            k.dma(pre[:, 0:W + 4], GPRE[j][:, gcol0 - 2:gcol0 + W + 2],
                  r=[("dram", "GPRE", j, x_) for x_ in (["p0", "p1", "p2"] + list(range(max(0, mi - 1), min(9, mi + 2))))], w=[pre.k()])
            for t in range(5):
                k.pe("matmul", pcv[:, 0:W], lhsT=DG[:, (j * 5 + t) * 128:(j * 5 + t + 1) * 128], rhs=pre[:, t:t + W], start=(t == 0), stop=(t == 4),
                     r=[DG.k(j), pre.k()], w=[pcv.k()])
            un = UN[sl]
            if j < 8:
                u, sq, rs, pss = U[sl], SQ[sl], RS[sl], ps_ss[sl]
                k.act("activation", out=u[:, 0:W], in_=pcv[:, 0:W], func=AF.Silu, r=[pcv.k()], w=[u.k()])
                yield
                k.pool("tensor_tensor", out=sq[:, 0:W], in0=u[:, 0:W], in1=u[:, 0:W], op=ALU.mult, r=[u.k()], w=[sq.k()])
                yield
                k.pe("matmul", pss[:, 0:W], lhsT=onesb[:, :], rhs=sq[:, 0:W], start=True, stop=True, r=[onesb.k(), sq.k()], w=[pss.k()])
                yield
                k.act("activation", out=rs[:, 0:W], in_=pss[:, 0:W], func=AF.Sqrt, bias=NORM_EPS, scale=1.0, r=[pss.k()], w=[rs.k()])
                yield
                k.dve("reciprocal", out=rs[:, 0:W], in_=rs[:, 0:W], r=[rs.k()], w=[rs.k()])
                yield
                if j < 4:
                    k.dve("scalar_tensor_tensor", out=un[:, 0:W], in0=u[:, 0:W], scalar=128.0 ** -0.5, in1=rs[:, 0:W], op0=ALU.mult, op1=ALU.mult,
                          r=[u.k(), rs.k()], w=[un.k()])
                    k.dma(GQT[j][:, col0:col0 + W], un[:, 0:W], r=[un.k()], w=[("dram", "GQT", j, mi)])
                    yield
                else:
                    k.dve("tensor_tensor", out=un[:, 0:W], in0=u[:, 0:W], in1=rs[:, 0:W], op=ALU.mult, r=[u.k(), rs.k()], w=[un.k()])
                    yield
                    k.dma(GKT[j - 4][:, col0:col0 + W], un[:, 0:W], r=[un.k()], w=[("dram", "GKT", j - 4, mi)])
                    yield
            else:
                k.act("activation", out=un[:, 0:W], in_=pcv[:, 0:W], func=AF.Silu, r=[pcv.k()], w=[un.k()])
                yield
            if j >= 4:
                h = (j - 4) % 4
                ptr = ps_tr[sl]
                ptb = ptr.ap.bitcast(BF16)
                for s in range(nsub):
                    k.pe("transpose", ptb[:, s * 128:(s + 1) * 128], un[:, s * 128:(s + 1) * 128], c["identb"][:, :], r=[un.k(), c["identb"].k()], w=[ptr.k()])
                    yield
                dst = (gks if j < 8 else gvs)
                dview = dst[:, :].rearrange("p (s f) -> p s f", s=4)[:, 0:nsub, h * 128:(h + 1) * 128]
                sview = ptb[:, 0:nsub * 128].rearrange("p (s f) -> p s f", f=128)
                if cc % 2 == 0:
                    k.act("copy", out=dview, in_=sview, r=[ptr.k()], w=[dst.k()])
                    yield
                else:
                    k.dve("tensor_copy", out=dview, in_=sview, r=[ptr.k()], w=[dst.k()])
                    yield
            yield
        run_interleaved([functools.partial(chunk, j, cc + j) for j in range(12)], 8, slotted=True)
        cc += 12
        for s in range(nsub):
            k.dma(GK[tile0 + s], gks[:, s * 512:(s + 1) * 512], r=[gks.k()], w=[("dram", "GK", tile0 + s)])
            k.dma(GV[tile0 + s], gvs[:, s * 512:(s + 1) * 512], r=[gvs.k()], w=[("dram", "GV", tile0 + s)])
    k.release(m0)


GDN_LAG = 45


def p2a_gdn(k, c, nslots=4, as_gens=False):
    nc = k.nc
    m0 = None if as_gens else k.mark()
    GQT = k.dram("GQT", [4, 128, TK], BF16)
    GKT = k.dram("GKT", [4, 128, TK], BF16)
    GK = k.dram("GK", [NTILE, 128, 512], BF16)
    GV = k.dram("GV", [NTILE, 128, 512], BF16)
    ABd = k.dram("AB", [NTILE, 128, 16], F32)
    trid = k.dram("c_tri", [9, 128, 128], F32)
    OD = [k.dram("OF", [32, 128, 512], F32), k.dram("OB", [32, 128, 512], F32)]
    TRI = k.alloc("TRI", 9 * 128, F32)
    for i in range(9):
        k.dma(TRI[:, i * 128:(i + 1) * 128], trid[i], r=[("dram", "c_tri")], w=[TRI.k()])
    tri = lambda i: TRI[:, i * 128:(i + 1) * 128]
    bc4 = lambda ap: ap.unsqueeze(1).to_broadcast([128, 4, 128])
    col4 = lambda ap: ap.unsqueeze(2).to_broadcast([128, 4, 128])
    v3 = lambda t: t[:, :].rearrange("p (h f) -> p h f", h=4)
    ABs = k.alloc("ABs", NTILE * 16, F32)
    for t in range(NTILE):
        k.dma(ABs[:, t * 16:(t + 1) * 16], ABd[t], r=[("dram", "AB", t)], w=[ABs.k()])
    alog = k.alloc("alog", 8, F32)
    dtb = k.alloc("dtb", 8, F32)
    k.dma(alog[:, :], row_bc(k, "a_log"), r=[("dram", "rows")], w=[alog.k()])
    k.dma(dtb[:, :], row_bc(k, "dt_bias"), r=[("dram", "rows")], w=[dtb.k()])
    GALL = k.alloc("GALL", NTILE * 8, F32)
    BALL = k.alloc("BALL", NTILE * 8, F32)
    ab3 = ABs[:, :].rearrange("p (t f) -> p t f", f=16)
    g3 = GALL[:, :].rearrange("p (t f) -> p t f", f=8)
    b3 = BALL[:, :].rearrange("p (t f) -> p t f", f=8)
    bct = lambda ap: ap.unsqueeze(1).to_broadcast([128, NTILE, 8])
    k.dve("tensor_tensor", out=g3, in0=ab3[:, :, 0:8], in1=bct(dtb[:, :]), op=ALU.add, r=[ABs.k(), dtb.k()], w=[GALL.k()])
    k.act("activation", out=GALL[:, :], in_=GALL[:, :], func=AF.Exp, r=[GALL.k()], w=[GALL.k()])
    k.act("activation", out=GALL[:, :], in_=GALL[:, :], func=AF.Ln, bias=1.0, scale=1.0, r=[GALL.k()], w=[GALL.k()])
    k.act("activation", out=alog[:, :], in_=alog[:, :], func=AF.Exp, r=[alog.k()], w=[alog.k()])
    k.dve("scalar_tensor_tensor", out=g3, in0=g3, scalar=-1.0, in1=bct(alog[:, :]), op0=ALU.mult, op1=ALU.mult, r=[GALL.k(), alog.k()], w=[GALL.k()])
    k.act("activation", out=b3, in_=ab3[:, :, 8:16], func=AF.Exp, scale=-1.0, r=[ABs.k()], w=[BALL.k()])
    k.dve("tensor_scalar", out=BALL[:, :], in0=BALL[:, :], scalar1=1.0, scalar2=None, op0=ALU.add, r=[BALL.k()], w=[BALL.k()])
    k.dve("reciprocal", out=BALL[:, :], in_=BALL[:, :], r=[BALL.k()], w=[BALL.k()])
    Sf = [k.alloc(f"Sf{d}", 512, F32) for d in range(2)]
    Sb = [k.alloc(f"Sb{d}", 512, BF16) for d in range(2)]
    for d in range(2):
        k.dve("memset", Sf[d][:, :], 0.0, w=[Sf[d].k()])
        k.pool("memset", Sb[d][:, :], 0.0, w=[Sb[d].k()])
    def bufs(d):
        B = {}
        for n_ in ["qT4", "kT4", "ktok", "vtok", "X", "XT", "PT", "AINC", "AINCT", "KD", "ATn", "QEFF", "N1", "N1T", "N2", "P", "V1", "U1"]:
            B[n_] = k.alloc(f"{n_}{d}", 512, BF16)
        for n_ in ["WUR", "WU"]:
            B[n_] = k.alloc(f"{n_}{d}", 1024, BF16)
        for n_ in ["DIFF", "E", "DMS", "DMI", "T1", "ER", "QD", "OUT"]:
            B[n_] = k.alloc(f"{n_}{d}", 512, F32)
        B["SM"] = k.alloc(f"SM{d}", 32, F32)
        return B
    BUF = [bufs(i) for i in range(nslots)]
    order = [[32, 33] + list(range(32)), [33, 32] + list(range(31, -1, -1))]
    cfg = [dict(tri=2, mi=0, ms=1, jl=127, m1a=5, m1b=6, m2a=7), dict(tri=0, mi=2, ms=3, jl=0, m1a=6, m1b=5, m2a=8)]
    identb = c["identb"]
    sdone = {}

    def unit(n, d, slot):
        Q = k.PS[2 * slot:2 * slot + 2]
        P_GR = P_B = P_W0 = P_Z = Q[0]
        P_SM = P_A = P_T = P_W1 = P_Z2 = Q[1]
        if n == 1 and nslots == 4:
            for _ in range(GDN_LAG):
                yield
        g = order[d][n]
        B = BUF[slot]
        cf = cfg[d]
        lat = g < 32
        sm = B["SM"]
        GC, GL, EG, BE, KDS, GT, TMP = (sm[:, 0:4], sm[:, 4:8], sm[:, 8:12], sm[:, 12:16], sm[:, 16:20], sm[:, 20:24], sm[:, 24:28])
        gcol = GALL[:, g * 8 + d * 4:g * 8 + d * 4 + 4]
        bcol = BALL[:, g * 8 + d * 4:g * 8 + d * 4 + 4]
        for h in range(4):
            k.dma(B["qT4"][:, h * 128:(h + 1) * 128], GQT[h][:, g * 128:(g + 1) * 128], r=[("dram", "GQT", h, mi_) for mi_ in range(9)], w=[B["qT4"].k()])
            k.dma(B["kT4"][:, h * 128:(h + 1) * 128], GKT[h][:, g * 128:(g + 1) * 128], r=[("dram", "GKT", h, mi_) for mi_ in range(9)], w=[B["kT4"].k()])
        k.dma(B["ktok"][:, :], GK[g], r=[("dram", "GK", g)], w=[B["ktok"].k()])
        yield
        k.dma(B["vtok"][:, :], GV[g], r=[("dram", "GV", g)], w=[B["vtok"].k()])
        yield
        for h in range(4):
            k.pe("matmul", P_GR[:, h * 128:(h + 1) * 128], lhsT=gcol[:, h:h + 1].to_broadcast([128, 128]), rhs=tri(cf["tri"]), start=True, stop=True,
                 r=[GALL.k(), TRI.k()], w=[P_GR.k()])
        k.pe("matmul", P_SM[:, 0:4], lhsT=tri(cf["tri"]), rhs=gcol, start=True, stop=True, r=[GALL.k(), TRI.k()], w=[P_SM.k()])
        yield
        k.act("copy", out=GC, in_=P_SM[:, 0:4], r=[P_SM.k()], w=[sm.k()])
        yield
        k.dve("tensor_copy", out=GL, in_=v3(P_GR)[:, :, cf["jl"]], r=[P_GR.k()], w=[sm.k()])
        yield
        k.dve("tensor_tensor", out=v3(B["DIFF"]), in0=col4(GC), in1=v3(P_GR), op=ALU.subtract, r=[sm.k(), P_GR.k()], w=[B["DIFF"].k()])
        yield
        k.act("activation", out=B["ER"][:, :], in_=P_GR[:, :], func=AF.Exp, r=[P_GR.k()], w=[B["ER"].k()])
        yield
        k.pool("tensor_scalar", out=B["DIFF"][:, :], in0=B["DIFF"][:, :], scalar1=0.0, scalar2=None, op0=ALU.min, r=[B["DIFF"].k()], w=[B["DIFF"].k()])
        yield
        k.act("activation", out=B["E"][:, :], in_=B["DIFF"][:, :], func=AF.Exp, r=[B["DIFF"].k()], w=[B["E"].k()])
        yield
        k.pool("tensor_tensor", out=v3(B["DMS"]), in0=v3(B["E"]), in1=bc4(tri(cf["ms"])), op=ALU.mult, r=[B["E"].k(), TRI.k()], w=[B["DMS"].k()])
        yield
        k.pool("tensor_tensor", out=v3(B["DMI"]), in0=v3(B["E"]), in1=bc4(tri(cf["mi"])), op=ALU.mult, r=[B["E"].k(), TRI.k()], w=[B["DMI"].k()])
        yield
        k.act("activation", out=EG, in_=GC, func=AF.Exp, r=[sm.k()], w=[sm.k()])
        yield
        k.dve("tensor_tensor", out=BE, in0=EG, in1=bcol, op=ALU.mult, r=[sm.k(), BALL.k()], w=[sm.k()])
        yield
        k.dve("tensor_tensor", out=TMP, in0=GL, in1=GC, op=ALU.subtract, r=[sm.k()], w=[sm.k()])
        yield
        k.act("activation", out=KDS, in_=TMP, func=AF.Exp, r=[sm.k()], w=[sm.k()])
        yield
        k.act("activation", out=GT, in_=GL, func=AF.Exp, r=[sm.k()], w=[sm.k()])
        yield
        for h in range(4):
            k.pe("matmul", P_A[:, h * 128:(h + 1) * 128], lhsT=B["kT4"][:, h * 128:(h + 1) * 128], rhs=B["kT4"][:, h * 128:(h + 1) * 128], start=True, stop=True,
                 r=[B["kT4"].k()], w=[P_A.k()])
        for h in range(4):
            k.pe("matmul", P_B[:, h * 128:(h + 1) * 128], lhsT=B["qT4"][:, h * 128:(h + 1) * 128], rhs=B["kT4"][:, h * 128:(h + 1) * 128], start=True, stop=True,
                 r=[B["qT4"].k(), B["kT4"].k()], w=[P_B.k()])
        k.dve("tensor_tensor", out=B["T1"][:, :], in0=P_A[:, :], in1=B["DMS"][:, :], op=ALU.mult, r=[P_A.k(), B["DMS"].k()], w=[B["T1"].k()])
        yield
        k.dve("scalar_tensor_tensor", out=v3(B["X"]), in0=v3(B["T1"]), scalar=-1.0, in1=col4(bcol), op0=ALU.mult, op1=ALU.mult,
               r=[B["T1"].k(), BALL.k()], w=[B["X"].k()])
        k.dve("tensor_tensor", out=B["AINC"][:, :], in0=P_B[:, :], in1=B["DMI"][:, :], op=ALU.mult, r=[P_B.k(), B["DMI"].k()], w=[B["AINC"].k()])
        yield
        ptb = P_T.ap.bitcast(BF16)
        for h in range(4):
            k.pe("transpose", ptb[:, h * 128:(h + 1) * 128], B["X"][:, h * 128:(h + 1) * 128], identb[:, :], r=[B["X"].k(), identb.k()], w=[P_T.k()])
        for h in range(4):
            k.pe("transpose", ptb[:, 512 + h * 128:512 + (h + 1) * 128], B["AINC"][:, h * 128:(h + 1) * 128], identb[:, :], r=[B["AINC"].k(), identb.k()], w=[P_T.k()])
        k.act("copy", out=B["XT"][:, :], in_=ptb[:, 0:512], r=[P_T.k()], w=[B["XT"].k()])
        yield
        k.act("copy", out=B["AINCT"][:, :], in_=ptb[:, 512:1024], r=[P_T.k()], w=[B["AINCT"].k()])
        yield
        k.pool("tensor_tensor", out=v3(B["N1"]), in0=v3(B["X"]), in1=bc4(tri(cf["m1a"])), op=ALU.mult, r=[B["X"].k(), TRI.k()], w=[B["N1"].k()])
        yield
        k.pool("tensor_tensor", out=v3(B["N1T"]), in0=v3(B["XT"]), in1=bc4(tri(cf["m1b"])), op=ALU.mult, r=[B["XT"].k(), TRI.k()], w=[B["N1T"].k()])
        yield
        k.pool("tensor_tensor", out=v3(B["N2"]), in0=v3(B["X"]), in1=bc4(tri(cf["m2a"])), op=ALU.mult, r=[B["X"].k(), TRI.k()], w=[B["N2"].k()])
        yield
        k.dve("tensor_tensor", out=v3(B["X"]), in0=v3(B["X"]), in1=bc4(tri(4)), op=ALU.mult, r=[B["X"].k(), TRI.k()], w=[B["X"].k()])
        yield
        k.dve("tensor_tensor", out=v3(B["XT"]), in0=v3(B["XT"]), in1=bc4(tri(4)), op=ALU.mult, r=[B["XT"].k(), TRI.k()], w=[B["XT"].k()])
        yield
        k.pool("tensor_tensor", out=v3(B["P"]), in0=v3(B["X"]), in1=bc4(identb[:, :]), op=ALU.add, r=[B["X"].k(), identb.k()], w=[B["P"].k()])
        yield
        k.dve("tensor_tensor", out=v3(B["PT"]), in0=v3(B["XT"]), in1=bc4(identb[:, :]), op=ALU.add, r=[B["XT"].k(), identb.k()], w=[B["PT"].k()])
        yield

        def mm4(ps, lhs, rhs, acc=None):
            for h in range(4):
                sl = slice(h * 128, (h + 1) * 128)
                k.pe("matmul", ps[:, sl], lhsT=B[lhs][:, sl], rhs=B[rhs][:, sl], start=True, stop=(acc is None), r=[B[lhs].k(), B[rhs].k()], w=[ps.k()])
                if acc is not None:
                    k.pe("matmul", ps[:, sl], lhsT=identb[:, :], rhs=B[acc][:, sl], start=False, stop=True, r=[identb.k(), B[acc].k()], w=[ps.k()])

        for l in range(1, 5):
            mm4(P_A, "XT", "X")
            yield
            mm4(P_B, "X", "XT")
            yield
            k.act("copy", out=B["X"][:, :], in_=P_A[:, :], r=[P_A.k()], w=[B["X"].k()])
            yield
            k.act("copy", out=B["XT"][:, :], in_=P_B[:, :], r=[P_B.k()], w=[B["XT"].k()])
            yield
            mm4(P_T, "XT", "P", acc="P")
            yield
            mm4(P_W0, "X", "PT", acc="PT")
            yield
            k.act("copy", out=B["P"][:, :], in_=P_T[:, :], r=[P_T.k()], w=[B["P"].k()])
            yield
            k.dve("tensor_copy", out=B["PT"][:, :], in_=P_W0[:, :], r=[P_W0.k()], w=[B["PT"].k()])
            yield
        mm4(P_A, "N1T", "P")
        yield
        mm4(P_B, "N1", "PT")
        yield
        k.act("copy", out=B["V1"][:, :], in_=P_A[:, :], r=[P_A.k()], w=[B["V1"].k()])
        yield
        k.act("copy", out=B["U1"][:, :], in_=P_B[:, :], r=[P_B.k()], w=[B["U1"].k()])
        yield
        mm4(P_T, "PT", "V1", acc="P")
        yield
        mm4(P_W0, "P", "U1", acc="PT")
        yield
        k.act("copy", out=B["P"][:, :], in_=P_T[:, :], r=[P_T.k()], w=[B["P"].k()])
        yield
        k.dve("tensor_copy", out=B["PT"][:, :], in_=P_W0[:, :], r=[P_W0.k()], w=[B["PT"].k()])
        yield
        mm4(P_A, "N2", "PT")
        yield
        k.act("copy", out=B["U1"][:, :], in_=P_A[:, :], r=[P_A.k()], w=[B["U1"].k()])
        yield
        mm4(P_T, "P", "U1", acc="PT")
        yield
        k.act("copy", out=B["PT"][:, :], in_=P_T[:, :], r=[P_T.k()], w=[B["PT"].k()])
        yield
        wur = B["WUR"][:, :].rearrange("p (h f) -> p h f", h=4)
        for h in range(4):
            hs = slice(h * 128, (h + 1) * 128)
            k.act("activation", out=wur[:, h, 0:128], in_=B["ktok"][:, hs], func=AF.Copy, scale=BE[:, h:h + 1], r=[B["ktok"].k(), sm.k()], w=[B["WUR"].k()])
            k.act("activation", out=wur[:, h, 128:256], in_=B["vtok"][:, hs], func=AF.Copy, scale=bcol[:, h:h + 1], r=[B["vtok"].k(), BALL.k()], w=[B["WUR"].k()])
            yield
            k.act("activation", out=B["KD"][:, hs], in_=B["ktok"][:, hs], func=AF.Copy, scale=KDS[:, h:h + 1], r=[B["ktok"].k(), sm.k()], w=[B["KD"].k()])
            yield
        for h in range(4):
            pw = P_W0 if h < 2 else P_W1
            k.pe("matmul", pw[:, (h % 2) * 256:(h % 2 + 1) * 256], lhsT=B["PT"][:, h * 128:(h + 1) * 128], rhs=B["WUR"][:, h * 256:(h + 1) * 256], start=True, stop=True,
                 r=[B["PT"].k(), B["WUR"].k()], w=[pw.k()])
        k.act("copy", out=B["WU"][:, 0:512], in_=P_W0[:, :], r=[P_W0.k()], w=[B["WU"].k()])
        yield
        k.act("copy", out=B["WU"][:, 512:1024], in_=P_W1[:, :], r=[P_W1.k()], w=[B["WU"].k()])
        yield
        wv = lambda h: B["WU"][:, h * 256:h * 256 + 128]
        uv = lambda h: B["WU"][:, h * 256 + 128:h * 256 + 256]
        for h in range(4):
            k.pe("matmul", P_Z[:, h * 128:(h + 1) * 128], lhsT=wv(h), rhs=B["KD"][:, h * 128:(h + 1) * 128], start=True, stop=True,
                 r=[B["WU"].k(), B["KD"].k()], w=[P_Z.k()])
        k.act("activation", out=B["ATn"][:, :], in_=P_Z[:, :], func=AF.Copy, scale=-1.0, r=[P_Z.k()], w=[B["ATn"].k()])
        yield
        while n > 0 and not sdone.get((n - 1, d)):
            yield
        if lat:
            k.pool("tensor_tensor", out=B["QD"][:, :], in0=B["qT4"][:, :], in1=B["ER"][:, :], op=ALU.mult, r=[B["qT4"].k(), B["ER"].k()], w=[B["QD"].k()])
            for h in range(4):
                k.pe("matmul", P_Z2[:, h * 128:(h + 1) * 128], lhsT=wv(h), rhs=B["AINCT"][:, h * 128:(h + 1) * 128], start=True, stop=True,
                     r=[B["WU"].k(), B["AINCT"].k()], w=[P_Z2.k()])
            k.dve("tensor_tensor", out=B["QEFF"][:, :], in0=B["QD"][:, :], in1=P_Z2[:, :], op=ALU.subtract, r=[B["QD"].k(), P_Z2.k()], w=[B["QEFF"].k()])
            assert n == 0 or sdone.get((n - 1, d)), f"GDN interleave order violated (o) at n={n} d={d}"
            for h in range(4):
                sl = slice(h * 128, (h + 1) * 128)
                k.pe("matmul", P_Z[:, sl], lhsT=B["QEFF"][:, sl], rhs=Sb[d][:, sl], start=True, stop=False, r=[B["QEFF"].k(), Sb[d].k()], w=[P_Z.k()])
                k.pe("matmul", P_Z[:, sl], lhsT=B["AINCT"][:, sl], rhs=uv(h), start=False, stop=True, r=[B["AINCT"].k(), B["WU"].k()], w=[P_Z.k()])
            k.act("copy", out=B["OUT"][:, :], in_=P_Z[:, :], r=[P_Z.k()], w=[B["OUT"].k()])
            k.dma(OD[d][g], B["OUT"][:, :], r=[B["OUT"].k()], w=[("dram", "O", d, g)])
        assert n == 0 or sdone.get((n - 1, d)), f"GDN interleave order violated at n={n} d={d}"
        for h in range(4):
            sl = slice(h * 128, (h + 1) * 128)
            k.pe("matmul", P_Z2[:, sl], lhsT=B["ATn"][:, sl], rhs=Sb[d][:, sl], start=True, stop=False, r=[B["ATn"].k(), Sb[d].k()], w=[P_Z2.k()])
            k.pe("matmul", P_Z2[:, sl], lhsT=B["KD"][:, sl], rhs=uv(h), start=False, stop=True, r=[B["KD"].k(), B["WU"].k()], w=[P_Z2.k()])
        for h in range(4):
            sl = slice(h * 128, (h + 1) * 128)
            k.dve("scalar_tensor_tensor", out=Sf[d][:, sl], in0=Sf[d][:, sl], scalar=GT[:, h:h + 1], in1=P_Z2[:, sl], op0=ALU.mult, op1=ALU.add,
                  r=[Sf[d].k(), sm.k(), P_Z2.k()], w=[Sf[d].k()])
        yield
        k.act("copy", out=Sb[d][:, :], in_=Sf[d][:, :], r=[Sf[d].k()], w=[Sb[d].k()])
        sdone[(n, d)] = True
        yield

    gens = []
    for n in range(NTILE):
        for d in range(2):
            gens.append(functools.partial(unit, n, d))
    if as_gens:
        return gens
    run_interleaved(gens, nslots, slotted=True)
    k.release(m0)


def gdn_consts():
    idx = np.arange(128)
    ge = (idx[:, None] >= idx[None, :]).astype(np.float32)
    gt = (idx[:, None] > idx[None, :]).astype(np.float32)
    bd32 = (idx[:, None] // 32 == idx[None, :] // 32).astype(np.float32)
    m1l = ((idx[:, None] // 64 == idx[None, :] // 64) & (idx[:, None] // 32 == idx[None, :] // 32 + 1)).astype(np.float32)
    m2l = ((idx[:, None] >= 64) & (idx[None, :] < 64)).astype(np.float32)
    return np.ascontiguousarray(np.stack([ge, gt, ge.T, gt.T, bd32, m1l, m1l.T, m2l, m2l.T]))


def p2c_gdnout(k, c):
    nc = k.nc
    m0 = k.mark()
    OF = k.dram("OF", [32, 128, 512], F32)
    OB = k.dram("OB", [32, 128, 512], F32)
    ZA = k.dram("ZA", [T, 512], BF16)
    YT = k.dram("YT", [8, 128, T], BF16)
    GG = k.alloc("GGn", 512, F32)
    for h in range(4):
        k.dma(GG[:, h * 128:(h + 1) * 128], row_bc(k, "gdn_g"), r=[("dram", "rows")], w=[GG.k()])
    NS = 4
    of = [k.alloc(f"of{i}", 512, F32) for i in range(NS)]
    ob = [k.alloc(f"ob{i}", 512, F32) for i in range(NS)]
    za = [k.alloc(f"zac{i}", 512, BF16) for i in range(NS)]
    o = [k.alloc(f"oc{i}", 512, F32) for i in range(NS)]
    sq = [k.alloc(f"sqc{i}", 512, F32) for i in range(NS)]
    st = [k.alloc(f"stc{i}", 16, F32) for i in range(NS)]
    yb = [k.alloc(f"yc{i}", 512, BF16) for i in range(NS)]
    yts = [k.alloc(f"ytc{i}", 4 * 512, BF16) for i in range(2)]
    ps_tr = [k.PS[0], k.PS[1], k.PS[2], k.PS[3]]
    v3 = lambda t: t[:, :].rearrange("p (h f) -> p h f", h=4)

    def tile(g, i):
        m = g // 4
        s = g % 4
        yt = yts[m % 2]
        k.dma(of[i][:, :], OF[g], r=[("dram", "O", 0, g)], w=[of[i].k()])
        k.dma(ob[i][:, :], OB[g], r=[("dram", "O", 1, g)], w=[ob[i].k()])
        k.dma(za[i][:, :], ZA[g * 128:(g + 1) * 128, :], r=[("dram", "ZA", g)], w=[za[i].k()])
        k.dve("tensor_tensor", out=o[i][:, :], in0=of[i][:, :], in1=ob[i][:, :], op=ALU.add, r=[of[i].k(), ob[i].k()], w=[o[i].k()])
        yield
        k.pool("tensor_tensor", out=sq[i][:, :], in0=o[i][:, :], in1=o[i][:, :], op=ALU.mult, r=[o[i].k()], w=[sq[i].k()])
        yield
        k.dve("tensor_reduce", out=st[i][:, 0:4], in_=v3(sq[i]), axis=AX.X, op=ALU.add, r=[sq[i].k()], w=[st[i].k()])
        yield
        k.act("activation", out=st[i][:, 4:8], in_=st[i][:, 0:4], func=AF.Sqrt, scale=1.0 / 128, bias=NORM_EPS, r=[st[i].k()], w=[st[i].k()])
        yield
        k.dve("reciprocal", out=st[i][:, 8:12], in_=st[i][:, 4:8], r=[st[i].k()], w=[st[i].k()])
        yield
        k.dve("tensor_tensor", out=v3(o[i]), in0=v3(o[i]), in1=st[i][:, 8:12].unsqueeze(2).to_broadcast([128, 4, 128]), op=ALU.mult,
              r=[o[i].k(), st[i].k()], w=[o[i].k()])
        yield
        k.pool("tensor_tensor", out=o[i][:, :], in0=o[i][:, :], in1=GG[:, :], op=ALU.mult, r=[o[i].k(), GG.k()], w=[o[i].k()])
        yield
        k.dve("tensor_tensor", out=yb[i][:, :], in0=o[i][:, :], in1=za[i][:, :], op=ALU.mult, r=[o[i].k(), za[i].k()], w=[yb[i].k()])
        yield
        ptr = ps_tr[i]
        ptb = ptr.ap.bitcast(BF16)
        for h in range(4):
            k.pe("transpose", ptb[:, h * 128:(h + 1) * 128], yb[i][:, h * 128:(h + 1) * 128], c["identb"][:, :], r=[yb[i].k(), c["identb"].k()], w=[ptr.k()])
        yield
        dview = yt[:, :].rearrange("p (h t) -> p h t", h=4)[:, :, s * 128:(s + 1) * 128]
        sview = ptb[:, 0:512].rearrange("p (h t) -> p h t", h=4)
        k.act("copy", out=dview, in_=sview, r=[ptr.k()], w=[yt.k()])
        yield

    for m in range(8):
        run_interleaved([functools.partial(tile, 4 * m + s_) for s_ in range(4)], NS, slotted=True)
        yt = yts[m % 2]
        for h in range(4):
            k.dma(YT[h][:, m * 512:(m + 1) * 512], yt[:, h * 512:(h + 1) * 512], r=[yt.k()], w=[("dram", "YT", h, m)])
    k.release(m0)


def p2ab(k, c):
    m0 = k.mark()
    gens = p2a_gdn(k, c, nslots=2, as_gens=True)
    att = p2b_attn(k, c, banks=k.PS[4:8], as_gen=True)
    il = Interleaver(gens, 2, slotted=True)
    g_alive, a_alive, r = True, True, 0
    while g_alive or a_alive:
        if g_alive:
            g_alive = il.step()
        if a_alive and (r % ATT_EVERY == 0 or not g_alive):
            try:
                next(att)
            except StopIteration:
                a_alive = False
        r += 1
    k.release(m0)


ATT_EVERY = 3
ALL_PASSES = None


def all_passes():
    return [p0_mod, p1_inproj, p1b_gdnprep, p2a_gdn, p2c_gdnout, p2b_attn, p3_outproj, l1_pass_a, l1_pass_b]


EXT_IN = ["x", "ctx", "cvec", "c_sel", "c_ident", "c_tri", "ada_w_r", "rows", "ev_w_in_r", "ev_w_out_r", "od_w_in_r", "od_w_out_r", "rope_cs", "rope_sn", "pv"]


def kernel(**inputs):
    maps = host_prep(inputs)
    tri = gdn_consts()
    for m in maps:
        m["c_tri"] = tri
    nc, _ = build_program(all_passes(), ext_in=EXT_IN, ext_out=["y"])
    in_maps = [{k_: m[k_] for k_ in EXT_IN} for m in maps]
    res = run_bass_kernel_spmd(nc, in_maps, core_ids=list(range(8)))
    return np.stack([np.asarray(r["y"], dtype=np.float32) for r in res.results], axis=0)
```

```python
import contextlib
import functools
import numpy as np
import concourse.bass as bass
import concourse.mybir as mybir
from concourse.bass_utils import run_bass_kernel_spmd

F32 = mybir.dt.float32
BF16 = mybir.dt.bfloat16
AF = mybir.ActivationFunctionType
ALU = mybir.AluOpType
AX = mybir.AxisListType

ENGS = ["pe", "act", "dve", "pool", "sp"]
NDMA_Q = {"sp": 20, "pool": 40}
EPOCH = 20000

D = 1024
T = 4096
CTX = 256
NORM_EPS = 1e-6


class Sched:
    def __init__(self, nc):
        self.nc = nc
        self.ops = []
        self.last_w = {}
        self.readers = {}
        self.cur = {e: {} for e in ENGS}
        self.pos = {e: 0 for e in ENGS}
        self.ndma = {"sp": 0, "pool": 0}
        self.dma_ops = {"sp": [], "pool": []}
        self.seen = set()
        self.inherit = {}

    def retire(self, names):
        names = set(names)
        for key in list(self.seen):
            if key[0] in names:
                cand = list(self.readers.get(key, ()))
                w = self.last_w.get(key)
                if w is not None:
                    cand.append(w)
                for c in cand:
                    o = self.ops[c]
                    old = self.inherit.get(o["src"])
                    if old is None or self.ops[old]["p"] < o["p"]:
                        self.inherit[o["src"]] = c
                self.seen.discard(key)
                self.readers.pop(key, None)
                self.last_w.pop(key, None)

    def _touch(self, key):
        if key not in self.seen:
            self.seen.add(key)
            if self.inherit:
                self.readers[key] = list(self.inherit.values())

    def add(self, eng, fn, reads=(), writes=(), dma=False):
        idx = len(self.ops)
        deps = []
        for r in reads:
            self._touch(r)
            w = self.last_w.get(r)
            if w is not None:
                deps.append((w, True))
        for k in writes:
            self._touch(k)
            w = self.last_w.get(k)
            if w is not None:
                deps.append((w, False))
            for rd in self.readers.get(k, ()):
                deps.append((rd, False))
        if dma:
            nd = self.ndma[eng]
            ns = NDMA_Q[eng]
            slot = nd % ns
            cnt = nd // ns + 1
            if nd >= ns:
                deps.append((self.dma_ops[eng][nd - ns], True))
            src = ("d", eng, slot)
            p = cnt
            self.ndma[eng] += 1
        else:
            self.pos[eng] += 1
            src = eng
            p = self.pos[eng]
        cur = self.cur[eng]
        waits = []
        for d, raw in deps:
            o = self.ops[d]
            s, v = o["src"], o["p"]
            if s == eng and not dma and not o["dma"]:
                if eng == "pe":
                    continue
            if cur.get(s, 0) >= v:
                continue
            waits.append((s, v))
            o["needed"] = True
            for ks, kv in o["vc"].items():
                if cur.get(ks, 0) < kv:
                    cur[ks] = kv
            cur[s] = v
        vc = dict(cur)
        vc[src] = p
        op = dict(eng=eng, fn=fn, waits=waits, src=src, p=p, dma=dma, vc=vc, needed=False, deps=deps)
        self.ops.append(op)
        if dma:
            self.dma_ops[eng].append(idx)
        for r in reads:
            self.readers.setdefault(r, []).append(idx)
        for k in writes:
            self.last_w[k] = idx
            self.readers[k] = []
        return idx

    def emit(self):
        nc = self.nc
        rank = {e: {} for e in ENGS}
        cnt = {e: 0 for e in ENGS}
        for o in self.ops:
            if not o["dma"] and o["needed"]:
                cnt[o["eng"]] += 1
                rank[o["eng"]][o["p"]] = cnt[o["eng"]]
        nsem = {e: max(1, (cnt[e] + EPOCH - 1) // EPOCH) for e in ENGS}
        with contextlib.ExitStack() as st:
            sems = {e: [st.enter_context(nc.semaphore(f"s_{e}{i}")) for i in range(nsem[e])] for e in ENGS}
            dsem = {q: [st.enter_context(nc.semaphore(f"s_d{q}{i}")) for i in range(NDMA_Q[q])] for q in NDMA_Q}
            block = st.enter_context(nc.Block())
            engobj = {"pe": nc.tensor, "act": nc.scalar, "dve": nc.vector, "pool": nc.gpsimd, "sp": nc.sync}
            per = {e: [o for o in self.ops if o["eng"] == e] for e in ENGS}

            def run(e):
                eo = engobj[e]
                for o in per[e]:
                    for s, v in o["waits"]:
                        if isinstance(s, tuple):
                            eo.wait_ge(dsem[s[1]][s[2]], 16 * v)
                        else:
                            r = rank[s][v] - 1
                            eo.wait_ge(sems[s][r // EPOCH], r % EPOCH + 1)
                    ins = o["fn"]()
                    if o["dma"]:
                        ins.then_inc(dsem[o["src"][1]][o["src"][2]], 16)
                    elif o["needed"]:
                        r = rank[e][o["p"]] - 1
                        ins.then_inc(sems[e][r // EPOCH], 1)
                if e == "sp":
                    for q in NDMA_Q:
                        for sl in range(min(self.ndma[q], NDMA_Q[q])):
                            last = (self.ndma[q] - 1 - sl) // NDMA_Q[q] + 1
                            eo.wait_ge(dsem[q][sl], 16 * last)

            @block.tensor
            def _(x):
                run("pe")

            @block.scalar
            def _(x):
                run("act")

            @block.vector
            def _(x):
                run("dve")

            @block.gpsimd
            def _(x):
                run("pool")

            @block.sync
            def _(x):
                run("sp")
        return cnt


class Interleaver:
    def __init__(self, gens, width, slotted=False):
        self.it = iter(gens)
        self.width = width
        self.slotted = slotted
        self.active = []
        self.free = list(range(width))

    def step(self):
        while len(self.active) < self.width:
            try:
                g = next(self.it)
            except StopIteration:
                break
            if self.slotted:
                sl = self.free.pop(0)
                self.active.append((g(sl), sl))
            else:
                self.active.append((g, None))
        if not self.active:
            return False
        for item in list(self.active):
            try:
                next(item[0])
            except StopIteration:
                self.active.remove(item)
                if self.slotted:
                    self.free.append(item[1])
        return True


def run_interleaved(gens, width, slotted=False):
    il = Interleaver(gens, width, slotted)
    while il.step():
        pass


class Tl:
    def __init__(self, name, ap):
        self.name = name
        self.ap = ap

    def k(self, sub=None):
        return (self.name, sub)

    def __getitem__(self, idx):
        return self.ap[idx]


ARENA_COLS = 52000


class K:
    def __init__(self, nc, ext_in=(), ext_out=()):
        self.nc = nc
        self.S = Sched(nc)
        self.st = contextlib.ExitStack()
        self.big = self.st.enter_context(nc.sbuf_tensor("arena", [128, ARENA_COLS], F32))
        self.off = 0
        self.live = []
        self.ext_in = set(ext_in)
        self.ext_out = set(ext_out)
        self.drams = {}
        self.uid = 0
        self.PS = [Tl(f"ps{i}", self.st.enter_context(nc.psum_tensor(f"ps{i}", [128, 512], F32))[:, :]) for i in range(8)]

    def alloc(self, name, cols, dt=F32):
        size = 4 if dt == F32 else 2
        n32 = (cols * size + 3) // 4
        n32 = (n32 + 7) // 8 * 8
        assert self.off + n32 <= ARENA_COLS, f"SBUF arena overflow at {name}: {self.off}+{n32}"
        ap = self.big[:, self.off:self.off + n32]
        if dt != F32:
            ap = ap.bitcast(dt)[:, :cols]
        else:
            ap = ap[:, :cols]
        self.off += n32
        self.uid += 1
        t = Tl(f"{name}#{self.uid}", ap)
        self.live.append(t.name)
        return t

    def mark(self):
        return (self.off, len(self.live))

    def release(self, mark):
        off, n = mark
        self.S.retire(self.live[n:])
        del self.live[n:]
        self.off = off

    def dram(self, name, shape, dt):
        if name in self.drams:
            return self.drams[name]
        if name in self.ext_in:
            t = self.nc.dram_tensor(name, list(shape), dt, kind="ExternalInput")
        elif name in self.ext_out:
            t = self.nc.dram_tensor(name, list(shape), dt, kind="ExternalOutput")
        else:
            t = self.nc.dram_tensor(name, list(shape), dt)
        self.drams[name] = t.ap()
        return self.drams[name]

    def _op(self, eng, name, a, kw):
        r = kw.pop("r", ())
        w = kw.pop("w", ())
        w = list(w) + [key for key in r if key[0].startswith("ps") and key not in w]
        obj = {"pe": self.nc.tensor, "act": self.nc.scalar, "dve": self.nc.vector, "pool": self.nc.gpsimd}[eng]
        return self.S.add(eng, functools.partial(getattr(obj, name), *a, **kw), r, w)

    def pe(self, name, *a, **kw):
        return self._op("pe", name, a, kw)

    def act(self, name, *a, **kw):
        return self._op("act", name, a, kw)

    def dve(self, name, *a, **kw):
        return self._op("dve", name, a, kw)

    def pool(self, name, *a, **kw):
        return self._op("pool", name, a, kw)

    def any(self, eng, name, *a, **kw):
        return self._op(eng, name, a, kw)

    def dma(self, out, in_, r=(), w=(), q="sp"):
        nc = self.nc
        if q == "sp":
            return self.S.add("sp", functools.partial(nc.sync.dma_start, out=out, in_=in_), r, w, dma=True)
        return self.S.add("pool", functools.partial(nc.gpsimd.dma_start, out=out, in_=in_), r, w, dma=True)

    def veng(self, eng):
        return {"dve": self.nc.vector, "pool": self.nc.gpsimd}[eng]


def load_consts(k):
    nc = k.nc
    identd = k.dram("c_ident", [128, 128], F32)
    c = {}
    c["identf"] = k.alloc("identf", 128, F32)
    c["identb"] = k.alloc("identb", 128, BF16)
    k.dma(c["identf"][:, :], identd, r=[("dram", "c_ident")], w=[c["identf"].k()])
    k.dma(c["identb"][:, :], identd, r=[("dram", "c_ident")], w=[c["identb"].k()], q="pool")
    return c


def prep_rows(k, c, xrows_ap, xkey, Amod, Bmod, hT, col0, nrows, tmp, ps_t, idx):
    nc = k.nc
    xt, junk, ss, t1, hb = tmp
    n = nrows
    k.dma(xt[:n, :], xrows_ap, r=[xkey], w=[xt.k()])
    k.act("activation", out=junk[:n, :], in_=xt[:n, :], func=AF.Square, accum_out=ss[:n, 0:1],
          r=[xt.k()], w=[junk.k(), ss.k()])
    k.act("activation", out=ss[:n, 1:2], in_=ss[:n, 0:1], func=AF.Sqrt, scale=1.0 / D, bias=NORM_EPS,
          r=[ss.k()], w=[ss.k()])
    k.dve("reciprocal", out=ss[:n, 2:3], in_=ss[:n, 1:2], r=[ss.k()], w=[ss.k()])
    k.dve("scalar_tensor_tensor", out=t1[:n, :], in0=xt[:n, :], scalar=ss[:n, 2:3], in1=Amod[:n, :],
                                                 op0=ALU.mult, op1=ALU.mult,
          r=[xt.k(), ss.k(), Amod.k()], w=[t1.k()])
    k.pool("tensor_tensor", out=hb[:n, :], in0=t1[:n, :], in1=Bmod[:n, :], op=ALU.add,
           r=[t1.k(), Bmod.k()], w=[hb.k()])
    pT = ps_t.ap.bitcast(BF16)
    for kc in range(8):
        k.pe("transpose", pT[:, kc * 128:kc * 128 + n], hb[:n, kc * 128:(kc + 1) * 128], c["identb"][:n, :n],
             r=[hb.k(), c["identb"].k()], w=[ps_t.k()])
    src = pT.rearrange("p (a b) -> p a b", a=8)[:, :, :n]
    dst = hT[:, :].rearrange("p (a b) -> p a b", a=8)[:, :, col0:col0 + n]
    if idx % 2 == 0:
        k.act("copy", out=dst, in_=src, r=[ps_t.k()], w=[hT.k()])
    else:
        k.dve("tensor_copy", out=dst, in_=src, r=[ps_t.k()], w=[hT.k()])


def prep_rows_gen(k, c, xrows_ap, xkey, Amod, Bmod, hT, col0, nrows, tmp, ps_t, idx):
    nc = k.nc
    xt, junk, ss, t1, hb = tmp
    n = nrows
    k.dma(xt[:n, :], xrows_ap, r=[xkey], w=[xt.k()])
    yield
    k.act("activation", out=junk[:n, :], in_=xt[:n, :], func=AF.Square, accum_out=ss[:n, 0:1],
          r=[xt.k()], w=[junk.k(), ss.k()])
    yield
    k.act("activation", out=ss[:n, 1:2], in_=ss[:n, 0:1], func=AF.Sqrt, scale=1.0 / D, bias=NORM_EPS,
          r=[ss.k()], w=[ss.k()])
    yield
    k.dve("reciprocal", out=ss[:n, 2:3], in_=ss[:n, 1:2], r=[ss.k()], w=[ss.k()])
    yield
    k.dve("scalar_tensor_tensor", out=t1[:n, :], in0=xt[:n, :], scalar=ss[:n, 2:3], in1=Amod[:n, :],
                                                 op0=ALU.mult, op1=ALU.mult,
          r=[xt.k(), ss.k(), Amod.k()], w=[t1.k()])
    yield
    k.pool("tensor_tensor", out=hb[:n, :], in0=t1[:n, :], in1=Bmod[:n, :], op=ALU.add,
           r=[t1.k(), Bmod.k()], w=[hb.k()])
    yield
    pT = ps_t.ap.bitcast(BF16)
    for kc in range(8):
        k.pe("transpose", pT[:, kc * 128:kc * 128 + n], hb[:n, kc * 128:(kc + 1) * 128], c["identb"][:n, :n],
             r=[hb.k(), c["identb"].k()], w=[ps_t.k()])
    yield
    src = pT.rearrange("p (a b) -> p a b", a=8)[:, :, :n]
    dst = hT[:, :].rearrange("p (a b) -> p a b", a=8)[:, :, col0:col0 + n]
    if idx % 2 == 0:
        k.act("copy", out=dst, in_=src, r=[ps_t.k()], w=[hT.k()])
        yield
    else:
        k.dve("tensor_copy", out=dst, in_=src, r=[ps_t.k()], w=[hT.k()])
        yield


def alloc_prep_tmp(k, tag):
    xt = k.alloc(f"xt{tag}", 1024, F32)
    junk = k.alloc(f"junk{tag}", 1024, BF16)
    ss = k.alloc(f"ss{tag}", 8, F32)
    t1 = k.alloc(f"t1{tag}", 1024, F32)
    hb = k.alloc(f"hb{tag}", 1024, BF16)
    return (xt, junk, ss, t1, hb)


PV_BIN = 0
PV_DWB = 24
PV_LNG = 32
PV_LNB = 40
PV_DWW = 48
PV_C5W = 48 + 248
PV_N = PV_C5W + 60
U1PAD = 15
U1W = T + 2 * U1PAD


def l1_pass_a(k, c):
    nc = k.nc
    m0 = k.mark()
    x1 = k.dram("x1", [T, D], F32)
    modd = k.dram("mod", [8, 128, D], F32)
    w_in_d = k.dram("od_w_in_r", [128, 8 * 3072], F32)
    pvd = k.dram("pv", [128, PV_N], F32)
    U1 = k.dram("U1", [8, 128, U1W], BF16)
    ZG1 = k.dram("ZG1", [8, 128, T], BF16)

    w_in = k.alloc("w_in1", 8 * 3072, BF16)
    for kc in range(8):
        k.dma(w_in[:, kc * 3072:(kc + 1) * 3072], w_in_d[:, kc * 3072:(kc + 1) * 3072], r=[("dram", "od_w_in_r")], w=[w_in.k(kc)], q="pool")
    pv = k.alloc("pv", PV_N, F32)
    k.dma(pv[:, :], pvd, r=[("dram", "pv")], w=[pv.k()])
    A1 = k.alloc("A1", D, F32)
    B1 = k.alloc("B1", D, F32)
    k.dma(A1[:, :], modd[5], r=[("dram", "mod")], w=[A1.k()])
    k.dma(B1[:, :], modd[6], r=[("dram", "mod")], w=[B1.k()])
    zt = k.alloc("zt", 16, BF16)
    k.dve("memset", zt[:, :], 0.0, w=[zt.k()])
    for j in range(8):
        k.dma(U1[j][:, 0:U1PAD], zt[:, 0:U1PAD], r=[zt.k()], w=[("dram", "U1", j, "padl")])
        k.dma(U1[j][:, U1PAD + T:U1W], zt[:, 0:U1PAD], r=[zt.k()], w=[("dram", "U1", j, "padr")])
    tmps = [alloc_prep_tmp(k, i) for i in range(2)]
    hTs = [k.alloc(f"hT{i}", 8 * 512, BF16) for i in range(2)]
    sg = [k.alloc(f"sg{i}", 512, F32) for i in range(2)]
    ub = [k.alloc(f"ub{i}", 512, BF16) for i in range(3)]
    zb = [k.alloc(f"zb{i}", 512, BF16) for i in range(3)]
    ps_t = [k.PS[0], k.PS[1]]
    ps_mm = [k.PS[2], k.PS[3], k.PS[4], k.PS[5], k.PS[6], k.PS[7]]
    nmt = T // 512

    def prep_m(m):
        hT = hTs[m % 2]

        def sub(s, slot):
            r0 = m * 512 + s * 128
            return prep_rows_gen(k, c, x1[r0:r0 + 128, :], ("dram", "x1", r0 // 128), A1, B1, hT, s * 128, 128, tmps[slot], ps_t[slot], 4 * m + s)
        il = Interleaver([functools.partial(sub, s) for s in range(4)], 2, slotted=True)
        while il.step():
            yield

    def comp_m(m):
        hT = hTs[m % 2]
        hT3 = hT[:, :].rearrange("p (a b) -> p a b", a=8)

        def mmgroup(pst, col0):
            for kc in range(8):
                k.pe("matmul", pst[:, :], lhsT=w_in[:, kc * 3072 + col0:kc * 3072 + col0 + 128], rhs=hT3[:, kc, :],
                     start=(kc == 0), stop=(kc == 7), r=[w_in.k(kc), hT.k()], w=[pst.k()])

        for j in range(8):
            pa = ps_mm[(2 * j) % 4]
            pg = ps_mm[(2 * j + 1) % 4]
            pz = ps_mm[4 + j % 2]
            mmgroup(pa, j * 128)
            yield
            mmgroup(pg, 1024 + j * 128)
            yield
            mmgroup(pz, 2048 + j * 128)
            yield
            sgt = sg[j % 2]
            ubt = ub[j % 3]
            zbt = zb[j % 3]
            k.act("activation", out=sgt[:, :], in_=pg[:, :], func=AF.Sigmoid, bias=pv[:, PV_BIN + 8 + j:PV_BIN + 9 + j], scale=1.0,
                  r=[pg.k(), pv.k()], w=[sgt.k()])
            yield
            k.dve("scalar_tensor_tensor", out=ubt[:, :], in0=pa[:, :], scalar=pv[:, PV_BIN + j:PV_BIN + j + 1], in1=sgt[:, :],
                  op0=ALU.add, op1=ALU.mult, r=[pa.k(), sgt.k(), pv.k()], w=[ubt.k()])
            k.dma(U1[j][:, U1PAD + m * 512:U1PAD + (m + 1) * 512], ubt[:, :], r=[ubt.k()], w=[("dram", "U1", j, m)])
            yield
            k.act("activation", out=zbt[:, :], in_=pz[:, :], func=AF.Silu, bias=pv[:, PV_BIN + 16 + j:PV_BIN + 17 + j], scale=1.0,
                  r=[pz.k(), pv.k()], w=[zbt.k()])
            k.dma(ZG1[j][:, m * 512:(m + 1) * 512], zbt[:, :], r=[zbt.k()], w=[("dram", "ZG1", j, m)])
            yield

    for _ in prep_m(0):
        pass
    for m in range(nmt):
        gl = [comp_m(m)] + ([prep_m(m + 1)] if m + 1 < nmt else [])
        run_interleaved(gl, 2)
    k.release(m0)


def l1_pass_b(k, c):
    nc = k.nc
    m0 = k.mark()
    x1 = k.dram("x1", [T, D], F32)
    y = k.dram("y", [T, D], F32)
    modd = k.dram("mod", [8, 128, D], F32)
    w_out_d = k.dram("od_w_out_r", [128, 8 * 1024], F32)
    pvd = k.dram("pv", [128, PV_N], F32)
    U1 = k.dram("U1", [8, 128, U1W], BF16)
    ZG1 = k.dram("ZG1", [8, 128, T], BF16)

    w_out = k.alloc("w_out1", 8 * 1024, BF16)
    k.dma(w_out[:, :], w_out_d, r=[("dram", "od_w_out_r")], w=[w_out.k()], q="pool")
    pv = k.alloc("pv", PV_N, F32)
    k.dma(pv[:, :], pvd, r=[("dram", "pv")], w=[pv.k()])
    G1 = k.alloc("G1", D, F32)
    k.dma(G1[:, :], modd[7], r=[("dram", "mod")], w=[G1.k()])
    BOUT = k.alloc("BOUT", D, F32)
    k.dma(BOUT[:, :], row_bc(k, "b_out"), r=[("dram", "rows")], w=[BOUT.k()])
    onesm = k.alloc("onesm", 128, BF16)
    k.dve("memset", onesm[:, :], 1.0 / 1024.0, w=[onesm.k()])
    DG = k.alloc("DG", 8 * 31 * 128, BF16)
    for j in range(8):
        for t in range(31):
            i = j * 31 + t
            eng = "dve" if i % 2 == 0 else "pool"
            ve = k.veng(eng)
            k.any(eng, "tensor_scalar", out=DG[:, i * 128:(i + 1) * 128], in0=c["identf"][:, :],
                                                           scalar1=pv[:, PV_DWW + i:PV_DWW + i + 1], scalar2=None, op0=ALU.mult,
                  r=[c["identf"].k(), pv.k()], w=[DG.k(j)])
    PW = 512 + 2 * U1PAD
    PRE = [k.alloc(f"PRE{i}", 8 * PW, BF16) for i in range(2)]
    ZGt = [k.alloc(f"ZGt{i}", 8 * 512, BF16) for i in range(2)]
    UC = k.alloc("UC", 8 * 512, F32)
    UCb = k.alloc("UCb", 8 * 512, BF16)
    SQ = k.alloc("SQ", 8 * 512, BF16)
    MEAN = k.alloc("MEAN", 512, F32)
    M2 = k.alloc("M2", 512, F32)
    RSTD = k.alloc("RSTD", 512, F32)
    TA = [k.alloc(f"TA{i}", 512, F32) for i in range(4)]
    TB = TA
    YS = [k.alloc(f"YS{i}", 512, BF16) for i in range(4)]
    YT = k.alloc("YT", 8 * 512, BF16)
    O = [k.alloc(f"O{i}", D, F32) for i in range(2)]
    junk = k.alloc("junkb", D, BF16)
    ss = [k.alloc(f"ssb{i}", 8, F32) for i in range(2)]
    XR = [k.alloc(f"XR{i}", D, F32) for i in range(2)]
    T2 = O
    OUT = XR
    ps_cv = [k.PS[0], k.PS[1]]
    ps_mean, ps_msq = k.PS[2], k.PS[3]
    ps_o = [(k.PS[4], k.PS[5]), (k.PS[6], k.PS[7])]
    nmt = T // 512
    oc = 0
    for m in range(nmt):
        pre = PRE[m % 2]
        zgt = ZGt[m % 2]
        for j in range(8):
            k.dma(pre[:, j * PW:(j + 1) * PW], U1[j][:, m * 512:m * 512 + PW],
                  r=[("dram", "U1", j, mm) for mm in range(max(0, m - 1), min(nmt, m + 2))] + [("dram", "U1", j, "padl"), ("dram", "U1", j, "padr")],
                  w=[pre.k(j)])
            k.dma(zgt[:, j * 512:(j + 1) * 512], ZG1[j][:, m * 512:(m + 1) * 512], r=[("dram", "ZG1", j, m)], w=[zgt.k(j)])
        for j in range(8):
            pcv = ps_cv[j % 2]
            for t in range(31):
                i = j * 31 + t
                k.pe("matmul", pcv[:, :], lhsT=DG[:, i * 128:(i + 1) * 128],
                                                                    rhs=pre[:, j * PW + t:j * PW + t + 512], start=(t == 0), stop=(t == 30),
                     r=[DG.k(j), pre.k(j)], w=[pcv.k()])
            k.act("activation", out=UC[:, j * 512:(j + 1) * 512], in_=pcv[:, :], func=AF.Identity,
                                                           bias=pv[:, PV_DWB + j:PV_DWB + j + 1], scale=1.0,
                  r=[pcv.k(), pv.k()], w=[UC.k(j)])
            k.act("activation", out=SQ[:, j * 512:(j + 1) * 512], in_=pcv[:, :], func=AF.Square,
                                                           bias=pv[:, PV_DWB + j:PV_DWB + j + 1], scale=1.0,
                  r=[pcv.k(), pv.k()], w=[SQ.k(j)])
            k.pool("tensor_copy", out=UCb[:, j * 512:(j + 1) * 512], in_=UC[:, j * 512:(j + 1) * 512],
                   r=[UC.k(j)], w=[UCb.k(j)])
        for j in range(8):
            k.pe("matmul", ps_mean[:, :], lhsT=onesm[:, :], rhs=UCb[:, j * 512:(j + 1) * 512], start=(j == 0), stop=(j == 7),
                 r=[onesm.k(), UCb.k(j)], w=[ps_mean.k()])
        for j in range(8):
            k.pe("matmul", ps_msq[:, :], lhsT=onesm[:, :], rhs=SQ[:, j * 512:(j + 1) * 512], start=(j == 0), stop=(j == 7),
                 r=[onesm.k(), SQ.k(j)], w=[ps_msq.k()])
        k.act("copy", out=MEAN[:, :], in_=ps_mean[:, :], r=[ps_mean.k()], w=[MEAN.k()])
        k.dve("tensor_tensor", out=M2[:, :], in0=MEAN[:, :], in1=MEAN[:, :], op=ALU.mult, r=[MEAN.k()], w=[M2.k()])
        k.dve("tensor_tensor", out=M2[:, :], in0=ps_msq[:, :], in1=M2[:, :], op=ALU.subtract, r=[ps_msq.k(), M2.k()], w=[M2.k()])
        k.act("activation", out=RSTD[:, :], in_=M2[:, :], func=AF.Sqrt, bias=1e-5, scale=1.0, r=[M2.k()], w=[RSTD.k()])
        k.dve("reciprocal", out=RSTD[:, :], in_=RSTD[:, :], r=[RSTD.k()], w=[RSTD.k()])
        def ln_chunk(j, sl):
            ta, tb, ys = TA[sl], TB[sl], YS[sl]
            k.dve("tensor_tensor", out=ta[:, :], in0=UC[:, j * 512:(j + 1) * 512], in1=MEAN[:, :], op=ALU.subtract,
                  r=[UC.k(j), MEAN.k()], w=[ta.k()])
            yield
            k.pool("tensor_tensor", out=tb[:, :], in0=ta[:, :], in1=RSTD[:, :], op=ALU.mult,
                   r=[ta.k(), RSTD.k()], w=[tb.k()])
            yield
            k.act("activation", out=ys[:, :], in_=tb[:, :], func=AF.Silu,
                  scale=pv[:, PV_LNG + j:PV_LNG + j + 1], bias=pv[:, PV_LNB + j:PV_LNB + j + 1],
                  r=[tb.k(), pv.k()], w=[ys.k()])
            yield
            k.dve("tensor_tensor", out=YT[:, j * 512:(j + 1) * 512], in0=ys[:, :], in1=zgt[:, j * 512:(j + 1) * 512], op=ALU.mult,
                  r=[ys.k(), zgt.k(j)], w=[YT.k(j)])
            yield
        run_interleaved([functools.partial(ln_chunk, j) for j in range(8)], 4, slotted=True)
        for s in range(4):
            r0 = m * 512 + s * 128
            po = ps_o[oc % 2]
            o, sst, xr, t2, out = O[oc % 2], ss[oc % 2], XR[oc % 2], T2[oc % 2], OUT[oc % 2]
            oc += 1
            k.dma(xr[:, :], x1[r0:r0 + 128, :], r=[("dram", "x1", r0 // 128)], w=[xr.k()])
            for nh in range(2):
                for j in range(8):
                    k.pe("matmul", po[nh][:, :], lhsT=YT[:, j * 512 + s * 128:j * 512 + (s + 1) * 128],
                                                                        rhs=w_out[:, j * 1024 + nh * 512:j * 1024 + (nh + 1) * 512],
                                                                        start=(j == 0), stop=(j == 7),
                         r=[YT.k(j), w_out.k()], w=[po[nh].k()])
            for nh in range(2):
                k.dve("tensor_tensor", out=o[:, nh * 512:(nh + 1) * 512], in0=po[nh][:, :],
                                                                       in1=BOUT[:, nh * 512:(nh + 1) * 512], op=ALU.add,
                      r=[po[nh].k(), BOUT.k()], w=[o.k()])
            post_res(k, o, sst, junk, G1, xr, t2, out, y[r0:r0 + 128, :], ("dram", "y", r0 // 128))
    k.release(m0)


def post_res(k, o, sst, junk, G, xr, t2, out, ydst, ykey):
    nc = k.nc
    k.act("activation", out=junk[:, :], in_=o[:, :], func=AF.Square, accum_out=sst[:, 0:1],
          r=[o.k()], w=[junk.k(), sst.k()])
    k.act("activation", out=sst[:, 1:2], in_=sst[:, 0:1], func=AF.Sqrt, scale=1.0 / D, bias=NORM_EPS,
          r=[sst.k()], w=[sst.k()])
    k.dve("reciprocal", out=sst[:, 2:3], in_=sst[:, 1:2], r=[sst.k()], w=[sst.k()])
    k.dve("scalar_tensor_tensor", out=t2[:, :], in0=o[:, :], scalar=sst[:, 2:3], in1=G[:, :], op0=ALU.mult, op1=ALU.mult,
          r=[o.k(), sst.k(), G.k()], w=[t2.k()])
    k.pool("tensor_tensor", out=out[:, :], in0=t2[:, :], in1=xr[:, :], op=ALU.add, r=[t2.k(), xr.k()], w=[out.k()])
    k.dma(ydst, out[:, :], r=[out.k()], w=[ykey])


def post_res_gen(k, o, sst, junk, G, xr, t2, out, ydst, ykey):
    nc = k.nc
    k.act("activation", out=junk[:, :], in_=o[:, :], func=AF.Square, accum_out=sst[:, 0:1],
          r=[o.k()], w=[junk.k(), sst.k()])
    yield
    k.act("activation", out=sst[:, 1:2], in_=sst[:, 0:1], func=AF.Sqrt, scale=1.0 / D, bias=NORM_EPS,
          r=[sst.k()], w=[sst.k()])
    yield
    k.dve("reciprocal", out=sst[:, 2:3], in_=sst[:, 1:2], r=[sst.k()], w=[sst.k()])
    yield
    k.dve("scalar_tensor_tensor", out=t2[:, :], in0=o[:, :], scalar=sst[:, 2:3], in1=G[:, :], op0=ALU.mult, op1=ALU.mult,
          r=[o.k(), sst.k(), G.k()], w=[t2.k()])
    yield
    k.pool("tensor_tensor", out=out[:, :], in0=t2[:, :], in1=xr[:, :], op=ALU.add, r=[t2.k(), xr.k()], w=[out.k()])
    yield
    k.dma(ydst, out[:, :], r=[out.k()], w=[ykey])
    yield


def build_program(passes, ext_in, ext_out):
    nc = bass.Bass("TRN2", target_bir_lowering=False)
    k = K(nc, ext_in, ext_out)
    c = load_consts(k)
    for p in passes:
        p(k, c)
    cnt = k.S.emit()
    k.st.close()
    return nc, cnt


ROW_OFF = {}
_o = 0
for _n, _l in [("ada_b0", 3072), ("ada_b1", 3072), ("pre_g0", 1024), ("pre_g1", 1024), ("post_g0", 1024), ("post_g1", 1024),
               ("b_out", 1024), ("gdn_g", 128), ("qn_g", 128), ("kn_g", 128), ("a_log", 8), ("dt_bias", 8)]:
    ROW_OFF[_n] = (_o, _l)
    _o += _l
ROWS_N = _o


def row_bc(k, name, off=0, n=None):
    rows = k.dram("rows", [1, ROWS_N], F32)
    o, l = ROW_OFF[name]
    if n is None:
        n = l
    return rows[0, o + off:o + off + n].partition_broadcast(128)


def p0_mod(k, c):
    nc = k.nc
    m0 = k.mark()
    modd = k.dram("mod", [8, 128, D], F32)
    cvec = k.dram("cvec", [128, 16], F32)
    seld = k.dram("c_sel", [2, 256], F32)
    adaw = k.dram("ada_w_r", [2, 128, 8 * 3072], F32)
    cv = k.alloc("cv", 16, F32)
    k.dma(cv[:, :], cvec, r=[("dram", "cvec")], w=[cv.k()])
    scb = k.alloc("scb", 16, BF16)
    k.act("activation", out=scb[:, :], in_=cv[:, :], func=AF.Silu, r=[cv.k()], w=[scb.k()])
    sel = k.alloc("sel", 256, F32)
    k.dma(sel[0:2, :], seld, r=[("dram", "c_sel")], w=[sel.k()])
    mrow = k.alloc("mrow", 6144, F32)
    aw = [k.alloc(f"aw{l}", 8 * 3072, BF16) for l in range(2)]
    for l in range(2):
        for kc in range(8):
            k.dma(aw[l][:, kc * 3072:(kc + 1) * 3072], adaw[l][:, kc * 3072:(kc + 1) * 3072], r=[("dram", "ada_w_r")], w=[aw[l].k(kc)], q="pool")
    adab = [k.alloc(f"adab{l}", 3072, F32) for l in range(2)]
    preg = [k.alloc(f"preg{l}", 1024, F32) for l in range(2)]
    postg = [k.alloc(f"postg{l}", 1024, F32) for l in range(2)]
    for l in range(2):
        k.dma(adab[l][:, :], row_bc(k, f"ada_b{l}"), r=[("dram", "rows")], w=[adab[l].k()])
        k.dma(preg[l][:, :], row_bc(k, f"pre_g{l}"), r=[("dram", "rows")], w=[preg[l].k()])
        k.dma(postg[l][:, :], row_bc(k, f"post_g{l}"), r=[("dram", "rows")], w=[postg[l].k()])
    for l in range(2):
        for nt in range(6):
            ps = k.PS[nt % 2]
            for kc in range(8):
                k.pe("matmul", ps[0:2, :], lhsT=scb[:, 2 * kc:2 * kc + 2], rhs=aw[l][:, kc * 3072 + nt * 512:kc * 3072 + (nt + 1) * 512],
                     start=(kc == 0), stop=(kc == 7), r=[scb.k(), aw[l].k(kc)], w=[ps.k()])
            k.act("copy", out=mrow[0:2, l * 3072 + nt * 512:l * 3072 + (nt + 1) * 512], in_=ps[0:2, :], r=[ps.k()], w=[mrow.k((l, nt))])
    tmp = [k.alloc(f"mt{i}", 512, F32) for i in range(2)]
    outt = [k.alloc(f"mo{i}", 1024, F32) for i in range(2)]
    plan = [(0, 0, 1, 0), (1, 0, 0, 0), (2, 0, 2, 0), (3, 0, 1, 1), (4, 0, 0, 1), (5, 1, 1, 0), (6, 1, 0, 0), (7, 1, 2, 0)]
    cnt = 0
    for (mi, l, part, si) in plan:
        ot = outt[mi % 2]
        for nh in range(2):
            ps = k.PS[2 + cnt % 2]
            tt = tmp[cnt % 2]
            cnt += 1
            seg = l * 3072 + part * 1024 + nh * 512
            nt = (part * 1024 + nh * 512) // 512
            k.pe("matmul", ps[:, :], lhsT=sel[0:2, si * 128:(si + 1) * 128], rhs=mrow[0:2, seg:seg + 512], start=True, stop=True,
                 r=[sel.k(), mrow.k((l, nt))], w=[ps.k()])
            ab_ = adab[l][:, part * 1024 + nh * 512:part * 1024 + (nh + 1) * 512]
            osl = ot[:, nh * 512:(nh + 1) * 512]
            if part == 0:
                k.dve("tensor_tensor", out=osl, in0=ps[:, :], in1=ab_, op=ALU.add, r=[ps.k(), adab[l].k()], w=[ot.k()])
            elif part == 1:
                k.dve("scalar_tensor_tensor", out=tt[:, :], in0=ps[:, :], scalar=1.0, in1=ab_, op0=ALU.add, op1=ALU.add,
                      r=[ps.k(), adab[l].k()], w=[tt.k()])
                k.pool("tensor_tensor", out=osl, in0=tt[:, :], in1=preg[l][:, nh * 512:(nh + 1) * 512], op=ALU.mult,
                       r=[tt.k(), preg[l].k()], w=[ot.k()])
            else:
                k.dve("tensor_tensor", out=tt[:, :], in0=ps[:, :], in1=ab_, op=ALU.add, r=[ps.k(), adab[l].k()], w=[tt.k()])
                k.pool("tensor_tensor", out=osl, in0=tt[:, :], in1=postg[l][:, nh * 512:(nh + 1) * 512], op=ALU.mult,
                       r=[tt.k(), postg[l].k()], w=[ot.k()])
        k.dma(modd[mi], ot[:, :], r=[ot.k()], w=[("dram", "mod")])
    k.release(m0)


NTILE = 34
TK = T + CTX
GP_CTX0 = 2
GP_LAT0 = 2 + CTX + 2 + 2
GP_W = GP_LAT0 + T + 2
C_QKV, C_ZA, C_AB, C_QB, C_KB, C_VB, C_ZB, EVEN_IN = 0, 1536, 2048, 2064, 2576, 2832, 3088, 3600


P1_FLAGS = {"pads": True, "tm": True, "fm": True, "qk": True, "ab": True, "norm": True, "tr": True}


def p1_inproj(k, c):
    nc = k.nc
    m0 = k.mark()
    x = k.dram("x", [T, D], F32)
    ctxd = k.dram("ctx", [CTX, D], F32)
    modd = k.dram("mod", [8, 128, D], F32)
    w_in_d = k.dram("ev_w_in_r", [128, 8 * EVEN_IN], F32)
    ropecs = k.dram("rope_cs", [32, 128, 128], F32)
    ropesn = k.dram("rope_sn", [32, 128, 128], F32)
    QT = k.dram("QT", [4, 128, T], BF16)
    KT = k.dram("KT", [2, 128, TK], BF16)
    V = k.dram("V", [NTILE, 128, 256], BF16)
    ZBT = k.dram("ZBT", [4, 128, T], BF16)
    ZA = k.dram("ZA", [T, 512], BF16)
    GPRE = k.dram("GPRE", [12, 128, GP_W], BF16)
    AB = k.dram("AB", [NTILE, 128, 16], F32)

    w_in = k.alloc("w_in0", 8 * EVEN_IN, BF16)
    for kc in range(8):
        k.dma(w_in[:, kc * EVEN_IN:(kc + 1) * EVEN_IN], w_in_d[:, kc * EVEN_IN:(kc + 1) * EVEN_IN], r=[("dram", "ev_w_in_r")], w=[w_in.k(kc)], q="pool")
    Am = k.alloc("Am", D, F32)
    Bm = k.alloc("Bm", D, F32)
    G6 = k.alloc("G6", 768, F32)
    for h in range(4):
        k.dma(G6[:, h * 128:(h + 1) * 128], row_bc(k, "qn_g"), r=[("dram", "rows")], w=[G6.k()])
    for h in range(2):
        k.dma(G6[:, 512 + h * 128:512 + (h + 1) * 128], row_bc(k, "kn_g"), r=[("dram", "rows")], w=[G6.k()])
    zt = k.alloc("zt", 16, BF16)
    k.dve("memset", zt[:, :], 0.0, w=[zt.k()])
    for j in range(12 if P1_FLAGS["pads"] else 0):
        k.dma(GPRE[j][:, 0:2], zt[:, 0:2], r=[zt.k()], w=[("dram", "GPRE", j, "p0")])
        k.dma(GPRE[j][:, 2 + CTX:GP_LAT0], zt[:, 0:4], r=[zt.k()], w=[("dram", "GPRE", j, "p1")])
        k.dma(GPRE[j][:, GP_LAT0 + T:GP_W], zt[:, 0:2], r=[zt.k()], w=[("dram", "GPRE", j, "p2")])
    tmps = [alloc_prep_tmp(k, i) for i in range(2)]
    hTs = [k.alloc(f"hT{i}", 8 * 512, BF16) for i in range(2)]
    CS = [k.alloc(f"CS{i}", 128, F32) for i in range(2)]
    SN = [k.alloc(f"SN{i}", 128, F32) for i in range(2)]
    QK = [k.alloc(f"QK{i}", 768, F32) for i in range(2)]
    SQ = [k.alloc(f"SQ{i}", 768, F32) for i in range(2)]
    ssq = [k.alloc(f"ssq{i}", 24, F32) for i in range(2)]
    QN = [k.alloc(f"QN{i}", 768, F32) for i in range(2)]
    R1 = [k.alloc(f"R1{i}", 768, F32) for i in range(2)]
    R2 = [k.alloc(f"R2{i}", 768, F32) for i in range(2)]
    QKb = [k.alloc(f"QKb{i}", 768, BF16) for i in range(2)]
    zas = [k.alloc(f"zas{i}", 512, BF16) for i in range(2)]
    vs = [k.alloc(f"vs{i}", 256, BF16) for i in range(2)]
    abs_ = [k.alloc(f"abs{i}", 16, F32) for i in range(2)]
    QTs = [k.alloc(f"QTs{i}", 4 * 512, BF16) for i in range(2)]
    KTs = [k.alloc(f"KTs{i}", 2 * 512, BF16) for i in range(2)]
    gps = [k.alloc(f"gps{i}", 512, BF16) for i in range(3)]
    zbs = [k.alloc(f"zbs{i}", 512, BF16) for i in range(3)]
    ps_t = [k.PS[0], k.PS[0]]
    ps_za, ps_q, ps_kv = k.PS[2], k.PS[3], k.PS[4]
    PSX = [k.PS[5], k.PS[1]]
    ps_f = [k.PS[6], k.PS[7]]
    cnt = 0
    fcnt = 0
    mts = [("ctx", 0, 256)] + [("lat", m * 512, 512) for m in range(T // 512)]
    mt_list = mts[:P1_FLAGS.get("nmt", 9)]

    def prep_m(mi):
        kind, t0, W = mt_list[mi]
        lat = kind == "lat"
        if mi == 0:
            k.dma(Am[:, :], modd[3], r=[("dram", "mod")], w=[Am.k()])
            k.dma(Bm[:, :], modd[4], r=[("dram", "mod")], w=[Bm.k()])
        elif mi == 1:
            k.dma(Am[:, :], modd[0], r=[("dram", "mod")], w=[Am.k()])
            k.dma(Bm[:, :], modd[1], r=[("dram", "mod")], w=[Bm.k()])
        hT = hTs[mi % 2]
        cnt0 = 0 if mi == 0 else 2 + 4 * (mi - 1)

        def sub(s, slot):
            r0 = t0 + s * 128
            src = x[r0:r0 + 128, :] if lat else ctxd[r0:r0 + 128, :]
            skey = ("dram", "x" if lat else "ctx", r0 // 128)
            return prep_rows_gen(k, c, src, skey, Am, Bm, hT, s * 128, 128, tmps[slot], ps_t[slot], cnt0 + s)
        il = Interleaver([functools.partial(sub, s) for s in range(W // 128)], 1, slotted=True)
        while il.step():
            yield

    def comp_m(mi):
        kind, t0, W = mt_list[mi]
        lat = kind == "lat"
        hT = hTs[mi % 2]
        hT3 = hT[:, :].rearrange("p (a b) -> p a b", a=8)
        nsub = W // 128
        qts, kts = QTs[mi % 2], KTs[mi % 2]
        fcnt = 0 if mi == 0 else 12 + 16 * (mi - 1)
        def tm_group(s, ps_ap, pskey, col0, n):
            for kc in range(8):
                k.pe("matmul", ps_ap, lhsT=hT3[:, kc, s * 128:(s + 1) * 128], rhs=w_in[:, kc * EVEN_IN + col0:kc * EVEN_IN + col0 + n],
                     start=(kc == 0), stop=(kc == 7), r=[hT.k(), w_in.k(kc)], w=[pskey])

        def stage_a(s):
            sl = s % 2
            r0 = t0 + s * 128
            tile_id = (r0 // 128) if lat else (32 + r0 // 128)
            psx = PSX[sl]
            qk = QK[sl]
            if lat:
                tm_group(s, ps_za[:, :], ps_za.k(), C_ZA, 512)
                zst = zas[sl]
                k.act("activation", out=zst[:, :], in_=ps_za[:, :], func=AF.Silu, r=[ps_za.k()], w=[zst.k()])
                k.dma(ZA[r0:r0 + 128, :], zst[:, :], r=[zst.k()], w=[("dram", "ZA", r0 // 128)])
                yield
                tm_group(s, ps_q[:, :], ps_q.k(), C_QB, 512)
                k.act("copy", out=qk[:, 0:512], in_=ps_q[:, :], r=[ps_q.k()], w=[qk.k()])
                yield
            tm_group(s, psx[:, 496:512], psx.k(), C_AB, 16)
            abst = abs_[sl]
            k.dve("tensor_copy", out=abst[:, :], in_=psx[:, 496:512], r=[psx.k()], w=[abst.k()])
            k.dma(AB[tile_id], abst[:, :], r=[abst.k()], w=[("dram", "AB", tile_id)])
            yield
            tm_group(s, ps_kv[:, :], ps_kv.k(), C_KB, 512)
            vst = vs[sl]
            k.act("copy", out=vst[:, :], in_=ps_kv[:, 256:512], r=[ps_kv.k()], w=[vst.k()])
            k.dma(V[tile_id], vst[:, :], r=[vst.k()], w=[("dram", "V", tile_id)])
            yield
            k.dve("tensor_copy", out=qk[:, 512:768], in_=ps_kv[:, 0:256], r=[ps_kv.k()], w=[qk.k()])
            yield

        def stage_b(s):
            sl = s % 2
            r0 = t0 + s * 128
            psx = PSX[sl]
            pxT = psx.ap.bitcast(BF16)
            qk, ssq_t, qkb, sq_, qn_, r1_, r2_ = QK[sl], ssq[sl], QKb[sl], SQ[sl], QN[sl], R1[sl], R2[sl]
            c0 = 0 if lat else 512
            nh = 6 if lat else 2
            h0 = 0 if lat else 4
            k.pool("tensor_tensor", out=sq_[:, c0:768], in0=qk[:, c0:768], in1=qk[:, c0:768], op=ALU.mult, r=[qk.k()], w=[sq_.k()])
            yield
            k.dve("tensor_reduce", out=ssq_t[:, h0:6], in_=sq_[:, c0:768].rearrange("p (h d) -> p h d", d=128), axis=AX.X, op=ALU.add,
                  r=[sq_.k()], w=[ssq_t.k()])
            yield
            k.act("activation", out=ssq_t[:, 8 + h0:14], in_=ssq_t[:, h0:6], func=AF.Sqrt, scale=1.0 / 128, bias=NORM_EPS,
                  r=[ssq_t.k()], w=[ssq_t.k()])
            yield
            k.dve("reciprocal", out=ssq_t[:, 16 + h0:22], in_=ssq_t[:, 8 + h0:14], r=[ssq_t.k()], w=[ssq_t.k()])
            yield
            k.dve("tensor_tensor", out=qn_[:, c0:768].rearrange("p (h d) -> p h d", d=128), in0=qk[:, c0:768].rearrange("p (h d) -> p h d", d=128),
                  in1=ssq_t[:, 16 + h0:22].unsqueeze(2).to_broadcast([128, nh, 128]), op=ALU.mult, r=[qk.k(), ssq_t.k()], w=[qn_.k()])
            yield
            if not lat:
                k.pool("tensor_tensor", out=qkb[:, c0:768], in0=qn_[:, c0:768], in1=G6[:, c0:768], op=ALU.mult, r=[qn_.k(), G6.k()], w=[qkb.k()])
                yield
            else:
                cs, sn = CS[sl], SN[sl]
                k.dma(cs[:, :], ropecs[r0 // 128], r=[("dram", "rope_cs")], w=[cs.k()])
                k.dma(sn[:, :], ropesn[r0 // 128], r=[("dram", "rope_sn")], w=[sn.k()])
                k.pool("tensor_tensor", out=qn_[:, :], in0=qn_[:, :], in1=G6[:, :], op=ALU.mult, r=[qn_.k(), G6.k()], w=[qn_.k()])
                yield
                k.dve("tensor_tensor", out=r1_[:, :].rearrange("p (h d) -> p h d", d=128), in0=qn_[:, :].rearrange("p (h d) -> p h d", d=128),
                      in1=cs[:, :].unsqueeze(1).to_broadcast([128, 6, 128]), op=ALU.mult, r=[qn_.k(), cs.k()], w=[r1_.k()])
                yield
                qn5 = qn_[:, :].rearrange("p (h a b e) -> p h a b e", h=6, a=2, b=2)
                r25 = r2_[:, :].rearrange("p (h a b e) -> p h a b e", h=6, a=2, b=2)
                sn4 = sn[:, :].rearrange("p (a b e) -> p a b e", a=2, b=2)
                for bsel in range(2):
                    k.pool("tensor_tensor", out=r25[:, :, :, bsel, :], in0=qn5[:, :, :, 1 - bsel, :],
                           in1=sn4[:, :, bsel, :].unsqueeze(1).to_broadcast([128, 6, 2, 32]), op=ALU.mult, r=[qn_.k(), sn.k()], w=[r2_.k()])
                    yield
                k.dve("tensor_tensor", out=qkb[:, :], in0=r1_[:, :], in1=r2_[:, :], op=ALU.add, r=[r1_.k(), r2_.k()], w=[qkb.k()])
                yield
            for h in range(h0, 6):
                k.pe("transpose", pxT[:, h * 128:(h + 1) * 128], qkb[:, h * 128:(h + 1) * 128], c["identb"][:, :],
                     r=[qkb.k(), c["identb"].k()], w=[psx.k()])
            yield
            if lat:
                k.act("copy", out=qts[:, :].rearrange("p (h t) -> p h t", h=4)[:, :, s * 128:(s + 1) * 128],
                      in_=pxT[:, 0:512].rearrange("p (h t) -> p h t", h=4), r=[psx.k()], w=[qts.k()])
                yield
            k.dve("tensor_copy", out=kts[:, :].rearrange("p (h t) -> p h t", h=2)[:, :, s * 128:(s + 1) * 128],
                  in_=pxT[:, 512:768].rearrange("p (h t) -> p h t", h=2), r=[psx.k()], w=[kts.k()])
            yield

        for _ in stage_a(0):
            yield
        for s in range(nsub):
            gl = [stage_b(s)] + ([stage_a(s + 1)] if s + 1 < nsub else [])
            il = Interleaver(gl, 2)
            while il.step():
                yield
        kcol0 = t0 if lat else T + t0
        for h in range(2):
            k.dma(KT[h][:, kcol0:kcol0 + W], kts[:, h * 512:h * 512 + W], r=[kts.k()], w=[("dram", "KT", h, mi)])
        if lat:
            for h in range(4):
                k.dma(QT[h][:, t0:t0 + W], qts[:, h * 512:h * 512 + W], r=[qts.k()], w=[("dram", "QT", h, mi)])
        gcol0 = (GP_LAT0 + t0) if lat else (GP_CTX0 + t0)
        for j in range((12 + (4 if lat else 0)) if P1_FLAGS["fm"] else 0):
            yield
            pf = ps_f[fcnt % 2]
            col0 = j * 128 if j < 12 else C_ZB + (j - 12) * 128
            for kc in range(8):
                k.pe("matmul", pf[:, 0:W], lhsT=w_in[:, kc * EVEN_IN + col0:kc * EVEN_IN + col0 + 128], rhs=hT3[:, kc, 0:W],
                     start=(kc == 0), stop=(kc == 7), r=[hT.k(), w_in.k(kc)], w=[pf.k()])
            yield
            if j < 12:
                g = gps[fcnt % 3]
                if fcnt % 2 == 0:
                    k.dve("tensor_copy", out=g[:, 0:W], in_=pf[:, 0:W], r=[pf.k()], w=[g.k()])
                else:
                    k.act("copy", out=g[:, 0:W], in_=pf[:, 0:W], r=[pf.k()], w=[g.k()])
                k.dma(GPRE[j][:, gcol0:gcol0 + W], g[:, 0:W], r=[g.k()], w=[("dram", "GPRE", j, mi)])
            else:
                g = zbs[fcnt % 3]
                k.act("activation", out=g[:, 0:W], in_=pf[:, 0:W], func=AF.Silu, r=[pf.k()], w=[g.k()])
                k.dma(ZBT[j - 12][:, t0:t0 + W], g[:, 0:W], r=[g.k()], w=[("dram", "ZBT", j - 12, mi)])
            fcnt += 1

    for _ in prep_m(0):
        pass
    for mi in range(len(mt_list)):
        gl = [comp_m(mi)] + ([prep_m(mi + 1)] if mi + 1 < len(mt_list) else [])
        run_interleaved(gl, 2)
    k.release(m0)


def _rearr_w(w):
    n = w.shape[1]
    return np.ascontiguousarray(w.reshape(8, 128, n).transpose(1, 0, 2).reshape(128, 8 * n))


def _fm(v):
    return np.ascontiguousarray(v.reshape(-1, 128).T)


def _rope_tables():
    t = np.arange(T)
    row = (t // 64).astype(np.float32)
    col = (t % 64).astype(np.float32)
    inv = (10000.0 ** (-np.arange(0, 64, 2, dtype=np.float32) / 64)).astype(np.float32)
    ar = row[:, None] * inv
    ac = col[:, None] * inv
    cs = np.concatenate([np.cos(ar), np.cos(ar), np.cos(ac), np.cos(ac)], 1).astype(np.float32)
    sn = np.concatenate([-np.sin(ar), np.sin(ar), -np.sin(ac), np.sin(ac)], 1).astype(np.float32)
    return cs.reshape(32, 128, 128), sn.reshape(32, 128, 128)


def host_prep(inp):
    f = lambda a: np.asarray(a, dtype=np.float32)
    shared = {}
    shared["c_ident"] = np.eye(128, dtype=np.float32)
    sel = np.zeros((2, 256), np.float32)
    sel[0, :128] = 1.0
    sel[1, 128:] = 1.0
    shared["c_sel"] = sel
    shared["ada_w_r"] = np.stack([_rearr_w(f(inp["ada_w"][l])) for l in range(2)])
    rows = np.zeros((1, ROWS_N), np.float32)
    vals = {"ada_b0": inp["ada_b"][0], "ada_b1": inp["ada_b"][1], "pre_g0": inp["pre_norm_g"][0], "pre_g1": inp["pre_norm_g"][1],
            "post_g0": inp["post_norm_g"][0], "post_g1": inp["post_norm_g"][1], "b_out": inp["od_b_out"][0], "gdn_g": inp["ev_gdn_norm_g"][0],
            "qn_g": inp["ev_q_norm_g"][0], "kn_g": inp["ev_k_norm_g"][0], "a_log": f(inp["ev_a_log"][0]).reshape(-1), "dt_bias": f(inp["ev_dt_bias"][0]).reshape(-1)}
    for n_, v in vals.items():
        o, l = ROW_OFF[n_]
        rows[0, o:o + l] = f(v).reshape(-1)
    shared["rows"] = rows
    shared["ev_w_in_r"] = _rearr_w(f(inp["ev_w_in"][0]))
    shared["ev_w_out_r"] = _rearr_w(f(inp["ev_w_out"][0]))
    shared["od_w_in_r"] = _rearr_w(f(inp["od_w_in"][0]))
    shared["od_w_out_r"] = _rearr_w(f(inp["od_w_out"][0]))
    cs, sn = _rope_tables()
    shared["rope_cs"] = cs
    shared["rope_sn"] = sn
    pv = np.zeros((128, PV_N), np.float32)
    pv[:, PV_BIN:PV_BIN + 24] = _fm(f(inp["od_b_in"][0]))
    pv[:, PV_DWB:PV_DWB + 8] = _fm(f(inp["od_dw_b"][0]))
    pv[:, PV_LNG:PV_LNG + 8] = _fm(f(inp["od_ln_g"][0]))
    pv[:, PV_LNB:PV_LNB + 8] = _fm(f(inp["od_ln_b"][0]))
    pv[:, PV_DWW:PV_DWW + 248] = f(inp["od_dw_w"][0]).T.reshape(8, 128, 31).transpose(1, 0, 2).reshape(128, 248)
    pv[:, PV_C5W:PV_C5W + 60] = f(inp["ev_short_conv_w"][0]).T.reshape(12, 128, 5).transpose(1, 0, 2).reshape(128, 60)
    shared["pv"] = pv
    maps = []
    cctx = _fm(f(inp["c_ctx"]))
    for b in range(8):
        m = dict(shared)
        m["x"] = np.ascontiguousarray(f(inp["x"][b]))
        m["ctx"] = np.ascontiguousarray(f(inp["ctx"][b]))
        cv = np.zeros((128, 16), np.float32)
        cv[:, 0::2] = _fm(f(inp["c"][b]))
        cv[:, 1::2] = cctx
        m["cvec"] = cv
        maps.append(m)
    return maps


NKC = TK // 128


def p2b_attn(k, c, banks=None, as_gen=False):
    nc = k.nc
    m0 = None if as_gen else k.mark()
    QTd = k.dram("QT", [4, 128, T], BF16)
    KTd = k.dram("KT", [2, 128, TK], BF16)
    Vd = k.dram("V", [NTILE, 128, 256], BF16)
    ZBT = k.dram("ZBT", [4, 128, T], BF16)
    YT = k.dram("YT", [8, 128, T], BF16)
    QTs = k.alloc("QTa", 4 * T, BF16)
    KTs = k.alloc("KTa", 2 * TK, BF16)
    Vs = k.alloc("Va", NTILE * 256, BF16)
    for h in range(4):
        for hf in range(2):
            k.dma(QTs[:, h * T + hf * 2048:h * T + (hf + 1) * 2048], QTd[h][:, hf * 2048:(hf + 1) * 2048],
                  r=[("dram", "QT", h, m) for m in range(1, 9)], w=[QTs.k(h)])
    for h in range(2):
        k.dma(KTs[:, h * TK:(h + 1) * TK], KTd[h], r=[("dram", "KT", h, m) for m in range(9)], w=[KTs.k(h)])
    for t in range(NTILE):
        k.dma(Vs[:, t * 256:(t + 1) * 256], Vd[t], r=[("dram", "V", t)], w=[Vs.k(t)])
    onesb = k.alloc("onesb", 128, BF16)
    k.dve("memset", onesb[:, :], 1.0, w=[onesb.k()])
    PT = [k.alloc(f"PT{i}", 512, BF16) for i in range(4)]
    RD = [k.alloc(f"RD{i}", 512, F32) for i in range(2)]
    OO = [k.alloc(f"OO{i}", 512, F32) for i in range(2)]
    ZG = [k.alloc(f"ZGa{i}", 512, BF16) for i in range(2)]
    YB = [k.alloc(f"YB{i}", 512, BF16) for i in range(2)]
    if banks is None:
        ps_s = [k.PS[0], k.PS[1], k.PS[2], k.PS[3]]
        ps_o = [k.PS[4], k.PS[5]]
        ps_d = [k.PS[6], k.PS[7]]
    else:
        ps_s = [banks[0], banks[1]]
        ps_o = [banks[2], banks[2]]
        ps_d = [banks[3], banks[3]]
    NPS = len(ps_s)
    scale = 128.0 ** -0.5

    def body():
      it = 0
      sc = 0
      for qi in range(T // 512):
        for h in range(4):
            kv = h // 2
            po, pd = ps_o[it % 2], ps_d[it % 2]
            rd, oo, zg, yb = RD[it % 2], OO[it % 2], ZG[it % 2], YB[it % 2]
            k.dma(zg[:, :], ZBT[h][:, qi * 512:(qi + 1) * 512], r=[("dram", "ZBT", h, qi + 1)], w=[zg.k()])
            qsl = QTs[:, h * T + qi * 512:h * T + (qi + 1) * 512]

            def score(kc):
                ps = ps_s[(sc + kc) % NPS]
                k.pe("matmul", ps[:, :], lhsT=KTs[:, kv * TK + kc * 128:kv * TK + (kc + 1) * 128], rhs=qsl, start=True, stop=True,
                     r=[KTs.k(kv), QTs.k(h)], w=[ps.k()])
                pt = PT[(sc + kc) % 4]
                k.act("activation", out=pt[:, :], in_=ps[:, :], func=AF.Exp, scale=scale, r=[ps.k()], w=[pt.k()])

            score(0)
            score(1)
            for kc in range(NKC):
                if kc + 2 < NKC:
                    score(kc + 2)
                pt = PT[(sc + kc) % 4]
                tile_id = kc if kc < 32 else kc
                k.pe("matmul", po[:, :], lhsT=Vs[:, kc * 256 + kv * 128:kc * 256 + (kv + 1) * 128], rhs=pt[:, :], start=(kc == 0), stop=(kc == NKC - 1),
                     r=[Vs.k(kc), pt.k()], w=[po.k()])
                k.pe("matmul", pd[:, :], lhsT=onesb[:, :], rhs=pt[:, :], start=(kc == 0), stop=(kc == NKC - 1),
                     r=[onesb.k(), pt.k()], w=[pd.k()])
                yield
            sc += NKC
            k.dve("reciprocal", out=rd[:, :], in_=pd[:, :], r=[pd.k()], w=[rd.k()])
            k.dve("tensor_tensor", out=oo[:, :], in0=po[:, :], in1=rd[:, :], op=ALU.mult, r=[po.k(), rd.k()], w=[oo.k()])
            k.pool("tensor_tensor", out=yb[:, :], in0=oo[:, :], in1=zg[:, :], op=ALU.mult, r=[oo.k(), zg.k()], w=[yb.k()])
            k.dma(YT[4 + h][:, qi * 512:(qi + 1) * 512], yb[:, :], r=[yb.k()], w=[("dram", "YT", 4 + h, qi)])
            it += 1
            yield

    if as_gen:
        return body()
    for _ in body():
        pass
    k.release(m0)


def p3_outproj(k, c):
    nc = k.nc
    m0 = k.mark()
    x = k.dram("x", [T, D], F32)
    x1 = k.dram("x1", [T, D], F32)
    modd = k.dram("mod", [8, 128, D], F32)
    YT = k.dram("YT", [8, 128, T], BF16)
    w_out_d = k.dram("ev_w_out_r", [128, 8 * 1024], F32)
    w_out = k.alloc("w_out0", 8 * 1024, BF16)
    k.dma(w_out[:, :], w_out_d, r=[("dram", "ev_w_out_r")], w=[w_out.k()], q="pool")
    G0 = k.alloc("G0", D, F32)
    k.dma(G0[:, :], modd[2], r=[("dram", "mod")], w=[G0.k()])
    YTt = [k.alloc(f"YTt{i}", 8 * 512, BF16) for i in range(2)]
    O = [k.alloc(f"O{i}", D, F32) for i in range(2)]
    junks = [k.alloc(f"junkp3{i}", D, BF16) for i in range(2)]
    ss = [k.alloc(f"ssp{i}", 8, F32) for i in range(2)]
    XR = [k.alloc(f"XR{i}", D, F32) for i in range(2)]
    ps_o = [(k.PS[0], k.PS[1]), (k.PS[2], k.PS[3])]
    oc = 0
    for m in range(T // 512):
        yt = YTt[m % 2]
        for j in range(8):
            k.dma(yt[:, j * 512:(j + 1) * 512], YT[j][:, m * 512:(m + 1) * 512], r=[("dram", "YT", j, m)], w=[yt.k(j)])
        def subtile(s, slot):
            r0 = m * 512 + s * 128
            po = ps_o[slot]
            o, sst, xr, jk = O[slot], ss[slot], XR[slot], junks[slot]
            k.dma(xr[:, :], x[r0:r0 + 128, :], r=[("dram", "x", r0 // 128)], w=[xr.k()])
            for nh in range(2):
                for j in range(8):
                    k.pe("matmul", po[nh][:, :], lhsT=yt[:, j * 512 + s * 128:j * 512 + (s + 1) * 128], rhs=w_out[:, j * 1024 + nh * 512:j * 1024 + (nh + 1) * 512],
                         start=(j == 0), stop=(j == 7), r=[yt.k(j), w_out.k()], w=[po[nh].k()])
                yield
            k.act("copy", out=o[:, 0:512], in_=po[0][:, :], r=[po[0].k()], w=[o.k()])
            yield
            k.dve("tensor_copy", out=o[:, 512:1024], in_=po[1][:, :], r=[po[1].k()], w=[o.k()])
            yield
            yield from post_res_gen(k, o, sst, jk, G0, xr, o, xr, x1[r0:r0 + 128, :], ("dram", "x1", r0 // 128))
        run_interleaved([functools.partial(subtile, s) for s in range(4)], 2, slotted=True)
    k.release(m0)


def p1b_gdnprep(k, c):
    nc = k.nc
    m0 = k.mark()
    GPRE = k.dram("GPRE", [12, 128, GP_W], BF16)
    pvd = k.dram("pv", [128, PV_N], F32)
    GQT = k.dram("GQT", [4, 128, TK], BF16)
    GKT = k.dram("GKT", [4, 128, TK], BF16)
    GK = k.dram("GK", [NTILE, 128, 512], BF16)
    GV = k.dram("GV", [NTILE, 128, 512], BF16)
    pv = k.alloc("pv", PV_N, F32)
    k.dma(pv[:, :], pvd, r=[("dram", "pv")], w=[pv.k()])
    DG = k.alloc("DG5", 60 * 128, BF16)
    for i in range(60):
        eng = "dve" if i % 2 == 0 else "pool"
        k.any(eng, "tensor_scalar", out=DG[:, i * 128:(i + 1) * 128], in0=c["identf"][:, :], scalar1=pv[:, PV_C5W + i:PV_C5W + i + 1], scalar2=None,
              op0=ALU.mult, r=[c["identf"].k(), pv.k()], w=[DG.k(i // 5)])
    onesb = k.alloc("onesb", 128, BF16)
    k.dve("memset", onesb[:, :], 1.0, w=[onesb.k()])
    PRE = [k.alloc(f"PRE5{i}", 516, BF16) for i in range(8)]
    U = [k.alloc(f"U5{i}", 512, F32) for i in range(8)]
    SQ = [k.alloc(f"SQ5{i}", 512, BF16) for i in range(8)]
    RS = [k.alloc(f"RS5{i}", 512, F32) for i in range(8)]
    UN = [k.alloc(f"UN5{i}", 512, BF16) for i in range(8)]
    GKs = [k.alloc(f"GKs{i}", 4 * 512, BF16) for i in range(2)]
    GVs = [k.alloc(f"GVs{i}", 4 * 512, BF16) for i in range(2)]
    ps_cv = list(k.PS)
    ps_ss = ps_cv
    ps_tr = ps_cv
    mts = [("ctx", 0, 256)] + [("lat", m * 512, 512) for m in range(T // 512)]
    cc = 0
    for mi, (kind, t0, W) in enumerate(mts):
        lat = kind == "lat"
        gcol0 = (GP_LAT0 + t0) if lat else (GP_CTX0 + t0)
        tile0 = (t0 // 128) if lat else 32
        col0 = tile0 * 128
        nsub = W // 128
        gks, gvs = GKs[mi % 2], GVs[mi % 2]
        def chunk(j, cc, sl):
            pre = PRE[sl]
            pcv = ps_cv[sl]
            k.dma(pre[:, 0:W + 4], GPRE[j][:, gcol0 - 2:gcol0 + W + 2],
                  r=[("dram", "GPRE", j, x_) for x_ in (["p0", "p1", "p2"] + list(range(max(0, mi - 1), min(9, mi + 2))))], w=[pre.k()])
            for t in range(5):
                k.pe("matmul", pcv[:, 0:W], lhsT=DG[:, (j * 5 + t) * 128:(j * 5 + t + 1) * 128], rhs=pre[:, t:t + W], start=(t == 0), stop=(t == 4),
                     r=[DG.k(j), pre.k()], w=[pcv.k()])
            un = UN[sl]
            if j < 8:
                u, sq, rs, pss = U[sl], SQ[sl], RS[sl], ps_ss[sl]
                k.act("activation", out=u[:, 0:W], in_=pcv[:, 0:W], func=AF.Silu, r=[pcv.k()], w=[u.k()])
                yield
                k.pool("tensor_tensor", out=sq[:, 0:W], in0=u[:, 0:W], in1=u[:, 0:W], op=ALU.mult, r=[u.k()], w=[sq.k()])
                yield
                k.pe("matmul", pss[:, 0:W], lhsT=onesb[:, :], rhs=sq[:, 0:W], start=True, stop=True, r=[onesb.k(), sq.k()], w=[pss.k()])
                yield
                k.act("activation", out=rs[:, 0:W], in_=pss[:, 0:W], func=AF.Sqrt, bias=NORM_EPS, scale=1.0, r=[pss.k()], w=[rs.k()])
                yield
                k.dve("reciprocal", out=rs[:, 0:W], in_=rs[:, 0:W], r=[rs.k()], w=[rs.k()])
                yield
                if j < 4:
                    k.dve("scalar_tensor_tensor", out=un[:, 0:W], in0=u[:, 0:W], scalar=128.0 ** -0.5, in1=rs[:, 0:W], op0=ALU.mult, op1=ALU.mult,
                          r=[u.k(), rs.k()], w=[un.k()])
                    k.dma(GQT[j][:, col0:col0 + W], un[:, 0:W], r=[un.k()], w=[("dram", "GQT", j, mi)])
                    yield
                else:
                    k.dve("tensor_tensor", out=un[:, 0:W], in0=u[:, 0:W], in1=rs[:, 0:W], op=ALU.mult, r=[u.k(), rs.k()], w=[un.k()])
                    yield
                    k.dma(GKT[j - 4][:, col0:col0 + W], un[:, 0:W], r=[un.k()], w=[("dram", "GKT", j - 4, mi)])
                    yield
            else:
                k.act("activation", out=un[:, 0:W], in_=pcv[:, 0:W], func=AF.Silu, r=[pcv.k()], w=[un.k()])
                yield
            if j >= 4:
                h = (j - 4) % 4
                ptr = ps_tr[sl]
                ptb = ptr.ap.bitcast(BF16)
                for s in range(nsub):
                    k.pe("transpose", ptb[:, s * 128:(s + 1) * 128], un[:, s * 128:(s + 1) * 128], c["identb"][:, :], r=[un.k(), c["identb"].k()], w=[ptr.k()])
                    yield
                dst = (gks if j < 8 else gvs)
                dview = dst[:, :].rearrange("p (s f) -> p s f", s=4)[:, 0:nsub, h * 128:(h + 1) * 128]
                sview = ptb[:, 0:nsub * 128].rearrange("p (s f) -> p s f", f=128)
                if cc % 2 == 0:
                    k.act("copy", out=dview, in_=sview, r=[ptr.k()], w=[dst.k()])
                    yield
                else:
                    k.dve("tensor_copy", out=dview, in_=sview, r=[ptr.k()], w=[dst.k()])
                    yield
            yield
        run_interleaved([functools.partial(chunk, j, cc + j) for j in range(12)], 8, slotted=True)
        cc += 12
        for s in range(nsub):
            k.dma(GK[tile0 + s], gks[:, s * 512:(s + 1) * 512], r=[gks.k()], w=[("dram", "GK", tile0 + s)])
            k.dma(GV[tile0 + s], gvs[:, s * 512:(s + 1) * 512], r=[gvs.k()], w=[("dram", "GV", tile0 + s)])
    k.release(m0)


GDN_LAG = 45


def p2a_gdn(k, c, nslots=4, as_gens=False):
    nc = k.nc
    m0 = None if as_gens else k.mark()
    GQT = k.dram("GQT", [4, 128, TK], BF16)
    GKT = k.dram("GKT", [4, 128, TK], BF16)
    GK = k.dram("GK", [NTILE, 128, 512], BF16)
    GV = k.dram("GV", [NTILE, 128, 512], BF16)
    ABd = k.dram("AB", [NTILE, 128, 16], F32)
    trid = k.dram("c_tri", [9, 128, 128], F32)
    OD = [k.dram("OF", [32, 128, 512], F32), k.dram("OB", [32, 128, 512], F32)]
    TRI = k.alloc("TRI", 9 * 128, F32)
    for i in range(9):
        k.dma(TRI[:, i * 128:(i + 1) * 128], trid[i], r=[("dram", "c_tri")], w=[TRI.k()])
    tri = lambda i: TRI[:, i * 128:(i + 1) * 128]
    bc4 = lambda ap: ap.unsqueeze(1).to_broadcast([128, 4, 128])
    col4 = lambda ap: ap.unsqueeze(2).to_broadcast([128, 4, 128])
    v3 = lambda t: t[:, :].rearrange("p (h f) -> p h f", h=4)
    ABs = k.alloc("ABs", NTILE * 16, F32)
    for t in range(NTILE):
        k.dma(ABs[:, t * 16:(t + 1) * 16], ABd[t], r=[("dram", "AB", t)], w=[ABs.k()])
    alog = k.alloc("alog", 8, F32)
    dtb = k.alloc("dtb", 8, F32)
    k.dma(alog[:, :], row_bc(k, "a_log"), r=[("dram", "rows")], w=[alog.k()])
    k.dma(dtb[:, :], row_bc(k, "dt_bias"), r=[("dram", "rows")], w=[dtb.k()])
    GALL = k.alloc("GALL", NTILE * 8, F32)
    BALL = k.alloc("BALL", NTILE * 8, F32)
    ab3 = ABs[:, :].rearrange("p (t f) -> p t f", f=16)
    g3 = GALL[:, :].rearrange("p (t f) -> p t f", f=8)
    b3 = BALL[:, :].rearrange("p (t f) -> p t f", f=8)
    bct = lambda ap: ap.unsqueeze(1).to_broadcast([128, NTILE, 8])
    k.dve("tensor_tensor", out=g3, in0=ab3[:, :, 0:8], in1=bct(dtb[:, :]), op=ALU.add, r=[ABs.k(), dtb.k()], w=[GALL.k()])
    k.act("activation", out=GALL[:, :], in_=GALL[:, :], func=AF.Exp, r=[GALL.k()], w=[GALL.k()])
    k.act("activation", out=GALL[:, :], in_=GALL[:, :], func=AF.Ln, bias=1.0, scale=1.0, r=[GALL.k()], w=[GALL.k()])
    k.act("activation", out=alog[:, :], in_=alog[:, :], func=AF.Exp, r=[alog.k()], w=[alog.k()])
    k.dve("scalar_tensor_tensor", out=g3, in0=g3, scalar=-1.0, in1=bct(alog[:, :]), op0=ALU.mult, op1=ALU.mult, r=[GALL.k(), alog.k()], w=[GALL.k()])
    k.act("activation", out=b3, in_=ab3[:, :, 8:16], func=AF.Exp, scale=-1.0, r=[ABs.k()], w=[BALL.k()])
    k.dve("tensor_scalar", out=BALL[:, :], in0=BALL[:, :], scalar1=1.0, scalar2=None, op0=ALU.add, r=[BALL.k()], w=[BALL.k()])
    k.dve("reciprocal", out=BALL[:, :], in_=BALL[:, :], r=[BALL.k()], w=[BALL.k()])
    Sf = [k.alloc(f"Sf{d}", 512, F32) for d in range(2)]
    Sb = [k.alloc(f"Sb{d}", 512, BF16) for d in range(2)]
    for d in range(2):
        k.dve("memset", Sf[d][:, :], 0.0, w=[Sf[d].k()])
        k.pool("memset", Sb[d][:, :], 0.0, w=[Sb[d].k()])
    def bufs(d):
        B = {}
        for n_ in ["qT4", "kT4", "ktok", "vtok", "X", "XT", "PT", "AINC", "AINCT", "KD", "ATn", "QEFF", "N1", "N1T", "N2", "P", "V1", "U1"]:
            B[n_] = k.alloc(f"{n_}{d}", 512, BF16)
        for n_ in ["WUR", "WU"]:
            B[n_] = k.alloc(f"{n_}{d}", 1024, BF16)
        for n_ in ["DIFF", "E", "DMS", "DMI", "T1", "ER", "QD", "OUT"]:
            B[n_] = k.alloc(f"{n_}{d}", 512, F32)
        B["SM"] = k.alloc(f"SM{d}", 32, F32)
        return B
    BUF = [bufs(i) for i in range(nslots)]
    order = [[32, 33] + list(range(32)), [33, 32] + list(range(31, -1, -1))]
    cfg = [dict(tri=2, mi=0, ms=1, jl=127, m1a=5, m1b=6, m2a=7), dict(tri=0, mi=2, ms=3, jl=0, m1a=6, m1b=5, m2a=8)]
    identb = c["identb"]
    sdone = {}

    def unit(n, d, slot):
        Q = k.PS[2 * slot:2 * slot + 2]
        P_GR = P_B = P_W0 = P_Z = Q[0]
        P_SM = P_A = P_T = P_W1 = P_Z2 = Q[1]
        if n == 1 and nslots == 4:
            for _ in range(GDN_LAG):
                yield
        g = order[d][n]
        B = BUF[slot]
        cf = cfg[d]
        lat = g < 32
        sm = B["SM"]
        GC, GL, EG, BE, KDS, GT, TMP = (sm[:, 0:4], sm[:, 4:8], sm[:, 8:12], sm[:, 12:16], sm[:, 16:20], sm[:, 20:24], sm[:, 24:28])
        gcol = GALL[:, g * 8 + d * 4:g * 8 + d * 4 + 4]
        bcol = BALL[:, g * 8 + d * 4:g * 8 + d * 4 + 4]
        for h in range(4):
            k.dma(B["qT4"][:, h * 128:(h + 1) * 128], GQT[h][:, g * 128:(g + 1) * 128], r=[("dram", "GQT", h, mi_) for mi_ in range(9)], w=[B["qT4"].k()])
            k.dma(B["kT4"][:, h * 128:(h + 1) * 128], GKT[h][:, g * 128:(g + 1) * 128], r=[("dram", "GKT", h, mi_) for mi_ in range(9)], w=[B["kT4"].k()])
        k.dma(B["ktok"][:, :], GK[g], r=[("dram", "GK", g)], w=[B["ktok"].k()])
        yield
        k.dma(B["vtok"][:, :], GV[g], r=[("dram", "GV", g)], w=[B["vtok"].k()])
        yield
        for h in range(4):
            k.pe("matmul", P_GR[:, h * 128:(h + 1) * 128], lhsT=gcol[:, h:h + 1].to_broadcast([128, 128]), rhs=tri(cf["tri"]), start=True, stop=True,
                 r=[GALL.k(), TRI.k()], w=[P_GR.k()])
        k.pe("matmul", P_SM[:, 0:4], lhsT=tri(cf["tri"]), rhs=gcol, start=True, stop=True, r=[GALL.k(), TRI.k()], w=[P_SM.k()])
        yield
        k.act("copy", out=GC, in_=P_SM[:, 0:4], r=[P_SM.k()], w=[sm.k()])
        yield
        k.dve("tensor_copy", out=GL, in_=v3(P_GR)[:, :, cf["jl"]], r=[P_GR.k()], w=[sm.k()])
        yield
        k.dve("tensor_tensor", out=v3(B["DIFF"]), in0=col4(GC), in1=v3(P_GR), op=ALU.subtract, r=[sm.k(), P_GR.k()], w=[B["DIFF"].k()])
        yield
        k.act("activation", out=B["ER"][:, :], in_=P_GR[:, :], func=AF.Exp, r=[P_GR.k()], w=[B["ER"].k()])
        yield
        k.pool("tensor_scalar", out=B["DIFF"][:, :], in0=B["DIFF"][:, :], scalar1=0.0, scalar2=None, op0=ALU.min, r=[B["DIFF"].k()], w=[B["DIFF"].k()])
        yield
        k.act("activation", out=B["E"][:, :], in_=B["DIFF"][:, :], func=AF.Exp, r=[B["DIFF"].k()], w=[B["E"].k()])
        yield
        k.pool("tensor_tensor", out=v3(B["DMS"]), in0=v3(B["E"]), in1=bc4(tri(cf["ms"])), op=ALU.mult, r=[B["E"].k(), TRI.k()], w=[B["DMS"].k()])
        yield
        k.pool("tensor_tensor", out=v3(B["DMI"]), in0=v3(B["E"]), in1=bc4(tri(cf["mi"])), op=ALU.mult, r=[B["E"].k(), TRI.k()], w=[B["DMI"].k()])
        yield
        k.act("activation", out=EG, in_=GC, func=AF.Exp, r=[sm.k()], w=[sm.k()])
        yield
        k.dve("tensor_tensor", out=BE, in0=EG, in1=bcol, op=ALU.mult, r=[sm.k(), BALL.k()], w=[sm.k()])
        yield
        k.dve("tensor_tensor", out=TMP, in0=GL, in1=GC, op=ALU.subtract, r=[sm.k()], w=[sm.k()])
        yield
        k.act("activation", out=KDS, in_=TMP, func=AF.Exp, r=[sm.k()], w=[sm.k()])
        yield
        k.act("activation", out=GT, in_=GL, func=AF.Exp, r=[sm.k()], w=[sm.k()])
        yield
        for h in range(4):
            k.pe("matmul", P_A[:, h * 128:(h + 1) * 128], lhsT=B["kT4"][:, h * 128:(h + 1) * 128], rhs=B["kT4"][:, h * 128:(h + 1) * 128], start=True, stop=True,
                 r=[B["kT4"].k()], w=[P_A.k()])
        for h in range(4):
            k.pe("matmul", P_B[:, h * 128:(h + 1) * 128], lhsT=B["qT4"][:, h * 128:(h + 1) * 128], rhs=B["kT4"][:, h * 128:(h + 1) * 128], start=True, stop=True,
                 r=[B["qT4"].k(), B["kT4"].k()], w=[P_B.k()])
        k.dve("tensor_tensor", out=B["T1"][:, :], in0=P_A[:, :], in1=B["DMS"][:, :], op=ALU.mult, r=[P_A.k(), B["DMS"].k()], w=[B["T1"].k()])
        yield
        k.dve("scalar_tensor_tensor", out=v3(B["X"]), in0=v3(B["T1"]), scalar=-1.0, in1=col4(bcol), op0=ALU.mult, op1=ALU.mult,
               r=[B["T1"].k(), BALL.k()], w=[B["X"].k()])
        k.dve("tensor_tensor", out=B["AINC"][:, :], in0=P_B[:, :], in1=B["DMI"][:, :], op=ALU.mult, r=[P_B.k(), B["DMI"].k()], w=[B["AINC"].k()])
        yield
        ptb = P_T.ap.bitcast(BF16)
        for h in range(4):
            k.pe("transpose", ptb[:, h * 128:(h + 1) * 128], B["X"][:, h * 128:(h + 1) * 128], identb[:, :], r=[B["X"].k(), identb.k()], w=[P_T.k()])
        for h in range(4):
            k.pe("transpose", ptb[:, 512 + h * 128:512 + (h + 1) * 128], B["AINC"][:, h * 128:(h + 1) * 128], identb[:, :], r=[B["AINC"].k(), identb.k()], w=[P_T.k()])
        k.act("copy", out=B["XT"][:, :], in_=ptb[:, 0:512], r=[P_T.k()], w=[B["XT"].k()])
        yield
        k.act("copy", out=B["AINCT"][:, :], in_=ptb[:, 512:1024], r=[P_T.k()], w=[B["AINCT"].k()])
        yield
        k.pool("tensor_tensor", out=v3(B["N1"]), in0=v3(B["X"]), in1=bc4(tri(cf["m1a"])), op=ALU.mult, r=[B["X"].k(), TRI.k()], w=[B["N1"].k()])
        yield
        k.pool("tensor_tensor", out=v3(B["N1T"]), in0=v3(B["XT"]), in1=bc4(tri(cf["m1b"])), op=ALU.mult, r=[B["XT"].k(), TRI.k()], w=[B["N1T"].k()])
        yield
        k.pool("tensor_tensor", out=v3(B["N2"]), in0=v3(B["X"]), in1=bc4(tri(cf["m2a"])), op=ALU.mult, r=[B["X"].k(), TRI.k()], w=[B["N2"].k()])
        yield
        k.dve("tensor_tensor", out=v3(B["X"]), in0=v3(B["X"]), in1=bc4(tri(4)), op=ALU.mult, r=[B["X"].k(), TRI.k()], w=[B["X"].k()])
        yield
        k.dve("tensor_tensor", out=v3(B["XT"]), in0=v3(B["XT"]), in1=bc4(tri(4)), op=ALU.mult, r=[B["XT"].k(), TRI.k()], w=[B["XT"].k()])
        yield
        k.pool("tensor_tensor", out=v3(B["P"]), in0=v3(B["X"]), in1=bc4(identb[:, :]), op=ALU.add, r=[B["X"].k(), identb.k()], w=[B["P"].k()])
        yield
        k.dve("tensor_tensor", out=v3(B["PT"]), in0=v3(B["XT"]), in1=bc4(identb[:, :]), op=ALU.add, r=[B["XT"].k(), identb.k()], w=[B["PT"].k()])
        yield

        def mm4(ps, lhs, rhs, acc=None):
            for h in range(4):
                sl = slice(h * 128, (h + 1) * 128)
                k.pe("matmul", ps[:, sl], lhsT=B[lhs][:, sl], rhs=B[rhs][:, sl], start=True, stop=(acc is None), r=[B[lhs].k(), B[rhs].k()], w=[ps.k()])
                if acc is not None:
                    k.pe("matmul", ps[:, sl], lhsT=identb[:, :], rhs=B[acc][:, sl], start=False, stop=True, r=[identb.k(), B[acc].k()], w=[ps.k()])

        for l in range(1, 5):
            mm4(P_A, "XT", "X")
            yield
            mm4(P_B, "X", "XT")
            yield
            k.act("copy", out=B["X"][:, :], in_=P_A[:, :], r=[P_A.k()], w=[B["X"].k()])
            yield
            k.act("copy", out=B["XT"][:, :], in_=P_B[:, :], r=[P_B.k()], w=[B["XT"].k()])
            yield
            mm4(P_T, "XT", "P", acc="P")
            yield
            mm4(P_W0, "X", "PT", acc="PT")
            yield
            k.act("copy", out=B["P"][:, :], in_=P_T[:, :], r=[P_T.k()], w=[B["P"].k()])
            yield
            k.dve("tensor_copy", out=B["PT"][:, :], in_=P_W0[:, :], r=[P_W0.k()], w=[B["PT"].k()])
            yield
        mm4(P_A, "N1T", "P")
        yield
        mm4(P_B, "N1", "PT")
        yield
        k.act("copy", out=B["V1"][:, :], in_=P_A[:, :], r=[P_A.k()], w=[B["V1"].k()])
        yield
        k.act("copy", out=B["U1"][:, :], in_=P_B[:, :], r=[P_B.k()], w=[B["U1"].k()])
        yield
        mm4(P_T, "PT", "V1", acc="P")
        yield
        mm4(P_W0, "P", "U1", acc="PT")
        yield
        k.act("copy", out=B["P"][:, :], in_=P_T[:, :], r=[P_T.k()], w=[B["P"].k()])
        yield
        k.dve("tensor_copy", out=B["PT"][:, :], in_=P_W0[:, :], r=[P_W0.k()], w=[B["PT"].k()])
        yield
        mm4(P_A, "N2", "PT")
        yield
        k.act("copy", out=B["U1"][:, :], in_=P_A[:, :], r=[P_A.k()], w=[B["U1"].k()])
        yield
        mm4(P_T, "P", "U1", acc="PT")
        yield
        k.act("copy", out=B["PT"][:, :], in_=P_T[:, :], r=[P_T.k()], w=[B["PT"].k()])
        yield
        wur = B["WUR"][:, :].rearrange("p (h f) -> p h f", h=4)
        k.pool("tensor_tensor", out=wur[:, :, 0:128], in0=v3(B["ktok"]), in1=col4(BE), op=ALU.mult, r=[B["ktok"].k(), sm.k()], w=[B["WUR"].k()])
        yield
        k.pool("tensor_tensor", out=wur[:, :, 128:256], in0=v3(B["vtok"]), in1=col4(bcol), op=ALU.mult, r=[B["vtok"].k(), BALL.k()], w=[B["WUR"].k()])
        yield
        k.pool("tensor_tensor", out=v3(B["KD"]), in0=v3(B["ktok"]), in1=col4(KDS), op=ALU.mult, r=[B["ktok"].k(), sm.k()], w=[B["KD"].k()])
        yield
        for h in range(4):
            pw = P_W0 if h < 2 else P_W1
            k.pe("matmul", pw[:, (h % 2) * 256:(h % 2 + 1) * 256], lhsT=B["PT"][:, h * 128:(h + 1) * 128], rhs=B["WUR"][:, h * 256:(h + 1) * 256], start=True, stop=True,
                 r=[B["PT"].k(), B["WUR"].k()], w=[pw.k()])
        k.act("copy", out=B["WU"][:, 0:512], in_=P_W0[:, :], r=[P_W0.k()], w=[B["WU"].k()])
        yield
        k.act("copy", out=B["WU"][:, 512:1024], in_=P_W1[:, :], r=[P_W1.k()], w=[B["WU"].k()])
        yield
        wv = lambda h: B["WU"][:, h * 256:h * 256 + 128]
        uv = lambda h: B["WU"][:, h * 256 + 128:h * 256 + 256]
        for h in range(4):
            k.pe("matmul", P_Z[:, h * 128:(h + 1) * 128], lhsT=wv(h), rhs=B["KD"][:, h * 128:(h + 1) * 128], start=True, stop=True,
                 r=[B["WU"].k(), B["KD"].k()], w=[P_Z.k()])
        k.act("activation", out=B["ATn"][:, :], in_=P_Z[:, :], func=AF.Copy, scale=-1.0, r=[P_Z.k()], w=[B["ATn"].k()])
        yield
        while n > 0 and not sdone.get((n - 1, d)):
            yield
        if lat:
            k.pool("tensor_tensor", out=B["QD"][:, :], in0=B["qT4"][:, :], in1=B["ER"][:, :], op=ALU.mult, r=[B["qT4"].k(), B["ER"].k()], w=[B["QD"].k()])
            for h in range(4):
                k.pe("matmul", P_Z2[:, h * 128:(h + 1) * 128], lhsT=wv(h), rhs=B["AINCT"][:, h * 128:(h + 1) * 128], start=True, stop=True,
                     r=[B["WU"].k(), B["AINCT"].k()], w=[P_Z2.k()])
            k.dve("tensor_tensor", out=B["QEFF"][:, :], in0=B["QD"][:, :], in1=P_Z2[:, :], op=ALU.subtract, r=[B["QD"].k(), P_Z2.k()], w=[B["QEFF"].k()])
            assert n == 0 or sdone.get((n - 1, d)), f"GDN interleave order violated (o) at n={n} d={d}"
            for h in range(4):
                sl = slice(h * 128, (h + 1) * 128)
                k.pe("matmul", P_Z[:, sl], lhsT=B["QEFF"][:, sl], rhs=Sb[d][:, sl], start=True, stop=False, r=[B["QEFF"].k(), Sb[d].k()], w=[P_Z.k()])
                k.pe("matmul", P_Z[:, sl], lhsT=B["AINCT"][:, sl], rhs=uv(h), start=False, stop=True, r=[B["AINCT"].k(), B["WU"].k()], w=[P_Z.k()])
            k.act("copy", out=B["OUT"][:, :], in_=P_Z[:, :], r=[P_Z.k()], w=[B["OUT"].k()])
            k.dma(OD[d][g], B["OUT"][:, :], r=[B["OUT"].k()], w=[("dram", "O", d, g)])
        assert n == 0 or sdone.get((n - 1, d)), f"GDN interleave order violated at n={n} d={d}"
        for h in range(4):
            sl = slice(h * 128, (h + 1) * 128)
            k.pe("matmul", P_Z2[:, sl], lhsT=B["ATn"][:, sl], rhs=Sb[d][:, sl], start=True, stop=False, r=[B["ATn"].k(), Sb[d].k()], w=[P_Z2.k()])
            k.pe("matmul", P_Z2[:, sl], lhsT=B["KD"][:, sl], rhs=uv(h), start=False, stop=True, r=[B["KD"].k(), B["WU"].k()], w=[P_Z2.k()])
        k.pool("tensor_tensor", out=v3(Sf[d]), in0=v3(Sf[d]), in1=col4(GT), op=ALU.mult, r=[Sf[d].k(), sm.k()], w=[Sf[d].k()])
        yield
        k.dve("tensor_tensor", out=Sf[d][:, :], in0=Sf[d][:, :], in1=P_Z2[:, :], op=ALU.add, r=[Sf[d].k(), P_Z2.k()], w=[Sf[d].k()])
        yield
        k.act("copy", out=Sb[d][:, :], in_=Sf[d][:, :], r=[Sf[d].k()], w=[Sb[d].k()])
        sdone[(n, d)] = True
        yield

    gens = []
    for n in range(NTILE):
        for d in range(2):
            gens.append(functools.partial(unit, n, d))
    if as_gens:
        return gens
    run_interleaved(gens, nslots, slotted=True)
    k.release(m0)


def gdn_consts():
    idx = np.arange(128)
    ge = (idx[:, None] >= idx[None, :]).astype(np.float32)
    gt = (idx[:, None] > idx[None, :]).astype(np.float32)
    bd32 = (idx[:, None] // 32 == idx[None, :] // 32).astype(np.float32)
    m1l = ((idx[:, None] // 64 == idx[None, :] // 64) & (idx[:, None] // 32 == idx[None, :] // 32 + 1)).astype(np.float32)
    m2l = ((idx[:, None] >= 64) & (idx[None, :] < 64)).astype(np.float32)
    return np.ascontiguousarray(np.stack([ge, gt, ge.T, gt.T, bd32, m1l, m1l.T, m2l, m2l.T]))


def p2c_gdnout(k, c):
    nc = k.nc
    m0 = k.mark()
    OF = k.dram("OF", [32, 128, 512], F32)
    OB = k.dram("OB", [32, 128, 512], F32)
    ZA = k.dram("ZA", [T, 512], BF16)
    YT = k.dram("YT", [8, 128, T], BF16)
    GG = k.alloc("GGn", 512, F32)
    for h in range(4):
        k.dma(GG[:, h * 128:(h + 1) * 128], row_bc(k, "gdn_g"), r=[("dram", "rows")], w=[GG.k()])
    NS = 4
    of = [k.alloc(f"of{i}", 512, F32) for i in range(NS)]
    ob = [k.alloc(f"ob{i}", 512, F32) for i in range(NS)]
    za = [k.alloc(f"zac{i}", 512, BF16) for i in range(NS)]
    o = [k.alloc(f"oc{i}", 512, F32) for i in range(NS)]
    sq = [k.alloc(f"sqc{i}", 512, F32) for i in range(NS)]
    st = [k.alloc(f"stc{i}", 16, F32) for i in range(NS)]
    yb = [k.alloc(f"yc{i}", 512, BF16) for i in range(NS)]
    yts = [k.alloc(f"ytc{i}", 4 * 512, BF16) for i in range(2)]
    ps_tr = [k.PS[0], k.PS[1], k.PS[2], k.PS[3]]
    v3 = lambda t: t[:, :].rearrange("p (h f) -> p h f", h=4)

    def tile(g, i):
        m = g // 4
        s = g % 4
        yt = yts[m % 2]
        k.dma(of[i][:, :], OF[g], r=[("dram", "O", 0, g)], w=[of[i].k()])
        k.dma(ob[i][:, :], OB[g], r=[("dram", "O", 1, g)], w=[ob[i].k()])
        k.dma(za[i][:, :], ZA[g * 128:(g + 1) * 128, :], r=[("dram", "ZA", g)], w=[za[i].k()])
        k.dve("tensor_tensor", out=o[i][:, :], in0=of[i][:, :], in1=ob[i][:, :], op=ALU.add, r=[of[i].k(), ob[i].k()], w=[o[i].k()])
        yield
        k.pool("tensor_tensor", out=sq[i][:, :], in0=o[i][:, :], in1=o[i][:, :], op=ALU.mult, r=[o[i].k()], w=[sq[i].k()])
        yield
        k.dve("tensor_reduce", out=st[i][:, 0:4], in_=v3(sq[i]), axis=AX.X, op=ALU.add, r=[sq[i].k()], w=[st[i].k()])
        yield
        k.act("activation", out=st[i][:, 4:8], in_=st[i][:, 0:4], func=AF.Sqrt, scale=1.0 / 128, bias=NORM_EPS, r=[st[i].k()], w=[st[i].k()])
        yield
        k.dve("reciprocal", out=st[i][:, 8:12], in_=st[i][:, 4:8], r=[st[i].k()], w=[st[i].k()])
        yield
        k.dve("tensor_tensor", out=v3(o[i]), in0=v3(o[i]), in1=st[i][:, 8:12].unsqueeze(2).to_broadcast([128, 4, 128]), op=ALU.mult,
              r=[o[i].k(), st[i].k()], w=[o[i].k()])
        yield
        k.pool("tensor_tensor", out=o[i][:, :], in0=o[i][:, :], in1=GG[:, :], op=ALU.mult, r=[o[i].k(), GG.k()], w=[o[i].k()])
        yield
        k.dve("tensor_tensor", out=yb[i][:, :], in0=o[i][:, :], in1=za[i][:, :], op=ALU.mult, r=[o[i].k(), za[i].k()], w=[yb[i].k()])
        yield
        ptr = ps_tr[i]
        ptb = ptr.ap.bitcast(BF16)
        for h in range(4):
            k.pe("transpose", ptb[:, h * 128:(h + 1) * 128], yb[i][:, h * 128:(h + 1) * 128], c["identb"][:, :], r=[yb[i].k(), c["identb"].k()], w=[ptr.k()])
        yield
        dview = yt[:, :].rearrange("p (h t) -> p h t", h=4)[:, :, s * 128:(s + 1) * 128]
        sview = ptb[:, 0:512].rearrange("p (h t) -> p h t", h=4)
        k.act("copy", out=dview, in_=sview, r=[ptr.k()], w=[yt.k()])
        yield

    for m in range(8):
        run_interleaved([functools.partial(tile, 4 * m + s_) for s_ in range(4)], NS, slotted=True)
        yt = yts[m % 2]
        for h in range(4):
            k.dma(YT[h][:, m * 512:(m + 1) * 512], yt[:, h * 512:(h + 1) * 512], r=[yt.k()], w=[("dram", "YT", h, m)])
    k.release(m0)


def p2ab(k, c):
    m0 = k.mark()
    gens = p2a_gdn(k, c, nslots=2, as_gens=True)
    att = p2b_attn(k, c, banks=k.PS[4:8], as_gen=True)
    il = Interleaver(gens, 2, slotted=True)
    g_alive, a_alive, r = True, True, 0
    while g_alive or a_alive:
        if g_alive:
            g_alive = il.step()
        if a_alive and (r % ATT_EVERY == 0 or not g_alive):
            try:
                next(att)
            except StopIteration:
                a_alive = False
        r += 1
    k.release(m0)


ATT_EVERY = 3
ALL_PASSES = None


def all_passes():
    return [p0_mod, p1_inproj, p1b_gdnprep, p2a_gdn, p2c_gdnout, p2b_attn, p3_outproj, l1_pass_a, l1_pass_b]


EXT_IN = ["x", "ctx", "cvec", "c_sel", "c_ident", "c_tri", "ada_w_r", "rows", "ev_w_in_r", "ev_w_out_r", "od_w_in_r", "od_w_out_r", "rope_cs", "rope_sn", "pv"]


def kernel(**inputs):
    maps = host_prep(inputs)
    tri = gdn_consts()
    for m in maps:
        m["c_tri"] = tri
    nc, _ = build_program(all_passes(), ext_in=EXT_IN, ext_out=["y"])
    in_maps = [{k_: m[k_] for k_ in EXT_IN} for m in maps]
    res = run_bass_kernel_spmd(nc, in_maps, core_ids=list(range(8)))
    return np.stack([np.asarray(r["y"], dtype=np.float32) for r in res.results], axis=0)
```

```python
import contextlib
import functools
import numpy as np
import concourse.bass as bass
import concourse.mybir as mybir
from concourse.bass_utils import run_bass_kernel_spmd

F32 = mybir.dt.float32
BF16 = mybir.dt.bfloat16
AF = mybir.ActivationFunctionType
ALU = mybir.AluOpType
AX = mybir.AxisListType

ENGS = ["pe", "act", "dve", "pool", "sp"]
NDMA_Q = {"sp": 20, "pool": 40}
EPOCH = 20000

D = 1024
T = 4096
CTX = 256
NORM_EPS = 1e-6


class Sched:
    def __init__(self, nc):
        self.nc = nc
        self.ops = []
        self.last_w = {}
        self.readers = {}
        self.cur = {e: {} for e in ENGS}
        self.pos = {e: 0 for e in ENGS}
        self.ndma = {"sp": 0, "pool": 0}
        self.dma_ops = {"sp": [], "pool": []}
        self.seen = set()
        self.inherit = {}

    def retire(self, names):
        names = set(names)
        for key in list(self.seen):
            if key[0] in names:
                cand = list(self.readers.get(key, ()))
                w = self.last_w.get(key)
                if w is not None:
                    cand.append(w)
                for c in cand:
                    o = self.ops[c]
                    old = self.inherit.get(o["src"])
                    if old is None or self.ops[old]["p"] < o["p"]:
                        self.inherit[o["src"]] = c
                self.seen.discard(key)
                self.readers.pop(key, None)
                self.last_w.pop(key, None)

    def _touch(self, key):
        if key not in self.seen:
            self.seen.add(key)
            if self.inherit:
                self.readers[key] = list(self.inherit.values())

    def add(self, eng, fn, reads=(), writes=(), dma=False):
        idx = len(self.ops)
        deps = []
        for r in reads:
            self._touch(r)
            w = self.last_w.get(r)
            if w is not None:
                deps.append((w, True))
        for k in writes:
            self._touch(k)
            w = self.last_w.get(k)
            if w is not None:
                deps.append((w, False))
            for rd in self.readers.get(k, ()):
                deps.append((rd, False))
        if dma:
            nd = self.ndma[eng]
            ns = NDMA_Q[eng]
            slot = nd % ns
            cnt = nd // ns + 1
            if nd >= ns:
                deps.append((self.dma_ops[eng][nd - ns], True))
            src = ("d", eng, slot)
            p = cnt
            self.ndma[eng] += 1
        else:
            self.pos[eng] += 1
            src = eng
            p = self.pos[eng]
        cur = self.cur[eng]
        waits = []
        for d, raw in deps:
            o = self.ops[d]
            s, v = o["src"], o["p"]
            if s == eng and not dma and not o["dma"]:
                if eng == "pe":
                    continue
            if cur.get(s, 0) >= v:
                continue
            waits.append((s, v))
            o["needed"] = True
            for ks, kv in o["vc"].items():
                if cur.get(ks, 0) < kv:
                    cur[ks] = kv
            cur[s] = v
        vc = dict(cur)
        vc[src] = p
        op = dict(eng=eng, fn=fn, waits=waits, src=src, p=p, dma=dma, vc=vc, needed=False, deps=deps)
        self.ops.append(op)
        if dma:
            self.dma_ops[eng].append(idx)
        for r in reads:
            self.readers.setdefault(r, []).append(idx)
        for k in writes:
            self.last_w[k] = idx
            self.readers[k] = []
        return idx

    def emit(self):
        nc = self.nc
        rank = {e: {} for e in ENGS}
        cnt = {e: 0 for e in ENGS}
        for o in self.ops:
            if not o["dma"] and o["needed"]:
                cnt[o["eng"]] += 1
                rank[o["eng"]][o["p"]] = cnt[o["eng"]]
        nsem = {e: max(1, (cnt[e] + EPOCH - 1) // EPOCH) for e in ENGS}
        with contextlib.ExitStack() as st:
            sems = {e: [st.enter_context(nc.semaphore(f"s_{e}{i}")) for i in range(nsem[e])] for e in ENGS}
            dsem = {q: [st.enter_context(nc.semaphore(f"s_d{q}{i}")) for i in range(NDMA_Q[q])] for q in NDMA_Q}
            block = st.enter_context(nc.Block())
            engobj = {"pe": nc.tensor, "act": nc.scalar, "dve": nc.vector, "pool": nc.gpsimd, "sp": nc.sync}
            per = {e: [o for o in self.ops if o["eng"] == e] for e in ENGS}

            def run(e):
                eo = engobj[e]
                for o in per[e]:
                    for s, v in o["waits"]:
                        if isinstance(s, tuple):
                            eo.wait_ge(dsem[s[1]][s[2]], 16 * v)
                        else:
                            r = rank[s][v] - 1
                            eo.wait_ge(sems[s][r // EPOCH], r % EPOCH + 1)
                    ins = o["fn"]()
                    if o["dma"]:
                        ins.then_inc(dsem[o["src"][1]][o["src"][2]], 16)
                    elif o["needed"]:
                        r = rank[e][o["p"]] - 1
                        ins.then_inc(sems[e][r // EPOCH], 1)
                if e == "sp":
                    for q in NDMA_Q:
                        for sl in range(min(self.ndma[q], NDMA_Q[q])):
                            last = (self.ndma[q] - 1 - sl) // NDMA_Q[q] + 1
                            eo.wait_ge(dsem[q][sl], 16 * last)

            @block.tensor
            def _(x):
                run("pe")

            @block.scalar
            def _(x):
                run("act")

            @block.vector
            def _(x):
                run("dve")

            @block.gpsimd
            def _(x):
                run("pool")

            @block.sync
            def _(x):
                run("sp")
        return cnt


class Interleaver:
    def __init__(self, gens, width, slotted=False):
        self.it = iter(gens)
        self.width = width
        self.slotted = slotted
        self.active = []
        self.free = list(range(width))

    def step(self):
        while len(self.active) < self.width:
            try:
                g = next(self.it)
            except StopIteration:
                break
            if self.slotted:
                sl = self.free.pop(0)
                self.active.append((g(sl), sl))
            else:
                self.active.append((g, None))
        if not self.active:
            return False
        for item in list(self.active):
            try:
                next(item[0])
            except StopIteration:
                self.active.remove(item)
                if self.slotted:
                    self.free.append(item[1])
        return True


def run_interleaved(gens, width, slotted=False):
    il = Interleaver(gens, width, slotted)
    while il.step():
        pass


class Tl:
    def __init__(self, name, ap):
        self.name = name
        self.ap = ap

    def k(self, sub=None):
        return (self.name, sub)

    def __getitem__(self, idx):
        return self.ap[idx]


ARENA_COLS = 52000


class K:
    def __init__(self, nc, ext_in=(), ext_out=()):
        self.nc = nc
        self.S = Sched(nc)
        self.st = contextlib.ExitStack()
        self.big = self.st.enter_context(nc.sbuf_tensor("arena", [128, ARENA_COLS], F32))
        self.off = 0
        self.live = []
        self.ext_in = set(ext_in)
        self.ext_out = set(ext_out)
        self.drams = {}
        self.uid = 0
        self.PS = [Tl(f"ps{i}", self.st.enter_context(nc.psum_tensor(f"ps{i}", [128, 512], F32))[:, :]) for i in range(8)]

    def alloc(self, name, cols, dt=F32):
        size = 4 if dt == F32 else 2
        n32 = (cols * size + 3) // 4
        n32 = (n32 + 7) // 8 * 8
        assert self.off + n32 <= ARENA_COLS, f"SBUF arena overflow at {name}: {self.off}+{n32}"
        ap = self.big[:, self.off:self.off + n32]
        if dt != F32:
            ap = ap.bitcast(dt)[:, :cols]
        else:
            ap = ap[:, :cols]
        self.off += n32
        self.uid += 1
        t = Tl(f"{name}#{self.uid}", ap)
        self.live.append(t.name)
        return t

    def mark(self):
        return (self.off, len(self.live))

    def release(self, mark):
        off, n = mark
        self.S.retire(self.live[n:])
        del self.live[n:]
        self.off = off

    def dram(self, name, shape, dt):
        if name in self.drams:
            return self.drams[name]
        if name in self.ext_in:
            t = self.nc.dram_tensor(name, list(shape), dt, kind="ExternalInput")
        elif name in self.ext_out:
            t = self.nc.dram_tensor(name, list(shape), dt, kind="ExternalOutput")
        else:
            t = self.nc.dram_tensor(name, list(shape), dt)
        self.drams[name] = t.ap()
        return self.drams[name]

    def _op(self, eng, name, a, kw):
        r = kw.pop("r", ())
        w = kw.pop("w", ())
        w = list(w) + [key for key in r if key[0].startswith("ps") and key not in w]
        obj = {"pe": self.nc.tensor, "act": self.nc.scalar, "dve": self.nc.vector, "pool": self.nc.gpsimd}[eng]
        return self.S.add(eng, functools.partial(getattr(obj, name), *a, **kw), r, w)

    def pe(self, name, *a, **kw):
        return self._op("pe", name, a, kw)

    def act(self, name, *a, **kw):
        return self._op("act", name, a, kw)

    def dve(self, name, *a, **kw):
        return self._op("dve", name, a, kw)

    def pool(self, name, *a, **kw):
        return self._op("pool", name, a, kw)

    def any(self, eng, name, *a, **kw):
        return self._op(eng, name, a, kw)

    def dma(self, out, in_, r=(), w=(), q="sp"):
        nc = self.nc
        if q == "sp":
            return self.S.add("sp", functools.partial(nc.sync.dma_start, out=out, in_=in_), r, w, dma=True)
        return self.S.add("pool", functools.partial(nc.gpsimd.dma_start, out=out, in_=in_), r, w, dma=True)

    def veng(self, eng):
        return {"dve": self.nc.vector, "pool": self.nc.gpsimd}[eng]


def load_consts(k):
    nc = k.nc
    identd = k.dram("c_ident", [128, 128], F32)
    c = {}
    c["identf"] = k.alloc("identf", 128, F32)
    c["identb"] = k.alloc("identb", 128, BF16)
    k.dma(c["identf"][:, :], identd, r=[("dram", "c_ident")], w=[c["identf"].k()])
    k.dma(c["identb"][:, :], identd, r=[("dram", "c_ident")], w=[c["identb"].k()], q="pool")
    return c


def prep_rows(k, c, xrows_ap, xkey, Amod, Bmod, hT, col0, nrows, tmp, ps_t, idx):
    nc = k.nc
    xt, junk, ss, t1, hb = tmp
    n = nrows
    k.dma(xt[:n, :], xrows_ap, r=[xkey], w=[xt.k()])
    k.act("activation", out=junk[:n, :], in_=xt[:n, :], func=AF.Square, accum_out=ss[:n, 0:1],
          r=[xt.k()], w=[junk.k(), ss.k()])
    k.act("activation", out=ss[:n, 1:2], in_=ss[:n, 0:1], func=AF.Sqrt, scale=1.0 / D, bias=NORM_EPS,
          r=[ss.k()], w=[ss.k()])
    k.dve("reciprocal", out=ss[:n, 2:3], in_=ss[:n, 1:2], r=[ss.k()], w=[ss.k()])
    k.dve("scalar_tensor_tensor", out=t1[:n, :], in0=xt[:n, :], scalar=ss[:n, 2:3], in1=Amod[:n, :],
                                                 op0=ALU.mult, op1=ALU.mult,
          r=[xt.k(), ss.k(), Amod.k()], w=[t1.k()])
    k.pool("tensor_tensor", out=hb[:n, :], in0=t1[:n, :], in1=Bmod[:n, :], op=ALU.add,
           r=[t1.k(), Bmod.k()], w=[hb.k()])
    pT = ps_t.ap.bitcast(BF16)
    for kc in range(8):
        k.pe("transpose", pT[:, kc * 128:kc * 128 + n], hb[:n, kc * 128:(kc + 1) * 128], c["identb"][:n, :n],
             r=[hb.k(), c["identb"].k()], w=[ps_t.k()])
    src = pT.rearrange("p (a b) -> p a b", a=8)[:, :, :n]
    dst = hT[:, :].rearrange("p (a b) -> p a b", a=8)[:, :, col0:col0 + n]
    if idx % 2 == 0:
        k.act("copy", out=dst, in_=src, r=[ps_t.k()], w=[hT.k()])
    else:
        k.dve("tensor_copy", out=dst, in_=src, r=[ps_t.k()], w=[hT.k()])


def prep_rows_gen(k, c, xrows_ap, xkey, Amod, Bmod, hT, col0, nrows, tmp, ps_t, idx):
    nc = k.nc
    xt, junk, ss, t1, hb = tmp
    n = nrows
    k.dma(xt[:n, :], xrows_ap, r=[xkey], w=[xt.k()])
    yield
    k.act("activation", out=junk[:n, :], in_=xt[:n, :], func=AF.Square, accum_out=ss[:n, 0:1],
          r=[xt.k()], w=[junk.k(), ss.k()])
    yield
    k.act("activation", out=ss[:n, 1:2], in_=ss[:n, 0:1], func=AF.Sqrt, scale=1.0 / D, bias=NORM_EPS,
          r=[ss.k()], w=[ss.k()])
    yield
    k.dve("reciprocal", out=ss[:n, 2:3], in_=ss[:n, 1:2], r=[ss.k()], w=[ss.k()])
    yield
    k.dve("scalar_tensor_tensor", out=t1[:n, :], in0=xt[:n, :], scalar=ss[:n, 2:3], in1=Amod[:n, :],
                                                 op0=ALU.mult, op1=ALU.mult,
          r=[xt.k(), ss.k(), Amod.k()], w=[t1.k()])
    yield
    k.pool("tensor_tensor", out=hb[:n, :], in0=t1[:n, :], in1=Bmod[:n, :], op=ALU.add,
           r=[t1.k(), Bmod.k()], w=[hb.k()])
    yield
    pT = ps_t.ap.bitcast(BF16)
    for kc in range(8):
        k.pe("transpose", pT[:, kc * 128:kc * 128 + n], hb[:n, kc * 128:(kc + 1) * 128], c["identb"][:n, :n],
             r=[hb.k(), c["identb"].k()], w=[ps_t.k()])
    yield
    src = pT.rearrange("p (a b) -> p a b", a=8)[:, :, :n]
    dst = hT[:, :].rearrange("p (a b) -> p a b", a=8)[:, :, col0:col0 + n]
    if idx % 2 == 0:
        k.act("copy", out=dst, in_=src, r=[ps_t.k()], w=[hT.k()])
        yield
    else:
        k.dve("tensor_copy", out=dst, in_=src, r=[ps_t.k()], w=[hT.k()])
        yield


def alloc_prep_tmp(k, tag):
    xt = k.alloc(f"xt{tag}", 1024, F32)
    junk = k.alloc(f"junk{tag}", 1024, BF16)
    ss = k.alloc(f"ss{tag}", 8, F32)
    t1 = k.alloc(f"t1{tag}", 1024, F32)
    hb = k.alloc(f"hb{tag}", 1024, BF16)
    return (xt, junk, ss, t1, hb)


PV_BIN = 0
PV_DWB = 24
PV_LNG = 32
PV_LNB = 40
PV_DWW = 48
PV_C5W = 48 + 248
PV_N = PV_C5W + 60
U1PAD = 15
U1W = T + 2 * U1PAD


def l1_pass_a(k, c):
    nc = k.nc
    m0 = k.mark()
    x1 = k.dram("x1", [T, D], F32)
    modd = k.dram("mod", [8, 128, D], F32)
    w_in_d = k.dram("od_w_in_r", [128, 8 * 3072], F32)
    pvd = k.dram("pv", [128, PV_N], F32)
    U1 = k.dram("U1", [8, 128, U1W], BF16)
    ZG1 = k.dram("ZG1", [8, 128, T], BF16)

    w_in = k.alloc("w_in1", 8 * 3072, BF16)
    for kc in range(8):
        k.dma(w_in[:, kc * 3072:(kc + 1) * 3072], w_in_d[:, kc * 3072:(kc + 1) * 3072], r=[("dram", "od_w_in_r")], w=[w_in.k(kc)], q="pool")
    pv = k.alloc("pv", PV_N, F32)
    k.dma(pv[:, :], pvd, r=[("dram", "pv")], w=[pv.k()])
    A1 = k.alloc("A1", D, F32)
    B1 = k.alloc("B1", D, F32)
    k.dma(A1[:, :], modd[5], r=[("dram", "mod")], w=[A1.k()])
    k.dma(B1[:, :], modd[6], r=[("dram", "mod")], w=[B1.k()])
    zt = k.alloc("zt", 16, BF16)
    k.dve("memset", zt[:, :], 0.0, w=[zt.k()])
    for j in range(8):
        k.dma(U1[j][:, 0:U1PAD], zt[:, 0:U1PAD], r=[zt.k()], w=[("dram", "U1", j, "padl")])
        k.dma(U1[j][:, U1PAD + T:U1W], zt[:, 0:U1PAD], r=[zt.k()], w=[("dram", "U1", j, "padr")])
    tmps = [alloc_prep_tmp(k, i) for i in range(2)]
    hTs = [k.alloc(f"hT{i}", 8 * 512, BF16) for i in range(2)]
    sg = [k.alloc(f"sg{i}", 512, F32) for i in range(2)]
    ub = [k.alloc(f"ub{i}", 512, BF16) for i in range(3)]
    zb = [k.alloc(f"zb{i}", 512, BF16) for i in range(3)]
    ps_t = [k.PS[0], k.PS[1]]
    ps_mm = [k.PS[2], k.PS[3], k.PS[4], k.PS[5], k.PS[6], k.PS[7]]
    nmt = T // 512

    def prep_m(m):
        hT = hTs[m % 2]

        def sub(s, slot):
            r0 = m * 512 + s * 128
            return prep_rows_gen(k, c, x1[r0:r0 + 128, :], ("dram", "x1", r0 // 128), A1, B1, hT, s * 128, 128, tmps[slot], ps_t[slot], 4 * m + s)
        il = Interleaver([functools.partial(sub, s) for s in range(4)], 2, slotted=True)
        while il.step():
            yield

    def comp_m(m):
        hT = hTs[m % 2]
        hT3 = hT[:, :].rearrange("p (a b) -> p a b", a=8)

        def mmgroup(pst, col0):
            for kc in range(8):
                k.pe("matmul", pst[:, :], lhsT=w_in[:, kc * 3072 + col0:kc * 3072 + col0 + 128], rhs=hT3[:, kc, :],
                     start=(kc == 0), stop=(kc == 7), r=[w_in.k(kc), hT.k()], w=[pst.k()])

        for j in range(8):
            pa = ps_mm[(2 * j) % 4]
            pg = ps_mm[(2 * j + 1) % 4]
            pz = ps_mm[4 + j % 2]
            mmgroup(pa, j * 128)
            yield
            mmgroup(pg, 1024 + j * 128)
            yield
            mmgroup(pz, 2048 + j * 128)
            yield
            sgt = sg[j % 2]
            ubt = ub[j % 3]
            zbt = zb[j % 3]
            k.act("activation", out=sgt[:, :], in_=pg[:, :], func=AF.Sigmoid, bias=pv[:, PV_BIN + 8 + j:PV_BIN + 9 + j], scale=1.0,
                  r=[pg.k(), pv.k()], w=[sgt.k()])
            yield
            k.dve("scalar_tensor_tensor", out=ubt[:, :], in0=pa[:, :], scalar=pv[:, PV_BIN + j:PV_BIN + j + 1], in1=sgt[:, :],
                  op0=ALU.add, op1=ALU.mult, r=[pa.k(), sgt.k(), pv.k()], w=[ubt.k()])
            k.dma(U1[j][:, U1PAD + m * 512:U1PAD + (m + 1) * 512], ubt[:, :], r=[ubt.k()], w=[("dram", "U1", j, m)])
            yield
            k.act("activation", out=zbt[:, :], in_=pz[:, :], func=AF.Silu, bias=pv[:, PV_BIN + 16 + j:PV_BIN + 17 + j], scale=1.0,
                  r=[pz.k(), pv.k()], w=[zbt.k()])
            k.dma(ZG1[j][:, m * 512:(m + 1) * 512], zbt[:, :], r=[zbt.k()], w=[("dram", "ZG1", j, m)])
            yield

    for _ in prep_m(0):
        pass
    for m in range(nmt):
        gl = [comp_m(m)] + ([prep_m(m + 1)] if m + 1 < nmt else [])
        run_interleaved(gl, 2)
    k.release(m0)


def l1_pass_b(k, c):
    nc = k.nc
    m0 = k.mark()
    x1 = k.dram("x1", [T, D], F32)
    y = k.dram("y", [T, D], F32)
    modd = k.dram("mod", [8, 128, D], F32)
    w_out_d = k.dram("od_w_out_r", [128, 8 * 1024], F32)
    pvd = k.dram("pv", [128, PV_N], F32)
    U1 = k.dram("U1", [8, 128, U1W], BF16)
    ZG1 = k.dram("ZG1", [8, 128, T], BF16)

    w_out = k.alloc("w_out1", 8 * 1024, BF16)
    k.dma(w_out[:, :], w_out_d, r=[("dram", "od_w_out_r")], w=[w_out.k()], q="pool")
    pv = k.alloc("pv", PV_N, F32)
    k.dma(pv[:, :], pvd, r=[("dram", "pv")], w=[pv.k()])
    G1 = k.alloc("G1", D, F32)
    k.dma(G1[:, :], modd[7], r=[("dram", "mod")], w=[G1.k()])
    BOUT = k.alloc("BOUT", D, F32)
    k.dma(BOUT[:, :], row_bc(k, "b_out"), r=[("dram", "rows")], w=[BOUT.k()])
    onesm = k.alloc("onesm", 128, BF16)
    k.dve("memset", onesm[:, :], 1.0 / 1024.0, w=[onesm.k()])
    DG = k.alloc("DG", 8 * 31 * 128, BF16)
    for j in range(8):
        for t in range(31):
            i = j * 31 + t
            eng = "dve" if i % 2 == 0 else "pool"
            ve = k.veng(eng)
            k.any(eng, "tensor_scalar", out=DG[:, i * 128:(i + 1) * 128], in0=c["identf"][:, :],
                                                           scalar1=pv[:, PV_DWW + i:PV_DWW + i + 1], scalar2=None, op0=ALU.mult,
                  r=[c["identf"].k(), pv.k()], w=[DG.k(j)])
    PW = 512 + 2 * U1PAD
    PRE = [k.alloc(f"PRE{i}", 8 * PW, BF16) for i in range(2)]
    ZGt = [k.alloc(f"ZGt{i}", 8 * 512, BF16) for i in range(2)]
    UC = k.alloc("UC", 8 * 512, F32)
    UCb = k.alloc("UCb", 8 * 512, BF16)
    SQ = k.alloc("SQ", 8 * 512, BF16)
    MEAN = k.alloc("MEAN", 512, F32)
    M2 = k.alloc("M2", 512, F32)
    RSTD = k.alloc("RSTD", 512, F32)
    TA = [k.alloc(f"TA{i}", 512, F32) for i in range(4)]
    TB = TA
    YS = [k.alloc(f"YS{i}", 512, BF16) for i in range(4)]
    YT = k.alloc("YT", 8 * 512, BF16)
    O = [k.alloc(f"O{i}", D, F32) for i in range(2)]
    junk = k.alloc("junkb", D, BF16)
    ss = [k.alloc(f"ssb{i}", 8, F32) for i in range(2)]
    XR = [k.alloc(f"XR{i}", D, F32) for i in range(2)]
    T2 = O
    OUT = XR
    ps_cv = [k.PS[0], k.PS[1]]
    ps_mean, ps_msq = k.PS[2], k.PS[3]
    ps_o = [(k.PS[4], k.PS[5]), (k.PS[6], k.PS[7])]
    nmt = T // 512

    def phase1(m):
        pre = PRE[m % 2]
        zgt = ZGt[m % 2]
        for j in range(8):
            k.dma(pre[:, j * PW:(j + 1) * PW], U1[j][:, m * 512:m * 512 + PW],
                  r=[("dram", "U1", j, mm) for mm in range(max(0, m - 1), min(nmt, m + 2))] + [("dram", "U1", j, "padl"), ("dram", "U1", j, "padr")],
                  w=[pre.k(j)])
            k.dma(zgt[:, j * 512:(j + 1) * 512], ZG1[j][:, m * 512:(m + 1) * 512], r=[("dram", "ZG1", j, m)], w=[zgt.k(j)])
        yield
        for j in range(8):
            pcv = ps_cv[j % 2]
            for t in range(31):
                i = j * 31 + t
                k.pe("matmul", pcv[:, :], lhsT=DG[:, i * 128:(i + 1) * 128], rhs=pre[:, j * PW + t:j * PW + t + 512], start=(t == 0), stop=(t == 30),
                     r=[DG.k(j), pre.k(j)], w=[pcv.k()])
                if t % 8 == 7:
                    yield
            yield
            k.act("activation", out=UC[:, j * 512:(j + 1) * 512], in_=pcv[:, :], func=AF.Identity, bias=pv[:, PV_DWB + j:PV_DWB + j + 1], scale=1.0,
                  r=[pcv.k(), pv.k()], w=[UC.k(j)])
            yield
            k.act("activation", out=SQ[:, j * 512:(j + 1) * 512], in_=pcv[:, :], func=AF.Square, bias=pv[:, PV_DWB + j:PV_DWB + j + 1], scale=1.0,
                  r=[pcv.k(), pv.k()], w=[SQ.k(j)])
            yield
            k.pool("tensor_copy", out=UCb[:, j * 512:(j + 1) * 512], in_=UC[:, j * 512:(j + 1) * 512], r=[UC.k(j)], w=[UCb.k(j)])
            yield

    def phase234(m):
        zgt = ZGt[m % 2]
        for j in range(8):
            k.pe("matmul", ps_mean[:, :], lhsT=onesm[:, :], rhs=UCb[:, j * 512:(j + 1) * 512], start=(j == 0), stop=(j == 7),
                 r=[onesm.k(), UCb.k(j)], w=[ps_mean.k()])
        for j in range(8):
            k.pe("matmul", ps_msq[:, :], lhsT=onesm[:, :], rhs=SQ[:, j * 512:(j + 1) * 512], start=(j == 0), stop=(j == 7),
                 r=[onesm.k(), SQ.k(j)], w=[ps_msq.k()])
        k.act("copy", out=MEAN[:, :], in_=ps_mean[:, :], r=[ps_mean.k()], w=[MEAN.k()])
        k.dve("tensor_tensor", out=M2[:, :], in0=MEAN[:, :], in1=MEAN[:, :], op=ALU.mult, r=[MEAN.k()], w=[M2.k()])
        k.dve("tensor_tensor", out=M2[:, :], in0=ps_msq[:, :], in1=M2[:, :], op=ALU.subtract, r=[ps_msq.k(), M2.k()], w=[M2.k()])
        k.act("activation", out=RSTD[:, :], in_=M2[:, :], func=AF.Sqrt, bias=1e-5, scale=1.0, r=[M2.k()], w=[RSTD.k()])
        k.dve("reciprocal", out=RSTD[:, :], in_=RSTD[:, :], r=[RSTD.k()], w=[RSTD.k()])

        def ln_chunk(j, sl):
            ta, tb, ys = TA[sl], TB[sl], YS[sl]
            k.dve("tensor_tensor", out=ta[:, :], in0=UC[:, j * 512:(j + 1) * 512], in1=MEAN[:, :], op=ALU.subtract,
                  r=[UC.k(j), MEAN.k()], w=[ta.k()])
            yield
            k.pool("tensor_tensor", out=tb[:, :], in0=ta[:, :], in1=RSTD[:, :], op=ALU.mult, r=[ta.k(), RSTD.k()], w=[tb.k()])
            yield
            k.act("activation", out=ys[:, :], in_=tb[:, :], func=AF.Silu,
                  scale=pv[:, PV_LNG + j:PV_LNG + j + 1], bias=pv[:, PV_LNB + j:PV_LNB + j + 1], r=[tb.k(), pv.k()], w=[ys.k()])
            yield
            k.dve("tensor_tensor", out=YT[:, j * 512:(j + 1) * 512], in0=ys[:, :], in1=zgt[:, j * 512:(j + 1) * 512], op=ALU.mult,
                  r=[ys.k(), zgt.k(j)], w=[YT.k(j)])
            yield
        run_interleaved([functools.partial(ln_chunk, j) for j in range(8)], 4, slotted=True)

    def phase5(m):
        for s in range(4):
            oc = 4 * m + s
            r0 = m * 512 + s * 128
            po = ps_o[oc % 2]
            o, sst, xr, t2, out = O[oc % 2], ss[oc % 2], XR[oc % 2], T2[oc % 2], OUT[oc % 2]
            k.dma(xr[:, :], x1[r0:r0 + 128, :], r=[("dram", "x1", r0 // 128)], w=[xr.k()])
            for nh in range(2):
                for j in range(8):
                    k.pe("matmul", po[nh][:, :], lhsT=YT[:, j * 512 + s * 128:j * 512 + (s + 1) * 128], rhs=w_out[:, j * 1024 + nh * 512:j * 1024 + (nh + 1) * 512],
                         start=(j == 0), stop=(j == 7), r=[YT.k(j), w_out.k()], w=[po[nh].k()])
                yield
            for nh in range(2):
                k.dve("tensor_tensor", out=o[:, nh * 512:(nh + 1) * 512], in0=po[nh][:, :], in1=BOUT[:, nh * 512:(nh + 1) * 512], op=ALU.add,
                      r=[po[nh].k(), BOUT.k()], w=[o.k()])
                yield
            yield from post_res_gen(k, o, sst, junk, G1, xr, t2, out, y[r0:r0 + 128, :], ("dram", "y", r0 // 128))

    for _ in phase1(0):
        pass
    for m in range(nmt):
        phase234(m)
        gl = [phase5(m)] + ([phase1(m + 1)] if m + 1 < nmt else [])
        run_interleaved(gl, 2)
    k.release(m0)


def post_res(k, o, sst, junk, G, xr, t2, out, ydst, ykey):
    nc = k.nc
    k.act("activation", out=junk[:, :], in_=o[:, :], func=AF.Square, accum_out=sst[:, 0:1],
          r=[o.k()], w=[junk.k(), sst.k()])
    k.act("activation", out=sst[:, 1:2], in_=sst[:, 0:1], func=AF.Sqrt, scale=1.0 / D, bias=NORM_EPS,
          r=[sst.k()], w=[sst.k()])
    k.dve("reciprocal", out=sst[:, 2:3], in_=sst[:, 1:2], r=[sst.k()], w=[sst.k()])
    k.dve("scalar_tensor_tensor", out=t2[:, :], in0=o[:, :], scalar=sst[:, 2:3], in1=G[:, :], op0=ALU.mult, op1=ALU.mult,
          r=[o.k(), sst.k(), G.k()], w=[t2.k()])
    k.pool("tensor_tensor", out=out[:, :], in0=t2[:, :], in1=xr[:, :], op=ALU.add, r=[t2.k(), xr.k()], w=[out.k()])
    k.dma(ydst, out[:, :], r=[out.k()], w=[ykey])


def post_res_gen(k, o, sst, junk, G, xr, t2, out, ydst, ykey):
    nc = k.nc
    k.act("activation", out=junk[:, :], in_=o[:, :], func=AF.Square, accum_out=sst[:, 0:1],
          r=[o.k()], w=[junk.k(), sst.k()])
    yield
    k.act("activation", out=sst[:, 1:2], in_=sst[:, 0:1], func=AF.Sqrt, scale=1.0 / D, bias=NORM_EPS,
          r=[sst.k()], w=[sst.k()])
    yield
    k.dve("reciprocal", out=sst[:, 2:3], in_=sst[:, 1:2], r=[sst.k()], w=[sst.k()])
    yield
    k.dve("scalar_tensor_tensor", out=t2[:, :], in0=o[:, :], scalar=sst[:, 2:3], in1=G[:, :], op0=ALU.mult, op1=ALU.mult,
          r=[o.k(), sst.k(), G.k()], w=[t2.k()])
    yield
    k.pool("tensor_tensor", out=out[:, :], in0=t2[:, :], in1=xr[:, :], op=ALU.add, r=[t2.k(), xr.k()], w=[out.k()])
    yield
    k.dma(ydst, out[:, :], r=[out.k()], w=[ykey])
    yield


def build_program(passes, ext_in, ext_out):
    nc = bass.Bass("TRN2", target_bir_lowering=False)
    k = K(nc, ext_in, ext_out)
    c = load_consts(k)
    for p in passes:
        p(k, c)
    cnt = k.S.emit()
    k.st.close()
    return nc, cnt


ROW_OFF = {}
_o = 0
for _n, _l in [("ada_b0", 3072), ("ada_b1", 3072), ("pre_g0", 1024), ("pre_g1", 1024), ("post_g0", 1024), ("post_g1", 1024),
               ("b_out", 1024), ("gdn_g", 128), ("qn_g", 128), ("kn_g", 128), ("a_log", 8), ("dt_bias", 8)]:
    ROW_OFF[_n] = (_o, _l)
    _o += _l
ROWS_N = _o


def row_bc(k, name, off=0, n=None):
    rows = k.dram("rows", [1, ROWS_N], F32)
    o, l = ROW_OFF[name]
    if n is None:
        n = l
    return rows[0, o + off:o + off + n].partition_broadcast(128)


def p0_mod(k, c):
    nc = k.nc
    m0 = k.mark()
    modd = k.dram("mod", [8, 128, D], F32)
    cvec = k.dram("cvec", [128, 16], F32)
    seld = k.dram("c_sel", [2, 256], F32)
    adaw = k.dram("ada_w_r", [2, 128, 8 * 3072], F32)
    cv = k.alloc("cv", 16, F32)
    k.dma(cv[:, :], cvec, r=[("dram", "cvec")], w=[cv.k()])
    scb = k.alloc("scb", 16, BF16)
    k.act("activation", out=scb[:, :], in_=cv[:, :], func=AF.Silu, r=[cv.k()], w=[scb.k()])
    sel = k.alloc("sel", 256, F32)
    k.dma(sel[0:2, :], seld, r=[("dram", "c_sel")], w=[sel.k()])
    mrow = k.alloc("mrow", 6144, F32)
    aw = [k.alloc(f"aw{l}", 8 * 3072, BF16) for l in range(2)]
    for l in range(2):
        for kc in range(8):
            k.dma(aw[l][:, kc * 3072:(kc + 1) * 3072], adaw[l][:, kc * 3072:(kc + 1) * 3072], r=[("dram", "ada_w_r")], w=[aw[l].k(kc)], q="pool")
    adab = [k.alloc(f"adab{l}", 3072, F32) for l in range(2)]
    preg = [k.alloc(f"preg{l}", 1024, F32) for l in range(2)]
    postg = [k.alloc(f"postg{l}", 1024, F32) for l in range(2)]
    for l in range(2):
        k.dma(adab[l][:, :], row_bc(k, f"ada_b{l}"), r=[("dram", "rows")], w=[adab[l].k()])
        k.dma(preg[l][:, :], row_bc(k, f"pre_g{l}"), r=[("dram", "rows")], w=[preg[l].k()])
        k.dma(postg[l][:, :], row_bc(k, f"post_g{l}"), r=[("dram", "rows")], w=[postg[l].k()])
    for l in range(2):
        for nt in range(6):
            ps = k.PS[nt % 2]
            for kc in range(8):
                k.pe("matmul", ps[0:2, :], lhsT=scb[:, 2 * kc:2 * kc + 2], rhs=aw[l][:, kc * 3072 + nt * 512:kc * 3072 + (nt + 1) * 512],
                     start=(kc == 0), stop=(kc == 7), r=[scb.k(), aw[l].k(kc)], w=[ps.k()])
            k.act("copy", out=mrow[0:2, l * 3072 + nt * 512:l * 3072 + (nt + 1) * 512], in_=ps[0:2, :], r=[ps.k()], w=[mrow.k((l, nt))])
    tmp = [k.alloc(f"mt{i}", 512, F32) for i in range(2)]
    outt = [k.alloc(f"mo{i}", 1024, F32) for i in range(2)]
    plan = [(0, 0, 1, 0), (1, 0, 0, 0), (2, 0, 2, 0), (3, 0, 1, 1), (4, 0, 0, 1), (5, 1, 1, 0), (6, 1, 0, 0), (7, 1, 2, 0)]
    cnt = 0
    for (mi, l, part, si) in plan:
        ot = outt[mi % 2]
        for nh in range(2):
            ps = k.PS[2 + cnt % 2]
            tt = tmp[cnt % 2]
            cnt += 1
            seg = l * 3072 + part * 1024 + nh * 512
            nt = (part * 1024 + nh * 512) // 512
            k.pe("matmul", ps[:, :], lhsT=sel[0:2, si * 128:(si + 1) * 128], rhs=mrow[0:2, seg:seg + 512], start=True, stop=True,
                 r=[sel.k(), mrow.k((l, nt))], w=[ps.k()])
            ab_ = adab[l][:, part * 1024 + nh * 512:part * 1024 + (nh + 1) * 512]
            osl = ot[:, nh * 512:(nh + 1) * 512]
            if part == 0:
                k.dve("tensor_tensor", out=osl, in0=ps[:, :], in1=ab_, op=ALU.add, r=[ps.k(), adab[l].k()], w=[ot.k()])
            elif part == 1:
                k.dve("scalar_tensor_tensor", out=tt[:, :], in0=ps[:, :], scalar=1.0, in1=ab_, op0=ALU.add, op1=ALU.add,
                      r=[ps.k(), adab[l].k()], w=[tt.k()])
                k.pool("tensor_tensor", out=osl, in0=tt[:, :], in1=preg[l][:, nh * 512:(nh + 1) * 512], op=ALU.mult,
                       r=[tt.k(), preg[l].k()], w=[ot.k()])
            else:
                k.dve("tensor_tensor", out=tt[:, :], in0=ps[:, :], in1=ab_, op=ALU.add, r=[ps.k(), adab[l].k()], w=[tt.k()])
                k.pool("tensor_tensor", out=osl, in0=tt[:, :], in1=postg[l][:, nh * 512:(nh + 1) * 512], op=ALU.mult,
                       r=[tt.k(), postg[l].k()], w=[ot.k()])
        k.dma(modd[mi], ot[:, :], r=[ot.k()], w=[("dram", "mod")])
    k.release(m0)


NTILE = 34
TK = T + CTX
GP_CTX0 = 2
GP_LAT0 = 2 + CTX + 2 + 2
GP_W = GP_LAT0 + T + 2
C_QKV, C_ZA, C_AB, C_QB, C_KB, C_VB, C_ZB, EVEN_IN = 0, 1536, 2048, 2064, 2576, 2832, 3088, 3600


P1_FLAGS = {"pads": True, "tm": True, "fm": True, "qk": True, "ab": True, "norm": True, "tr": True}


def p1_inproj(k, c):
    nc = k.nc
    m0 = k.mark()
    x = k.dram("x", [T, D], F32)
    ctxd = k.dram("ctx", [CTX, D], F32)
    modd = k.dram("mod", [8, 128, D], F32)
    w_in_d = k.dram("ev_w_in_r", [128, 8 * EVEN_IN], F32)
    ropecs = k.dram("rope_cs", [32, 128, 128], F32)
    ropesn = k.dram("rope_sn", [32, 128, 128], F32)
    QT = k.dram("QT", [4, 128, T], BF16)
    KT = k.dram("KT", [2, 128, TK], BF16)
    V = k.dram("V", [NTILE, 128, 256], BF16)
    ZBT = k.dram("ZBT", [4, 128, T], BF16)
    ZA = k.dram("ZA", [T, 512], BF16)
    GPRE = k.dram("GPRE", [12, 128, GP_W], BF16)
    AB = k.dram("AB", [NTILE, 128, 16], F32)

    w_in = k.alloc("w_in0", 8 * EVEN_IN, BF16)
    for kc in range(8):
        k.dma(w_in[:, kc * EVEN_IN:(kc + 1) * EVEN_IN], w_in_d[:, kc * EVEN_IN:(kc + 1) * EVEN_IN], r=[("dram", "ev_w_in_r")], w=[w_in.k(kc)], q="pool")
    Am = k.alloc("Am", D, F32)
    Bm = k.alloc("Bm", D, F32)
    G6 = k.alloc("G6", 768, F32)
    for h in range(4):
        k.dma(G6[:, h * 128:(h + 1) * 128], row_bc(k, "qn_g"), r=[("dram", "rows")], w=[G6.k()])
    for h in range(2):
        k.dma(G6[:, 512 + h * 128:512 + (h + 1) * 128], row_bc(k, "kn_g"), r=[("dram", "rows")], w=[G6.k()])
    zt = k.alloc("zt", 16, BF16)
    k.dve("memset", zt[:, :], 0.0, w=[zt.k()])
    for j in range(12 if P1_FLAGS["pads"] else 0):
        k.dma(GPRE[j][:, 0:2], zt[:, 0:2], r=[zt.k()], w=[("dram", "GPRE", j, "p0")])
        k.dma(GPRE[j][:, 2 + CTX:GP_LAT0], zt[:, 0:4], r=[zt.k()], w=[("dram", "GPRE", j, "p1")])
        k.dma(GPRE[j][:, GP_LAT0 + T:GP_W], zt[:, 0:2], r=[zt.k()], w=[("dram", "GPRE", j, "p2")])
    tmps = [alloc_prep_tmp(k, i) for i in range(2)]
    hTs = [k.alloc(f"hT{i}", 8 * 512, BF16) for i in range(2)]
    CS = [k.alloc(f"CS{i}", 128, F32) for i in range(2)]
    SN = [k.alloc(f"SN{i}", 128, F32) for i in range(2)]
    QK = [k.alloc(f"QK{i}", 768, F32) for i in range(2)]
    SQ = [k.alloc(f"SQ{i}", 768, F32) for i in range(2)]
    ssq = [k.alloc(f"ssq{i}", 24, F32) for i in range(2)]
    QN = [k.alloc(f"QN{i}", 768, F32) for i in range(2)]
    R1 = [k.alloc(f"R1{i}", 768, F32) for i in range(2)]
    R2 = [k.alloc(f"R2{i}", 768, F32) for i in range(2)]
    QKb = [k.alloc(f"QKb{i}", 768, BF16) for i in range(2)]
    zas = [k.alloc(f"zas{i}", 512, BF16) for i in range(2)]
    vs = [k.alloc(f"vs{i}", 256, BF16) for i in range(2)]
    abs_ = [k.alloc(f"abs{i}", 16, F32) for i in range(2)]
    QTs = [k.alloc(f"QTs{i}", 4 * 512, BF16) for i in range(2)]
    KTs = [k.alloc(f"KTs{i}", 2 * 512, BF16) for i in range(2)]
    gps = [k.alloc(f"gps{i}", 512, BF16) for i in range(3)]
    zbs = [k.alloc(f"zbs{i}", 512, BF16) for i in range(3)]
    ps_t = [k.PS[0], k.PS[0]]
    ps_za, ps_q, ps_kv = k.PS[2], k.PS[3], k.PS[4]
    PSX = [k.PS[5], k.PS[1]]
    ps_f = [k.PS[6], k.PS[7]]
    cnt = 0
    fcnt = 0
    mts = [("ctx", 0, 256)] + [("lat", m * 512, 512) for m in range(T // 512)]
    mt_list = mts[:P1_FLAGS.get("nmt", 9)]

    def prep_m(mi):
        kind, t0, W = mt_list[mi]
        lat = kind == "lat"
        if mi == 0:
            k.dma(Am[:, :], modd[3], r=[("dram", "mod")], w=[Am.k()])
            k.dma(Bm[:, :], modd[4], r=[("dram", "mod")], w=[Bm.k()])
        elif mi == 1:
            k.dma(Am[:, :], modd[0], r=[("dram", "mod")], w=[Am.k()])
            k.dma(Bm[:, :], modd[1], r=[("dram", "mod")], w=[Bm.k()])
        hT = hTs[mi % 2]
        cnt0 = 0 if mi == 0 else 2 + 4 * (mi - 1)

        def sub(s, slot):
            r0 = t0 + s * 128
            src = x[r0:r0 + 128, :] if lat else ctxd[r0:r0 + 128, :]
            skey = ("dram", "x" if lat else "ctx", r0 // 128)
            return prep_rows_gen(k, c, src, skey, Am, Bm, hT, s * 128, 128, tmps[slot], ps_t[slot], cnt0 + s)
        il = Interleaver([functools.partial(sub, s) for s in range(W // 128)], 1, slotted=True)
        while il.step():
            yield

    def comp_m(mi):
        kind, t0, W = mt_list[mi]
        lat = kind == "lat"
        hT = hTs[mi % 2]
        hT3 = hT[:, :].rearrange("p (a b) -> p a b", a=8)
        nsub = W // 128
        qts, kts = QTs[mi % 2], KTs[mi % 2]
        fcnt = 0 if mi == 0 else 12 + 16 * (mi - 1)
        def tm_group(s, ps_ap, pskey, col0, n):
            for kc in range(8):
                k.pe("matmul", ps_ap, lhsT=hT3[:, kc, s * 128:(s + 1) * 128], rhs=w_in[:, kc * EVEN_IN + col0:kc * EVEN_IN + col0 + n],
                     start=(kc == 0), stop=(kc == 7), r=[hT.k(), w_in.k(kc)], w=[pskey])

        def stage_a(s):
            sl = s % 2
            r0 = t0 + s * 128
            tile_id = (r0 // 128) if lat else (32 + r0 // 128)
            psx = PSX[sl]
            qk = QK[sl]
            if lat:
                tm_group(s, ps_za[:, :], ps_za.k(), C_ZA, 512)
                zst = zas[sl]
                k.act("activation", out=zst[:, :], in_=ps_za[:, :], func=AF.Silu, r=[ps_za.k()], w=[zst.k()])
                k.dma(ZA[r0:r0 + 128, :], zst[:, :], r=[zst.k()], w=[("dram", "ZA", r0 // 128)])
                yield
                tm_group(s, ps_q[:, :], ps_q.k(), C_QB, 512)
                k.act("copy", out=qk[:, 0:512], in_=ps_q[:, :], r=[ps_q.k()], w=[qk.k()])
                yield
            tm_group(s, psx[:, 496:512], psx.k(), C_AB, 16)
            abst = abs_[sl]
            k.dve("tensor_copy", out=abst[:, :], in_=psx[:, 496:512], r=[psx.k()], w=[abst.k()])
            k.dma(AB[tile_id], abst[:, :], r=[abst.k()], w=[("dram", "AB", tile_id)])
            yield
            tm_group(s, ps_kv[:, :], ps_kv.k(), C_KB, 512)
            vst = vs[sl]
            k.act("copy", out=vst[:, :], in_=ps_kv[:, 256:512], r=[ps_kv.k()], w=[vst.k()])
            k.dma(V[tile_id], vst[:, :], r=[vst.k()], w=[("dram", "V", tile_id)])
            yield
            k.dve("tensor_copy", out=qk[:, 512:768], in_=ps_kv[:, 0:256], r=[ps_kv.k()], w=[qk.k()])
            yield

        def stage_b(s):
            sl = s % 2
            r0 = t0 + s * 128
            psx = PSX[sl]
            pxT = psx.ap.bitcast(BF16)
            qk, ssq_t, qkb, sq_, qn_, r1_, r2_ = QK[sl], ssq[sl], QKb[sl], SQ[sl], QN[sl], R1[sl], R2[sl]
            c0 = 0 if lat else 512
            nh = 6 if lat else 2
            h0 = 0 if lat else 4
            k.pool("tensor_tensor", out=sq_[:, c0:768], in0=qk[:, c0:768], in1=qk[:, c0:768], op=ALU.mult, r=[qk.k()], w=[sq_.k()])
            yield
            k.dve("tensor_reduce", out=ssq_t[:, h0:6], in_=sq_[:, c0:768].rearrange("p (h d) -> p h d", d=128), axis=AX.X, op=ALU.add,
                  r=[sq_.k()], w=[ssq_t.k()])
            yield
            k.act("activation", out=ssq_t[:, 8 + h0:14], in_=ssq_t[:, h0:6], func=AF.Sqrt, scale=1.0 / 128, bias=NORM_EPS,
                  r=[ssq_t.k()], w=[ssq_t.k()])
            yield
            k.dve("reciprocal", out=ssq_t[:, 16 + h0:22], in_=ssq_t[:, 8 + h0:14], r=[ssq_t.k()], w=[ssq_t.k()])
            yield
            k.dve("tensor_tensor", out=qn_[:, c0:768].rearrange("p (h d) -> p h d", d=128), in0=qk[:, c0:768].rearrange("p (h d) -> p h d", d=128),
                  in1=ssq_t[:, 16 + h0:22].unsqueeze(2).to_broadcast([128, nh, 128]), op=ALU.mult, r=[qk.k(), ssq_t.k()], w=[qn_.k()])
            yield
            if not lat:
                k.pool("tensor_tensor", out=qkb[:, c0:768], in0=qn_[:, c0:768], in1=G6[:, c0:768], op=ALU.mult, r=[qn_.k(), G6.k()], w=[qkb.k()])
                yield
            else:
                cs, sn = CS[sl], SN[sl]
                k.dma(cs[:, :], ropecs[r0 // 128], r=[("dram", "rope_cs")], w=[cs.k()])
                k.dma(sn[:, :], ropesn[r0 // 128], r=[("dram", "rope_sn")], w=[sn.k()])
                k.pool("tensor_tensor", out=qn_[:, :], in0=qn_[:, :], in1=G6[:, :], op=ALU.mult, r=[qn_.k(), G6.k()], w=[qn_.k()])
                yield
                k.dve("tensor_tensor", out=r1_[:, :].rearrange("p (h d) -> p h d", d=128), in0=qn_[:, :].rearrange("p (h d) -> p h d", d=128),
                      in1=cs[:, :].unsqueeze(1).to_broadcast([128, 6, 128]), op=ALU.mult, r=[qn_.k(), cs.k()], w=[r1_.k()])
                yield
                qn5 = qn_[:, :].rearrange("p (h a b e) -> p h a b e", h=6, a=2, b=2)
                r25 = r2_[:, :].rearrange("p (h a b e) -> p h a b e", h=6, a=2, b=2)
                sn4 = sn[:, :].rearrange("p (a b e) -> p a b e", a=2, b=2)
                for bsel in range(2):
                    k.pool("tensor_tensor", out=r25[:, :, :, bsel, :], in0=qn5[:, :, :, 1 - bsel, :],
                           in1=sn4[:, :, bsel, :].unsqueeze(1).to_broadcast([128, 6, 2, 32]), op=ALU.mult, r=[qn_.k(), sn.k()], w=[r2_.k()])
                    yield
                k.dve("tensor_tensor", out=qkb[:, :], in0=r1_[:, :], in1=r2_[:, :], op=ALU.add, r=[r1_.k(), r2_.k()], w=[qkb.k()])
                yield
            for h in range(h0, 6):
                k.pe("transpose", pxT[:, h * 128:(h + 1) * 128], qkb[:, h * 128:(h + 1) * 128], c["identb"][:, :],
                     r=[qkb.k(), c["identb"].k()], w=[psx.k()])
            yield
            if lat:
                k.act("copy", out=qts[:, :].rearrange("p (h t) -> p h t", h=4)[:, :, s * 128:(s + 1) * 128],
                      in_=pxT[:, 0:512].rearrange("p (h t) -> p h t", h=4), r=[psx.k()], w=[qts.k()])
                yield
            k.dve("tensor_copy", out=kts[:, :].rearrange("p (h t) -> p h t", h=2)[:, :, s * 128:(s + 1) * 128],
                  in_=pxT[:, 512:768].rearrange("p (h t) -> p h t", h=2), r=[psx.k()], w=[kts.k()])
            yield

        def tm_part():
            for _ in stage_a(0):
                yield
            for s in range(nsub):
                gl = [stage_b(s)] + ([stage_a(s + 1)] if s + 1 < nsub else [])
                il = Interleaver(gl, 2)
                while il.step():
                    yield
            kcol0 = t0 if lat else T + t0
            for h in range(2):
                k.dma(KT[h][:, kcol0:kcol0 + W], kts[:, h * 512:h * 512 + W], r=[kts.k()], w=[("dram", "KT", h, mi)])
            if lat:
                for h in range(4):
                    k.dma(QT[h][:, t0:t0 + W], qts[:, h * 512:h * 512 + W], r=[qts.k()], w=[("dram", "QT", h, mi)])

        def fm_part():
            nonlocal fcnt
            gcol0 = (GP_LAT0 + t0) if lat else (GP_CTX0 + t0)
            for j in range((12 + (4 if lat else 0)) if P1_FLAGS["fm"] else 0):
                yield
                pf = ps_f[fcnt % 2]
                col0 = j * 128 if j < 12 else C_ZB + (j - 12) * 128
                for kc in range(8):
                    k.pe("matmul", pf[:, 0:W], lhsT=w_in[:, kc * EVEN_IN + col0:kc * EVEN_IN + col0 + 128], rhs=hT3[:, kc, 0:W],
                         start=(kc == 0), stop=(kc == 7), r=[hT.k(), w_in.k(kc)], w=[pf.k()])
                yield
                if j < 12:
                    g = gps[fcnt % 3]
                    if fcnt % 2 == 0:
                        k.dve("tensor_copy", out=g[:, 0:W], in_=pf[:, 0:W], r=[pf.k()], w=[g.k()])
                    else:
                        k.act("copy", out=g[:, 0:W], in_=pf[:, 0:W], r=[pf.k()], w=[g.k()])
                    k.dma(GPRE[j][:, gcol0:gcol0 + W], g[:, 0:W], r=[g.k()], w=[("dram", "GPRE", j, mi)])
                else:
                    g = zbs[fcnt % 3]
                    k.act("activation", out=g[:, 0:W], in_=pf[:, 0:W], func=AF.Silu, r=[pf.k()], w=[g.k()])
                    k.dma(ZBT[j - 12][:, t0:t0 + W], g[:, 0:W], r=[g.k()], w=[("dram", "ZBT", j - 12, mi)])
                fcnt += 1

        il2 = Interleaver([tm_part(), fm_part()], 2)
        while il2.step():
            yield

    for _ in prep_m(0):
        pass
    for mi in range(len(mt_list)):
        gl = [comp_m(mi)] + ([prep_m(mi + 1)] if mi + 1 < len(mt_list) else [])
        run_interleaved(gl, 2)
    k.release(m0)


def _rearr_w(w):
    n = w.shape[1]
    return np.ascontiguousarray(w.reshape(8, 128, n).transpose(1, 0, 2).reshape(128, 8 * n))


def _fm(v):
    return np.ascontiguousarray(v.reshape(-1, 128).T)


def _rope_tables():
    t = np.arange(T)
    row = (t // 64).astype(np.float32)
    col = (t % 64).astype(np.float32)
    inv = (10000.0 ** (-np.arange(0, 64, 2, dtype=np.float32) / 64)).astype(np.float32)
    ar = row[:, None] * inv
    ac = col[:, None] * inv
    cs = np.concatenate([np.cos(ar), np.cos(ar), np.cos(ac), np.cos(ac)], 1).astype(np.float32)
    sn = np.concatenate([-np.sin(ar), np.sin(ar), -np.sin(ac), np.sin(ac)], 1).astype(np.float32)
    return cs.reshape(32, 128, 128), sn.reshape(32, 128, 128)


def host_prep(inp):
    f = lambda a: np.asarray(a, dtype=np.float32)
    shared = {}
    shared["c_ident"] = np.eye(128, dtype=np.float32)
    sel = np.zeros((2, 256), np.float32)
    sel[0, :128] = 1.0
    sel[1, 128:] = 1.0
    shared["c_sel"] = sel
    shared["ada_w_r"] = np.stack([_rearr_w(f(inp["ada_w"][l])) for l in range(2)])
    rows = np.zeros((1, ROWS_N), np.float32)
    vals = {"ada_b0": inp["ada_b"][0], "ada_b1": inp["ada_b"][1], "pre_g0": inp["pre_norm_g"][0], "pre_g1": inp["pre_norm_g"][1],
            "post_g0": inp["post_norm_g"][0], "post_g1": inp["post_norm_g"][1], "b_out": inp["od_b_out"][0], "gdn_g": inp["ev_gdn_norm_g"][0],
            "qn_g": inp["ev_q_norm_g"][0], "kn_g": inp["ev_k_norm_g"][0], "a_log": f(inp["ev_a_log"][0]).reshape(-1), "dt_bias": f(inp["ev_dt_bias"][0]).reshape(-1)}
    for n_, v in vals.items():
        o, l = ROW_OFF[n_]
        rows[0, o:o + l] = f(v).reshape(-1)
    shared["rows"] = rows
    shared["ev_w_in_r"] = _rearr_w(f(inp["ev_w_in"][0]))
    shared["ev_w_out_r"] = _rearr_w(f(inp["ev_w_out"][0]))
    shared["od_w_in_r"] = _rearr_w(f(inp["od_w_in"][0]))
    shared["od_w_out_r"] = _rearr_w(f(inp["od_w_out"][0]))
    cs, sn = _rope_tables()
    shared["rope_cs"] = cs
    shared["rope_sn"] = sn
    pv = np.zeros((128, PV_N), np.float32)
    pv[:, PV_BIN:PV_BIN + 24] = _fm(f(inp["od_b_in"][0]))
    pv[:, PV_DWB:PV_DWB + 8] = _fm(f(inp["od_dw_b"][0]))
    pv[:, PV_LNG:PV_LNG + 8] = _fm(f(inp["od_ln_g"][0]))
    pv[:, PV_LNB:PV_LNB + 8] = _fm(f(inp["od_ln_b"][0]))
    pv[:, PV_DWW:PV_DWW + 248] = f(inp["od_dw_w"][0]).T.reshape(8, 128, 31).transpose(1, 0, 2).reshape(128, 248)
    pv[:, PV_C5W:PV_C5W + 60] = f(inp["ev_short_conv_w"][0]).T.reshape(12, 128, 5).transpose(1, 0, 2).reshape(128, 60)
    shared["pv"] = pv
    maps = []
    cctx = _fm(f(inp["c_ctx"]))
    for b in range(8):
        m = dict(shared)
        m["x"] = np.ascontiguousarray(f(inp["x"][b]))
        m["ctx"] = np.ascontiguousarray(f(inp["ctx"][b]))
        cv = np.zeros((128, 16), np.float32)
        cv[:, 0::2] = _fm(f(inp["c"][b]))
        cv[:, 1::2] = cctx
        m["cvec"] = cv
        maps.append(m)
    return maps


NKC = TK // 128


def p2b_attn(k, c, banks=None, as_gen=False):
    nc = k.nc
    m0 = None if as_gen else k.mark()
    QTd = k.dram("QT", [4, 128, T], BF16)
    KTd = k.dram("KT", [2, 128, TK], BF16)
    Vd = k.dram("V", [NTILE, 128, 256], BF16)
    ZBT = k.dram("ZBT", [4, 128, T], BF16)
    YT = k.dram("YT", [8, 128, T], BF16)
    QTs = k.alloc("QTa", 4 * T, BF16)
    KTs = k.alloc("KTa", 2 * TK, BF16)
    Vs = k.alloc("Va", NTILE * 256, BF16)
    for h in range(4):
        for hf in range(2):
            k.dma(QTs[:, h * T + hf * 2048:h * T + (hf + 1) * 2048], QTd[h][:, hf * 2048:(hf + 1) * 2048],
                  r=[("dram", "QT", h, m) for m in range(1, 9)], w=[QTs.k(h)])
    for h in range(2):
        k.dma(KTs[:, h * TK:(h + 1) * TK], KTd[h], r=[("dram", "KT", h, m) for m in range(9)], w=[KTs.k(h)])
    for t in range(NTILE):
        k.dma(Vs[:, t * 256:(t + 1) * 256], Vd[t], r=[("dram", "V", t)], w=[Vs.k(t)])
    onesb = k.alloc("onesb", 128, BF16)
    k.dve("memset", onesb[:, :], 1.0, w=[onesb.k()])
    PT = [k.alloc(f"PT{i}", 512, BF16) for i in range(4)]
    RD = [k.alloc(f"RD{i}", 512, F32) for i in range(2)]
    OO = [k.alloc(f"OO{i}", 512, F32) for i in range(2)]
    ZG = [k.alloc(f"ZGa{i}", 512, BF16) for i in range(2)]
    YB = [k.alloc(f"YB{i}", 512, BF16) for i in range(2)]
    if banks is None:
        ps_s = [k.PS[0], k.PS[1], k.PS[2], k.PS[3]]
        ps_o = [k.PS[4], k.PS[5]]
        ps_d = [k.PS[6], k.PS[7]]
    else:
        ps_s = [banks[0], banks[1]]
        ps_o = [banks[2], banks[2]]
        ps_d = [banks[3], banks[3]]
    NPS = len(ps_s)
    scale = 128.0 ** -0.5

    def body():
      it = 0
      sc = 0
      for qi in range(T // 512):
        for h in range(4):
            kv = h // 2
            po, pd = ps_o[it % 2], ps_d[it % 2]
            rd, oo, zg, yb = RD[it % 2], OO[it % 2], ZG[it % 2], YB[it % 2]
            k.dma(zg[:, :], ZBT[h][:, qi * 512:(qi + 1) * 512], r=[("dram", "ZBT", h, qi + 1)], w=[zg.k()])
            qsl = QTs[:, h * T + qi * 512:h * T + (qi + 1) * 512]

            def score(kc):
                ps = ps_s[(sc + kc) % NPS]
                k.pe("matmul", ps[:, :], lhsT=KTs[:, kv * TK + kc * 128:kv * TK + (kc + 1) * 128], rhs=qsl, start=True, stop=True,
                     r=[KTs.k(kv), QTs.k(h)], w=[ps.k()])
                pt = PT[(sc + kc) % 4]
                k.act("activation", out=pt[:, :], in_=ps[:, :], func=AF.Exp, scale=scale, r=[ps.k()], w=[pt.k()])

            score(0)
            score(1)
            for kc in range(NKC):
                if kc + 2 < NKC:
                    score(kc + 2)
                pt = PT[(sc + kc) % 4]
                tile_id = kc if kc < 32 else kc
                k.pe("matmul", po[:, :], lhsT=Vs[:, kc * 256 + kv * 128:kc * 256 + (kv + 1) * 128], rhs=pt[:, :], start=(kc == 0), stop=(kc == NKC - 1),
                     r=[Vs.k(kc), pt.k()], w=[po.k()])
                k.pe("matmul", pd[:, :], lhsT=onesb[:, :], rhs=pt[:, :], start=(kc == 0), stop=(kc == NKC - 1),
                     r=[onesb.k(), pt.k()], w=[pd.k()])
                yield
            sc += NKC
            k.dve("reciprocal", out=rd[:, :], in_=pd[:, :], r=[pd.k()], w=[rd.k()])
            k.dve("tensor_tensor", out=oo[:, :], in0=po[:, :], in1=rd[:, :], op=ALU.mult, r=[po.k(), rd.k()], w=[oo.k()])
            k.pool("tensor_tensor", out=yb[:, :], in0=oo[:, :], in1=zg[:, :], op=ALU.mult, r=[oo.k(), zg.k()], w=[yb.k()])
            k.dma(YT[4 + h][:, qi * 512:(qi + 1) * 512], yb[:, :], r=[yb.k()], w=[("dram", "YT", 4 + h, qi)])
            it += 1
            yield

    if as_gen:
        return body()
    for _ in body():
        pass
    k.release(m0)


def p3_outproj(k, c):
    nc = k.nc
    m0 = k.mark()
    x = k.dram("x", [T, D], F32)
    x1 = k.dram("x1", [T, D], F32)
    modd = k.dram("mod", [8, 128, D], F32)
    YT = k.dram("YT", [8, 128, T], BF16)
    w_out_d = k.dram("ev_w_out_r", [128, 8 * 1024], F32)
    w_out = k.alloc("w_out0", 8 * 1024, BF16)
    k.dma(w_out[:, :], w_out_d, r=[("dram", "ev_w_out_r")], w=[w_out.k()], q="pool")
    G0 = k.alloc("G0", D, F32)
    k.dma(G0[:, :], modd[2], r=[("dram", "mod")], w=[G0.k()])
    YTt = [k.alloc(f"YTt{i}", 8 * 512, BF16) for i in range(2)]
    O = [k.alloc(f"O{i}", D, F32) for i in range(2)]
    junks = [k.alloc(f"junkp3{i}", D, BF16) for i in range(2)]
    ss = [k.alloc(f"ssp{i}", 8, F32) for i in range(2)]
    XR = [k.alloc(f"XR{i}", D, F32) for i in range(2)]
    ps_o = [(k.PS[0], k.PS[1]), (k.PS[2], k.PS[3])]
    oc = 0
    for m in range(T // 512):
        yt = YTt[m % 2]
        for j in range(8):
            k.dma(yt[:, j * 512:(j + 1) * 512], YT[j][:, m * 512:(m + 1) * 512], r=[("dram", "YT", j, m)], w=[yt.k(j)])
        def subtile(s, slot):
            r0 = m * 512 + s * 128
            po = ps_o[slot]
            o, sst, xr, jk = O[slot], ss[slot], XR[slot], junks[slot]
            k.dma(xr[:, :], x[r0:r0 + 128, :], r=[("dram", "x", r0 // 128)], w=[xr.k()])
            for nh in range(2):
                for j in range(8):
                    k.pe("matmul", po[nh][:, :], lhsT=yt[:, j * 512 + s * 128:j * 512 + (s + 1) * 128], rhs=w_out[:, j * 1024 + nh * 512:j * 1024 + (nh + 1) * 512],
                         start=(j == 0), stop=(j == 7), r=[yt.k(j), w_out.k()], w=[po[nh].k()])
                yield
            k.act("copy", out=o[:, 0:512], in_=po[0][:, :], r=[po[0].k()], w=[o.k()])
            yield
            k.dve("tensor_copy", out=o[:, 512:1024], in_=po[1][:, :], r=[po[1].k()], w=[o.k()])
            yield
            yield from post_res_gen(k, o, sst, jk, G0, xr, o, xr, x1[r0:r0 + 128, :], ("dram", "x1", r0 // 128))
        run_interleaved([functools.partial(subtile, s) for s in range(4)], 2, slotted=True)
    k.release(m0)


def p1b_gdnprep(k, c):
    nc = k.nc
    m0 = k.mark()
    GPRE = k.dram("GPRE", [12, 128, GP_W], BF16)
    pvd = k.dram("pv", [128, PV_N], F32)
    GQT = k.dram("GQT", [4, 128, TK], BF16)
    GKT = k.dram("GKT", [4, 128, TK], BF16)
    GK = k.dram("GK", [NTILE, 128, 512], BF16)
    GV = k.dram("GV", [NTILE, 128, 512], BF16)
    pv = k.alloc("pv", PV_N, F32)
    k.dma(pv[:, :], pvd, r=[("dram", "pv")], w=[pv.k()])
    DG = k.alloc("DG5", 60 * 128, BF16)
    for i in range(60):
        eng = "dve" if i % 2 == 0 else "pool"
        k.any(eng, "tensor_scalar", out=DG[:, i * 128:(i + 1) * 128], in0=c["identf"][:, :], scalar1=pv[:, PV_C5W + i:PV_C5W + i + 1], scalar2=None,
              op0=ALU.mult, r=[c["identf"].k(), pv.k()], w=[DG.k(i // 5)])
    onesb = k.alloc("onesb", 128, BF16)
    k.dve("memset", onesb[:, :], 1.0, w=[onesb.k()])
    PRE = [k.alloc(f"PRE5{i}", 516, BF16) for i in range(8)]
    U = [k.alloc(f"U5{i}", 512, F32) for i in range(8)]
    SQ = [k.alloc(f"SQ5{i}", 512, BF16) for i in range(8)]
    RS = [k.alloc(f"RS5{i}", 512, F32) for i in range(8)]
    UN = [k.alloc(f"UN5{i}", 512, BF16) for i in range(8)]
    GKs = [k.alloc(f"GKs{i}", 4 * 512, BF16) for i in range(2)]
    GVs = [k.alloc(f"GVs{i}", 4 * 512, BF16) for i in range(2)]
    ps_cv = list(k.PS)
    ps_ss = ps_cv
    ps_tr = ps_cv
    mts = [("ctx", 0, 256)] + [("lat", m * 512, 512) for m in range(T // 512)]
    cc = 0
    for mi, (kind, t0, W) in enumerate(mts):
        lat = kind == "lat"
        gcol0 = (GP_LAT0 + t0) if lat else (GP_CTX0 + t0)
        tile0 = (t0 // 128) if lat else 32
        col0 = tile0 * 128
        nsub = W // 128
        gks, gvs = GKs[mi % 2], GVs[mi % 2]
        def chunk(j, cc, sl):
            pre = PRE[sl]
            pcv = ps_cv[sl]
            k.dma(pre[:, 0:W + 4], GPRE[j][:, gcol0 - 2:gcol0 + W + 2],
                  r=[("dram", "GPRE", j, x_) for x_ in (["p0", "p1", "p2"] + list(range(max(0, mi - 1), min(9, mi + 2))))], w=[pre.k()])
            for t in range(5):
                k.pe("matmul", pcv[:, 0:W], lhsT=DG[:, (j * 5 + t) * 128:(j * 5 + t + 1) * 128], rhs=pre[:, t:t + W], start=(t == 0), stop=(t == 4),
                     r=[DG.k(j), pre.k()], w=[pcv.k()])
            un = UN[sl]
            if j < 8:
                u, sq, rs, pss = U[sl], SQ[sl], RS[sl], ps_ss[sl]
                k.act("activation", out=u[:, 0:W], in_=pcv[:, 0:W], func=AF.Silu, r=[pcv.k()], w=[u.k()])
                yield
                k.pool("tensor_tensor", out=sq[:, 0:W], in0=u[:, 0:W], in1=u[:, 0:W], op=ALU.mult, r=[u.k()], w=[sq.k()])
                yield
                k.pe("matmul", pss[:, 0:W], lhsT=onesb[:, :], rhs=sq[:, 0:W], start=True, stop=True, r=[onesb.k(), sq.k()], w=[pss.k()])
                yield
                k.act("activation", out=rs[:, 0:W], in_=pss[:, 0:W], func=AF.Sqrt, bias=NORM_EPS, scale=1.0, r=[pss.k()], w=[rs.k()])
                yield
                k.dve("reciprocal", out=rs[:, 0:W], in_=rs[:, 0:W], r=[rs.k()], w=[rs.k()])
                yield
                if j < 4:
                    k.dve("scalar_tensor_tensor", out=un[:, 0:W], in0=u[:, 0:W], scalar=128.0 ** -0.5, in1=rs[:, 0:W], op0=ALU.mult, op1=ALU.mult,
                          r=[u.k(), rs.k()], w=[un.k()])
                    k.dma(GQT[j][:, col0:col0 + W], un[:, 0:W], r=[un.k()], w=[("dram", "GQT", j, mi)])
                    yield
                else:
                    k.dve("tensor_tensor", out=un[:, 0:W], in0=u[:, 0:W], in1=rs[:, 0:W], op=ALU.mult, r=[u.k(), rs.k()], w=[un.k()])
                    yield
                    k.dma(GKT[j - 4][:, col0:col0 + W], un[:, 0:W], r=[un.k()], w=[("dram", "GKT", j - 4, mi)])
                    yield
            else:
                k.act("activation", out=un[:, 0:W], in_=pcv[:, 0:W], func=AF.Silu, r=[pcv.k()], w=[un.k()])
                yield
            if j >= 4:
                h = (j - 4) % 4
                ptr = ps_tr[sl]
                ptb = ptr.ap.bitcast(BF16)
                for s in range(nsub):
                    k.pe("transpose", ptb[:, s * 128:(s + 1) * 128], un[:, s * 128:(s + 1) * 128], c["identb"][:, :], r=[un.k(), c["identb"].k()], w=[ptr.k()])
                    yield
                dst = (gks if j < 8 else gvs)
                dview = dst[:, :].rearrange("p (s f) -> p s f", s=4)[:, 0:nsub, h * 128:(h + 1) * 128]
                sview = ptb[:, 0:nsub * 128].rearrange("p (s f) -> p s f", f=128)
                if cc % 2 == 0:
                    k.act("copy", out=dview, in_=sview, r=[ptr.k()], w=[dst.k()])
                    yield
                else:
                    k.dve("tensor_copy", out=dview, in_=sview, r=[ptr.k()], w=[dst.k()])
                    yield
            yield
        run_interleaved([functools.partial(chunk, j, cc + j) for j in range(12)], 8, slotted=True)
        cc += 12
        for s in range(nsub):
            k.dma(GK[tile0 + s], gks[:, s * 512:(s + 1) * 512], r=[gks.k()], w=[("dram", "GK", tile0 + s)])
            k.dma(GV[tile0 + s], gvs[:, s * 512:(s + 1) * 512], r=[gvs.k()], w=[("dram", "GV", tile0 + s)])
    k.release(m0)


GDN_LAG = 45


def p2a_gdn(k, c, nslots=4, as_gens=False):
    nc = k.nc
    m0 = None if as_gens else k.mark()
    GQT = k.dram("GQT", [4, 128, TK], BF16)
    GKT = k.dram("GKT", [4, 128, TK], BF16)
    GK = k.dram("GK", [NTILE, 128, 512], BF16)
    GV = k.dram("GV", [NTILE, 128, 512], BF16)
    ABd = k.dram("AB", [NTILE, 128, 16], F32)
    trid = k.dram("c_tri", [9, 128, 128], F32)
    OD = [k.dram("OF", [32, 128, 512], F32), k.dram("OB", [32, 128, 512], F32)]
    TRI = k.alloc("TRI", 9 * 128, F32)
    for i in range(9):
        k.dma(TRI[:, i * 128:(i + 1) * 128], trid[i], r=[("dram", "c_tri")], w=[TRI.k()])
    tri = lambda i: TRI[:, i * 128:(i + 1) * 128]
    bc4 = lambda ap: ap.unsqueeze(1).to_broadcast([128, 4, 128])
    col4 = lambda ap: ap.unsqueeze(2).to_broadcast([128, 4, 128])
    v3 = lambda t: t[:, :].rearrange("p (h f) -> p h f", h=4)
    ABs = k.alloc("ABs", NTILE * 16, F32)
    for t in range(NTILE):
        k.dma(ABs[:, t * 16:(t + 1) * 16], ABd[t], r=[("dram", "AB", t)], w=[ABs.k()])
    alog = k.alloc("alog", 8, F32)
    dtb = k.alloc("dtb", 8, F32)
    k.dma(alog[:, :], row_bc(k, "a_log"), r=[("dram", "rows")], w=[alog.k()])
    k.dma(dtb[:, :], row_bc(k, "dt_bias"), r=[("dram", "rows")], w=[dtb.k()])
    GALL = k.alloc("GALL", NTILE * 8, F32)
    BALL = k.alloc("BALL", NTILE * 8, F32)
    ab3 = ABs[:, :].rearrange("p (t f) -> p t f", f=16)
    g3 = GALL[:, :].rearrange("p (t f) -> p t f", f=8)
    b3 = BALL[:, :].rearrange("p (t f) -> p t f", f=8)
    bct = lambda ap: ap.unsqueeze(1).to_broadcast([128, NTILE, 8])
    k.dve("tensor_tensor", out=g3, in0=ab3[:, :, 0:8], in1=bct(dtb[:, :]), op=ALU.add, r=[ABs.k(), dtb.k()], w=[GALL.k()])
    k.act("activation", out=GALL[:, :], in_=GALL[:, :], func=AF.Exp, r=[GALL.k()], w=[GALL.k()])
    k.act("activation", out=GALL[:, :], in_=GALL[:, :], func=AF.Ln, bias=1.0, scale=1.0, r=[GALL.k()], w=[GALL.k()])
    k.act("activation", out=alog[:, :], in_=alog[:, :], func=AF.Exp, r=[alog.k()], w=[alog.k()])
    k.dve("scalar_tensor_tensor", out=g3, in0=g3, scalar=-1.0, in1=bct(alog[:, :]), op0=ALU.mult, op1=ALU.mult, r=[GALL.k(), alog.k()], w=[GALL.k()])
    k.act("activation", out=b3, in_=ab3[:, :, 8:16], func=AF.Exp, scale=-1.0, r=[ABs.k()], w=[BALL.k()])
    k.dve("tensor_scalar", out=BALL[:, :], in0=BALL[:, :], scalar1=1.0, scalar2=None, op0=ALU.add, r=[BALL.k()], w=[BALL.k()])
    k.dve("reciprocal", out=BALL[:, :], in_=BALL[:, :], r=[BALL.k()], w=[BALL.k()])
    Sf = [k.alloc(f"Sf{d}", 512, F32) for d in range(2)]
    Sb = [k.alloc(f"Sb{d}", 512, BF16) for d in range(2)]
    for d in range(2):
        k.dve("memset", Sf[d][:, :], 0.0, w=[Sf[d].k()])
        k.pool("memset", Sb[d][:, :], 0.0, w=[Sb[d].k()])
    def bufs(d):
        B = {}
        for n_ in ["qT4", "kT4", "ktok", "vtok", "X", "XT", "PT", "AINC", "AINCT", "KD", "ATn", "QEFF", "N1", "N1T", "N2", "P", "V1", "U1"]:
            B[n_] = k.alloc(f"{n_}{d}", 512, BF16)
        for n_ in ["WUR", "WU"]:
            B[n_] = k.alloc(f"{n_}{d}", 1024, BF16)
        for n_ in ["DIFF", "E", "DMS", "DMI", "T1", "ER", "QD", "OUT"]:
            B[n_] = k.alloc(f"{n_}{d}", 512, F32)
        B["SM"] = k.alloc(f"SM{d}", 32, F32)
        return B
    BUF = [bufs(i) for i in range(nslots)]
    order = [[32, 33] + list(range(32)), [33, 32] + list(range(31, -1, -1))]
    cfg = [dict(tri=2, mi=0, ms=1, jl=127, m1a=5, m1b=6, m2a=7), dict(tri=0, mi=2, ms=3, jl=0, m1a=6, m1b=5, m2a=8)]
    identb = c["identb"]
    sdone = {}

    def unit(n, d, slot):
        Q = k.PS[2 * slot:2 * slot + 2]
        P_GR = P_B = P_W0 = P_Z = Q[0]
        P_SM = P_A = P_T = P_W1 = P_Z2 = Q[1]
        if n == 1 and nslots == 4:
            for _ in range(GDN_LAG):
                yield
        g = order[d][n]
        B = BUF[slot]
        cf = cfg[d]
        lat = g < 32
        sm = B["SM"]
        GC, GL, EG, BE, KDS, GT, TMP = (sm[:, 0:4], sm[:, 4:8], sm[:, 8:12], sm[:, 12:16], sm[:, 16:20], sm[:, 20:24], sm[:, 24:28])
        gcol = GALL[:, g * 8 + d * 4:g * 8 + d * 4 + 4]
        bcol = BALL[:, g * 8 + d * 4:g * 8 + d * 4 + 4]
        for h in range(4):
            k.dma(B["qT4"][:, h * 128:(h + 1) * 128], GQT[h][:, g * 128:(g + 1) * 128], r=[("dram", "GQT", h, mi_) for mi_ in range(9)], w=[B["qT4"].k()])
            k.dma(B["kT4"][:, h * 128:(h + 1) * 128], GKT[h][:, g * 128:(g + 1) * 128], r=[("dram", "GKT", h, mi_) for mi_ in range(9)], w=[B["kT4"].k()])
        k.dma(B["ktok"][:, :], GK[g], r=[("dram", "GK", g)], w=[B["ktok"].k()])
        yield
        k.dma(B["vtok"][:, :], GV[g], r=[("dram", "GV", g)], w=[B["vtok"].k()])
        yield
        for h in range(4):
            k.pe("matmul", P_GR[:, h * 128:(h + 1) * 128], lhsT=gcol[:, h:h + 1].to_broadcast([128, 128]), rhs=tri(cf["tri"]), start=True, stop=True,
                 r=[GALL.k(), TRI.k()], w=[P_GR.k()])
        k.pe("matmul", P_SM[:, 0:4], lhsT=tri(cf["tri"]), rhs=gcol, start=True, stop=True, r=[GALL.k(), TRI.k()], w=[P_SM.k()])
        yield
        k.act("copy", out=GC, in_=P_SM[:, 0:4], r=[P_SM.k()], w=[sm.k()])
        yield
        k.dve("tensor_copy", out=GL, in_=v3(P_GR)[:, :, cf["jl"]], r=[P_GR.k()], w=[sm.k()])
        yield
        k.dve("tensor_tensor", out=v3(B["DIFF"]), in0=col4(GC), in1=v3(P_GR), op=ALU.subtract, r=[sm.k(), P_GR.k()], w=[B["DIFF"].k()])
        yield
        k.act("activation", out=B["ER"][:, :], in_=P_GR[:, :], func=AF.Exp, r=[P_GR.k()], w=[B["ER"].k()])
        yield
        k.pool("tensor_scalar", out=B["DIFF"][:, :], in0=B["DIFF"][:, :], scalar1=0.0, scalar2=None, op0=ALU.min, r=[B["DIFF"].k()], w=[B["DIFF"].k()])
        yield
        k.act("activation", out=B["E"][:, :], in_=B["DIFF"][:, :], func=AF.Exp, r=[B["DIFF"].k()], w=[B["E"].k()])
        yield
        k.pool("tensor_tensor", out=v3(B["DMS"]), in0=v3(B["E"]), in1=bc4(tri(cf["ms"])), op=ALU.mult, r=[B["E"].k(), TRI.k()], w=[B["DMS"].k()])
        yield
        k.pool("tensor_tensor", out=v3(B["DMI"]), in0=v3(B["E"]), in1=bc4(tri(cf["mi"])), op=ALU.mult, r=[B["E"].k(), TRI.k()], w=[B["DMI"].k()])
        yield
        k.act("activation", out=EG, in_=GC, func=AF.Exp, r=[sm.k()], w=[sm.k()])
        yield
        k.dve("tensor_tensor", out=BE, in0=EG, in1=bcol, op=ALU.mult, r=[sm.k(), BALL.k()], w=[sm.k()])
        yield
        k.dve("tensor_tensor", out=TMP, in0=GL, in1=GC, op=ALU.subtract, r=[sm.k()], w=[sm.k()])
        yield
        k.act("activation", out=KDS, in_=TMP, func=AF.Exp, r=[sm.k()], w=[sm.k()])
        yield
        k.act("activation", out=GT, in_=GL, func=AF.Exp, r=[sm.k()], w=[sm.k()])
        yield
        for h in range(4):
            k.pe("matmul", P_A[:, h * 128:(h + 1) * 128], lhsT=B["kT4"][:, h * 128:(h + 1) * 128], rhs=B["kT4"][:, h * 128:(h + 1) * 128], start=True, stop=True,
                 r=[B["kT4"].k()], w=[P_A.k()])
        for h in range(4):
            k.pe("matmul", P_B[:, h * 128:(h + 1) * 128], lhsT=B["qT4"][:, h * 128:(h + 1) * 128], rhs=B["kT4"][:, h * 128:(h + 1) * 128], start=True, stop=True,
                 r=[B["qT4"].k(), B["kT4"].k()], w=[P_B.k()])
        k.dve("tensor_tensor", out=B["T1"][:, :], in0=P_A[:, :], in1=B["DMS"][:, :], op=ALU.mult, r=[P_A.k(), B["DMS"].k()], w=[B["T1"].k()])
        yield
        k.dve("scalar_tensor_tensor", out=v3(B["X"]), in0=v3(B["T1"]), scalar=-1.0, in1=col4(bcol), op0=ALU.mult, op1=ALU.mult,
               r=[B["T1"].k(), BALL.k()], w=[B["X"].k()])
        k.dve("tensor_tensor", out=B["AINC"][:, :], in0=P_B[:, :], in1=B["DMI"][:, :], op=ALU.mult, r=[P_B.k(), B["DMI"].k()], w=[B["AINC"].k()])
        yield
        ptb = P_T.ap.bitcast(BF16)
        for h in range(4):
            k.pe("transpose", ptb[:, h * 128:(h + 1) * 128], B["X"][:, h * 128:(h + 1) * 128], identb[:, :], r=[B["X"].k(), identb.k()], w=[P_T.k()])
        for h in range(4):
            k.pe("transpose", ptb[:, 512 + h * 128:512 + (h + 1) * 128], B["AINC"][:, h * 128:(h + 1) * 128], identb[:, :], r=[B["AINC"].k(), identb.k()], w=[P_T.k()])
        k.act("copy", out=B["XT"][:, :], in_=ptb[:, 0:512], r=[P_T.k()], w=[B["XT"].k()])
        yield
        k.act("copy", out=B["AINCT"][:, :], in_=ptb[:, 512:1024], r=[P_T.k()], w=[B["AINCT"].k()])
        yield
        k.pool("tensor_tensor", out=v3(B["N1"]), in0=v3(B["X"]), in1=bc4(tri(cf["m1a"])), op=ALU.mult, r=[B["X"].k(), TRI.k()], w=[B["N1"].k()])
        yield
        k.pool("tensor_tensor", out=v3(B["N1T"]), in0=v3(B["XT"]), in1=bc4(tri(cf["m1b"])), op=ALU.mult, r=[B["XT"].k(), TRI.k()], w=[B["N1T"].k()])
        yield
        k.pool("tensor_tensor", out=v3(B["N2"]), in0=v3(B["X"]), in1=bc4(tri(cf["m2a"])), op=ALU.mult, r=[B["X"].k(), TRI.k()], w=[B["N2"].k()])
        yield
        k.dve("tensor_tensor", out=v3(B["X"]), in0=v3(B["X"]), in1=bc4(tri(4)), op=ALU.mult, r=[B["X"].k(), TRI.k()], w=[B["X"].k()])
        yield
        k.dve("tensor_tensor", out=v3(B["XT"]), in0=v3(B["XT"]), in1=bc4(tri(4)), op=ALU.mult, r=[B["XT"].k(), TRI.k()], w=[B["XT"].k()])
        yield
        k.pool("tensor_tensor", out=v3(B["P"]), in0=v3(B["X"]), in1=bc4(identb[:, :]), op=ALU.add, r=[B["X"].k(), identb.k()], w=[B["P"].k()])
        yield
        k.dve("tensor_tensor", out=v3(B["PT"]), in0=v3(B["XT"]), in1=bc4(identb[:, :]), op=ALU.add, r=[B["XT"].k(), identb.k()], w=[B["PT"].k()])
        yield

        def mm4(ps, lhs, rhs, acc=None):
            for h in range(4):
                sl = slice(h * 128, (h + 1) * 128)
                k.pe("matmul", ps[:, sl], lhsT=B[lhs][:, sl], rhs=B[rhs][:, sl], start=True, stop=(acc is None), r=[B[lhs].k(), B[rhs].k()], w=[ps.k()])
                if acc is not None:
                    k.pe("matmul", ps[:, sl], lhsT=identb[:, :], rhs=B[acc][:, sl], start=False, stop=True, r=[identb.k(), B[acc].k()], w=[ps.k()])

        for l in range(1, 5):
            mm4(P_A, "XT", "X")
            yield
            mm4(P_B, "X", "XT")
            yield
            k.act("copy", out=B["X"][:, :], in_=P_A[:, :], r=[P_A.k()], w=[B["X"].k()])
            yield
            k.act("copy", out=B["XT"][:, :], in_=P_B[:, :], r=[P_B.k()], w=[B["XT"].k()])
            yield
            mm4(P_T, "XT", "P", acc="P")
            yield
            mm4(P_W0, "X", "PT", acc="PT")
            yield
            k.act("copy", out=B["P"][:, :], in_=P_T[:, :], r=[P_T.k()], w=[B["P"].k()])
            yield
            k.dve("tensor_copy", out=B["PT"][:, :], in_=P_W0[:, :], r=[P_W0.k()], w=[B["PT"].k()])
            yield
        mm4(P_A, "N1T", "P")
        yield
        mm4(P_B, "N1", "PT")
        yield
        k.act("copy", out=B["V1"][:, :], in_=P_A[:, :], r=[P_A.k()], w=[B["V1"].k()])
        yield
        k.act("copy", out=B["U1"][:, :], in_=P_B[:, :], r=[P_B.k()], w=[B["U1"].k()])
        yield
        mm4(P_T, "PT", "V1", acc="P")
        yield
        mm4(P_W0, "P", "U1", acc="PT")
        yield
        k.act("copy", out=B["P"][:, :], in_=P_T[:, :], r=[P_T.k()], w=[B["P"].k()])
        yield
        k.dve("tensor_copy", out=B["PT"][:, :], in_=P_W0[:, :], r=[P_W0.k()], w=[B["PT"].k()])
        yield
        mm4(P_A, "N2", "PT")
        yield
        k.act("copy", out=B["U1"][:, :], in_=P_A[:, :], r=[P_A.k()], w=[B["U1"].k()])
        yield
        mm4(P_T, "P", "U1", acc="PT")
        yield
        k.act("copy", out=B["PT"][:, :], in_=P_T[:, :], r=[P_T.k()], w=[B["PT"].k()])
        yield
        wur = B["WUR"][:, :].rearrange("p (h f) -> p h f", h=4)
        k.pool("tensor_tensor", out=wur[:, :, 0:128], in0=v3(B["ktok"]), in1=col4(BE), op=ALU.mult, r=[B["ktok"].k(), sm.k()], w=[B["WUR"].k()])
        yield
        k.pool("tensor_tensor", out=wur[:, :, 128:256], in0=v3(B["vtok"]), in1=col4(bcol), op=ALU.mult, r=[B["vtok"].k(), BALL.k()], w=[B["WUR"].k()])
        yield
        k.pool("tensor_tensor", out=v3(B["KD"]), in0=v3(B["ktok"]), in1=col4(KDS), op=ALU.mult, r=[B["ktok"].k(), sm.k()], w=[B["KD"].k()])
        yield
        for h in range(4):
            pw = P_W0 if h < 2 else P_W1
            k.pe("matmul", pw[:, (h % 2) * 256:(h % 2 + 1) * 256], lhsT=B["PT"][:, h * 128:(h + 1) * 128], rhs=B["WUR"][:, h * 256:(h + 1) * 256], start=True, stop=True,
                 r=[B["PT"].k(), B["WUR"].k()], w=[pw.k()])
        k.act("copy", out=B["WU"][:, 0:512], in_=P_W0[:, :], r=[P_W0.k()], w=[B["WU"].k()])
        yield
        k.act("copy", out=B["WU"][:, 512:1024], in_=P_W1[:, :], r=[P_W1.k()], w=[B["WU"].k()])
        yield
        wv = lambda h: B["WU"][:, h * 256:h * 256 + 128]
        uv = lambda h: B["WU"][:, h * 256 + 128:h * 256 + 256]
        for h in range(4):
            k.pe("matmul", P_Z[:, h * 128:(h + 1) * 128], lhsT=wv(h), rhs=B["KD"][:, h * 128:(h + 1) * 128], start=True, stop=True,
                 r=[B["WU"].k(), B["KD"].k()], w=[P_Z.k()])
        k.act("activation", out=B["ATn"][:, :], in_=P_Z[:, :], func=AF.Copy, scale=-1.0, r=[P_Z.k()], w=[B["ATn"].k()])
        yield
        while n > 0 and not sdone.get((n - 1, d)):
            yield
        if lat:
            k.pool("tensor_tensor", out=B["QD"][:, :], in0=B["qT4"][:, :], in1=B["ER"][:, :], op=ALU.mult, r=[B["qT4"].k(), B["ER"].k()], w=[B["QD"].k()])
            for h in range(4):
                k.pe("matmul", P_Z2[:, h * 128:(h + 1) * 128], lhsT=wv(h), rhs=B["AINCT"][:, h * 128:(h + 1) * 128], start=True, stop=True,
                     r=[B["WU"].k(), B["AINCT"].k()], w=[P_Z2.k()])
            k.dve("tensor_tensor", out=B["QEFF"][:, :], in0=B["QD"][:, :], in1=P_Z2[:, :], op=ALU.subtract, r=[B["QD"].k(), P_Z2.k()], w=[B["QEFF"].k()])
            assert n == 0 or sdone.get((n - 1, d)), f"GDN interleave order violated (o) at n={n} d={d}"
            for h in range(4):
                sl = slice(h * 128, (h + 1) * 128)
                k.pe("matmul", P_Z[:, sl], lhsT=B["QEFF"][:, sl], rhs=Sb[d][:, sl], start=True, stop=False, r=[B["QEFF"].k(), Sb[d].k()], w=[P_Z.k()])
                k.pe("matmul", P_Z[:, sl], lhsT=B["AINCT"][:, sl], rhs=uv(h), start=False, stop=True, r=[B["AINCT"].k(), B["WU"].k()], w=[P_Z.k()])
            k.act("copy", out=B["OUT"][:, :], in_=P_Z[:, :], r=[P_Z.k()], w=[B["OUT"].k()])
            k.dma(OD[d][g], B["OUT"][:, :], r=[B["OUT"].k()], w=[("dram", "O", d, g)])
        assert n == 0 or sdone.get((n - 1, d)), f"GDN interleave order violated at n={n} d={d}"
        for h in range(4):
            sl = slice(h * 128, (h + 1) * 128)
            k.pe("matmul", P_Z2[:, sl], lhsT=B["ATn"][:, sl], rhs=Sb[d][:, sl], start=True, stop=False, r=[B["ATn"].k(), Sb[d].k()], w=[P_Z2.k()])
            k.pe("matmul", P_Z2[:, sl], lhsT=B["KD"][:, sl], rhs=uv(h), start=False, stop=True, r=[B["KD"].k(), B["WU"].k()], w=[P_Z2.k()])
        k.pool("tensor_tensor", out=v3(Sf[d]), in0=v3(Sf[d]), in1=col4(GT), op=ALU.mult, r=[Sf[d].k(), sm.k()], w=[Sf[d].k()])
        yield
        k.dve("tensor_tensor", out=Sf[d][:, :], in0=Sf[d][:, :], in1=P_Z2[:, :], op=ALU.add, r=[Sf[d].k(), P_Z2.k()], w=[Sf[d].k()])
        yield
        k.act("copy", out=Sb[d][:, :], in_=Sf[d][:, :], r=[Sf[d].k()], w=[Sb[d].k()])
        sdone[(n, d)] = True
        yield

    gens = []
    for n in range(NTILE):
        for d in range(2):
            gens.append(functools.partial(unit, n, d))
    if as_gens:
        return gens
    run_interleaved(gens, nslots, slotted=True)
    k.release(m0)


def gdn_consts():
    idx = np.arange(128)
    ge = (idx[:, None] >= idx[None, :]).astype(np.float32)
    gt = (idx[:, None] > idx[None, :]).astype(np.float32)
    bd32 = (idx[:, None] // 32 == idx[None, :] // 32).astype(np.float32)
    m1l = ((idx[:, None] // 64 == idx[None, :] // 64) & (idx[:, None] // 32 == idx[None, :] // 32 + 1)).astype(np.float32)
    m2l = ((idx[:, None] >= 64) & (idx[None, :] < 64)).astype(np.float32)
    return np.ascontiguousarray(np.stack([ge, gt, ge.T, gt.T, bd32, m1l, m1l.T, m2l, m2l.T]))


def p2c_gdnout(k, c):
    nc = k.nc
    m0 = k.mark()
    OF = k.dram("OF", [32, 128, 512], F32)
    OB = k.dram("OB", [32, 128, 512], F32)
    ZA = k.dram("ZA", [T, 512], BF16)
    YT = k.dram("YT", [8, 128, T], BF16)
    GG = k.alloc("GGn", 512, F32)
    for h in range(4):
        k.dma(GG[:, h * 128:(h + 1) * 128], row_bc(k, "gdn_g"), r=[("dram", "rows")], w=[GG.k()])
    NS = 4
    of = [k.alloc(f"of{i}", 512, F32) for i in range(NS)]
    ob = [k.alloc(f"ob{i}", 512, F32) for i in range(NS)]
    za = [k.alloc(f"zac{i}", 512, BF16) for i in range(NS)]
    o = [k.alloc(f"oc{i}", 512, F32) for i in range(NS)]
    sq = [k.alloc(f"sqc{i}", 512, F32) for i in range(NS)]
    st = [k.alloc(f"stc{i}", 16, F32) for i in range(NS)]
    yb = [k.alloc(f"yc{i}", 512, BF16) for i in range(NS)]
    yts = [k.alloc(f"ytc{i}", 4 * 512, BF16) for i in range(2)]
    ps_tr = [k.PS[0], k.PS[1], k.PS[2], k.PS[3]]
    v3 = lambda t: t[:, :].rearrange("p (h f) -> p h f", h=4)

    def tile(g, i):
        m = g // 4
        s = g % 4
        yt = yts[m % 2]
        k.dma(of[i][:, :], OF[g], r=[("dram", "O", 0, g)], w=[of[i].k()])
        k.dma(ob[i][:, :], OB[g], r=[("dram", "O", 1, g)], w=[ob[i].k()])
        k.dma(za[i][:, :], ZA[g * 128:(g + 1) * 128, :], r=[("dram", "ZA", g)], w=[za[i].k()])
        k.dve("tensor_tensor", out=o[i][:, :], in0=of[i][:, :], in1=ob[i][:, :], op=ALU.add, r=[of[i].k(), ob[i].k()], w=[o[i].k()])
        yield
        k.pool("tensor_tensor", out=sq[i][:, :], in0=o[i][:, :], in1=o[i][:, :], op=ALU.mult, r=[o[i].k()], w=[sq[i].k()])
        yield
        k.dve("tensor_reduce", out=st[i][:, 0:4], in_=v3(sq[i]), axis=AX.X, op=ALU.add, r=[sq[i].k()], w=[st[i].k()])
        yield
        k.act("activation", out=st[i][:, 4:8], in_=st[i][:, 0:4], func=AF.Sqrt, scale=1.0 / 128, bias=NORM_EPS, r=[st[i].k()], w=[st[i].k()])
        yield
        k.dve("reciprocal", out=st[i][:, 8:12], in_=st[i][:, 4:8], r=[st[i].k()], w=[st[i].k()])
        yield
        k.dve("tensor_tensor", out=v3(o[i]), in0=v3(o[i]), in1=st[i][:, 8:12].unsqueeze(2).to_broadcast([128, 4, 128]), op=ALU.mult,
              r=[o[i].k(), st[i].k()], w=[o[i].k()])
        yield
        k.pool("tensor_tensor", out=o[i][:, :], in0=o[i][:, :], in1=GG[:, :], op=ALU.mult, r=[o[i].k(), GG.k()], w=[o[i].k()])
        yield
        k.dve("tensor_tensor", out=yb[i][:, :], in0=o[i][:, :], in1=za[i][:, :], op=ALU.mult, r=[o[i].k(), za[i].k()], w=[yb[i].k()])
        yield
        ptr = ps_tr[i]
        ptb = ptr.ap.bitcast(BF16)
        for h in range(4):
            k.pe("transpose", ptb[:, h * 128:(h + 1) * 128], yb[i][:, h * 128:(h + 1) * 128], c["identb"][:, :], r=[yb[i].k(), c["identb"].k()], w=[ptr.k()])
        yield
        dview = yt[:, :].rearrange("p (h t) -> p h t", h=4)[:, :, s * 128:(s + 1) * 128]
        sview = ptb[:, 0:512].rearrange("p (h t) -> p h t", h=4)
        k.act("copy", out=dview, in_=sview, r=[ptr.k()], w=[yt.k()])
        yield

    for m in range(8):
        run_interleaved([functools.partial(tile, 4 * m + s_) for s_ in range(4)], NS, slotted=True)
        yt = yts[m % 2]
        for h in range(4):
            k.dma(YT[h][:, m * 512:(m + 1) * 512], yt[:, h * 512:(h + 1) * 512], r=[yt.k()], w=[("dram", "YT", h, m)])
    k.release(m0)


def p2ab(k, c):
    m0 = k.mark()
    gens = p2a_gdn(k, c, nslots=2, as_gens=True)
    att = p2b_attn(k, c, banks=k.PS[4:8], as_gen=True)
    il = Interleaver(gens, 2, slotted=True)
    g_alive, a_alive, r = True, True, 0
    while g_alive or a_alive:
        if g_alive:
            g_alive = il.step()
        if a_alive and (r % ATT_EVERY == 0 or not g_alive):
            try:
                next(att)
            except StopIteration:
                a_alive = False
        r += 1
    k.release(m0)


ATT_EVERY = 3
ALL_PASSES = None


def all_passes():
    return [p0_mod, p1_inproj, p1b_gdnprep, p2a_gdn, p2c_gdnout, p2b_attn, p3_outproj, l1_pass_a, l1_pass_b]


EXT_IN = ["x", "ctx", "cvec", "c_sel", "c_ident", "c_tri", "ada_w_r", "rows", "ev_w_in_r", "ev_w_out_r", "od_w_in_r", "od_w_out_r", "rope_cs", "rope_sn", "pv"]


def kernel(**inputs):
    maps = host_prep(inputs)
    tri = gdn_consts()
    for m in maps:
        m["c_tri"] = tri
    nc, _ = build_program(all_passes(), ext_in=EXT_IN, ext_out=["y"])
    in_maps = [{k_: m[k_] for k_ in EXT_IN} for m in maps]
    res = run_bass_kernel_spmd(nc, in_maps, core_ids=list(range(8)))
    return np.stack([np.asarray(r["y"], dtype=np.float32) for r in res.results], axis=0)
```

```python
import contextlib
import functools
import numpy as np
import concourse.bass as bass
import concourse.mybir as mybir
from concourse.bass_utils import run_bass_kernel_spmd

F32 = mybir.dt.float32
BF16 = mybir.dt.bfloat16
AF = mybir.ActivationFunctionType
ALU = mybir.AluOpType
AX = mybir.AxisListType

ENGS = ["pe", "act", "dve", "pool", "sp"]
NDMA_Q = {"sp": 20, "pool": 40}
EPOCH = 20000

D = 1024
T = 4096
CTX = 256
NORM_EPS = 1e-6


class Sched:
    def __init__(self, nc):
        self.nc = nc
        self.ops = []
        self.last_w = {}
        self.readers = {}
        self.cur = {e: {} for e in ENGS}
        self.pos = {e: 0 for e in ENGS}
        self.ndma = {"sp": 0, "pool": 0}
        self.dma_ops = {"sp": [], "pool": []}
        self.seen = set()
        self.inherit = {}

    def retire(self, names):
        names = set(names)
        for key in list(self.seen):
            if key[0] in names:
                cand = list(self.readers.get(key, ()))
                w = self.last_w.get(key)
                if w is not None:
                    cand.append(w)
                for c in cand:
                    o = self.ops[c]
                    old = self.inherit.get(o["src"])
                    if old is None or self.ops[old]["p"] < o["p"]:
                        self.inherit[o["src"]] = c
                self.seen.discard(key)
                self.readers.pop(key, None)
                self.last_w.pop(key, None)

    def _touch(self, key):
        if key not in self.seen:
            self.seen.add(key)
            if self.inherit:
                self.readers[key] = list(self.inherit.values())

    def add(self, eng, fn, reads=(), writes=(), dma=False):
        idx = len(self.ops)
        deps = []
        for r in reads:
            self._touch(r)
            w = self.last_w.get(r)
            if w is not None:
                deps.append((w, True))
        for k in writes:
            self._touch(k)
            w = self.last_w.get(k)
            if w is not None:
                deps.append((w, False))
            for rd in self.readers.get(k, ()):
                deps.append((rd, False))
        if dma:
            nd = self.ndma[eng]
            ns = NDMA_Q[eng]
            slot = nd % ns
            cnt = nd // ns + 1
            if nd >= ns:
                deps.append((self.dma_ops[eng][nd - ns], True))
            src = ("d", eng, slot)
            p = cnt
            self.ndma[eng] += 1
        else:
            self.pos[eng] += 1
            src = eng
            p = self.pos[eng]
        cur = self.cur[eng]
        waits = []
        for d, raw in deps:
            o = self.ops[d]
            s, v = o["src"], o["p"]
            if s == eng and not dma and not o["dma"]:
                if eng == "pe":
                    continue
            if cur.get(s, 0) >= v:
                continue
            waits.append((s, v))
            o["needed"] = True
            for ks, kv in o["vc"].items():
                if cur.get(ks, 0) < kv:
                    cur[ks] = kv
            cur[s] = v
        vc = dict(cur)
        vc[src] = p
        op = dict(eng=eng, fn=fn, waits=waits, src=src, p=p, dma=dma, vc=vc, needed=False, deps=deps)
        self.ops.append(op)
        if dma:
            self.dma_ops[eng].append(idx)
        for r in reads:
            self.readers.setdefault(r, []).append(idx)
        for k in writes:
            self.last_w[k] = idx
            self.readers[k] = []
        return idx

    def emit(self):
        nc = self.nc
        rank = {e: {} for e in ENGS}
        cnt = {e: 0 for e in ENGS}
        for o in self.ops:
            if not o["dma"] and o["needed"]:
                cnt[o["eng"]] += 1
                rank[o["eng"]][o["p"]] = cnt[o["eng"]]
        nsem = {e: max(1, (cnt[e] + EPOCH - 1) // EPOCH) for e in ENGS}
        with contextlib.ExitStack() as st:
            sems = {e: [st.enter_context(nc.semaphore(f"s_{e}{i}")) for i in range(nsem[e])] for e in ENGS}
            dsem = {q: [st.enter_context(nc.semaphore(f"s_d{q}{i}")) for i in range(NDMA_Q[q])] for q in NDMA_Q}
            block = st.enter_context(nc.Block())
            engobj = {"pe": nc.tensor, "act": nc.scalar, "dve": nc.vector, "pool": nc.gpsimd, "sp": nc.sync}
            per = {e: [o for o in self.ops if o["eng"] == e] for e in ENGS}

            def run(e):
                eo = engobj[e]
                for o in per[e]:
                    for s, v in o["waits"]:
                        if isinstance(s, tuple):
                            eo.wait_ge(dsem[s[1]][s[2]], 16 * v)
                        else:
                            r = rank[s][v] - 1
                            eo.wait_ge(sems[s][r // EPOCH], r % EPOCH + 1)
                    ins = o["fn"]()
                    if o["dma"]:
                        ins.then_inc(dsem[o["src"][1]][o["src"][2]], 16)
                    elif o["needed"]:
                        r = rank[e][o["p"]] - 1
                        ins.then_inc(sems[e][r // EPOCH], 1)
                if e == "sp":
                    for q in NDMA_Q:
                        for sl in range(min(self.ndma[q], NDMA_Q[q])):
                            last = (self.ndma[q] - 1 - sl) // NDMA_Q[q] + 1
                            eo.wait_ge(dsem[q][sl], 16 * last)

            @block.tensor
            def _(x):
                run("pe")

            @block.scalar
            def _(x):
                run("act")

            @block.vector
            def _(x):
                run("dve")

            @block.gpsimd
            def _(x):
                run("pool")

            @block.sync
            def _(x):
                run("sp")
        return cnt


class Interleaver:
    def __init__(self, gens, width, slotted=False):
        self.it = iter(gens)
        self.width = width
        self.slotted = slotted
        self.active = []
        self.free = list(range(width))

    def step(self):
        while len(self.active) < self.width:
            try:
                g = next(self.it)
            except StopIteration:
                break
            if self.slotted:
                sl = self.free.pop(0)
                self.active.append((g(sl), sl))
            else:
                self.active.append((g, None))
        if not self.active:
            return False
        for item in list(self.active):
            try:
                next(item[0])
            except StopIteration:
                self.active.remove(item)
                if self.slotted:
                    self.free.append(item[1])
        return True


def run_interleaved(gens, width, slotted=False):
    il = Interleaver(gens, width, slotted)
    while il.step():
        pass


class Tl:
    def __init__(self, name, ap):
        self.name = name
        self.ap = ap

    def k(self, sub=None):
        return (self.name, sub)

    def __getitem__(self, idx):
        return self.ap[idx]


ARENA_COLS = 52000


class K:
    def __init__(self, nc, ext_in=(), ext_out=()):
        self.nc = nc
        self.S = Sched(nc)
        self.st = contextlib.ExitStack()
        self.big = self.st.enter_context(nc.sbuf_tensor("arena", [128, ARENA_COLS], F32))
        self.off = 0
        self.live = []
        self.ext_in = set(ext_in)
        self.ext_out = set(ext_out)
        self.drams = {}
        self.uid = 0
        self.PS = [Tl(f"ps{i}", self.st.enter_context(nc.psum_tensor(f"ps{i}", [128, 512], F32))[:, :]) for i in range(8)]

    def alloc(self, name, cols, dt=F32):
        size = 4 if dt == F32 else 2
        n32 = (cols * size + 3) // 4
        n32 = (n32 + 7) // 8 * 8
        assert self.off + n32 <= ARENA_COLS, f"SBUF arena overflow at {name}: {self.off}+{n32}"
        ap = self.big[:, self.off:self.off + n32]
        if dt != F32:
            ap = ap.bitcast(dt)[:, :cols]
        else:
            ap = ap[:, :cols]
        self.off += n32
        self.uid += 1
        t = Tl(f"{name}#{self.uid}", ap)
        self.live.append(t.name)
        return t

    def mark(self):
        return (self.off, len(self.live))

    def release(self, mark):
        off, n = mark
        self.S.retire(self.live[n:])
        del self.live[n:]
        self.off = off

    def dram(self, name, shape, dt):
        if name in self.drams:
            return self.drams[name]
        if name in self.ext_in:
            t = self.nc.dram_tensor(name, list(shape), dt, kind="ExternalInput")
        elif name in self.ext_out:
            t = self.nc.dram_tensor(name, list(shape), dt, kind="ExternalOutput")
        else:
            t = self.nc.dram_tensor(name, list(shape), dt)
        self.drams[name] = t.ap()
        return self.drams[name]

    def _op(self, eng, name, a, kw):
        r = kw.pop("r", ())
        w = kw.pop("w", ())
        w = list(w) + [key for key in r if key[0].startswith("ps") and key not in w]
        obj = {"pe": self.nc.tensor, "act": self.nc.scalar, "dve": self.nc.vector, "pool": self.nc.gpsimd}[eng]
        return self.S.add(eng, functools.partial(getattr(obj, name), *a, **kw), r, w)

    def pe(self, name, *a, **kw):
        return self._op("pe", name, a, kw)

    def act(self, name, *a, **kw):
        return self._op("act", name, a, kw)

    def dve(self, name, *a, **kw):
        return self._op("dve", name, a, kw)

    def pool(self, name, *a, **kw):
        return self._op("pool", name, a, kw)

    def any(self, eng, name, *a, **kw):
        return self._op(eng, name, a, kw)

    def dma(self, out, in_, r=(), w=(), q="sp"):
        nc = self.nc
        if q == "sp":
            return self.S.add("sp", functools.partial(nc.sync.dma_start, out=out, in_=in_), r, w, dma=True)
        return self.S.add("pool", functools.partial(nc.gpsimd.dma_start, out=out, in_=in_), r, w, dma=True)

    def veng(self, eng):
        return {"dve": self.nc.vector, "pool": self.nc.gpsimd}[eng]


def load_consts(k):
    nc = k.nc
    identd = k.dram("c_ident", [128, 128], F32)
    c = {}
    c["identf"] = k.alloc("identf", 128, F32)
    c["identb"] = k.alloc("identb", 128, BF16)
    k.dma(c["identf"][:, :], identd, r=[("dram", "c_ident")], w=[c["identf"].k()])
    k.dma(c["identb"][:, :], identd, r=[("dram", "c_ident")], w=[c["identb"].k()], q="pool")
    return c


def prep_rows(k, c, xrows_ap, xkey, Amod, Bmod, hT, col0, nrows, tmp, ps_t, idx):
    nc = k.nc
    xt, junk, ss, t1, hb = tmp
    n = nrows
    k.dma(xt[:n, :], xrows_ap, r=[xkey], w=[xt.k()])
    k.act("activation", out=junk[:n, :], in_=xt[:n, :], func=AF.Square, accum_out=ss[:n, 0:1],
          r=[xt.k()], w=[junk.k(), ss.k()])
    k.act("activation", out=ss[:n, 1:2], in_=ss[:n, 0:1], func=AF.Sqrt, scale=1.0 / D, bias=NORM_EPS,
          r=[ss.k()], w=[ss.k()])
    k.dve("reciprocal", out=ss[:n, 2:3], in_=ss[:n, 1:2], r=[ss.k()], w=[ss.k()])
    k.dve("scalar_tensor_tensor", out=t1[:n, :], in0=xt[:n, :], scalar=ss[:n, 2:3], in1=Amod[:n, :],
                                                 op0=ALU.mult, op1=ALU.mult,
          r=[xt.k(), ss.k(), Amod.k()], w=[t1.k()])
    k.pool("tensor_tensor", out=hb[:n, :], in0=t1[:n, :], in1=Bmod[:n, :], op=ALU.add,
           r=[t1.k(), Bmod.k()], w=[hb.k()])
    pT = ps_t.ap.bitcast(BF16)
    for kc in range(8):
        k.pe("transpose", pT[:, kc * 128:kc * 128 + n], hb[:n, kc * 128:(kc + 1) * 128], c["identb"][:n, :n],
             r=[hb.k(), c["identb"].k()], w=[ps_t.k()])
    src = pT.rearrange("p (a b) -> p a b", a=8)[:, :, :n]
    dst = hT[:, :].rearrange("p (a b) -> p a b", a=8)[:, :, col0:col0 + n]
    if idx % 2 == 0:
        k.act("copy", out=dst, in_=src, r=[ps_t.k()], w=[hT.k()])
    else:
        k.dve("tensor_copy", out=dst, in_=src, r=[ps_t.k()], w=[hT.k()])


def prep_rows_gen(k, c, xrows_ap, xkey, Amod, Bmod, hT, col0, nrows, tmp, ps_t, idx):
    nc = k.nc
    xt, junk, ss, t1, hb = tmp
    n = nrows
    k.dma(xt[:n, :], xrows_ap, r=[xkey], w=[xt.k()])
    yield
    k.act("activation", out=junk[:n, :], in_=xt[:n, :], func=AF.Square, accum_out=ss[:n, 0:1],
          r=[xt.k()], w=[junk.k(), ss.k()])
    yield
    k.act("activation", out=ss[:n, 1:2], in_=ss[:n, 0:1], func=AF.Sqrt, scale=1.0 / D, bias=NORM_EPS,
          r=[ss.k()], w=[ss.k()])
    yield
    k.dve("reciprocal", out=ss[:n, 2:3], in_=ss[:n, 1:2], r=[ss.k()], w=[ss.k()])
    yield
    k.dve("scalar_tensor_tensor", out=t1[:n, :], in0=xt[:n, :], scalar=ss[:n, 2:3], in1=Amod[:n, :],
                                                 op0=ALU.mult, op1=ALU.mult,
          r=[xt.k(), ss.k(), Amod.k()], w=[t1.k()])
    yield
    k.pool("tensor_tensor", out=hb[:n, :], in0=t1[:n, :], in1=Bmod[:n, :], op=ALU.add,
           r=[t1.k(), Bmod.k()], w=[hb.k()])
    yield
    pT = ps_t.ap.bitcast(BF16)
    for kc in range(8):
        k.pe("transpose", pT[:, kc * 128:kc * 128 + n], hb[:n, kc * 128:(kc + 1) * 128], c["identb"][:n, :n],
             r=[hb.k(), c["identb"].k()], w=[ps_t.k()])
    yield
    src = pT.rearrange("p (a b) -> p a b", a=8)[:, :, :n]
    dst = hT[:, :].rearrange("p (a b) -> p a b", a=8)[:, :, col0:col0 + n]
    if idx % 2 == 0:
        k.act("copy", out=dst, in_=src, r=[ps_t.k()], w=[hT.k()])
        yield
    else:
        k.dve("tensor_copy", out=dst, in_=src, r=[ps_t.k()], w=[hT.k()])
        yield


def alloc_prep_tmp(k, tag):
    xt = k.alloc(f"xt{tag}", 1024, F32)
    junk = k.alloc(f"junk{tag}", 1024, BF16)
    ss = k.alloc(f"ss{tag}", 8, F32)
    t1 = k.alloc(f"t1{tag}", 1024, F32)
    hb = k.alloc(f"hb{tag}", 1024, BF16)
    return (xt, junk, ss, t1, hb)


PV_BIN = 0
PV_DWB = 24
PV_LNG = 32
PV_LNB = 40
PV_DWW = 48
PV_C5W = 48 + 248
PV_N = PV_C5W + 60
U1PAD = 15
U1W = T + 2 * U1PAD


def l1_pass_a(k, c):
    nc = k.nc
    m0 = k.mark()
    x1 = k.dram("x1", [T, D], F32)
    modd = k.dram("mod", [8, 128, D], F32)
    w_in_d = k.dram("od_w_in_r", [128, 8 * 3072], F32)
    pvd = k.dram("pv", [128, PV_N], F32)
    U1 = k.dram("U1", [8, 128, U1W], BF16)
    ZG1 = k.dram("ZG1", [8, 128, T], BF16)

    w_in = k.alloc("w_in1", 8 * 3072, BF16)
    for kc in range(8):
        k.dma(w_in[:, kc * 3072:(kc + 1) * 3072], w_in_d[:, kc * 3072:(kc + 1) * 3072], r=[("dram", "od_w_in_r")], w=[w_in.k(kc)], q="pool")
    pv = k.alloc("pv", PV_N, F32)
    k.dma(pv[:, :], pvd, r=[("dram", "pv")], w=[pv.k()])
    A1 = k.alloc("A1", D, F32)
    B1 = k.alloc("B1", D, F32)
    k.dma(A1[:, :], modd[5], r=[("dram", "mod")], w=[A1.k()])
    k.dma(B1[:, :], modd[6], r=[("dram", "mod")], w=[B1.k()])
    zt = k.alloc("zt", 16, BF16)
    k.dve("memset", zt[:, :], 0.0, w=[zt.k()])
    for j in range(8):
        k.dma(U1[j][:, 0:U1PAD], zt[:, 0:U1PAD], r=[zt.k()], w=[("dram", "U1", j, "padl")])
        k.dma(U1[j][:, U1PAD + T:U1W], zt[:, 0:U1PAD], r=[zt.k()], w=[("dram", "U1", j, "padr")])
    tmps = [alloc_prep_tmp(k, i) for i in range(2)]
    hTs = [k.alloc(f"hT{i}", 8 * 512, BF16) for i in range(2)]
    sg = [k.alloc(f"sg{i}", 512, F32) for i in range(2)]
    ub = [k.alloc(f"ub{i}", 512, BF16) for i in range(3)]
    zb = [k.alloc(f"zb{i}", 512, BF16) for i in range(3)]
    ps_t = [k.PS[0], k.PS[1]]
    ps_mm = [k.PS[2], k.PS[3], k.PS[4], k.PS[5], k.PS[6], k.PS[7]]
    nmt = T // 512

    def prep_m(m):
        hT = hTs[m % 2]

        def sub(s, slot):
            r0 = m * 512 + s * 128
            return prep_rows_gen(k, c, x1[r0:r0 + 128, :], ("dram", "x1", r0 // 128), A1, B1, hT, s * 128, 128, tmps[slot], ps_t[slot], 4 * m + s)
        il = Interleaver([functools.partial(sub, s) for s in range(4)], 2, slotted=True)
        while il.step():
            yield

    def comp_m(m):
        hT = hTs[m % 2]
        hT3 = hT[:, :].rearrange("p (a b) -> p a b", a=8)

        def mmgroup(pst, col0):
            for kc in range(8):
                k.pe("matmul", pst[:, :], lhsT=w_in[:, kc * 3072 + col0:kc * 3072 + col0 + 128], rhs=hT3[:, kc, :],
                     start=(kc == 0), stop=(kc == 7), r=[w_in.k(kc), hT.k()], w=[pst.k()])

        for j in range(8):
            pa = ps_mm[(2 * j) % 4]
            pg = ps_mm[(2 * j + 1) % 4]
            pz = ps_mm[4 + j % 2]
            mmgroup(pa, j * 128)
            yield
            mmgroup(pg, 1024 + j * 128)
            yield
            mmgroup(pz, 2048 + j * 128)
            yield
            sgt = sg[j % 2]
            ubt = ub[j % 3]
            zbt = zb[j % 3]
            k.act("activation", out=sgt[:, :], in_=pg[:, :], func=AF.Sigmoid, bias=pv[:, PV_BIN + 8 + j:PV_BIN + 9 + j], scale=1.0,
                  r=[pg.k(), pv.k()], w=[sgt.k()])
            yield
            k.dve("scalar_tensor_tensor", out=ubt[:, :], in0=pa[:, :], scalar=pv[:, PV_BIN + j:PV_BIN + j + 1], in1=sgt[:, :],
                  op0=ALU.add, op1=ALU.mult, r=[pa.k(), sgt.k(), pv.k()], w=[ubt.k()])
            k.dma(U1[j][:, U1PAD + m * 512:U1PAD + (m + 1) * 512], ubt[:, :], r=[ubt.k()], w=[("dram", "U1", j, m)])
            yield
            k.act("activation", out=zbt[:, :], in_=pz[:, :], func=AF.Silu, bias=pv[:, PV_BIN + 16 + j:PV_BIN + 17 + j], scale=1.0,
                  r=[pz.k(), pv.k()], w=[zbt.k()])
            k.dma(ZG1[j][:, m * 512:(m + 1) * 512], zbt[:, :], r=[zbt.k()], w=[("dram", "ZG1", j, m)])
            yield

    for _ in prep_m(0):
        pass
    for m in range(nmt):
        gl = [comp_m(m)] + ([prep_m(m + 1)] if m + 1 < nmt else [])
        run_interleaved(gl, 2)
    k.release(m0)


def l1_pass_b(k, c):
    nc = k.nc
    m0 = k.mark()
    x1 = k.dram("x1", [T, D], F32)
    y = k.dram("y", [T, D], F32)
    modd = k.dram("mod", [8, 128, D], F32)
    w_out_d = k.dram("od_w_out_r", [128, 8 * 1024], F32)
    pvd = k.dram("pv", [128, PV_N], F32)
    U1 = k.dram("U1", [8, 128, U1W], BF16)
    ZG1 = k.dram("ZG1", [8, 128, T], BF16)

    w_out = k.alloc("w_out1", 8 * 1024, BF16)
    k.dma(w_out[:, :], w_out_d, r=[("dram", "od_w_out_r")], w=[w_out.k()], q="pool")
    pv = k.alloc("pv", PV_N, F32)
    k.dma(pv[:, :], pvd, r=[("dram", "pv")], w=[pv.k()])
    G1 = k.alloc("G1", D, F32)
    k.dma(G1[:, :], modd[7], r=[("dram", "mod")], w=[G1.k()])
    BOUT = k.alloc("BOUT", D, F32)
    k.dma(BOUT[:, :], row_bc(k, "b_out"), r=[("dram", "rows")], w=[BOUT.k()])
    onesm = k.alloc("onesm", 128, BF16)
    k.dve("memset", onesm[:, :], 1.0 / 1024.0, w=[onesm.k()])
    DG = k.alloc("DG", 8 * 31 * 128, BF16)
    for j in range(8):
        for t in range(31):
            i = j * 31 + t
            eng = "dve" if i % 2 == 0 else "pool"
            ve = k.veng(eng)
            k.any(eng, "tensor_scalar", out=DG[:, i * 128:(i + 1) * 128], in0=c["identf"][:, :],
                                                           scalar1=pv[:, PV_DWW + i:PV_DWW + i + 1], scalar2=None, op0=ALU.mult,
                  r=[c["identf"].k(), pv.k()], w=[DG.k(j)])
    PW = 512 + 2 * U1PAD
    PRE = [k.alloc(f"PRE{i}", 8 * PW, BF16) for i in range(2)]
    ZGt = [k.alloc(f"ZGt{i}", 8 * 512, BF16) for i in range(2)]
    UC = k.alloc("UC", 8 * 512, F32)
    UCb = k.alloc("UCb", 8 * 512, BF16)
    SQ = k.alloc("SQ", 8 * 512, BF16)
    MEAN = k.alloc("MEAN", 512, F32)
    M2 = k.alloc("M2", 512, F32)
    RSTD = k.alloc("RSTD", 512, F32)
    TA = [k.alloc(f"TA{i}", 512, F32) for i in range(4)]
    TB = TA
    YS = [k.alloc(f"YS{i}", 512, BF16) for i in range(4)]
    YT = k.alloc("YT", 8 * 512, BF16)
    O = [k.alloc(f"O{i}", D, F32) for i in range(2)]
    junk = k.alloc("junkb", D, BF16)
    ss = [k.alloc(f"ssb{i}", 8, F32) for i in range(2)]
    XR = [k.alloc(f"XR{i}", D, F32) for i in range(2)]
    T2 = O
    OUT = XR
    ps_cv = [k.PS[0], k.PS[1]]
    ps_mean, ps_msq = k.PS[2], k.PS[3]
    ps_o = [(k.PS[4], k.PS[5]), (k.PS[6], k.PS[7])]
    nmt = T // 512

    def phase1(m):
        pre = PRE[m % 2]
        zgt = ZGt[m % 2]
        for j in range(8):
            k.dma(pre[:, j * PW:(j + 1) * PW], U1[j][:, m * 512:m * 512 + PW],
                  r=[("dram", "U1", j, mm) for mm in range(max(0, m - 1), min(nmt, m + 2))] + [("dram", "U1", j, "padl"), ("dram", "U1", j, "padr")],
                  w=[pre.k(j)])
            k.dma(zgt[:, j * 512:(j + 1) * 512], ZG1[j][:, m * 512:(m + 1) * 512], r=[("dram", "ZG1", j, m)], w=[zgt.k(j)])
        yield
        for j in range(8):
            pcv = ps_cv[j % 2]
            for t in range(31):
                i = j * 31 + t
                k.pe("matmul", pcv[:, :], lhsT=DG[:, i * 128:(i + 1) * 128], rhs=pre[:, j * PW + t:j * PW + t + 512], start=(t == 0), stop=(t == 30),
                     r=[DG.k(j), pre.k(j)], w=[pcv.k()])
                if t % 8 == 7:
                    yield
            yield
            k.act("activation", out=UC[:, j * 512:(j + 1) * 512], in_=pcv[:, :], func=AF.Identity, bias=pv[:, PV_DWB + j:PV_DWB + j + 1], scale=1.0,
                  r=[pcv.k(), pv.k()], w=[UC.k(j)])
            yield
            k.act("activation", out=SQ[:, j * 512:(j + 1) * 512], in_=pcv[:, :], func=AF.Square, bias=pv[:, PV_DWB + j:PV_DWB + j + 1], scale=1.0,
                  r=[pcv.k(), pv.k()], w=[SQ.k(j)])
            yield
            k.pool("tensor_copy", out=UCb[:, j * 512:(j + 1) * 512], in_=UC[:, j * 512:(j + 1) * 512], r=[UC.k(j)], w=[UCb.k(j)])
            yield

    def phase234(m):
        zgt = ZGt[m % 2]
        for j in range(8):
            k.pe("matmul", ps_mean[:, :], lhsT=onesm[:, :], rhs=UCb[:, j * 512:(j + 1) * 512], start=(j == 0), stop=(j == 7),
                 r=[onesm.k(), UCb.k(j)], w=[ps_mean.k()])
        for j in range(8):
            k.pe("matmul", ps_msq[:, :], lhsT=onesm[:, :], rhs=SQ[:, j * 512:(j + 1) * 512], start=(j == 0), stop=(j == 7),
                 r=[onesm.k(), SQ.k(j)], w=[ps_msq.k()])
        k.act("copy", out=MEAN[:, :], in_=ps_mean[:, :], r=[ps_mean.k()], w=[MEAN.k()])
        k.dve("tensor_tensor", out=M2[:, :], in0=MEAN[:, :], in1=MEAN[:, :], op=ALU.mult, r=[MEAN.k()], w=[M2.k()])
        k.dve("tensor_tensor", out=M2[:, :], in0=ps_msq[:, :], in1=M2[:, :], op=ALU.subtract, r=[ps_msq.k(), M2.k()], w=[M2.k()])
        k.act("activation", out=RSTD[:, :], in_=M2[:, :], func=AF.Sqrt, bias=1e-5, scale=1.0, r=[M2.k()], w=[RSTD.k()])
        k.dve("reciprocal", out=RSTD[:, :], in_=RSTD[:, :], r=[RSTD.k()], w=[RSTD.k()])

        def ln_chunk(j, sl):
            ta, tb, ys = TA[sl], TB[sl], YS[sl]
            k.dve("tensor_tensor", out=ta[:, :], in0=UC[:, j * 512:(j + 1) * 512], in1=MEAN[:, :], op=ALU.subtract,
                  r=[UC.k(j), MEAN.k()], w=[ta.k()])
            yield
            k.pool("tensor_tensor", out=tb[:, :], in0=ta[:, :], in1=RSTD[:, :], op=ALU.mult, r=[ta.k(), RSTD.k()], w=[tb.k()])
            yield
            k.act("activation", out=ys[:, :], in_=tb[:, :], func=AF.Silu,
                  scale=pv[:, PV_LNG + j:PV_LNG + j + 1], bias=pv[:, PV_LNB + j:PV_LNB + j + 1], r=[tb.k(), pv.k()], w=[ys.k()])
            yield
            k.dve("tensor_tensor", out=YT[:, j * 512:(j + 1) * 512], in0=ys[:, :], in1=zgt[:, j * 512:(j + 1) * 512], op=ALU.mult,
                  r=[ys.k(), zgt.k(j)], w=[YT.k(j)])
            yield
        run_interleaved([functools.partial(ln_chunk, j) for j in range(8)], 4, slotted=True)

    def phase5(m):
        for s in range(4):
            oc = 4 * m + s
            r0 = m * 512 + s * 128
            po = ps_o[oc % 2]
            o, sst, xr, t2, out = O[oc % 2], ss[oc % 2], XR[oc % 2], T2[oc % 2], OUT[oc % 2]
            k.dma(xr[:, :], x1[r0:r0 + 128, :], r=[("dram", "x1", r0 // 128)], w=[xr.k()])
            for nh in range(2):
                for j in range(8):
                    k.pe("matmul", po[nh][:, :], lhsT=YT[:, j * 512 + s * 128:j * 512 + (s + 1) * 128], rhs=w_out[:, j * 1024 + nh * 512:j * 1024 + (nh + 1) * 512],
                         start=(j == 0), stop=(j == 7), r=[YT.k(j), w_out.k()], w=[po[nh].k()])
                yield
            for nh in range(2):
                k.dve("tensor_tensor", out=o[:, nh * 512:(nh + 1) * 512], in0=po[nh][:, :], in1=BOUT[:, nh * 512:(nh + 1) * 512], op=ALU.add,
                      r=[po[nh].k(), BOUT.k()], w=[o.k()])
                yield
            yield from post_res_gen(k, o, sst, junk, G1, xr, t2, out, y[r0:r0 + 128, :], ("dram", "y", r0 // 128))

    for _ in phase1(0):
        pass
    for m in range(nmt):
        phase234(m)
        gl = [phase5(m)] + ([phase1(m + 1)] if m + 1 < nmt else [])
        run_interleaved(gl, 2)
    k.release(m0)


def post_res(k, o, sst, junk, G, xr, t2, out, ydst, ykey):
    nc = k.nc
    k.act("activation", out=junk[:, :], in_=o[:, :], func=AF.Square, accum_out=sst[:, 0:1],
          r=[o.k()], w=[junk.k(), sst.k()])
    k.act("activation", out=sst[:, 1:2], in_=sst[:, 0:1], func=AF.Sqrt, scale=1.0 / D, bias=NORM_EPS,
          r=[sst.k()], w=[sst.k()])
    k.dve("reciprocal", out=sst[:, 2:3], in_=sst[:, 1:2], r=[sst.k()], w=[sst.k()])
    k.dve("scalar_tensor_tensor", out=t2[:, :], in0=o[:, :], scalar=sst[:, 2:3], in1=G[:, :], op0=ALU.mult, op1=ALU.mult,
          r=[o.k(), sst.k(), G.k()], w=[t2.k()])
    k.pool("tensor_tensor", out=out[:, :], in0=t2[:, :], in1=xr[:, :], op=ALU.add, r=[t2.k(), xr.k()], w=[out.k()])
    k.dma(ydst, out[:, :], r=[out.k()], w=[ykey])


def post_res_gen(k, o, sst, junk, G, xr, t2, out, ydst, ykey):
    nc = k.nc
    k.act("activation", out=junk[:, :], in_=o[:, :], func=AF.Square, accum_out=sst[:, 0:1],
          r=[o.k()], w=[junk.k(), sst.k()])
    yield
    k.act("activation", out=sst[:, 1:2], in_=sst[:, 0:1], func=AF.Sqrt, scale=1.0 / D, bias=NORM_EPS,
          r=[sst.k()], w=[sst.k()])
    yield
    k.dve("reciprocal", out=sst[:, 2:3], in_=sst[:, 1:2], r=[sst.k()], w=[sst.k()])
    yield
    k.dve("scalar_tensor_tensor", out=t2[:, :], in0=o[:, :], scalar=sst[:, 2:3], in1=G[:, :], op0=ALU.mult, op1=ALU.mult,
          r=[o.k(), sst.k(), G.k()], w=[t2.k()])
    yield
    k.pool("tensor_tensor", out=out[:, :], in0=t2[:, :], in1=xr[:, :], op=ALU.add, r=[t2.k(), xr.k()], w=[out.k()])
    yield
    k.dma(ydst, out[:, :], r=[out.k()], w=[ykey])
    yield


def build_program(passes, ext_in, ext_out):
    nc = bass.Bass("TRN2", target_bir_lowering=False)
    k = K(nc, ext_in, ext_out)
    c = load_consts(k)
    for p in passes:
        p(k, c)
    cnt = k.S.emit()
    k.st.close()
    return nc, cnt


ROW_OFF = {}
_o = 0
for _n, _l in [("ada_b0", 3072), ("ada_b1", 3072), ("pre_g0", 1024), ("pre_g1", 1024), ("post_g0", 1024), ("post_g1", 1024),
               ("b_out", 1024), ("gdn_g", 128), ("qn_g", 128), ("kn_g", 128), ("a_log", 8), ("dt_bias", 8)]:
    ROW_OFF[_n] = (_o, _l)
    _o += _l
ROWS_N = _o


def row_bc(k, name, off=0, n=None):
    rows = k.dram("rows", [1, ROWS_N], F32)
    o, l = ROW_OFF[name]
    if n is None:
        n = l
    return rows[0, o + off:o + off + n].partition_broadcast(128)


def p0_mod(k, c):
    nc = k.nc
    m0 = k.mark()
    modd = k.dram("mod", [8, 128, D], F32)
    cvec = k.dram("cvec", [128, 16], F32)
    seld = k.dram("c_sel", [2, 256], F32)
    adaw = k.dram("ada_w_r", [2, 128, 8 * 3072], F32)
    cv = k.alloc("cv", 16, F32)
    k.dma(cv[:, :], cvec, r=[("dram", "cvec")], w=[cv.k()])
    scb = k.alloc("scb", 16, BF16)
    k.act("activation", out=scb[:, :], in_=cv[:, :], func=AF.Silu, r=[cv.k()], w=[scb.k()])
    sel = k.alloc("sel", 256, F32)
    k.dma(sel[0:2, :], seld, r=[("dram", "c_sel")], w=[sel.k()])
    mrow = k.alloc("mrow", 6144, F32)
    aw = [k.alloc(f"aw{l}", 8 * 3072, BF16) for l in range(2)]
    for l in range(2):
        for kc in range(8):
            k.dma(aw[l][:, kc * 3072:(kc + 1) * 3072], adaw[l][:, kc * 3072:(kc + 1) * 3072], r=[("dram", "ada_w_r")], w=[aw[l].k(kc)], q="pool")
    adab = [k.alloc(f"adab{l}", 3072, F32) for l in range(2)]
    preg = [k.alloc(f"preg{l}", 1024, F32) for l in range(2)]
    postg = [k.alloc(f"postg{l}", 1024, F32) for l in range(2)]
    for l in range(2):
        k.dma(adab[l][:, :], row_bc(k, f"ada_b{l}"), r=[("dram", "rows")], w=[adab[l].k()])
        k.dma(preg[l][:, :], row_bc(k, f"pre_g{l}"), r=[("dram", "rows")], w=[preg[l].k()])
        k.dma(postg[l][:, :], row_bc(k, f"post_g{l}"), r=[("dram", "rows")], w=[postg[l].k()])
    for l in range(2):
        for nt in range(6):
            ps = k.PS[nt % 2]
            for kc in range(8):
                k.pe("matmul", ps[0:2, :], lhsT=scb[:, 2 * kc:2 * kc + 2], rhs=aw[l][:, kc * 3072 + nt * 512:kc * 3072 + (nt + 1) * 512],
                     start=(kc == 0), stop=(kc == 7), r=[scb.k(), aw[l].k(kc)], w=[ps.k()])
            k.act("copy", out=mrow[0:2, l * 3072 + nt * 512:l * 3072 + (nt + 1) * 512], in_=ps[0:2, :], r=[ps.k()], w=[mrow.k((l, nt))])
    tmp = [k.alloc(f"mt{i}", 512, F32) for i in range(2)]
    outt = [k.alloc(f"mo{i}", 1024, F32) for i in range(2)]
    plan = [(0, 0, 1, 0), (1, 0, 0, 0), (2, 0, 2, 0), (3, 0, 1, 1), (4, 0, 0, 1), (5, 1, 1, 0), (6, 1, 0, 0), (7, 1, 2, 0)]
    cnt = 0
    for (mi, l, part, si) in plan:
        ot = outt[mi % 2]
        for nh in range(2):
            ps = k.PS[2 + cnt % 2]
            tt = tmp[cnt % 2]
            cnt += 1
            seg = l * 3072 + part * 1024 + nh * 512
            nt = (part * 1024 + nh * 512) // 512
            k.pe("matmul", ps[:, :], lhsT=sel[0:2, si * 128:(si + 1) * 128], rhs=mrow[0:2, seg:seg + 512], start=True, stop=True,
                 r=[sel.k(), mrow.k((l, nt))], w=[ps.k()])
            ab_ = adab[l][:, part * 1024 + nh * 512:part * 1024 + (nh + 1) * 512]
            osl = ot[:, nh * 512:(nh + 1) * 512]
            if part == 0:
                k.dve("tensor_tensor", out=osl, in0=ps[:, :], in1=ab_, op=ALU.add, r=[ps.k(), adab[l].k()], w=[ot.k()])
            elif part == 1:
                k.dve("scalar_tensor_tensor", out=tt[:, :], in0=ps[:, :], scalar=1.0, in1=ab_, op0=ALU.add, op1=ALU.add,
                      r=[ps.k(), adab[l].k()], w=[tt.k()])
                k.pool("tensor_tensor", out=osl, in0=tt[:, :], in1=preg[l][:, nh * 512:(nh + 1) * 512], op=ALU.mult,
                       r=[tt.k(), preg[l].k()], w=[ot.k()])
            else:
                k.dve("tensor_tensor", out=tt[:, :], in0=ps[:, :], in1=ab_, op=ALU.add, r=[ps.k(), adab[l].k()], w=[tt.k()])
                k.pool("tensor_tensor", out=osl, in0=tt[:, :], in1=postg[l][:, nh * 512:(nh + 1) * 512], op=ALU.mult,
                       r=[tt.k(), postg[l].k()], w=[ot.k()])
        k.dma(modd[mi], ot[:, :], r=[ot.k()], w=[("dram", "mod")])
    k.release(m0)


NTILE = 34
TK = T + CTX
GP_CTX0 = 2
GP_LAT0 = 2 + CTX + 2 + 2
GP_W = GP_LAT0 + T + 2
C_QKV, C_ZA, C_AB, C_QB, C_KB, C_VB, C_ZB, EVEN_IN = 0, 1536, 2048, 2064, 2576, 2832, 3088, 3600


P1_FLAGS = {"pads": True, "tm": True, "fm": True, "qk": True, "ab": True, "norm": True, "tr": True}


def p1_inproj(k, c):
    nc = k.nc
    m0 = k.mark()
    x = k.dram("x", [T, D], F32)
    ctxd = k.dram("ctx", [CTX, D], F32)
    modd = k.dram("mod", [8, 128, D], F32)
    w_in_d = k.dram("ev_w_in_r", [128, 8 * EVEN_IN], F32)
    ropecs = k.dram("rope_cs", [32, 128, 128], F32)
    ropesn = k.dram("rope_sn", [32, 128, 128], F32)
    QT = k.dram("QT", [4, 128, T], BF16)
    KT = k.dram("KT", [2, 128, TK], BF16)
    V = k.dram("V", [NTILE, 128, 256], BF16)
    ZBT = k.dram("ZBT", [4, 128, T], BF16)
    ZA = k.dram("ZA", [T, 512], BF16)
    GPRE = k.dram("GPRE", [12, 128, GP_W], BF16)
    AB = k.dram("AB", [NTILE, 128, 16], F32)

    w_in = k.alloc("w_in0", 8 * EVEN_IN, BF16)
    for kc in range(8):
        k.dma(w_in[:, kc * EVEN_IN:(kc + 1) * EVEN_IN], w_in_d[:, kc * EVEN_IN:(kc + 1) * EVEN_IN], r=[("dram", "ev_w_in_r")], w=[w_in.k(kc)], q="pool")
    Am = k.alloc("Am", D, F32)
    Bm = k.alloc("Bm", D, F32)
    G6 = k.alloc("G6", 768, F32)
    for h in range(4):
        k.dma(G6[:, h * 128:(h + 1) * 128], row_bc(k, "qn_g"), r=[("dram", "rows")], w=[G6.k()])
    for h in range(2):
        k.dma(G6[:, 512 + h * 128:512 + (h + 1) * 128], row_bc(k, "kn_g"), r=[("dram", "rows")], w=[G6.k()])
    zt = k.alloc("zt", 16, BF16)
    k.dve("memset", zt[:, :], 0.0, w=[zt.k()])
    for j in range(12 if P1_FLAGS["pads"] else 0):
        k.dma(GPRE[j][:, 0:2], zt[:, 0:2], r=[zt.k()], w=[("dram", "GPRE", j, "p0")])
        k.dma(GPRE[j][:, 2 + CTX:GP_LAT0], zt[:, 0:4], r=[zt.k()], w=[("dram", "GPRE", j, "p1")])
        k.dma(GPRE[j][:, GP_LAT0 + T:GP_W], zt[:, 0:2], r=[zt.k()], w=[("dram", "GPRE", j, "p2")])
    tmps = [alloc_prep_tmp(k, i) for i in range(2)]
    hTs = [k.alloc(f"hT{i}", 8 * 512, BF16) for i in range(2)]
    CS = [k.alloc(f"CS{i}", 128, F32) for i in range(2)]
    SN = [k.alloc(f"SN{i}", 128, F32) for i in range(2)]
    QK = [k.alloc(f"QK{i}", 768, F32) for i in range(2)]
    SQ = [k.alloc(f"SQ{i}", 768, F32) for i in range(2)]
    ssq = [k.alloc(f"ssq{i}", 24, F32) for i in range(2)]
    QN = [k.alloc(f"QN{i}", 768, F32) for i in range(2)]
    R1 = [k.alloc(f"R1{i}", 768, F32) for i in range(2)]
    R2 = [k.alloc(f"R2{i}", 768, F32) for i in range(2)]
    QKb = [k.alloc(f"QKb{i}", 768, BF16) for i in range(2)]
    zas = [k.alloc(f"zas{i}", 512, BF16) for i in range(2)]
    vs = [k.alloc(f"vs{i}", 256, BF16) for i in range(2)]
    abs_ = [k.alloc(f"abs{i}", 16, F32) for i in range(2)]
    QTs = [k.alloc(f"QTs{i}", 4 * 512, BF16) for i in range(2)]
    KTs = [k.alloc(f"KTs{i}", 2 * 512, BF16) for i in range(2)]
    gps = [k.alloc(f"gps{i}", 512, BF16) for i in range(3)]
    zbs = [k.alloc(f"zbs{i}", 512, BF16) for i in range(3)]
    ps_t = [k.PS[0], k.PS[0]]
    ps_za, ps_q, ps_kv = k.PS[2], k.PS[3], k.PS[4]
    PSX = [k.PS[5], k.PS[1]]
    ps_f = [k.PS[6], k.PS[7]]
    cnt = 0
    fcnt = 0
    mts = [("ctx", 0, 256)] + [("lat", m * 512, 512) for m in range(T // 512)]
    mt_list = mts[:P1_FLAGS.get("nmt", 9)]

    def prep_m(mi):
        kind, t0, W = mt_list[mi]
        lat = kind == "lat"
        if mi == 0:
            k.dma(Am[:, :], modd[3], r=[("dram", "mod")], w=[Am.k()])
            k.dma(Bm[:, :], modd[4], r=[("dram", "mod")], w=[Bm.k()])
        elif mi == 1:
            k.dma(Am[:, :], modd[0], r=[("dram", "mod")], w=[Am.k()])
            k.dma(Bm[:, :], modd[1], r=[("dram", "mod")], w=[Bm.k()])
        hT = hTs[mi % 2]
        cnt0 = 0 if mi == 0 else 2 + 4 * (mi - 1)

        def sub(s, slot):
            r0 = t0 + s * 128
            src = x[r0:r0 + 128, :] if lat else ctxd[r0:r0 + 128, :]
            skey = ("dram", "x" if lat else "ctx", r0 // 128)
            return prep_rows_gen(k, c, src, skey, Am, Bm, hT, s * 128, 128, tmps[slot], ps_t[slot], cnt0 + s)
        il = Interleaver([functools.partial(sub, s) for s in range(W // 128)], 1, slotted=True)
        while il.step():
            yield

    def comp_m(mi):
        kind, t0, W = mt_list[mi]
        lat = kind == "lat"
        hT = hTs[mi % 2]
        hT3 = hT[:, :].rearrange("p (a b) -> p a b", a=8)
        nsub = W // 128
        qts, kts = QTs[mi % 2], KTs[mi % 2]
        fcnt = 0 if mi == 0 else 12 + 16 * (mi - 1)
        def tm_group(s, ps_ap, pskey, col0, n):
            for kc in range(8):
                k.pe("matmul", ps_ap, lhsT=hT3[:, kc, s * 128:(s + 1) * 128], rhs=w_in[:, kc * EVEN_IN + col0:kc * EVEN_IN + col0 + n],
                     start=(kc == 0), stop=(kc == 7), r=[hT.k(), w_in.k(kc)], w=[pskey])

        def stage_a(s):
            sl = s % 2
            r0 = t0 + s * 128
            tile_id = (r0 // 128) if lat else (32 + r0 // 128)
            psx = PSX[sl]
            qk = QK[sl]
            if lat:
                tm_group(s, ps_za[:, :], ps_za.k(), C_ZA, 512)
                zst = zas[sl]
                k.act("activation", out=zst[:, :], in_=ps_za[:, :], func=AF.Silu, r=[ps_za.k()], w=[zst.k()])
                k.dma(ZA[r0:r0 + 128, :], zst[:, :], r=[zst.k()], w=[("dram", "ZA", r0 // 128)])
                yield
                tm_group(s, ps_q[:, :], ps_q.k(), C_QB, 512)
                k.act("copy", out=qk[:, 0:512], in_=ps_q[:, :], r=[ps_q.k()], w=[qk.k()])
                yield
            tm_group(s, psx[:, 496:512], psx.k(), C_AB, 16)
            abst = abs_[sl]
            k.dve("tensor_copy", out=abst[:, :], in_=psx[:, 496:512], r=[psx.k()], w=[abst.k()])
            k.dma(AB[tile_id], abst[:, :], r=[abst.k()], w=[("dram", "AB", tile_id)])
            yield
            tm_group(s, ps_kv[:, :], ps_kv.k(), C_KB, 512)
            vst = vs[sl]
            k.act("copy", out=vst[:, :], in_=ps_kv[:, 256:512], r=[ps_kv.k()], w=[vst.k()])
            k.dma(V[tile_id], vst[:, :], r=[vst.k()], w=[("dram", "V", tile_id)])
            yield
            k.dve("tensor_copy", out=qk[:, 512:768], in_=ps_kv[:, 0:256], r=[ps_kv.k()], w=[qk.k()])
            yield

        def stage_b(s):
            sl = s % 2
            r0 = t0 + s * 128
            psx = PSX[sl]
            pxT = psx.ap.bitcast(BF16)
            qk, ssq_t, qkb, sq_, qn_, r1_, r2_ = QK[sl], ssq[sl], QKb[sl], SQ[sl], QN[sl], R1[sl], R2[sl]
            c0 = 0 if lat else 512
            nh = 6 if lat else 2
            h0 = 0 if lat else 4
            k.pool("tensor_tensor", out=sq_[:, c0:768], in0=qk[:, c0:768], in1=qk[:, c0:768], op=ALU.mult, r=[qk.k()], w=[sq_.k()])
            yield
            k.dve("tensor_reduce", out=ssq_t[:, h0:6], in_=sq_[:, c0:768].rearrange("p (h d) -> p h d", d=128), axis=AX.X, op=ALU.add,
                  r=[sq_.k()], w=[ssq_t.k()])
            yield
            k.act("activation", out=ssq_t[:, 8 + h0:14], in_=ssq_t[:, h0:6], func=AF.Sqrt, scale=1.0 / 128, bias=NORM_EPS,
                  r=[ssq_t.k()], w=[ssq_t.k()])
            yield
            k.dve("reciprocal", out=ssq_t[:, 16 + h0:22], in_=ssq_t[:, 8 + h0:14], r=[ssq_t.k()], w=[ssq_t.k()])
            yield
            k.dve("tensor_tensor", out=qn_[:, c0:768].rearrange("p (h d) -> p h d", d=128), in0=qk[:, c0:768].rearrange("p (h d) -> p h d", d=128),
                  in1=ssq_t[:, 16 + h0:22].unsqueeze(2).to_broadcast([128, nh, 128]), op=ALU.mult, r=[qk.k(), ssq_t.k()], w=[qn_.k()])
            yield
            if not lat:
                k.pool("tensor_tensor", out=qkb[:, c0:768], in0=qn_[:, c0:768], in1=G6[:, c0:768], op=ALU.mult, r=[qn_.k(), G6.k()], w=[qkb.k()])
                yield
            else:
                cs, sn = CS[sl], SN[sl]
                k.dma(cs[:, :], ropecs[r0 // 128], r=[("dram", "rope_cs")], w=[cs.k()])
                k.dma(sn[:, :], ropesn[r0 // 128], r=[("dram", "rope_sn")], w=[sn.k()])
                k.pool("tensor_tensor", out=qn_[:, :], in0=qn_[:, :], in1=G6[:, :], op=ALU.mult, r=[qn_.k(), G6.k()], w=[qn_.k()])
                yield
                k.dve("tensor_tensor", out=r1_[:, :].rearrange("p (h d) -> p h d", d=128), in0=qn_[:, :].rearrange("p (h d) -> p h d", d=128),
                      in1=cs[:, :].unsqueeze(1).to_broadcast([128, 6, 128]), op=ALU.mult, r=[qn_.k(), cs.k()], w=[r1_.k()])
                yield
                qn5 = qn_[:, :].rearrange("p (h a b e) -> p h a b e", h=6, a=2, b=2)
                r25 = r2_[:, :].rearrange("p (h a b e) -> p h a b e", h=6, a=2, b=2)
                sn4 = sn[:, :].rearrange("p (a b e) -> p a b e", a=2, b=2)
                for bsel in range(2):
                    k.pool("tensor_tensor", out=r25[:, :, :, bsel, :], in0=qn5[:, :, :, 1 - bsel, :],
                           in1=sn4[:, :, bsel, :].unsqueeze(1).to_broadcast([128, 6, 2, 32]), op=ALU.mult, r=[qn_.k(), sn.k()], w=[r2_.k()])
                    yield
                k.dve("tensor_tensor", out=qkb[:, :], in0=r1_[:, :], in1=r2_[:, :], op=ALU.add, r=[r1_.k(), r2_.k()], w=[qkb.k()])
                yield
            for h in range(h0, 6):
                k.pe("transpose", pxT[:, h * 128:(h + 1) * 128], qkb[:, h * 128:(h + 1) * 128], c["identb"][:, :],
                     r=[qkb.k(), c["identb"].k()], w=[psx.k()])
            yield
            if lat:
                k.act("copy", out=qts[:, :].rearrange("p (h t) -> p h t", h=4)[:, :, s * 128:(s + 1) * 128],
                      in_=pxT[:, 0:512].rearrange("p (h t) -> p h t", h=4), r=[psx.k()], w=[qts.k()])
                yield
            k.dve("tensor_copy", out=kts[:, :].rearrange("p (h t) -> p h t", h=2)[:, :, s * 128:(s + 1) * 128],
                  in_=pxT[:, 512:768].rearrange("p (h t) -> p h t", h=2), r=[psx.k()], w=[kts.k()])
            yield

        def tm_part():
            for _ in stage_a(0):
                yield
            for s in range(nsub):
                gl = [stage_b(s)] + ([stage_a(s + 1)] if s + 1 < nsub else [])
                il = Interleaver(gl, 2)
                while il.step():
                    yield
            kcol0 = t0 if lat else T + t0
            for h in range(2):
                k.dma(KT[h][:, kcol0:kcol0 + W], kts[:, h * 512:h * 512 + W], r=[kts.k()], w=[("dram", "KT", h, mi)])
            if lat:
                for h in range(4):
                    k.dma(QT[h][:, t0:t0 + W], qts[:, h * 512:h * 512 + W], r=[qts.k()], w=[("dram", "QT", h, mi)])

        def fm_part():
            nonlocal fcnt
            gcol0 = (GP_LAT0 + t0) if lat else (GP_CTX0 + t0)
            for j in range((12 + (4 if lat else 0)) if P1_FLAGS["fm"] else 0):
                yield
                pf = ps_f[fcnt % 2]
                col0 = j * 128 if j < 12 else C_ZB + (j - 12) * 128
                for kc in range(8):
                    k.pe("matmul", pf[:, 0:W], lhsT=w_in[:, kc * EVEN_IN + col0:kc * EVEN_IN + col0 + 128], rhs=hT3[:, kc, 0:W],
                         start=(kc == 0), stop=(kc == 7), r=[hT.k(), w_in.k(kc)], w=[pf.k()])
                yield
                if j < 12:
                    g = gps[fcnt % 3]
                    if fcnt % 2 == 0:
                        k.dve("tensor_copy", out=g[:, 0:W], in_=pf[:, 0:W], r=[pf.k()], w=[g.k()])
                    else:
                        k.act("copy", out=g[:, 0:W], in_=pf[:, 0:W], r=[pf.k()], w=[g.k()])
                    k.dma(GPRE[j][:, gcol0:gcol0 + W], g[:, 0:W], r=[g.k()], w=[("dram", "GPRE", j, mi)])
                else:
                    g = zbs[fcnt % 3]
                    k.act("activation", out=g[:, 0:W], in_=pf[:, 0:W], func=AF.Silu, r=[pf.k()], w=[g.k()])
                    k.dma(ZBT[j - 12][:, t0:t0 + W], g[:, 0:W], r=[g.k()], w=[("dram", "ZBT", j - 12, mi)])
                fcnt += 1

        il2 = Interleaver([tm_part(), fm_part()], 2)
        while il2.step():
            yield

    for _ in prep_m(0):
        pass
    for mi in range(len(mt_list)):
        gl = [comp_m(mi)] + ([prep_m(mi + 1)] if mi + 1 < len(mt_list) else [])
        run_interleaved(gl, 2)
    k.release(m0)


def _rearr_w(w):
    n = w.shape[1]
    return np.ascontiguousarray(w.reshape(8, 128, n).transpose(1, 0, 2).reshape(128, 8 * n))


def _fm(v):
    return np.ascontiguousarray(v.reshape(-1, 128).T)


def _rope_tables():
    t = np.arange(T)
    row = (t // 64).astype(np.float32)
    col = (t % 64).astype(np.float32)
    inv = (10000.0 ** (-np.arange(0, 64, 2, dtype=np.float32) / 64)).astype(np.float32)
    ar = row[:, None] * inv
    ac = col[:, None] * inv
    cs = np.concatenate([np.cos(ar), np.cos(ar), np.cos(ac), np.cos(ac)], 1).astype(np.float32)
    sn = np.concatenate([-np.sin(ar), np.sin(ar), -np.sin(ac), np.sin(ac)], 1).astype(np.float32)
    return cs.reshape(32, 128, 128), sn.reshape(32, 128, 128)


def host_prep(inp):
    f = lambda a: np.asarray(a, dtype=np.float32)
    shared = {}
    shared["c_ident"] = np.eye(128, dtype=np.float32)
    sel = np.zeros((2, 256), np.float32)
    sel[0, :128] = 1.0
    sel[1, 128:] = 1.0
    shared["c_sel"] = sel
    shared["ada_w_r"] = np.stack([_rearr_w(f(inp["ada_w"][l])) for l in range(2)])
    rows = np.zeros((1, ROWS_N), np.float32)
    vals = {"ada_b0": inp["ada_b"][0], "ada_b1": inp["ada_b"][1], "pre_g0": inp["pre_norm_g"][0], "pre_g1": inp["pre_norm_g"][1],
            "post_g0": inp["post_norm_g"][0], "post_g1": inp["post_norm_g"][1], "b_out": inp["od_b_out"][0], "gdn_g": inp["ev_gdn_norm_g"][0],
            "qn_g": inp["ev_q_norm_g"][0], "kn_g": inp["ev_k_norm_g"][0], "a_log": f(inp["ev_a_log"][0]).reshape(-1), "dt_bias": f(inp["ev_dt_bias"][0]).reshape(-1)}
    for n_, v in vals.items():
        o, l = ROW_OFF[n_]
        rows[0, o:o + l] = f(v).reshape(-1)
    shared["rows"] = rows
    shared["ev_w_in_r"] = _rearr_w(f(inp["ev_w_in"][0]))
    shared["ev_w_out_r"] = _rearr_w(f(inp["ev_w_out"][0]))
    shared["od_w_in_r"] = _rearr_w(f(inp["od_w_in"][0]))
    shared["od_w_out_r"] = _rearr_w(f(inp["od_w_out"][0]))
    cs, sn = _rope_tables()
    shared["rope_cs"] = cs
    shared["rope_sn"] = sn
    pv = np.zeros((128, PV_N), np.float32)
    pv[:, PV_BIN:PV_BIN + 24] = _fm(f(inp["od_b_in"][0]))
    pv[:, PV_DWB:PV_DWB + 8] = _fm(f(inp["od_dw_b"][0]))
    pv[:, PV_LNG:PV_LNG + 8] = _fm(f(inp["od_ln_g"][0]))
    pv[:, PV_LNB:PV_LNB + 8] = _fm(f(inp["od_ln_b"][0]))
    pv[:, PV_DWW:PV_DWW + 248] = f(inp["od_dw_w"][0]).T.reshape(8, 128, 31).transpose(1, 0, 2).reshape(128, 248)
    pv[:, PV_C5W:PV_C5W + 60] = f(inp["ev_short_conv_w"][0]).T.reshape(12, 128, 5).transpose(1, 0, 2).reshape(128, 60)
    shared["pv"] = pv
    maps = []
    cctx = _fm(f(inp["c_ctx"]))
    for b in range(8):
        m = dict(shared)
        m["x"] = np.ascontiguousarray(f(inp["x"][b]))
        m["ctx"] = np.ascontiguousarray(f(inp["ctx"][b]))
        cv = np.zeros((128, 16), np.float32)
        cv[:, 0::2] = _fm(f(inp["c"][b]))
        cv[:, 1::2] = cctx
        m["cvec"] = cv
        maps.append(m)
    return maps


NKC = TK // 128


def p2b_attn(k, c, banks=None, as_gen=False):
    nc = k.nc
    m0 = None if as_gen else k.mark()
    QTd = k.dram("QT", [4, 128, T], BF16)
    KTd = k.dram("KT", [2, 128, TK], BF16)
    Vd = k.dram("V", [NTILE, 128, 256], BF16)
    ZBT = k.dram("ZBT", [4, 128, T], BF16)
    YT = k.dram("YT", [8, 128, T], BF16)
    QTs = k.alloc("QTa", 4 * T, BF16)
    KTs = k.alloc("KTa", 2 * TK, BF16)
    Vs = k.alloc("Va", NTILE * 256, BF16)
    for h in range(4):
        for hf in range(2):
            k.dma(QTs[:, h * T + hf * 2048:h * T + (hf + 1) * 2048], QTd[h][:, hf * 2048:(hf + 1) * 2048],
                  r=[("dram", "QT", h, m) for m in range(1, 9)], w=[QTs.k(h)])
    for h in range(2):
        k.dma(KTs[:, h * TK:(h + 1) * TK], KTd[h], r=[("dram", "KT", h, m) for m in range(9)], w=[KTs.k(h)])
    for t in range(NTILE):
        k.dma(Vs[:, t * 256:(t + 1) * 256], Vd[t], r=[("dram", "V", t)], w=[Vs.k(t)])
    onesb = k.alloc("onesb", 128, BF16)
    k.dve("memset", onesb[:, :], 1.0, w=[onesb.k()])
    PT = [k.alloc(f"PT{i}", 512, BF16) for i in range(4)]
    RD = [k.alloc(f"RD{i}", 512, F32) for i in range(2)]
    OO = [k.alloc(f"OO{i}", 512, F32) for i in range(2)]
    ZG = [k.alloc(f"ZGa{i}", 512, BF16) for i in range(2)]
    YB = [k.alloc(f"YB{i}", 512, BF16) for i in range(2)]
    if banks is None:
        ps_s = [k.PS[0], k.PS[1], k.PS[2], k.PS[3]]
        ps_o = [k.PS[4], k.PS[5]]
        ps_d = [k.PS[6], k.PS[7]]
    else:
        ps_s = [banks[0], banks[1]]
        ps_o = [banks[2], banks[2]]
        ps_d = [banks[3], banks[3]]
    NPS = len(ps_s)
    scale = 128.0 ** -0.5

    def body():
      it = 0
      sc = 0
      for qi in range(T // 512):
        for h in range(4):
            kv = h // 2
            po, pd = ps_o[it % 2], ps_d[it % 2]
            rd, oo, zg, yb = RD[it % 2], OO[it % 2], ZG[it % 2], YB[it % 2]
            k.dma(zg[:, :], ZBT[h][:, qi * 512:(qi + 1) * 512], r=[("dram", "ZBT", h, qi + 1)], w=[zg.k()])
            qsl = QTs[:, h * T + qi * 512:h * T + (qi + 1) * 512]

            def score(kc):
                ps = ps_s[(sc + kc) % NPS]
                k.pe("matmul", ps[:, :], lhsT=KTs[:, kv * TK + kc * 128:kv * TK + (kc + 1) * 128], rhs=qsl, start=True, stop=True,
                     r=[KTs.k(kv), QTs.k(h)], w=[ps.k()])
                pt = PT[(sc + kc) % 4]
                k.act("activation", out=pt[:, :], in_=ps[:, :], func=AF.Exp, scale=scale, r=[ps.k()], w=[pt.k()])

            score(0)
            score(1)
            for kc in range(NKC):
                if kc + 2 < NKC:
                    score(kc + 2)
                pt = PT[(sc + kc) % 4]
                tile_id = kc if kc < 32 else kc
                k.pe("matmul", po[:, :], lhsT=Vs[:, kc * 256 + kv * 128:kc * 256 + (kv + 1) * 128], rhs=pt[:, :], start=(kc == 0), stop=(kc == NKC - 1),
                     r=[Vs.k(kc), pt.k()], w=[po.k()])
                k.pe("matmul", pd[:, :], lhsT=onesb[:, :], rhs=pt[:, :], start=(kc == 0), stop=(kc == NKC - 1),
                     r=[onesb.k(), pt.k()], w=[pd.k()])
                yield
            sc += NKC
            k.dve("reciprocal", out=rd[:, :], in_=pd[:, :], r=[pd.k()], w=[rd.k()])
            k.dve("tensor_tensor", out=oo[:, :], in0=po[:, :], in1=rd[:, :], op=ALU.mult, r=[po.k(), rd.k()], w=[oo.k()])
            k.pool("tensor_tensor", out=yb[:, :], in0=oo[:, :], in1=zg[:, :], op=ALU.mult, r=[oo.k(), zg.k()], w=[yb.k()])
            k.dma(YT[4 + h][:, qi * 512:(qi + 1) * 512], yb[:, :], r=[yb.k()], w=[("dram", "YT", 4 + h, qi)])
            it += 1
            yield

    if as_gen:
        return body()
    for _ in body():
        pass
    k.release(m0)


def p3_outproj(k, c):
    nc = k.nc
    m0 = k.mark()
    x = k.dram("x", [T, D], F32)
    x1 = k.dram("x1", [T, D], F32)
    modd = k.dram("mod", [8, 128, D], F32)
    YT = k.dram("YT", [8, 128, T], BF16)
    w_out_d = k.dram("ev_w_out_r", [128, 8 * 1024], F32)
    w_out = k.alloc("w_out0", 8 * 1024, BF16)
    k.dma(w_out[:, :], w_out_d, r=[("dram", "ev_w_out_r")], w=[w_out.k()], q="pool")
    G0 = k.alloc("G0", D, F32)
    k.dma(G0[:, :], modd[2], r=[("dram", "mod")], w=[G0.k()])
    YTt = [k.alloc(f"YTt{i}", 8 * 512, BF16) for i in range(2)]
    O = [k.alloc(f"O{i}", D, F32) for i in range(2)]
    junks = [k.alloc(f"junkp3{i}", D, BF16) for i in range(2)]
    ss = [k.alloc(f"ssp{i}", 8, F32) for i in range(2)]
    XR = [k.alloc(f"XR{i}", D, F32) for i in range(2)]
    ps_o = [(k.PS[0], k.PS[1]), (k.PS[2], k.PS[3])]
    oc = 0
    for m in range(T // 512):
        yt = YTt[m % 2]
        for j in range(8):
            k.dma(yt[:, j * 512:(j + 1) * 512], YT[j][:, m * 512:(m + 1) * 512], r=[("dram", "YT", j, m)], w=[yt.k(j)])
        def subtile(s, slot):
            r0 = m * 512 + s * 128
            po = ps_o[slot]
            o, sst, xr, jk = O[slot], ss[slot], XR[slot], junks[slot]
            k.dma(xr[:, :], x[r0:r0 + 128, :], r=[("dram", "x", r0 // 128)], w=[xr.k()])
            for nh in range(2):
                for j in range(8):
                    k.pe("matmul", po[nh][:, :], lhsT=yt[:, j * 512 + s * 128:j * 512 + (s + 1) * 128], rhs=w_out[:, j * 1024 + nh * 512:j * 1024 + (nh + 1) * 512],
                         start=(j == 0), stop=(j == 7), r=[yt.k(j), w_out.k()], w=[po[nh].k()])
                yield
            k.act("copy", out=o[:, 0:512], in_=po[0][:, :], r=[po[0].k()], w=[o.k()])
            yield
            k.dve("tensor_copy", out=o[:, 512:1024], in_=po[1][:, :], r=[po[1].k()], w=[o.k()])
            yield
            yield from post_res_gen(k, o, sst, jk, G0, xr, o, xr, x1[r0:r0 + 128, :], ("dram", "x1", r0 // 128))
        run_interleaved([functools.partial(subtile, s) for s in range(4)], 2, slotted=True)
    k.release(m0)


def p1b_gdnprep(k, c):
    nc = k.nc
    m0 = k.mark()
    GPRE = k.dram("GPRE", [12, 128, GP_W], BF16)
    pvd = k.dram("pv", [128, PV_N], F32)
    GQT = k.dram("GQT", [4, 128, TK], BF16)
    GKT = k.dram("GKT", [4, 128, TK], BF16)
    GK = k.dram("GK", [NTILE, 128, 512], BF16)
    GV = k.dram("GV", [NTILE, 128, 512], BF16)
    pv = k.alloc("pv", PV_N, F32)
    k.dma(pv[:, :], pvd, r=[("dram", "pv")], w=[pv.k()])
    DG = k.alloc("DG5", 60 * 128, BF16)
    for i in range(60):
        eng = "dve" if i % 2 == 0 else "pool"
        k.any(eng, "tensor_scalar", out=DG[:, i * 128:(i + 1) * 128], in0=c["identf"][:, :], scalar1=pv[:, PV_C5W + i:PV_C5W + i + 1], scalar2=None,
              op0=ALU.mult, r=[c["identf"].k(), pv.k()], w=[DG.k(i // 5)])
    onesb = k.alloc("onesb", 128, BF16)
    k.dve("memset", onesb[:, :], 1.0, w=[onesb.k()])
    PRE = [k.alloc(f"PRE5{i}", 516, BF16) for i in range(8)]
    U = [k.alloc(f"U5{i}", 512, F32) for i in range(8)]
    SQ = [k.alloc(f"SQ5{i}", 512, BF16) for i in range(8)]
    RS = [k.alloc(f"RS5{i}", 512, F32) for i in range(8)]
    UN = [k.alloc(f"UN5{i}", 512, BF16) for i in range(8)]
    GKs = [k.alloc(f"GKs{i}", 4 * 512, BF16) for i in range(2)]
    GVs = [k.alloc(f"GVs{i}", 4 * 512, BF16) for i in range(2)]
    ps_cv = list(k.PS)
    ps_ss = ps_cv
    ps_tr = ps_cv
    mts = [("ctx", 0, 256)] + [("lat", m * 512, 512) for m in range(T // 512)]
    cc = 0
    for mi, (kind, t0, W) in enumerate(mts):
        lat = kind == "lat"
        gcol0 = (GP_LAT0 + t0) if lat else (GP_CTX0 + t0)
        tile0 = (t0 // 128) if lat else 32
        col0 = tile0 * 128
        nsub = W // 128
        gks, gvs = GKs[mi % 2], GVs[mi % 2]
        def chunk(j, cc, sl):
            pre = PRE[sl]
            pcv = ps_cv[sl]
            k.dma(pre[:, 0:W + 4], GPRE[j][:, gcol0 - 2:gcol0 + W + 2],
                  r=[("dram", "GPRE", j, x_) for x_ in (["p0", "p1", "p2"] + list(range(max(0, mi - 1), min(9, mi + 2))))], w=[pre.k()])
            for t in range(5):
                k.pe("matmul", pcv[:, 0:W], lhsT=DG[:, (j * 5 + t) * 128:(j * 5 + t + 1) * 128], rhs=pre[:, t:t + W], start=(t == 0), stop=(t == 4),
                     r=[DG.k(j), pre.k()], w=[pcv.k()])
            un = UN[sl]
            if j < 8:
                u, sq, rs, pss = U[sl], SQ[sl], RS[sl], ps_ss[sl]
                k.act("activation", out=u[:, 0:W], in_=pcv[:, 0:W], func=AF.Silu, r=[pcv.k()], w=[u.k()])
                yield
                k.act("activation", out=sq[:, 0:W], in_=u[:, 0:W], func=AF.Square, r=[u.k()], w=[sq.k()])
                yield
                k.pe("matmul", pss[:, 0:W], lhsT=onesb[:, :], rhs=sq[:, 0:W], start=True, stop=True, r=[onesb.k(), sq.k()], w=[pss.k()])
                yield
                k.act("activation", out=rs[:, 0:W], in_=pss[:, 0:W], func=AF.Sqrt, bias=NORM_EPS, scale=1.0, r=[pss.k()], w=[rs.k()])
                yield
                k.dve("reciprocal", out=rs[:, 0:W], in_=rs[:, 0:W], r=[rs.k()], w=[rs.k()])
                yield
                if j < 4:
                    k.dve("scalar_tensor_tensor", out=un[:, 0:W], in0=u[:, 0:W], scalar=128.0 ** -0.5, in1=rs[:, 0:W], op0=ALU.mult, op1=ALU.mult,
                          r=[u.k(), rs.k()], w=[un.k()])
                    k.dma(GQT[j][:, col0:col0 + W], un[:, 0:W], r=[un.k()], w=[("dram", "GQT", j, mi)])
                    yield
                else:
                    k.dve("tensor_tensor", out=un[:, 0:W], in0=u[:, 0:W], in1=rs[:, 0:W], op=ALU.mult, r=[u.k(), rs.k()], w=[un.k()])
                    yield
                    k.dma(GKT[j - 4][:, col0:col0 + W], un[:, 0:W], r=[un.k()], w=[("dram", "GKT", j - 4, mi)])
                    yield
            else:
                k.act("activation", out=un[:, 0:W], in_=pcv[:, 0:W], func=AF.Silu, r=[pcv.k()], w=[un.k()])
                yield
            if j >= 4:
                h = (j - 4) % 4
                ptr = ps_tr[sl]
                ptb = ptr.ap.bitcast(BF16)
                for s in range(nsub):
                    k.pe("transpose", ptb[:, s * 128:(s + 1) * 128], un[:, s * 128:(s + 1) * 128], c["identb"][:, :], r=[un.k(), c["identb"].k()], w=[ptr.k()])
                    yield
                dst = (gks if j < 8 else gvs)
                dview = dst[:, :].rearrange("p (s f) -> p s f", s=4)[:, 0:nsub, h * 128:(h + 1) * 128]
                sview = ptb[:, 0:nsub * 128].rearrange("p (s f) -> p s f", f=128)
                if cc % 2 == 0:
                    k.act("copy", out=dview, in_=sview, r=[ptr.k()], w=[dst.k()])
                    yield
                else:
                    k.dve("tensor_copy", out=dview, in_=sview, r=[ptr.k()], w=[dst.k()])
                    yield
            yield
        run_interleaved([functools.partial(chunk, j, cc + j) for j in range(12)], 8, slotted=True)
        cc += 12
        for s in range(nsub):
            k.dma(GK[tile0 + s], gks[:, s * 512:(s + 1) * 512], r=[gks.k()], w=[("dram", "GK", tile0 + s)])
            k.dma(GV[tile0 + s], gvs[:, s * 512:(s + 1) * 512], r=[gvs.k()], w=[("dram", "GV", tile0 + s)])
    k.release(m0)


GDN_LAG = 45


def p2a_gdn(k, c, nslots=4, as_gens=False):
    nc = k.nc
    m0 = None if as_gens else k.mark()
    GQT = k.dram("GQT", [4, 128, TK], BF16)
    GKT = k.dram("GKT", [4, 128, TK], BF16)
    GK = k.dram("GK", [NTILE, 128, 512], BF16)
    GV = k.dram("GV", [NTILE, 128, 512], BF16)
    ABd = k.dram("AB", [NTILE, 128, 16], F32)
    trid = k.dram("c_tri", [9, 128, 128], F32)
    OD = [k.dram("OF", [32, 128, 512], F32), k.dram("OB", [32, 128, 512], F32)]
    TRI = k.alloc("TRI", 9 * 128, F32)
    for i in range(9):
        k.dma(TRI[:, i * 128:(i + 1) * 128], trid[i], r=[("dram", "c_tri")], w=[TRI.k()])
    tri = lambda i: TRI[:, i * 128:(i + 1) * 128]
    bc4 = lambda ap: ap.unsqueeze(1).to_broadcast([128, 4, 128])
    col4 = lambda ap: ap.unsqueeze(2).to_broadcast([128, 4, 128])
    v3 = lambda t: t[:, :].rearrange("p (h f) -> p h f", h=4)
    ABs = k.alloc("ABs", NTILE * 16, F32)
    for t in range(NTILE):
        k.dma(ABs[:, t * 16:(t + 1) * 16], ABd[t], r=[("dram", "AB", t)], w=[ABs.k()])
    alog = k.alloc("alog", 8, F32)
    dtb = k.alloc("dtb", 8, F32)
    k.dma(alog[:, :], row_bc(k, "a_log"), r=[("dram", "rows")], w=[alog.k()])
    k.dma(dtb[:, :], row_bc(k, "dt_bias"), r=[("dram", "rows")], w=[dtb.k()])
    GALL = k.alloc("GALL", NTILE * 8, F32)
    BALL = k.alloc("BALL", NTILE * 8, F32)
    ab3 = ABs[:, :].rearrange("p (t f) -> p t f", f=16)
    g3 = GALL[:, :].rearrange("p (t f) -> p t f", f=8)
    b3 = BALL[:, :].rearrange("p (t f) -> p t f", f=8)
    bct = lambda ap: ap.unsqueeze(1).to_broadcast([128, NTILE, 8])
    k.dve("tensor_tensor", out=g3, in0=ab3[:, :, 0:8], in1=bct(dtb[:, :]), op=ALU.add, r=[ABs.k(), dtb.k()], w=[GALL.k()])
    k.act("activation", out=GALL[:, :], in_=GALL[:, :], func=AF.Exp, r=[GALL.k()], w=[GALL.k()])
    k.act("activation", out=GALL[:, :], in_=GALL[:, :], func=AF.Ln, bias=1.0, scale=1.0, r=[GALL.k()], w=[GALL.k()])
    k.act("activation", out=alog[:, :], in_=alog[:, :], func=AF.Exp, r=[alog.k()], w=[alog.k()])
    k.dve("scalar_tensor_tensor", out=g3, in0=g3, scalar=-1.0, in1=bct(alog[:, :]), op0=ALU.mult, op1=ALU.mult, r=[GALL.k(), alog.k()], w=[GALL.k()])
    k.act("activation", out=b3, in_=ab3[:, :, 8:16], func=AF.Exp, scale=-1.0, r=[ABs.k()], w=[BALL.k()])
    k.dve("tensor_scalar", out=BALL[:, :], in0=BALL[:, :], scalar1=1.0, scalar2=None, op0=ALU.add, r=[BALL.k()], w=[BALL.k()])
    k.dve("reciprocal", out=BALL[:, :], in_=BALL[:, :], r=[BALL.k()], w=[BALL.k()])
    Sf = [k.alloc(f"Sf{d}", 512, F32) for d in range(2)]
    Sb = [k.alloc(f"Sb{d}", 512, BF16) for d in range(2)]
    for d in range(2):
        k.dve("memset", Sf[d][:, :], 0.0, w=[Sf[d].k()])
        k.pool("memset", Sb[d][:, :], 0.0, w=[Sb[d].k()])
    def bufs(d):
        B = {}
        for n_ in ["qT4", "kT4", "ktok", "vtok", "X", "XT", "PT", "AINC", "AINCT", "KD", "ATn", "QEFF", "N1", "N1T", "N2", "P", "V1", "U1"]:
            B[n_] = k.alloc(f"{n_}{d}", 512, BF16)
        for n_ in ["WUR", "WU"]:
            B[n_] = k.alloc(f"{n_}{d}", 1024, BF16)
        for n_ in ["DIFF", "E", "DMS", "DMI", "T1", "ER", "QD", "OUT"]:
            B[n_] = k.alloc(f"{n_}{d}", 512, F32)
        B["SM"] = k.alloc(f"SM{d}", 32, F32)
        return B
    BUF = [bufs(i) for i in range(nslots)]
    order = [[32, 33] + list(range(32)), [33, 32] + list(range(31, -1, -1))]
    cfg = [dict(tri=2, mi=0, ms=1, jl=127, m1a=5, m1b=6, m2a=7), dict(tri=0, mi=2, ms=3, jl=0, m1a=6, m1b=5, m2a=8)]
    identb = c["identb"]
    sdone = {}

    def unit(n, d, slot):
        Q = k.PS[2 * slot:2 * slot + 2]
        P_GR = P_B = P_W0 = P_Z = Q[0]
        P_SM = P_A = P_T = P_W1 = P_Z2 = Q[1]
        if n == 1 and nslots == 4:
            for _ in range(GDN_LAG):
                yield
        g = order[d][n]
        B = BUF[slot]
        cf = cfg[d]
        lat = g < 32
        sm = B["SM"]
        GC, GL, EG, BE, KDS, GT, TMP = (sm[:, 0:4], sm[:, 4:8], sm[:, 8:12], sm[:, 12:16], sm[:, 16:20], sm[:, 20:24], sm[:, 24:28])
        gcol = GALL[:, g * 8 + d * 4:g * 8 + d * 4 + 4]
        bcol = BALL[:, g * 8 + d * 4:g * 8 + d * 4 + 4]
        for h in range(4):
            k.dma(B["qT4"][:, h * 128:(h + 1) * 128], GQT[h][:, g * 128:(g + 1) * 128], r=[("dram", "GQT", h, mi_) for mi_ in range(9)], w=[B["qT4"].k()])
            k.dma(B["kT4"][:, h * 128:(h + 1) * 128], GKT[h][:, g * 128:(g + 1) * 128], r=[("dram", "GKT", h, mi_) for mi_ in range(9)], w=[B["kT4"].k()])
        k.dma(B["ktok"][:, :], GK[g], r=[("dram", "GK", g)], w=[B["ktok"].k()])
        yield
        k.dma(B["vtok"][:, :], GV[g], r=[("dram", "GV", g)], w=[B["vtok"].k()])
        yield
        for h in range(4):
            k.pe("matmul", P_GR[:, h * 128:(h + 1) * 128], lhsT=gcol[:, h:h + 1].to_broadcast([128, 128]), rhs=tri(cf["tri"]), start=True, stop=True,
                 r=[GALL.k(), TRI.k()], w=[P_GR.k()])
        k.pe("matmul", P_SM[:, 0:4], lhsT=tri(cf["tri"]), rhs=gcol, start=True, stop=True, r=[GALL.k(), TRI.k()], w=[P_SM.k()])
        yield
        k.act("copy", out=GC, in_=P_SM[:, 0:4], r=[P_SM.k()], w=[sm.k()])
        yield
        k.dve("tensor_copy", out=GL, in_=v3(P_GR)[:, :, cf["jl"]], r=[P_GR.k()], w=[sm.k()])
        yield
        k.dve("tensor_tensor", out=v3(B["DIFF"]), in0=col4(GC), in1=v3(P_GR), op=ALU.subtract, r=[sm.k(), P_GR.k()], w=[B["DIFF"].k()])
        yield
        k.act("activation", out=B["ER"][:, :], in_=P_GR[:, :], func=AF.Exp, r=[P_GR.k()], w=[B["ER"].k()])
        yield
        k.pool("tensor_scalar", out=B["DIFF"][:, :], in0=B["DIFF"][:, :], scalar1=0.0, scalar2=None, op0=ALU.min, r=[B["DIFF"].k()], w=[B["DIFF"].k()])
        yield
        k.act("activation", out=B["E"][:, :], in_=B["DIFF"][:, :], func=AF.Exp, r=[B["DIFF"].k()], w=[B["E"].k()])
        yield
        k.pool("tensor_tensor", out=v3(B["DMI"]), in0=v3(B["E"]), in1=bc4(tri(cf["mi"])), op=ALU.mult, r=[B["E"].k(), TRI.k()], w=[B["DMI"].k()])
        yield
        k.act("activation", out=EG, in_=GC, func=AF.Exp, r=[sm.k()], w=[sm.k()])
        yield
        k.dve("tensor_tensor", out=BE, in0=EG, in1=bcol, op=ALU.mult, r=[sm.k(), BALL.k()], w=[sm.k()])
        yield
        k.dve("tensor_tensor", out=TMP, in0=GL, in1=GC, op=ALU.subtract, r=[sm.k()], w=[sm.k()])
        yield
        k.act("activation", out=KDS, in_=TMP, func=AF.Exp, r=[sm.k()], w=[sm.k()])
        yield
        k.act("activation", out=GT, in_=GL, func=AF.Exp, r=[sm.k()], w=[sm.k()])
        yield
        for h in range(4):
            k.pe("matmul", P_A[:, h * 128:(h + 1) * 128], lhsT=B["kT4"][:, h * 128:(h + 1) * 128], rhs=B["kT4"][:, h * 128:(h + 1) * 128], start=True, stop=True,
                 r=[B["kT4"].k()], w=[P_A.k()])
        for h in range(4):
            k.pe("matmul", P_B[:, h * 128:(h + 1) * 128], lhsT=B["qT4"][:, h * 128:(h + 1) * 128], rhs=B["kT4"][:, h * 128:(h + 1) * 128], start=True, stop=True,
                 r=[B["qT4"].k(), B["kT4"].k()], w=[P_B.k()])
        k.dve("tensor_tensor", out=B["T1"][:, :], in0=P_A[:, :], in1=B["DMI"][:, :], op=ALU.mult, r=[P_A.k(), B["DMI"].k()], w=[B["T1"].k()])
        yield
        k.dve("scalar_tensor_tensor", out=v3(B["X"]), in0=v3(B["T1"]), scalar=-1.0, in1=col4(bcol), op0=ALU.mult, op1=ALU.mult,
               r=[B["T1"].k(), BALL.k()], w=[B["X"].k()])
        k.dve("tensor_tensor", out=B["AINC"][:, :], in0=P_B[:, :], in1=B["DMI"][:, :], op=ALU.mult, r=[P_B.k(), B["DMI"].k()], w=[B["AINC"].k()])
        yield
        ptb = P_T.ap.bitcast(BF16)
        for h in range(4):
            k.pe("transpose", ptb[:, h * 128:(h + 1) * 128], B["X"][:, h * 128:(h + 1) * 128], identb[:, :], r=[B["X"].k(), identb.k()], w=[P_T.k()])
        for h in range(4):
            k.pe("transpose", ptb[:, 512 + h * 128:512 + (h + 1) * 128], B["AINC"][:, h * 128:(h + 1) * 128], identb[:, :], r=[B["AINC"].k(), identb.k()], w=[P_T.k()])
        k.act("copy", out=B["XT"][:, :], in_=ptb[:, 0:512], r=[P_T.k()], w=[B["XT"].k()])
        yield
        k.act("copy", out=B["AINCT"][:, :], in_=ptb[:, 512:1024], r=[P_T.k()], w=[B["AINCT"].k()])
        yield
        k.pool("tensor_tensor", out=v3(B["N1"]), in0=v3(B["X"]), in1=bc4(tri(cf["m1a"])), op=ALU.mult, r=[B["X"].k(), TRI.k()], w=[B["N1"].k()])
        yield
        k.pool("tensor_tensor", out=v3(B["N1T"]), in0=v3(B["XT"]), in1=bc4(tri(cf["m1b"])), op=ALU.mult, r=[B["XT"].k(), TRI.k()], w=[B["N1T"].k()])
        yield
        k.pool("tensor_tensor", out=v3(B["N2"]), in0=v3(B["X"]), in1=bc4(tri(cf["m2a"])), op=ALU.mult, r=[B["X"].k(), TRI.k()], w=[B["N2"].k()])
        yield
        k.dve("tensor_tensor", out=v3(B["X"]), in0=v3(B["X"]), in1=bc4(tri(4)), op=ALU.mult, r=[B["X"].k(), TRI.k()], w=[B["X"].k()])
        yield
        k.dve("tensor_tensor", out=v3(B["XT"]), in0=v3(B["XT"]), in1=bc4(tri(4)), op=ALU.mult, r=[B["XT"].k(), TRI.k()], w=[B["XT"].k()])
        yield
        k.pool("tensor_tensor", out=v3(B["P"]), in0=v3(B["X"]), in1=bc4(identb[:, :]), op=ALU.add, r=[B["X"].k(), identb.k()], w=[B["P"].k()])
        yield
        k.dve("tensor_tensor", out=v3(B["PT"]), in0=v3(B["XT"]), in1=bc4(identb[:, :]), op=ALU.add, r=[B["XT"].k(), identb.k()], w=[B["PT"].k()])
        yield

        def mm4(ps, lhs, rhs, acc=None):
            for h in range(4):
                sl = slice(h * 128, (h + 1) * 128)
                k.pe("matmul", ps[:, sl], lhsT=B[lhs][:, sl], rhs=B[rhs][:, sl], start=True, stop=(acc is None), r=[B[lhs].k(), B[rhs].k()], w=[ps.k()])
                if acc is not None:
                    k.pe("matmul", ps[:, sl], lhsT=identb[:, :], rhs=B[acc][:, sl], start=False, stop=True, r=[identb.k(), B[acc].k()], w=[ps.k()])

        for l in range(1, 5):
            mm4(P_A, "XT", "X")
            yield
            mm4(P_B, "X", "XT")
            yield
            k.act("copy", out=B["X"][:, :], in_=P_A[:, :], r=[P_A.k()], w=[B["X"].k()])
            yield
            k.act("copy", out=B["XT"][:, :], in_=P_B[:, :], r=[P_B.k()], w=[B["XT"].k()])
            yield
            mm4(P_T, "XT", "P", acc="P")
            yield
            mm4(P_W0, "X", "PT", acc="PT")
            yield
            k.act("copy", out=B["P"][:, :], in_=P_T[:, :], r=[P_T.k()], w=[B["P"].k()])
            yield
            k.dve("tensor_copy", out=B["PT"][:, :], in_=P_W0[:, :], r=[P_W0.k()], w=[B["PT"].k()])
            yield
        mm4(P_A, "N1T", "P")
        yield
        mm4(P_B, "N1", "PT")
        yield
        k.act("copy", out=B["V1"][:, :], in_=P_A[:, :], r=[P_A.k()], w=[B["V1"].k()])
        yield
        k.act("copy", out=B["U1"][:, :], in_=P_B[:, :], r=[P_B.k()], w=[B["U1"].k()])
        yield
        mm4(P_T, "PT", "V1", acc="P")
        yield
        mm4(P_W0, "P", "U1", acc="PT")
        yield
        k.act("copy", out=B["P"][:, :], in_=P_T[:, :], r=[P_T.k()], w=[B["P"].k()])
        yield
        k.dve("tensor_copy", out=B["PT"][:, :], in_=P_W0[:, :], r=[P_W0.k()], w=[B["PT"].k()])
        yield
        mm4(P_A, "N2", "PT")
        yield
        k.act("copy", out=B["U1"][:, :], in_=P_A[:, :], r=[P_A.k()], w=[B["U1"].k()])
        yield
        mm4(P_T, "P", "U1", acc="PT")
        yield
        k.act("copy", out=B["PT"][:, :], in_=P_T[:, :], r=[P_T.k()], w=[B["PT"].k()])
        yield
        wur = B["WUR"][:, :].rearrange("p (h f) -> p h f", h=4)
        k.pool("tensor_tensor", out=wur[:, :, 0:128], in0=v3(B["ktok"]), in1=col4(BE), op=ALU.mult, r=[B["ktok"].k(), sm.k()], w=[B["WUR"].k()])
        yield
        k.pool("tensor_tensor", out=wur[:, :, 128:256], in0=v3(B["vtok"]), in1=col4(bcol), op=ALU.mult, r=[B["vtok"].k(), BALL.k()], w=[B["WUR"].k()])
        yield
        k.pool("tensor_tensor", out=v3(B["KD"]), in0=v3(B["ktok"]), in1=col4(KDS), op=ALU.mult, r=[B["ktok"].k(), sm.k()], w=[B["KD"].k()])
        yield
        for h in range(4):
            pw = P_W0 if h < 2 else P_W1
            k.pe("matmul", pw[:, (h % 2) * 256:(h % 2 + 1) * 256], lhsT=B["PT"][:, h * 128:(h + 1) * 128], rhs=B["WUR"][:, h * 256:(h + 1) * 256], start=True, stop=True,
                 r=[B["PT"].k(), B["WUR"].k()], w=[pw.k()])
        k.act("copy", out=B["WU"][:, 0:512], in_=P_W0[:, :], r=[P_W0.k()], w=[B["WU"].k()])
        yield
        k.act("copy", out=B["WU"][:, 512:1024], in_=P_W1[:, :], r=[P_W1.k()], w=[B["WU"].k()])
        yield
        wv = lambda h: B["WU"][:, h * 256:h * 256 + 128]
        uv = lambda h: B["WU"][:, h * 256 + 128:h * 256 + 256]
        for h in range(4):
            k.pe("matmul", P_Z[:, h * 128:(h + 1) * 128], lhsT=wv(h), rhs=B["KD"][:, h * 128:(h + 1) * 128], start=True, stop=True,
                 r=[B["WU"].k(), B["KD"].k()], w=[P_Z.k()])
        k.act("activation", out=B["ATn"][:, :], in_=P_Z[:, :], func=AF.Copy, scale=-1.0, r=[P_Z.k()], w=[B["ATn"].k()])
        yield
        while n > 0 and not sdone.get((n - 1, d)):
            yield
        if lat:
            k.pool("tensor_tensor", out=B["QD"][:, :], in0=B["qT4"][:, :], in1=B["ER"][:, :], op=ALU.mult, r=[B["qT4"].k(), B["ER"].k()], w=[B["QD"].k()])
            for h in range(4):
                k.pe("matmul", P_Z2[:, h * 128:(h + 1) * 128], lhsT=wv(h), rhs=B["AINCT"][:, h * 128:(h + 1) * 128], start=True, stop=True,
                     r=[B["WU"].k(), B["AINCT"].k()], w=[P_Z2.k()])
            k.dve("tensor_tensor", out=B["QEFF"][:, :], in0=B["QD"][:, :], in1=P_Z2[:, :], op=ALU.subtract, r=[B["QD"].k(), P_Z2.k()], w=[B["QEFF"].k()])
            assert n == 0 or sdone.get((n - 1, d)), f"GDN interleave order violated (o) at n={n} d={d}"
            for h in range(4):
                sl = slice(h * 128, (h + 1) * 128)
                k.pe("matmul", P_Z[:, sl], lhsT=B["QEFF"][:, sl], rhs=Sb[d][:, sl], start=True, stop=False, r=[B["QEFF"].k(), Sb[d].k()], w=[P_Z.k()])
                k.pe("matmul", P_Z[:, sl], lhsT=B["AINCT"][:, sl], rhs=uv(h), start=False, stop=True, r=[B["AINCT"].k(), B["WU"].k()], w=[P_Z.k()])
            k.act("copy", out=B["OUT"][:, :], in_=P_Z[:, :], r=[P_Z.k()], w=[B["OUT"].k()])
            k.dma(OD[d][g], B["OUT"][:, :], r=[B["OUT"].k()], w=[("dram", "O", d, g)])
        assert n == 0 or sdone.get((n - 1, d)), f"GDN interleave order violated at n={n} d={d}"
        for h in range(4):
            sl = slice(h * 128, (h + 1) * 128)
            k.pe("matmul", P_Z2[:, sl], lhsT=B["ATn"][:, sl], rhs=Sb[d][:, sl], start=True, stop=False, r=[B["ATn"].k(), Sb[d].k()], w=[P_Z2.k()])
            k.pe("matmul", P_Z2[:, sl], lhsT=B["KD"][:, sl], rhs=uv(h), start=False, stop=True, r=[B["KD"].k(), B["WU"].k()], w=[P_Z2.k()])
        k.pool("tensor_tensor", out=v3(Sf[d]), in0=v3(Sf[d]), in1=col4(GT), op=ALU.mult, r=[Sf[d].k(), sm.k()], w=[Sf[d].k()])
        yield
        k.dve("tensor_tensor", out=Sf[d][:, :], in0=Sf[d][:, :], in1=P_Z2[:, :], op=ALU.add, r=[Sf[d].k(), P_Z2.k()], w=[Sf[d].k()])
        yield
        k.act("copy", out=Sb[d][:, :], in_=Sf[d][:, :], r=[Sf[d].k()], w=[Sb[d].k()])
        sdone[(n, d)] = True
        yield

    gens = []
    for n in range(NTILE):
        for d in range(2):
            gens.append(functools.partial(unit, n, d))
    if as_gens:
        return gens
    run_interleaved(gens, nslots, slotted=True)
    k.release(m0)


def gdn_consts():
    idx = np.arange(128)
    ge = (idx[:, None] >= idx[None, :]).astype(np.float32)
    gt = (idx[:, None] > idx[None, :]).astype(np.float32)
    bd32 = ((idx[:, None] // 32 == idx[None, :] // 32) & (idx[:, None] != idx[None, :])).astype(np.float32)
    m1l = ((idx[:, None] // 64 == idx[None, :] // 64) & (idx[:, None] // 32 == idx[None, :] // 32 + 1)).astype(np.float32)
    m2l = ((idx[:, None] >= 64) & (idx[None, :] < 64)).astype(np.float32)
    return np.ascontiguousarray(np.stack([ge, gt, ge.T, gt.T, bd32, m1l, m1l.T, m2l, m2l.T]))


def p2c_gdnout(k, c):
    nc = k.nc
    m0 = k.mark()
    OF = k.dram("OF", [32, 128, 512], F32)
    OB = k.dram("OB", [32, 128, 512], F32)
    ZA = k.dram("ZA", [T, 512], BF16)
    YT = k.dram("YT", [8, 128, T], BF16)
    GG = k.alloc("GGn", 512, F32)
    for h in range(4):
        k.dma(GG[:, h * 128:(h + 1) * 128], row_bc(k, "gdn_g"), r=[("dram", "rows")], w=[GG.k()])
    NS = 4
    of = [k.alloc(f"of{i}", 512, F32) for i in range(NS)]
    ob = [k.alloc(f"ob{i}", 512, F32) for i in range(NS)]
    za = [k.alloc(f"zac{i}", 512, BF16) for i in range(NS)]
    o = [k.alloc(f"oc{i}", 512, F32) for i in range(NS)]
    sq = [k.alloc(f"sqc{i}", 512, F32) for i in range(NS)]
    st = [k.alloc(f"stc{i}", 16, F32) for i in range(NS)]
    yb = [k.alloc(f"yc{i}", 512, BF16) for i in range(NS)]
    yts = [k.alloc(f"ytc{i}", 4 * 512, BF16) for i in range(2)]
    ps_tr = [k.PS[0], k.PS[1], k.PS[2], k.PS[3]]
    v3 = lambda t: t[:, :].rearrange("p (h f) -> p h f", h=4)

    def tile(g, i):
        m = g // 4
        s = g % 4
        yt = yts[m % 2]
        k.dma(of[i][:, :], OF[g], r=[("dram", "O", 0, g)], w=[of[i].k()])
        k.dma(ob[i][:, :], OB[g], r=[("dram", "O", 1, g)], w=[ob[i].k()])
        k.dma(za[i][:, :], ZA[g * 128:(g + 1) * 128, :], r=[("dram", "ZA", g)], w=[za[i].k()])
        k.dve("tensor_tensor", out=o[i][:, :], in0=of[i][:, :], in1=ob[i][:, :], op=ALU.add, r=[of[i].k(), ob[i].k()], w=[o[i].k()])
        yield
        k.pool("tensor_tensor", out=sq[i][:, :], in0=o[i][:, :], in1=o[i][:, :], op=ALU.mult, r=[o[i].k()], w=[sq[i].k()])
        yield
        k.dve("tensor_reduce", out=st[i][:, 0:4], in_=v3(sq[i]), axis=AX.X, op=ALU.add, r=[sq[i].k()], w=[st[i].k()])
        yield
        k.act("activation", out=st[i][:, 4:8], in_=st[i][:, 0:4], func=AF.Sqrt, scale=1.0 / 128, bias=NORM_EPS, r=[st[i].k()], w=[st[i].k()])
        yield
        k.dve("reciprocal", out=st[i][:, 8:12], in_=st[i][:, 4:8], r=[st[i].k()], w=[st[i].k()])
        yield
        k.dve("tensor_tensor", out=v3(o[i]), in0=v3(o[i]), in1=st[i][:, 8:12].unsqueeze(2).to_broadcast([128, 4, 128]), op=ALU.mult,
              r=[o[i].k(), st[i].k()], w=[o[i].k()])
        yield
        k.pool("tensor_tensor", out=o[i][:, :], in0=o[i][:, :], in1=GG[:, :], op=ALU.mult, r=[o[i].k(), GG.k()], w=[o[i].k()])
        yield
        k.dve("tensor_tensor", out=yb[i][:, :], in0=o[i][:, :], in1=za[i][:, :], op=ALU.mult, r=[o[i].k(), za[i].k()], w=[yb[i].k()])
        yield
        ptr = ps_tr[i]
        ptb = ptr.ap.bitcast(BF16)
        for h in range(4):
            k.pe("transpose", ptb[:, h * 128:(h + 1) * 128], yb[i][:, h * 128:(h + 1) * 128], c["identb"][:, :], r=[yb[i].k(), c["identb"].k()], w=[ptr.k()])
        yield
        dview = yt[:, :].rearrange("p (h t) -> p h t", h=4)[:, :, s * 128:(s + 1) * 128]
        sview = ptb[:, 0:512].rearrange("p (h t) -> p h t", h=4)
        k.act("copy", out=dview, in_=sview, r=[ptr.k()], w=[yt.k()])
        yield

    for m in range(8):
        run_interleaved([functools.partial(tile, 4 * m + s_) for s_ in range(4)], NS, slotted=True)
        yt = yts[m % 2]
        for h in range(4):
            k.dma(YT[h][:, m * 512:(m + 1) * 512], yt[:, h * 512:(h + 1) * 512], r=[yt.k()], w=[("dram", "YT", h, m)])
    k.release(m0)


def p2ab(k, c):
    m0 = k.mark()
    gens = p2a_gdn(k, c, nslots=2, as_gens=True)
    att = p2b_attn(k, c, banks=k.PS[4:8], as_gen=True)
    il = Interleaver(gens, 2, slotted=True)
    g_alive, a_alive, r = True, True, 0
    while g_alive or a_alive:
        if g_alive:
            g_alive = il.step()
        if a_alive and (r % ATT_EVERY == 0 or not g_alive):
            try:
                next(att)
            except StopIteration:
                a_alive = False
        r += 1
    k.release(m0)


ATT_EVERY = 3
ALL_PASSES = None


def all_passes():
    return [p0_mod, p1_inproj, p1b_gdnprep, p2a_gdn, p2c_gdnout, p2b_attn, p3_outproj, l1_pass_a, l1_pass_b]


EXT_IN = ["x", "ctx", "cvec", "c_sel", "c_ident", "c_tri", "ada_w_r", "rows", "ev_w_in_r", "ev_w_out_r", "od_w_in_r", "od_w_out_r", "rope_cs", "rope_sn", "pv"]


def kernel(**inputs):
    maps = host_prep(inputs)
    tri = gdn_consts()
    for m in maps:
        m["c_tri"] = tri
    nc, _ = build_program(all_passes(), ext_in=EXT_IN, ext_out=["y"])
    in_maps = [{k_: m[k_] for k_ in EXT_IN} for m in maps]
    res = run_bass_kernel_spmd(nc, in_maps, core_ids=list(range(8)))
    return np.stack([np.asarray(r["y"], dtype=np.float32) for r in res.results], axis=0)
```

```python
import contextlib
import functools
import numpy as np
import concourse.bass as bass
import concourse.mybir as mybir
from concourse.bass_utils import run_bass_kernel_spmd

F32 = mybir.dt.float32
BF16 = mybir.dt.bfloat16
AF = mybir.ActivationFunctionType
ALU = mybir.AluOpType
AX = mybir.AxisListType

ENGS = ["pe", "act", "dve", "pool", "sp"]
NDMA_Q = {"sp": 20, "pool": 40}
EPOCH = 20000

D = 1024
T = 4096
CTX = 256
NORM_EPS = 1e-6


class Sched:
    def __init__(self, nc):
        self.nc = nc
        self.ops = []
        self.last_w = {}
        self.readers = {}
        self.cur = {e: {} for e in ENGS}
        self.pos = {e: 0 for e in ENGS}
        self.ndma = {"sp": 0, "pool": 0}
        self.dma_ops = {"sp": [], "pool": []}
        self.seen = set()
        self.inherit = {}

    def retire(self, names):
        names = set(names)
        for key in list(self.seen):
            if key[0] in names:
                cand = list(self.readers.get(key, ()))
                w = self.last_w.get(key)
                if w is not None:
                    cand.append(w)
                for c in cand:
                    o = self.ops[c]
                    old = self.inherit.get(o["src"])
                    if old is None or self.ops[old]["p"] < o["p"]:
                        self.inherit[o["src"]] = c
                self.seen.discard(key)
                self.readers.pop(key, None)
                self.last_w.pop(key, None)

    def _touch(self, key):
        if key not in self.seen:
            self.seen.add(key)
            if self.inherit:
                self.readers[key] = list(self.inherit.values())

    def add(self, eng, fn, reads=(), writes=(), dma=False):
        idx = len(self.ops)
        deps = []
        for r in reads:
            self._touch(r)
            w = self.last_w.get(r)
            if w is not None:
                deps.append((w, True))
        for k in writes:
            self._touch(k)
            w = self.last_w.get(k)
            if w is not None:
                deps.append((w, False))
            for rd in self.readers.get(k, ()):
                deps.append((rd, False))
        if dma:
            nd = self.ndma[eng]
            ns = NDMA_Q[eng]
            slot = nd % ns
            cnt = nd // ns + 1
            if nd >= ns:
                deps.append((self.dma_ops[eng][nd - ns], True))
            src = ("d", eng, slot)
            p = cnt
            self.ndma[eng] += 1
        else:
            self.pos[eng] += 1
            src = eng
            p = self.pos[eng]
        cur = self.cur[eng]
        waits = []
        for d, raw in deps:
            o = self.ops[d]
            s, v = o["src"], o["p"]
            if s == eng and not dma and not o["dma"]:
                if eng == "pe":
                    continue
            if cur.get(s, 0) >= v:
                continue
            waits.append((s, v))
            o["needed"] = True
            for ks, kv in o["vc"].items():
                if cur.get(ks, 0) < kv:
                    cur[ks] = kv
            cur[s] = v
        vc = dict(cur)
        vc[src] = p
        op = dict(eng=eng, fn=fn, waits=waits, src=src, p=p, dma=dma, vc=vc, needed=False, deps=deps)
        self.ops.append(op)
        if dma:
            self.dma_ops[eng].append(idx)
        for r in reads:
            self.readers.setdefault(r, []).append(idx)
        for k in writes:
            self.last_w[k] = idx
            self.readers[k] = []
        return idx

    def emit(self):
        nc = self.nc
        rank = {e: {} for e in ENGS}
        cnt = {e: 0 for e in ENGS}
        for o in self.ops:
            if not o["dma"] and o["needed"]:
                cnt[o["eng"]] += 1
                rank[o["eng"]][o["p"]] = cnt[o["eng"]]
        nsem = {e: max(1, (cnt[e] + EPOCH - 1) // EPOCH) for e in ENGS}
        with contextlib.ExitStack() as st:
            sems = {e: [st.enter_context(nc.semaphore(f"s_{e}{i}")) for i in range(nsem[e])] for e in ENGS}
            dsem = {q: [st.enter_context(nc.semaphore(f"s_d{q}{i}")) for i in range(NDMA_Q[q])] for q in NDMA_Q}
            block = st.enter_context(nc.Block())
            engobj = {"pe": nc.tensor, "act": nc.scalar, "dve": nc.vector, "pool": nc.gpsimd, "sp": nc.sync}
            per = {e: [o for o in self.ops if o["eng"] == e] for e in ENGS}

            def run(e):
                eo = engobj[e]
                for o in per[e]:
                    for s, v in o["waits"]:
                        if isinstance(s, tuple):
                            eo.wait_ge(dsem[s[1]][s[2]], 16 * v)
                        else:
                            r = rank[s][v] - 1
                            eo.wait_ge(sems[s][r // EPOCH], r % EPOCH + 1)
                    ins = o["fn"]()
                    if o["dma"]:
                        ins.then_inc(dsem[o["src"][1]][o["src"][2]], 16)
                    elif o["needed"]:
                        r = rank[e][o["p"]] - 1
                        ins.then_inc(sems[e][r // EPOCH], 1)
                if e == "sp":
                    for q in NDMA_Q:
                        for sl in range(min(self.ndma[q], NDMA_Q[q])):
                            last = (self.ndma[q] - 1 - sl) // NDMA_Q[q] + 1
                            eo.wait_ge(dsem[q][sl], 16 * last)

            @block.tensor
            def _(x):
                run("pe")

            @block.scalar
            def _(x):
                run("act")

            @block.vector
            def _(x):
                run("dve")

            @block.gpsimd
            def _(x):
                run("pool")

            @block.sync
            def _(x):
                run("sp")
        return cnt


class Interleaver:
    def __init__(self, gens, width, slotted=False):
        self.it = iter(gens)
        self.width = width
        self.slotted = slotted
        self.active = []
        self.free = list(range(width))

    def step(self):
        while len(self.active) < self.width:
            try:
                g = next(self.it)
            except StopIteration:
                break
            if self.slotted:
                sl = self.free.pop(0)
                self.active.append((g(sl), sl))
            else:
                self.active.append((g, None))
        if not self.active:
            return False
        for item in list(self.active):
            try:
                next(item[0])
            except StopIteration:
                self.active.remove(item)
                if self.slotted:
                    self.free.append(item[1])
        return True


def run_interleaved(gens, width, slotted=False):
    il = Interleaver(gens, width, slotted)
    while il.step():
        pass


class Tl:
    def __init__(self, name, ap):
        self.name = name
        self.ap = ap

    def k(self, sub=None):
        return (self.name, sub)

    def __getitem__(self, idx):
        return self.ap[idx]


ARENA_COLS = 52000


class K:
    def __init__(self, nc, ext_in=(), ext_out=()):
        self.nc = nc
        self.S = Sched(nc)
        self.st = contextlib.ExitStack()
        self.big = self.st.enter_context(nc.sbuf_tensor("arena", [128, ARENA_COLS], F32))
        self.off = 0
        self.live = []
        self.ext_in = set(ext_in)
        self.ext_out = set(ext_out)
        self.drams = {}
        self.uid = 0
        self.PS = [Tl(f"ps{i}", self.st.enter_context(nc.psum_tensor(f"ps{i}", [128, 512], F32))[:, :]) for i in range(8)]

    def alloc(self, name, cols, dt=F32):
        size = 4 if dt == F32 else 2
        n32 = (cols * size + 3) // 4
        n32 = (n32 + 7) // 8 * 8
        assert self.off + n32 <= ARENA_COLS, f"SBUF arena overflow at {name}: {self.off}+{n32}"
        ap = self.big[:, self.off:self.off + n32]
        if dt != F32:
            ap = ap.bitcast(dt)[:, :cols]
        else:
            ap = ap[:, :cols]
        self.off += n32
        self.uid += 1
        t = Tl(f"{name}#{self.uid}", ap)
        self.live.append(t.name)
        return t

    def mark(self):
        return (self.off, len(self.live))

    def release(self, mark):
        off, n = mark
        self.S.retire(self.live[n:])
        del self.live[n:]
        self.off = off

    def dram(self, name, shape, dt):
        if name in self.drams:
            return self.drams[name]
        if name in self.ext_in:
            t = self.nc.dram_tensor(name, list(shape), dt, kind="ExternalInput")
        elif name in self.ext_out:
            t = self.nc.dram_tensor(name, list(shape), dt, kind="ExternalOutput")
        else:
            t = self.nc.dram_tensor(name, list(shape), dt)
        self.drams[name] = t.ap()
        return self.drams[name]

    def _op(self, eng, name, a, kw):
        r = kw.pop("r", ())
        w = kw.pop("w", ())
        w = list(w) + [key for key in r if key[0].startswith("ps") and key not in w]
        obj = {"pe": self.nc.tensor, "act": self.nc.scalar, "dve": self.nc.vector, "pool": self.nc.gpsimd}[eng]
        return self.S.add(eng, functools.partial(getattr(obj, name), *a, **kw), r, w)

    def pe(self, name, *a, **kw):
        return self._op("pe", name, a, kw)

    def act(self, name, *a, **kw):
        return self._op("act", name, a, kw)

    def dve(self, name, *a, **kw):
        return self._op("dve", name, a, kw)

    def pool(self, name, *a, **kw):
        return self._op("pool", name, a, kw)

    def any(self, eng, name, *a, **kw):
        return self._op(eng, name, a, kw)

    def dma(self, out, in_, r=(), w=(), q="sp"):
        nc = self.nc
        if q == "sp":
            return self.S.add("sp", functools.partial(nc.sync.dma_start, out=out, in_=in_), r, w, dma=True)
        return self.S.add("pool", functools.partial(nc.gpsimd.dma_start, out=out, in_=in_), r, w, dma=True)

    def veng(self, eng):
        return {"dve": self.nc.vector, "pool": self.nc.gpsimd}[eng]


def load_consts(k):
    nc = k.nc
    identd = k.dram("c_ident", [128, 128], F32)
    c = {}
    c["identf"] = k.alloc("identf", 128, F32)
    c["identb"] = k.alloc("identb", 128, BF16)
    k.dma(c["identf"][:, :], identd, r=[("dram", "c_ident")], w=[c["identf"].k()])
    k.dma(c["identb"][:, :], identd, r=[("dram", "c_ident")], w=[c["identb"].k()], q="pool")
    return c


def prep_rows(k, c, xrows_ap, xkey, Amod, Bmod, hT, col0, nrows, tmp, ps_t, idx):
    nc = k.nc
    xt, junk, ss, t1, hb = tmp
    n = nrows
    k.dma(xt[:n, :], xrows_ap, r=[xkey], w=[xt.k()])
    k.act("activation", out=junk[:n, :], in_=xt[:n, :], func=AF.Square, accum_out=ss[:n, 0:1],
          r=[xt.k()], w=[junk.k(), ss.k()])
    k.act("activation", out=ss[:n, 1:2], in_=ss[:n, 0:1], func=AF.Sqrt, scale=1.0 / D, bias=NORM_EPS,
          r=[ss.k()], w=[ss.k()])
    k.dve("reciprocal", out=ss[:n, 2:3], in_=ss[:n, 1:2], r=[ss.k()], w=[ss.k()])
    k.dve("scalar_tensor_tensor", out=t1[:n, :], in0=xt[:n, :], scalar=ss[:n, 2:3], in1=Amod[:n, :],
                                                 op0=ALU.mult, op1=ALU.mult,
          r=[xt.k(), ss.k(), Amod.k()], w=[t1.k()])
    k.pool("tensor_tensor", out=hb[:n, :], in0=t1[:n, :], in1=Bmod[:n, :], op=ALU.add,
           r=[t1.k(), Bmod.k()], w=[hb.k()])
    pT = ps_t.ap.bitcast(BF16)
    for kc in range(8):
        k.pe("transpose", pT[:, kc * 128:kc * 128 + n], hb[:n, kc * 128:(kc + 1) * 128], c["identb"][:n, :n],
             r=[hb.k(), c["identb"].k()], w=[ps_t.k()])
    src = pT.rearrange("p (a b) -> p a b", a=8)[:, :, :n]
    dst = hT[:, :].rearrange("p (a b) -> p a b", a=8)[:, :, col0:col0 + n]
    if idx % 2 == 0:
        k.act("copy", out=dst, in_=src, r=[ps_t.k()], w=[hT.k()])
    else:
        k.dve("tensor_copy", out=dst, in_=src, r=[ps_t.k()], w=[hT.k()])


def prep_rows_gen(k, c, xrows_ap, xkey, Amod, Bmod, hT, col0, nrows, tmp, ps_t, idx):
    nc = k.nc
    xt, junk, ss, t1, hb = tmp
    n = nrows
    k.dma(xt[:n, :], xrows_ap, r=[xkey], w=[xt.k()])
    yield
    k.act("activation", out=junk[:n, :], in_=xt[:n, :], func=AF.Square, accum_out=ss[:n, 0:1],
          r=[xt.k()], w=[junk.k(), ss.k()])
    yield
    k.act("activation", out=ss[:n, 1:2], in_=ss[:n, 0:1], func=AF.Sqrt, scale=1.0 / D, bias=NORM_EPS,
          r=[ss.k()], w=[ss.k()])
    yield
    k.dve("reciprocal", out=ss[:n, 2:3], in_=ss[:n, 1:2], r=[ss.k()], w=[ss.k()])
    yield
    k.dve("scalar_tensor_tensor", out=t1[:n, :], in0=xt[:n, :], scalar=ss[:n, 2:3], in1=Amod[:n, :],
                                                 op0=ALU.mult, op1=ALU.mult,
          r=[xt.k(), ss.k(), Amod.k()], w=[t1.k()])
    yield
    k.pool("tensor_tensor", out=hb[:n, :], in0=t1[:n, :], in1=Bmod[:n, :], op=ALU.add,
           r=[t1.k(), Bmod.k()], w=[hb.k()])
    yield
    pT = ps_t.ap.bitcast(BF16)
    for kc in range(8):
        k.pe("transpose", pT[:, kc * 128:kc * 128 + n], hb[:n, kc * 128:(kc + 1) * 128], c["identb"][:n, :n],
             r=[hb.k(), c["identb"].k()], w=[ps_t.k()])
    yield
    src = pT.rearrange("p (a b) -> p a b", a=8)[:, :, :n]
    dst = hT[:, :].rearrange("p (a b) -> p a b", a=8)[:, :, col0:col0 + n]
    if idx % 2 == 0:
        k.act("copy", out=dst, in_=src, r=[ps_t.k()], w=[hT.k()])
        yield
    else:
        k.dve("tensor_copy", out=dst, in_=src, r=[ps_t.k()], w=[hT.k()])
        yield


def alloc_prep_tmp(k, tag):
    xt = k.alloc(f"xt{tag}", 1024, F32)
    junk = k.alloc(f"junk{tag}", 1024, BF16)
    ss = k.alloc(f"ss{tag}", 8, F32)
    t1 = k.alloc(f"t1{tag}", 1024, F32)
    hb = k.alloc(f"hb{tag}", 1024, BF16)
    return (xt, junk, ss, t1, hb)


PV_BIN = 0
PV_DWB = 24
PV_LNG = 32
PV_LNB = 40
PV_DWW = 48
PV_C5W = 48 + 248
PV_N = PV_C5W + 60
U1PAD = 15
U1W = T + 2 * U1PAD


def l1_pass_a(k, c):
    nc = k.nc
    m0 = k.mark()
    x1 = k.dram("x1", [T, D], F32)
    modd = k.dram("mod", [8, 128, D], F32)
    w_in_d = k.dram("od_w_in_r", [128, 8 * 3072], F32)
    pvd = k.dram("pv", [128, PV_N], F32)
    U1 = k.dram("U1", [8, 128, U1W], BF16)
    ZG1 = k.dram("ZG1", [8, 128, T], BF16)

    w_in = k.alloc("w_in1", 8 * 3072, BF16)
    for kc in range(8):
        k.dma(w_in[:, kc * 3072:(kc + 1) * 3072], w_in_d[:, kc * 3072:(kc + 1) * 3072], r=[("dram", "od_w_in_r")], w=[w_in.k(kc)], q="pool")
    pv = k.alloc("pv", PV_N, F32)
    k.dma(pv[:, :], pvd, r=[("dram", "pv")], w=[pv.k()])
    A1 = k.alloc("A1", D, F32)
    B1 = k.alloc("B1", D, F32)
    k.dma(A1[:, :], modd[5], r=[("dram", "mod")], w=[A1.k()])
    k.dma(B1[:, :], modd[6], r=[("dram", "mod")], w=[B1.k()])
    zt = k.alloc("zt", 16, BF16)
    k.dve("memset", zt[:, :], 0.0, w=[zt.k()])
    for j in range(8):
        k.dma(U1[j][:, 0:U1PAD], zt[:, 0:U1PAD], r=[zt.k()], w=[("dram", "U1", j, "padl")])
        k.dma(U1[j][:, U1PAD + T:U1W], zt[:, 0:U1PAD], r=[zt.k()], w=[("dram", "U1", j, "padr")])
    tmps = [alloc_prep_tmp(k, i) for i in range(2)]
    hTs = [k.alloc(f"hT{i}", 8 * 512, BF16) for i in range(2)]
    sg = [k.alloc(f"sg{i}", 512, F32) for i in range(2)]
    ub = [k.alloc(f"ub{i}", 512, BF16) for i in range(3)]
    zb = [k.alloc(f"zb{i}", 512, BF16) for i in range(3)]
    ps_t = [k.PS[0], k.PS[1]]
    ps_mm = [k.PS[2], k.PS[3], k.PS[4], k.PS[5], k.PS[6], k.PS[7]]
    nmt = T // 512

    def prep_m(m):
        hT = hTs[m % 2]

        def sub(s, slot):
            r0 = m * 512 + s * 128
            return prep_rows_gen(k, c, x1[r0:r0 + 128, :], ("dram", "x1", r0 // 128), A1, B1, hT, s * 128, 128, tmps[slot], ps_t[slot], 4 * m + s)
        il = Interleaver([functools.partial(sub, s) for s in range(4)], 2, slotted=True)
        while il.step():
            yield

    def comp_m(m):
        hT = hTs[m % 2]
        hT3 = hT[:, :].rearrange("p (a b) -> p a b", a=8)

        def mmgroup(pst, col0):
            for kc in range(8):
                k.pe("matmul", pst[:, :], lhsT=w_in[:, kc * 3072 + col0:kc * 3072 + col0 + 128], rhs=hT3[:, kc, :],
                     start=(kc == 0), stop=(kc == 7), r=[w_in.k(kc), hT.k()], w=[pst.k()])

        for j in range(8):
            pa = ps_mm[(2 * j) % 4]
            pg = ps_mm[(2 * j + 1) % 4]
            pz = ps_mm[4 + j % 2]
            mmgroup(pa, j * 128)
            yield
            mmgroup(pg, 1024 + j * 128)
            yield
            mmgroup(pz, 2048 + j * 128)
            yield
            sgt = sg[j % 2]
            ubt = ub[j % 3]
            zbt = zb[j % 3]
            k.act("activation", out=sgt[:, :], in_=pg[:, :], func=AF.Sigmoid, bias=pv[:, PV_BIN + 8 + j:PV_BIN + 9 + j], scale=1.0,
                  r=[pg.k(), pv.k()], w=[sgt.k()])
            yield
            k.dve("scalar_tensor_tensor", out=ubt[:, :], in0=pa[:, :], scalar=pv[:, PV_BIN + j:PV_BIN + j + 1], in1=sgt[:, :],
                  op0=ALU.add, op1=ALU.mult, r=[pa.k(), sgt.k(), pv.k()], w=[ubt.k()])
            k.dma(U1[j][:, U1PAD + m * 512:U1PAD + (m + 1) * 512], ubt[:, :], r=[ubt.k()], w=[("dram", "U1", j, m)])
            yield
            k.act("activation", out=zbt[:, :], in_=pz[:, :], func=AF.Silu, bias=pv[:, PV_BIN + 16 + j:PV_BIN + 17 + j], scale=1.0,
                  r=[pz.k(), pv.k()], w=[zbt.k()])
            k.dma(ZG1[j][:, m * 512:(m + 1) * 512], zbt[:, :], r=[zbt.k()], w=[("dram", "ZG1", j, m)])
            yield

    for _ in prep_m(0):
        pass
    for m in range(nmt):
        gl = [comp_m(m)] + ([prep_m(m + 1)] if m + 1 < nmt else [])
        run_interleaved(gl, 2)
    k.release(m0)


def l1_pass_b(k, c):
    nc = k.nc
    m0 = k.mark()
    x1 = k.dram("x1", [T, D], F32)
    y = k.dram("y", [T, D], F32)
    modd = k.dram("mod", [8, 128, D], F32)
    w_out_d = k.dram("od_w_out_r", [128, 8 * 1024], F32)
    pvd = k.dram("pv", [128, PV_N], F32)
    U1 = k.dram("U1", [8, 128, U1W], BF16)
    ZG1 = k.dram("ZG1", [8, 128, T], BF16)

    w_out = k.alloc("w_out1", 8 * 1024, BF16)
    k.dma(w_out[:, :], w_out_d, r=[("dram", "od_w_out_r")], w=[w_out.k()], q="pool")
    pv = k.alloc("pv", PV_N, F32)
    k.dma(pv[:, :], pvd, r=[("dram", "pv")], w=[pv.k()])
    G1 = k.alloc("G1", D, F32)
    k.dma(G1[:, :], modd[7], r=[("dram", "mod")], w=[G1.k()])
    BOUT = k.alloc("BOUT", D, F32)
    k.dma(BOUT[:, :], row_bc(k, "b_out"), r=[("dram", "rows")], w=[BOUT.k()])
    onesm = k.alloc("onesm", 128, BF16)
    k.dve("memset", onesm[:, :], 1.0 / 1024.0, w=[onesm.k()])
    DG = k.alloc("DG", 8 * 31 * 128, BF16)
    for j in range(8):
        for t in range(31):
            i = j * 31 + t
            eng = "dve" if i % 2 == 0 else "pool"
            ve = k.veng(eng)
            k.any(eng, "tensor_scalar", out=DG[:, i * 128:(i + 1) * 128], in0=c["identf"][:, :],
                                                           scalar1=pv[:, PV_DWW + i:PV_DWW + i + 1], scalar2=None, op0=ALU.mult,
                  r=[c["identf"].k(), pv.k()], w=[DG.k(j)])
    PW = 512 + 2 * U1PAD
    PRE = [k.alloc(f"PRE{i}", 8 * PW, BF16) for i in range(2)]
    ZGt = [k.alloc(f"ZGt{i}", 8 * 512, BF16) for i in range(2)]
    UC = k.alloc("UC", 8 * 512, F32)
    UCb = k.alloc("UCb", 8 * 512, BF16)
    SQ = k.alloc("SQ", 8 * 512, BF16)
    MEAN = k.alloc("MEAN", 512, F32)
    M2 = k.alloc("M2", 512, F32)
    RSTD = k.alloc("RSTD", 512, F32)
    TA = [k.alloc(f"TA{i}", 512, F32) for i in range(4)]
    TB = TA
    YS = [k.alloc(f"YS{i}", 512, BF16) for i in range(4)]
    YT = k.alloc("YT", 8 * 512, BF16)
    O = [k.alloc(f"O{i}", D, F32) for i in range(2)]
    junk = k.alloc("junkb", D, BF16)
    ss = [k.alloc(f"ssb{i}", 8, F32) for i in range(2)]
    XR = [k.alloc(f"XR{i}", D, F32) for i in range(2)]
    T2 = O
    OUT = XR
    ps_cv = [k.PS[0], k.PS[1]]
    ps_mean, ps_msq = k.PS[2], k.PS[3]
    ps_o = [(k.PS[4], k.PS[5]), (k.PS[6], k.PS[7])]
    nmt = T // 512

    def phase1(m):
        pre = PRE[m % 2]
        zgt = ZGt[m % 2]
        for j in range(8):
            k.dma(pre[:, j * PW:(j + 1) * PW], U1[j][:, m * 512:m * 512 + PW],
                  r=[("dram", "U1", j, mm) for mm in range(max(0, m - 1), min(nmt, m + 2))] + [("dram", "U1", j, "padl"), ("dram", "U1", j, "padr")],
                  w=[pre.k(j)])
            k.dma(zgt[:, j * 512:(j + 1) * 512], ZG1[j][:, m * 512:(m + 1) * 512], r=[("dram", "ZG1", j, m)], w=[zgt.k(j)])
        yield
        for j in range(8):
            pcv = ps_cv[j % 2]
            for t in range(31):
                i = j * 31 + t
                k.pe("matmul", pcv[:, :], lhsT=DG[:, i * 128:(i + 1) * 128], rhs=pre[:, j * PW + t:j * PW + t + 512], start=(t == 0), stop=(t == 30),
                     r=[DG.k(j), pre.k(j)], w=[pcv.k()])
                if t % 8 == 7:
                    yield
            yield
            k.act("activation", out=UC[:, j * 512:(j + 1) * 512], in_=pcv[:, :], func=AF.Identity, bias=pv[:, PV_DWB + j:PV_DWB + j + 1], scale=1.0,
                  r=[pcv.k(), pv.k()], w=[UC.k(j)])
            yield
            k.act("activation", out=SQ[:, j * 512:(j + 1) * 512], in_=pcv[:, :], func=AF.Square, bias=pv[:, PV_DWB + j:PV_DWB + j + 1], scale=1.0,
                  r=[pcv.k(), pv.k()], w=[SQ.k(j)])
            yield
            k.pool("tensor_copy", out=UCb[:, j * 512:(j + 1) * 512], in_=UC[:, j * 512:(j + 1) * 512], r=[UC.k(j)], w=[UCb.k(j)])
            yield

    def phase234(m):
        zgt = ZGt[m % 2]
        for j in range(8):
            k.pe("matmul", ps_mean[:, :], lhsT=onesm[:, :], rhs=UCb[:, j * 512:(j + 1) * 512], start=(j == 0), stop=(j == 7),
                 r=[onesm.k(), UCb.k(j)], w=[ps_mean.k()])
        for j in range(8):
            k.pe("matmul", ps_msq[:, :], lhsT=onesm[:, :], rhs=SQ[:, j * 512:(j + 1) * 512], start=(j == 0), stop=(j == 7),
                 r=[onesm.k(), SQ.k(j)], w=[ps_msq.k()])
        k.act("copy", out=MEAN[:, :], in_=ps_mean[:, :], r=[ps_mean.k()], w=[MEAN.k()])
        k.dve("tensor_tensor", out=M2[:, :], in0=MEAN[:, :], in1=MEAN[:, :], op=ALU.mult, r=[MEAN.k()], w=[M2.k()])
        k.dve("tensor_tensor", out=M2[:, :], in0=ps_msq[:, :], in1=M2[:, :], op=ALU.subtract, r=[ps_msq.k(), M2.k()], w=[M2.k()])
        k.act("activation", out=RSTD[:, :], in_=M2[:, :], func=AF.Sqrt, bias=1e-5, scale=1.0, r=[M2.k()], w=[RSTD.k()])
        k.dve("reciprocal", out=RSTD[:, :], in_=RSTD[:, :], r=[RSTD.k()], w=[RSTD.k()])

        def ln_chunk(j, sl):
            ta, tb, ys = TA[sl], TB[sl], YS[sl]
            k.dve("tensor_tensor", out=ta[:, :], in0=UC[:, j * 512:(j + 1) * 512], in1=MEAN[:, :], op=ALU.subtract,
                  r=[UC.k(j), MEAN.k()], w=[ta.k()])
            yield
            k.pool("tensor_tensor", out=tb[:, :], in0=ta[:, :], in1=RSTD[:, :], op=ALU.mult, r=[ta.k(), RSTD.k()], w=[tb.k()])
            yield
            k.act("activation", out=ys[:, :], in_=tb[:, :], func=AF.Silu,
                  scale=pv[:, PV_LNG + j:PV_LNG + j + 1], bias=pv[:, PV_LNB + j:PV_LNB + j + 1], r=[tb.k(), pv.k()], w=[ys.k()])
            yield
            k.dve("tensor_tensor", out=YT[:, j * 512:(j + 1) * 512], in0=ys[:, :], in1=zgt[:, j * 512:(j + 1) * 512], op=ALU.mult,
                  r=[ys.k(), zgt.k(j)], w=[YT.k(j)])
            yield
        run_interleaved([functools.partial(ln_chunk, j) for j in range(8)], 4, slotted=True)

    def phase5(m):
        for s in range(4):
            oc = 4 * m + s
            r0 = m * 512 + s * 128
            po = ps_o[oc % 2]
            o, sst, xr, t2, out = O[oc % 2], ss[oc % 2], XR[oc % 2], T2[oc % 2], OUT[oc % 2]
            k.dma(xr[:, :], x1[r0:r0 + 128, :], r=[("dram", "x1", r0 // 128)], w=[xr.k()])
            for nh in range(2):
                for j in range(8):
                    k.pe("matmul", po[nh][:, :], lhsT=YT[:, j * 512 + s * 128:j * 512 + (s + 1) * 128], rhs=w_out[:, j * 1024 + nh * 512:j * 1024 + (nh + 1) * 512],
                         start=(j == 0), stop=(j == 7), r=[YT.k(j), w_out.k()], w=[po[nh].k()])
                yield
            for nh in range(2):
                k.dve("tensor_tensor", out=o[:, nh * 512:(nh + 1) * 512], in0=po[nh][:, :], in1=BOUT[:, nh * 512:(nh + 1) * 512], op=ALU.add,
                      r=[po[nh].k(), BOUT.k()], w=[o.k()])
                yield
            yield from post_res_gen(k, o, sst, junk, G1, xr, t2, out, y[r0:r0 + 128, :], ("dram", "y", r0 // 128))

    for _ in phase1(0):
        pass
    for m in range(nmt):
        phase234(m)
        gl = [phase5(m)] + ([phase1(m + 1)] if m + 1 < nmt else [])
        run_interleaved(gl, 2)
    k.release(m0)


def post_res(k, o, sst, junk, G, xr, t2, out, ydst, ykey):
    nc = k.nc
    k.act("activation", out=junk[:, :], in_=o[:, :], func=AF.Square, accum_out=sst[:, 0:1],
          r=[o.k()], w=[junk.k(), sst.k()])
    k.act("activation", out=sst[:, 1:2], in_=sst[:, 0:1], func=AF.Sqrt, scale=1.0 / D, bias=NORM_EPS,
          r=[sst.k()], w=[sst.k()])
    k.dve("reciprocal", out=sst[:, 2:3], in_=sst[:, 1:2], r=[sst.k()], w=[sst.k()])
    k.dve("scalar_tensor_tensor", out=t2[:, :], in0=o[:, :], scalar=sst[:, 2:3], in1=G[:, :], op0=ALU.mult, op1=ALU.mult,
          r=[o.k(), sst.k(), G.k()], w=[t2.k()])
    k.pool("tensor_tensor", out=out[:, :], in0=t2[:, :], in1=xr[:, :], op=ALU.add, r=[t2.k(), xr.k()], w=[out.k()])
    k.dma(ydst, out[:, :], r=[out.k()], w=[ykey])


def post_res_gen(k, o, sst, junk, G, xr, t2, out, ydst, ykey):
    nc = k.nc
    k.act("activation", out=junk[:, :], in_=o[:, :], func=AF.Square, accum_out=sst[:, 0:1],
          r=[o.k()], w=[junk.k(), sst.k()])
    yield
    k.act("activation", out=sst[:, 1:2], in_=sst[:, 0:1], func=AF.Sqrt, scale=1.0 / D, bias=NORM_EPS,
          r=[sst.k()], w=[sst.k()])
    yield
    k.dve("reciprocal", out=sst[:, 2:3], in_=sst[:, 1:2], r=[sst.k()], w=[sst.k()])
    yield
    k.dve("scalar_tensor_tensor", out=t2[:, :], in0=o[:, :], scalar=sst[:, 2:3], in1=G[:, :], op0=ALU.mult, op1=ALU.mult,
          r=[o.k(), sst.k(), G.k()], w=[t2.k()])
    yield
    k.pool("tensor_tensor", out=out[:, :], in0=t2[:, :], in1=xr[:, :], op=ALU.add, r=[t2.k(), xr.k()], w=[out.k()])
    yield
    k.dma(ydst, out[:, :], r=[out.k()], w=[ykey])
    yield


def build_program(passes, ext_in, ext_out):
    nc = bass.Bass("TRN2", target_bir_lowering=False)
    k = K(nc, ext_in, ext_out)
    c = load_consts(k)
    for p in passes:
        p(k, c)
    cnt = k.S.emit()
    k.st.close()
    return nc, cnt


ROW_OFF = {}
_o = 0
for _n, _l in [("ada_b0", 3072), ("ada_b1", 3072), ("pre_g0", 1024), ("pre_g1", 1024), ("post_g0", 1024), ("post_g1", 1024),
               ("b_out", 1024), ("gdn_g", 128), ("qn_g", 128), ("kn_g", 128), ("a_log", 8), ("dt_bias", 8)]:
    ROW_OFF[_n] = (_o, _l)
    _o += _l
ROWS_N = _o


def row_bc(k, name, off=0, n=None):
    rows = k.dram("rows", [1, ROWS_N], F32)
    o, l = ROW_OFF[name]
    if n is None:
        n = l
    return rows[0, o + off:o + off + n].partition_broadcast(128)


def p0_mod(k, c):
    nc = k.nc
    m0 = k.mark()
    modd = k.dram("mod", [8, 128, D], F32)
    cvec = k.dram("cvec", [128, 16], F32)
    seld = k.dram("c_sel", [2, 256], F32)
    adaw = k.dram("ada_w_r", [2, 128, 8 * 3072], F32)
    cv = k.alloc("cv", 16, F32)
    k.dma(cv[:, :], cvec, r=[("dram", "cvec")], w=[cv.k()])
    scb = k.alloc("scb", 16, BF16)
    k.act("activation", out=scb[:, :], in_=cv[:, :], func=AF.Silu, r=[cv.k()], w=[scb.k()])
    sel = k.alloc("sel", 256, F32)
    k.dma(sel[0:2, :], seld, r=[("dram", "c_sel")], w=[sel.k()])
    mrow = k.alloc("mrow", 6144, F32)
    aw = [k.alloc(f"aw{l}", 8 * 3072, BF16) for l in range(2)]
    for l in range(2):
        for kc in range(8):
            k.dma(aw[l][:, kc * 3072:(kc + 1) * 3072], adaw[l][:, kc * 3072:(kc + 1) * 3072], r=[("dram", "ada_w_r")], w=[aw[l].k(kc)], q="pool")
    adab = [k.alloc(f"adab{l}", 3072, F32) for l in range(2)]
    preg = [k.alloc(f"preg{l}", 1024, F32) for l in range(2)]
    postg = [k.alloc(f"postg{l}", 1024, F32) for l in range(2)]
    for l in range(2):
        k.dma(adab[l][:, :], row_bc(k, f"ada_b{l}"), r=[("dram", "rows")], w=[adab[l].k()])
        k.dma(preg[l][:, :], row_bc(k, f"pre_g{l}"), r=[("dram", "rows")], w=[preg[l].k()])
        k.dma(postg[l][:, :], row_bc(k, f"post_g{l}"), r=[("dram", "rows")], w=[postg[l].k()])
    for l in range(2):
        for nt in range(6):
            ps = k.PS[nt % 2]
            for kc in range(8):
                k.pe("matmul", ps[0:2, :], lhsT=scb[:, 2 * kc:2 * kc + 2], rhs=aw[l][:, kc * 3072 + nt * 512:kc * 3072 + (nt + 1) * 512],
                     start=(kc == 0), stop=(kc == 7), r=[scb.k(), aw[l].k(kc)], w=[ps.k()])
            k.act("copy", out=mrow[0:2, l * 3072 + nt * 512:l * 3072 + (nt + 1) * 512], in_=ps[0:2, :], r=[ps.k()], w=[mrow.k((l, nt))])
    tmp = [k.alloc(f"mt{i}", 512, F32) for i in range(2)]
    outt = [k.alloc(f"mo{i}", 1024, F32) for i in range(2)]
    plan = [(0, 0, 1, 0), (1, 0, 0, 0), (2, 0, 2, 0), (3, 0, 1, 1), (4, 0, 0, 1), (5, 1, 1, 0), (6, 1, 0, 0), (7, 1, 2, 0)]
    cnt = 0
    for (mi, l, part, si) in plan:
        ot = outt[mi % 2]
        for nh in range(2):
            ps = k.PS[2 + cnt % 2]
            tt = tmp[cnt % 2]
            cnt += 1
            seg = l * 3072 + part * 1024 + nh * 512
            nt = (part * 1024 + nh * 512) // 512
            k.pe("matmul", ps[:, :], lhsT=sel[0:2, si * 128:(si + 1) * 128], rhs=mrow[0:2, seg:seg + 512], start=True, stop=True,
                 r=[sel.k(), mrow.k((l, nt))], w=[ps.k()])
            ab_ = adab[l][:, part * 1024 + nh * 512:part * 1024 + (nh + 1) * 512]
            osl = ot[:, nh * 512:(nh + 1) * 512]
            if part == 0:
                k.dve("tensor_tensor", out=osl, in0=ps[:, :], in1=ab_, op=ALU.add, r=[ps.k(), adab[l].k()], w=[ot.k()])
            elif part == 1:
                k.dve("scalar_tensor_tensor", out=tt[:, :], in0=ps[:, :], scalar=1.0, in1=ab_, op0=ALU.add, op1=ALU.add,
                      r=[ps.k(), adab[l].k()], w=[tt.k()])
                k.pool("tensor_tensor", out=osl, in0=tt[:, :], in1=preg[l][:, nh * 512:(nh + 1) * 512], op=ALU.mult,
                       r=[tt.k(), preg[l].k()], w=[ot.k()])
            else:
                k.dve("tensor_tensor", out=tt[:, :], in0=ps[:, :], in1=ab_, op=ALU.add, r=[ps.k(), adab[l].k()], w=[tt.k()])
                k.pool("tensor_tensor", out=osl, in0=tt[:, :], in1=postg[l][:, nh * 512:(nh + 1) * 512], op=ALU.mult,
                       r=[tt.k(), postg[l].k()], w=[ot.k()])
        k.dma(modd[mi], ot[:, :], r=[ot.k()], w=[("dram", "mod")])
    k.release(m0)


NTILE = 34
TK = T + CTX
GP_CTX0 = 2
GP_LAT0 = 2 + CTX + 2 + 2
GP_W = GP_LAT0 + T + 2
C_QKV, C_ZA, C_AB, C_QB, C_KB, C_VB, C_ZB, EVEN_IN = 0, 1536, 2048, 2064, 2576, 2832, 3088, 3600


P1_FLAGS = {"pads": True, "tm": True, "fm": True, "qk": True, "ab": True, "norm": True, "tr": True}


def p1_inproj(k, c):
    nc = k.nc
    m0 = k.mark()
    x = k.dram("x", [T, D], F32)
    ctxd = k.dram("ctx", [CTX, D], F32)
    modd = k.dram("mod", [8, 128, D], F32)
    w_in_d = k.dram("ev_w_in_r", [128, 8 * EVEN_IN], F32)
    ropecs = k.dram("rope_cs", [32, 128, 128], F32)
    ropesn = k.dram("rope_sn", [32, 128, 128], F32)
    QT = k.dram("QT", [4, 128, T], BF16)
    KT = k.dram("KT", [2, 128, TK], BF16)
    V = k.dram("V", [NTILE, 128, 256], BF16)
    ZBT = k.dram("ZBT", [4, 128, T], BF16)
    ZA = k.dram("ZA", [T, 512], BF16)
    GPRE = k.dram("GPRE", [12, 128, GP_W], BF16)
    AB = k.dram("AB", [NTILE, 128, 16], F32)

    w_in = k.alloc("w_in0", 8 * EVEN_IN, BF16)
    for kc in range(8):
        k.dma(w_in[:, kc * EVEN_IN:(kc + 1) * EVEN_IN], w_in_d[:, kc * EVEN_IN:(kc + 1) * EVEN_IN], r=[("dram", "ev_w_in_r")], w=[w_in.k(kc)], q="pool")
    Am = k.alloc("Am", D, F32)
    Bm = k.alloc("Bm", D, F32)
    G6 = k.alloc("G6", 768, F32)
    for h in range(4):
        k.dma(G6[:, h * 128:(h + 1) * 128], row_bc(k, "qn_g"), r=[("dram", "rows")], w=[G6.k()])
    for h in range(2):
        k.dma(G6[:, 512 + h * 128:512 + (h + 1) * 128], row_bc(k, "kn_g"), r=[("dram", "rows")], w=[G6.k()])
    zt = k.alloc("zt", 16, BF16)
    k.dve("memset", zt[:, :], 0.0, w=[zt.k()])
    for j in range(12 if P1_FLAGS["pads"] else 0):
        k.dma(GPRE[j][:, 0:2], zt[:, 0:2], r=[zt.k()], w=[("dram", "GPRE", j, "p0")])
        k.dma(GPRE[j][:, 2 + CTX:GP_LAT0], zt[:, 0:4], r=[zt.k()], w=[("dram", "GPRE", j, "p1")])
        k.dma(GPRE[j][:, GP_LAT0 + T:GP_W], zt[:, 0:2], r=[zt.k()], w=[("dram", "GPRE", j, "p2")])
    tmps = [alloc_prep_tmp(k, i) for i in range(2)]
    hTs = [k.alloc(f"hT{i}", 8 * 512, BF16) for i in range(2)]
    CS = [k.alloc(f"CS{i}", 128, F32) for i in range(2)]
    SN = [k.alloc(f"SN{i}", 128, F32) for i in range(2)]
    QK = [k.alloc(f"QK{i}", 768, F32) for i in range(2)]
    SQ = [k.alloc(f"SQ{i}", 768, F32) for i in range(2)]
    ssq = [k.alloc(f"ssq{i}", 24, F32) for i in range(2)]
    QN = [k.alloc(f"QN{i}", 768, F32) for i in range(2)]
    R1 = [k.alloc(f"R1{i}", 768, F32) for i in range(2)]
    R2 = [k.alloc(f"R2{i}", 768, F32) for i in range(2)]
    QKb = [k.alloc(f"QKb{i}", 768, BF16) for i in range(2)]
    zas = [k.alloc(f"zas{i}", 512, BF16) for i in range(2)]
    vs = [k.alloc(f"vs{i}", 256, BF16) for i in range(2)]
    abs_ = [k.alloc(f"abs{i}", 16, F32) for i in range(2)]
    QTs = [k.alloc(f"QTs{i}", 4 * 512, BF16) for i in range(2)]
    KTs = [k.alloc(f"KTs{i}", 2 * 512, BF16) for i in range(2)]
    gps = [k.alloc(f"gps{i}", 512, BF16) for i in range(3)]
    zbs = [k.alloc(f"zbs{i}", 512, BF16) for i in range(3)]
    ps_t = [k.PS[0], k.PS[0]]
    ps_za, ps_q, ps_kv = k.PS[2], k.PS[3], k.PS[4]
    PSX = [k.PS[5], k.PS[1]]
    ps_f = [k.PS[6], k.PS[7]]
    cnt = 0
    fcnt = 0
    mts = [("ctx", 0, 256)] + [("lat", m * 512, 512) for m in range(T // 512)]
    mt_list = mts[:P1_FLAGS.get("nmt", 9)]

    def prep_m(mi):
        kind, t0, W = mt_list[mi]
        lat = kind == "lat"
        if mi == 0:
            k.dma(Am[:, :], modd[3], r=[("dram", "mod")], w=[Am.k()])
            k.dma(Bm[:, :], modd[4], r=[("dram", "mod")], w=[Bm.k()])
        elif mi == 1:
            k.dma(Am[:, :], modd[0], r=[("dram", "mod")], w=[Am.k()])
            k.dma(Bm[:, :], modd[1], r=[("dram", "mod")], w=[Bm.k()])
        hT = hTs[mi % 2]
        cnt0 = 0 if mi == 0 else 2 + 4 * (mi - 1)

        def sub(s, slot):
            r0 = t0 + s * 128
            src = x[r0:r0 + 128, :] if lat else ctxd[r0:r0 + 128, :]
            skey = ("dram", "x" if lat else "ctx", r0 // 128)
            return prep_rows_gen(k, c, src, skey, Am, Bm, hT, s * 128, 128, tmps[slot], ps_t[slot], cnt0 + s)
        il = Interleaver([functools.partial(sub, s) for s in range(W // 128)], 1, slotted=True)
        while il.step():
            yield

    def comp_m(mi):
        kind, t0, W = mt_list[mi]
        lat = kind == "lat"
        hT = hTs[mi % 2]
        hT3 = hT[:, :].rearrange("p (a b) -> p a b", a=8)
        nsub = W // 128
        qts, kts = QTs[mi % 2], KTs[mi % 2]
        fcnt = 0 if mi == 0 else 12 + 16 * (mi - 1)
        def tm_group(s, ps_ap, pskey, col0, n):
            for kc in range(8):
                k.pe("matmul", ps_ap, lhsT=hT3[:, kc, s * 128:(s + 1) * 128], rhs=w_in[:, kc * EVEN_IN + col0:kc * EVEN_IN + col0 + n],
                     start=(kc == 0), stop=(kc == 7), r=[hT.k(), w_in.k(kc)], w=[pskey])

        def stage_a(s):
            sl = s % 2
            r0 = t0 + s * 128
            tile_id = (r0 // 128) if lat else (32 + r0 // 128)
            psx = PSX[sl]
            qk = QK[sl]
            if lat:
                tm_group(s, ps_za[:, :], ps_za.k(), C_ZA, 512)
                zst = zas[sl]
                k.act("activation", out=zst[:, :], in_=ps_za[:, :], func=AF.Silu, r=[ps_za.k()], w=[zst.k()])
                k.dma(ZA[r0:r0 + 128, :], zst[:, :], r=[zst.k()], w=[("dram", "ZA", r0 // 128)])
                yield
                tm_group(s, ps_q[:, :], ps_q.k(), C_QB, 512)
                k.act("copy", out=qk[:, 0:512], in_=ps_q[:, :], r=[ps_q.k()], w=[qk.k()])
                yield
            tm_group(s, psx[:, 496:512], psx.k(), C_AB, 16)
            abst = abs_[sl]
            k.dve("tensor_copy", out=abst[:, :], in_=psx[:, 496:512], r=[psx.k()], w=[abst.k()])
            k.dma(AB[tile_id], abst[:, :], r=[abst.k()], w=[("dram", "AB", tile_id)])
            yield
            tm_group(s, ps_kv[:, :], ps_kv.k(), C_KB, 512)
            vst = vs[sl]
            k.act("copy", out=vst[:, :], in_=ps_kv[:, 256:512], r=[ps_kv.k()], w=[vst.k()])
            k.dma(V[tile_id], vst[:, :], r=[vst.k()], w=[("dram", "V", tile_id)])
            yield
            k.dve("tensor_copy", out=qk[:, 512:768], in_=ps_kv[:, 0:256], r=[ps_kv.k()], w=[qk.k()])
            yield

        def stage_b(s):
            sl = s % 2
            r0 = t0 + s * 128
            psx = PSX[sl]
            pxT = psx.ap.bitcast(BF16)
            qk, ssq_t, qkb, sq_, qn_, r1_, r2_ = QK[sl], ssq[sl], QKb[sl], SQ[sl], QN[sl], R1[sl], R2[sl]
            c0 = 0 if lat else 512
            nh = 6 if lat else 2
            h0 = 0 if lat else 4
            k.pool("tensor_tensor", out=sq_[:, c0:768], in0=qk[:, c0:768], in1=qk[:, c0:768], op=ALU.mult, r=[qk.k()], w=[sq_.k()])
            yield
            k.dve("tensor_reduce", out=ssq_t[:, h0:6], in_=sq_[:, c0:768].rearrange("p (h d) -> p h d", d=128), axis=AX.X, op=ALU.add,
                  r=[sq_.k()], w=[ssq_t.k()])
            yield
            k.act("activation", out=ssq_t[:, 8 + h0:14], in_=ssq_t[:, h0:6], func=AF.Sqrt, scale=1.0 / 128, bias=NORM_EPS,
                  r=[ssq_t.k()], w=[ssq_t.k()])
            yield
            k.dve("reciprocal", out=ssq_t[:, 16 + h0:22], in_=ssq_t[:, 8 + h0:14], r=[ssq_t.k()], w=[ssq_t.k()])
            yield
            k.dve("tensor_tensor", out=qn_[:, c0:768].rearrange("p (h d) -> p h d", d=128), in0=qk[:, c0:768].rearrange("p (h d) -> p h d", d=128),
                  in1=ssq_t[:, 16 + h0:22].unsqueeze(2).to_broadcast([128, nh, 128]), op=ALU.mult, r=[qk.k(), ssq_t.k()], w=[qn_.k()])
            yield
            if not lat:
                k.pool("tensor_tensor", out=qkb[:, c0:768], in0=qn_[:, c0:768], in1=G6[:, c0:768], op=ALU.mult, r=[qn_.k(), G6.k()], w=[qkb.k()])
                yield
            else:
                cs, sn = CS[sl], SN[sl]
                k.dma(cs[:, :], ropecs[r0 // 128], r=[("dram", "rope_cs")], w=[cs.k()])
                k.dma(sn[:, :], ropesn[r0 // 128], r=[("dram", "rope_sn")], w=[sn.k()])
                k.pool("tensor_tensor", out=qn_[:, :], in0=qn_[:, :], in1=G6[:, :], op=ALU.mult, r=[qn_.k(), G6.k()], w=[qn_.k()])
                yield
                k.dve("tensor_tensor", out=r1_[:, :].rearrange("p (h d) -> p h d", d=128), in0=qn_[:, :].rearrange("p (h d) -> p h d", d=128),
                      in1=cs[:, :].unsqueeze(1).to_broadcast([128, 6, 128]), op=ALU.mult, r=[qn_.k(), cs.k()], w=[r1_.k()])
                yield
                qn5 = qn_[:, :].rearrange("p (h a b e) -> p h a b e", h=6, a=2, b=2)
                r25 = r2_[:, :].rearrange("p (h a b e) -> p h a b e", h=6, a=2, b=2)
                sn4 = sn[:, :].rearrange("p (a b e) -> p a b e", a=2, b=2)
                for bsel in range(2):
                    k.pool("tensor_tensor", out=r25[:, :, :, bsel, :], in0=qn5[:, :, :, 1 - bsel, :],
                           in1=sn4[:, :, bsel, :].unsqueeze(1).to_broadcast([128, 6, 2, 32]), op=ALU.mult, r=[qn_.k(), sn.k()], w=[r2_.k()])
                    yield
                k.dve("tensor_tensor", out=qkb[:, :], in0=r1_[:, :], in1=r2_[:, :], op=ALU.add, r=[r1_.k(), r2_.k()], w=[qkb.k()])
                yield
            for h in range(h0, 6):
                k.pe("transpose", pxT[:, h * 128:(h + 1) * 128], qkb[:, h * 128:(h + 1) * 128], c["identb"][:, :],
                     r=[qkb.k(), c["identb"].k()], w=[psx.k()])
            yield
            if lat:
                k.act("copy", out=qts[:, :].rearrange("p (h t) -> p h t", h=4)[:, :, s * 128:(s + 1) * 128],
                      in_=pxT[:, 0:512].rearrange("p (h t) -> p h t", h=4), r=[psx.k()], w=[qts.k()])
                yield
            k.dve("tensor_copy", out=kts[:, :].rearrange("p (h t) -> p h t", h=2)[:, :, s * 128:(s + 1) * 128],
                  in_=pxT[:, 512:768].rearrange("p (h t) -> p h t", h=2), r=[psx.k()], w=[kts.k()])
            yield

        def tm_part():
            for _ in stage_a(0):
                yield
            for s in range(nsub):
                gl = [stage_b(s)] + ([stage_a(s + 1)] if s + 1 < nsub else [])
                il = Interleaver(gl, 2)
                while il.step():
                    yield
            kcol0 = t0 if lat else T + t0
            for h in range(2):
                k.dma(KT[h][:, kcol0:kcol0 + W], kts[:, h * 512:h * 512 + W], r=[kts.k()], w=[("dram", "KT", h, mi)])
            if lat:
                for h in range(4):
                    k.dma(QT[h][:, t0:t0 + W], qts[:, h * 512:h * 512 + W], r=[qts.k()], w=[("dram", "QT", h, mi)])

        def fm_part():
            nonlocal fcnt
            gcol0 = (GP_LAT0 + t0) if lat else (GP_CTX0 + t0)
            for j in range((12 + (4 if lat else 0)) if P1_FLAGS["fm"] else 0):
                yield
                pf = ps_f[fcnt % 2]
                col0 = j * 128 if j < 12 else C_ZB + (j - 12) * 128
                for kc in range(8):
                    k.pe("matmul", pf[:, 0:W], lhsT=w_in[:, kc * EVEN_IN + col0:kc * EVEN_IN + col0 + 128], rhs=hT3[:, kc, 0:W],
                         start=(kc == 0), stop=(kc == 7), r=[hT.k(), w_in.k(kc)], w=[pf.k()])
                yield
                if j < 12:
                    g = gps[fcnt % 3]
                    if fcnt % 2 == 0:
                        k.dve("tensor_copy", out=g[:, 0:W], in_=pf[:, 0:W], r=[pf.k()], w=[g.k()])
                    else:
                        k.act("copy", out=g[:, 0:W], in_=pf[:, 0:W], r=[pf.k()], w=[g.k()])
                    k.dma(GPRE[j][:, gcol0:gcol0 + W], g[:, 0:W], r=[g.k()], w=[("dram", "GPRE", j, mi)])
                else:
                    g = zbs[fcnt % 3]
                    k.act("activation", out=g[:, 0:W], in_=pf[:, 0:W], func=AF.Silu, r=[pf.k()], w=[g.k()])
                    k.dma(ZBT[j - 12][:, t0:t0 + W], g[:, 0:W], r=[g.k()], w=[("dram", "ZBT", j - 12, mi)])
                fcnt += 1

        il2 = Interleaver([tm_part(), fm_part()], 2)
        while il2.step():
            yield

    for _ in prep_m(0):
        pass
    for mi in range(len(mt_list)):
        gl = [comp_m(mi)] + ([prep_m(mi + 1)] if mi + 1 < len(mt_list) else [])
        run_interleaved(gl, 2)
    k.release(m0)


def _rearr_w(w):
    n = w.shape[1]
    return np.ascontiguousarray(w.reshape(8, 128, n).transpose(1, 0, 2).reshape(128, 8 * n))


def _fm(v):
    return np.ascontiguousarray(v.reshape(-1, 128).T)


def _rope_tables():
    t = np.arange(T)
    row = (t // 64).astype(np.float32)
    col = (t % 64).astype(np.float32)
    inv = (10000.0 ** (-np.arange(0, 64, 2, dtype=np.float32) / 64)).astype(np.float32)
    ar = row[:, None] * inv
    ac = col[:, None] * inv
    cs = np.concatenate([np.cos(ar), np.cos(ar), np.cos(ac), np.cos(ac)], 1).astype(np.float32)
    sn = np.concatenate([-np.sin(ar), np.sin(ar), -np.sin(ac), np.sin(ac)], 1).astype(np.float32)
    return cs.reshape(32, 128, 128), sn.reshape(32, 128, 128)


def host_prep(inp):
    f = lambda a: np.asarray(a, dtype=np.float32)
    shared = {}
    shared["c_ident"] = np.eye(128, dtype=np.float32)
    sel = np.zeros((2, 256), np.float32)
    sel[0, :128] = 1.0
    sel[1, 128:] = 1.0
    shared["c_sel"] = sel
    shared["ada_w_r"] = np.stack([_rearr_w(f(inp["ada_w"][l])) for l in range(2)])
    rows = np.zeros((1, ROWS_N), np.float32)
    vals = {"ada_b0": inp["ada_b"][0], "ada_b1": inp["ada_b"][1], "pre_g0": inp["pre_norm_g"][0], "pre_g1": inp["pre_norm_g"][1],
            "post_g0": inp["post_norm_g"][0], "post_g1": inp["post_norm_g"][1], "b_out": inp["od_b_out"][0], "gdn_g": inp["ev_gdn_norm_g"][0],
            "qn_g": inp["ev_q_norm_g"][0], "kn_g": inp["ev_k_norm_g"][0], "a_log": f(inp["ev_a_log"][0]).reshape(-1), "dt_bias": f(inp["ev_dt_bias"][0]).reshape(-1)}
    for n_, v in vals.items():
        o, l = ROW_OFF[n_]
        rows[0, o:o + l] = f(v).reshape(-1)
    shared["rows"] = rows
    shared["ev_w_in_r"] = _rearr_w(f(inp["ev_w_in"][0]))
    shared["ev_w_out_r"] = _rearr_w(f(inp["ev_w_out"][0]))
    shared["od_w_in_r"] = _rearr_w(f(inp["od_w_in"][0]))
    shared["od_w_out_r"] = _rearr_w(f(inp["od_w_out"][0]))
    cs, sn = _rope_tables()
    shared["rope_cs"] = cs
    shared["rope_sn"] = sn
    pv = np.zeros((128, PV_N), np.float32)
    pv[:, PV_BIN:PV_BIN + 24] = _fm(f(inp["od_b_in"][0]))
    pv[:, PV_DWB:PV_DWB + 8] = _fm(f(inp["od_dw_b"][0]))
    pv[:, PV_LNG:PV_LNG + 8] = _fm(f(inp["od_ln_g"][0]))
    pv[:, PV_LNB:PV_LNB + 8] = _fm(f(inp["od_ln_b"][0]))
    pv[:, PV_DWW:PV_DWW + 248] = f(inp["od_dw_w"][0]).T.reshape(8, 128, 31).transpose(1, 0, 2).reshape(128, 248)
    pv[:, PV_C5W:PV_C5W + 60] = f(inp["ev_short_conv_w"][0]).T.reshape(12, 128, 5).transpose(1, 0, 2).reshape(128, 60)
    shared["pv"] = pv
    maps = []
    cctx = _fm(f(inp["c_ctx"]))
    for b in range(8):
        m = dict(shared)
        m["x"] = np.ascontiguousarray(f(inp["x"][b]))
        m["ctx"] = np.ascontiguousarray(f(inp["ctx"][b]))
        cv = np.zeros((128, 16), np.float32)
        cv[:, 0::2] = _fm(f(inp["c"][b]))
        cv[:, 1::2] = cctx
        m["cvec"] = cv
        maps.append(m)
    return maps


NKC = TK // 128


def p2b_attn(k, c, banks=None, as_gen=False):
    nc = k.nc
    m0 = None if as_gen else k.mark()
    QTd = k.dram("QT", [4, 128, T], BF16)
    KTd = k.dram("KT", [2, 128, TK], BF16)
    Vd = k.dram("V", [NTILE, 128, 256], BF16)
    ZBT = k.dram("ZBT", [4, 128, T], BF16)
    YT = k.dram("YT", [8, 128, T], BF16)
    QTs = k.alloc("QTa", 4 * T, BF16)
    KTs = k.alloc("KTa", 2 * TK, BF16)
    Vs = k.alloc("Va", NTILE * 256, BF16)
    for h in range(4):
        for hf in range(2):
            k.dma(QTs[:, h * T + hf * 2048:h * T + (hf + 1) * 2048], QTd[h][:, hf * 2048:(hf + 1) * 2048],
                  r=[("dram", "QT", h, m) for m in range(1, 9)], w=[QTs.k(h)])
    for h in range(2):
        k.dma(KTs[:, h * TK:(h + 1) * TK], KTd[h], r=[("dram", "KT", h, m) for m in range(9)], w=[KTs.k(h)])
    for t in range(NTILE):
        k.dma(Vs[:, t * 256:(t + 1) * 256], Vd[t], r=[("dram", "V", t)], w=[Vs.k(t)])
    onesb = k.alloc("onesb", 128, BF16)
    k.dve("memset", onesb[:, :], 1.0, w=[onesb.k()])
    PT = [k.alloc(f"PT{i}", 512, BF16) for i in range(4)]
    RD = [k.alloc(f"RD{i}", 512, F32) for i in range(2)]
    OO = [k.alloc(f"OO{i}", 512, F32) for i in range(2)]
    ZG = [k.alloc(f"ZGa{i}", 512, BF16) for i in range(2)]
    YB = [k.alloc(f"YB{i}", 512, BF16) for i in range(2)]
    if banks is None:
        ps_s = [k.PS[0], k.PS[1], k.PS[2], k.PS[3]]
        ps_o = [k.PS[4], k.PS[5]]
        ps_d = [k.PS[6], k.PS[7]]
    else:
        ps_s = [banks[0], banks[1]]
        ps_o = [banks[2], banks[2]]
        ps_d = [banks[3], banks[3]]
    NPS = len(ps_s)
    scale = 128.0 ** -0.5

    def body():
      it = 0
      sc = 0
      for qi in range(T // 512):
        for h in range(4):
            kv = h // 2
            po, pd = ps_o[it % 2], ps_d[it % 2]
            rd, oo, zg, yb = RD[it % 2], OO[it % 2], ZG[it % 2], YB[it % 2]
            k.dma(zg[:, :], ZBT[h][:, qi * 512:(qi + 1) * 512], r=[("dram", "ZBT", h, qi + 1)], w=[zg.k()])
            qsl = QTs[:, h * T + qi * 512:h * T + (qi + 1) * 512]

            def score(kc):
                ps = ps_s[(sc + kc) % NPS]
                k.pe("matmul", ps[:, :], lhsT=KTs[:, kv * TK + kc * 128:kv * TK + (kc + 1) * 128], rhs=qsl, start=True, stop=True,
                     r=[KTs.k(kv), QTs.k(h)], w=[ps.k()])
                pt = PT[(sc + kc) % 4]
                k.act("activation", out=pt[:, :], in_=ps[:, :], func=AF.Exp, scale=scale, r=[ps.k()], w=[pt.k()])

            score(0)
            score(1)
            for kc in range(NKC):
                if kc + 2 < NKC:
                    score(kc + 2)
                pt = PT[(sc + kc) % 4]
                tile_id = kc if kc < 32 else kc
                k.pe("matmul", po[:, :], lhsT=Vs[:, kc * 256 + kv * 128:kc * 256 + (kv + 1) * 128], rhs=pt[:, :], start=(kc == 0), stop=(kc == NKC - 1),
                     r=[Vs.k(kc), pt.k()], w=[po.k()])
                k.pe("matmul", pd[:, :], lhsT=onesb[:, :], rhs=pt[:, :], start=(kc == 0), stop=(kc == NKC - 1),
                     r=[onesb.k(), pt.k()], w=[pd.k()])
                yield
            sc += NKC
            k.dve("reciprocal", out=rd[:, :], in_=pd[:, :], r=[pd.k()], w=[rd.k()])
            k.dve("tensor_tensor", out=oo[:, :], in0=po[:, :], in1=rd[:, :], op=ALU.mult, r=[po.k(), rd.k()], w=[oo.k()])
            k.pool("tensor_tensor", out=yb[:, :], in0=oo[:, :], in1=zg[:, :], op=ALU.mult, r=[oo.k(), zg.k()], w=[yb.k()])
            k.dma(YT[4 + h][:, qi * 512:(qi + 1) * 512], yb[:, :], r=[yb.k()], w=[("dram", "YT", 4 + h, qi)])
            it += 1
            yield

    if as_gen:
        return body()
    for _ in body():
        pass
    k.release(m0)


def p3_outproj(k, c):
    nc = k.nc
    m0 = k.mark()
    x = k.dram("x", [T, D], F32)
    x1 = k.dram("x1", [T, D], F32)
    modd = k.dram("mod", [8, 128, D], F32)
    YT = k.dram("YT", [8, 128, T], BF16)
    w_out_d = k.dram("ev_w_out_r", [128, 8 * 1024], F32)
    w_out = k.alloc("w_out0", 8 * 1024, BF16)
    k.dma(w_out[:, :], w_out_d, r=[("dram", "ev_w_out_r")], w=[w_out.k()], q="pool")
    G0 = k.alloc("G0", D, F32)
    k.dma(G0[:, :], modd[2], r=[("dram", "mod")], w=[G0.k()])
    YTt = [k.alloc(f"YTt{i}", 8 * 512, BF16) for i in range(2)]
    O = [k.alloc(f"O{i}", D, F32) for i in range(2)]
    junks = [k.alloc(f"junkp3{i}", D, BF16) for i in range(2)]
    ss = [k.alloc(f"ssp{i}", 8, F32) for i in range(2)]
    XR = [k.alloc(f"XR{i}", D, F32) for i in range(2)]
    ps_o = [(k.PS[0], k.PS[1]), (k.PS[2], k.PS[3])]
    oc = 0
    for m in range(T // 512):
        yt = YTt[m % 2]
        for j in range(8):
            k.dma(yt[:, j * 512:(j + 1) * 512], YT[j][:, m * 512:(m + 1) * 512], r=[("dram", "YT", j, m)], w=[yt.k(j)])
        def subtile(s, slot):
            r0 = m * 512 + s * 128
            po = ps_o[slot]
            o, sst, xr, jk = O[slot], ss[slot], XR[slot], junks[slot]
            k.dma(xr[:, :], x[r0:r0 + 128, :], r=[("dram", "x", r0 // 128)], w=[xr.k()])
            for nh in range(2):
                for j in range(8):
                    k.pe("matmul", po[nh][:, :], lhsT=yt[:, j * 512 + s * 128:j * 512 + (s + 1) * 128], rhs=w_out[:, j * 1024 + nh * 512:j * 1024 + (nh + 1) * 512],
                         start=(j == 0), stop=(j == 7), r=[yt.k(j), w_out.k()], w=[po[nh].k()])
                yield
            k.act("copy", out=o[:, 0:512], in_=po[0][:, :], r=[po[0].k()], w=[o.k()])
            yield
            k.dve("tensor_copy", out=o[:, 512:1024], in_=po[1][:, :], r=[po[1].k()], w=[o.k()])
            yield
            yield from post_res_gen(k, o, sst, jk, G0, xr, o, xr, x1[r0:r0 + 128, :], ("dram", "x1", r0 // 128))
        run_interleaved([functools.partial(subtile, s) for s in range(4)], 2, slotted=True)
    k.release(m0)


def p1b_gdnprep(k, c):
    nc = k.nc
    m0 = k.mark()
    GPRE = k.dram("GPRE", [12, 128, GP_W], BF16)
    pvd = k.dram("pv", [128, PV_N], F32)
    GQT = k.dram("GQT", [4, 128, TK], BF16)
    GKT = k.dram("GKT", [4, 128, TK], BF16)
    GK = k.dram("GK", [NTILE, 128, 512], BF16)
    GV = k.dram("GV", [NTILE, 128, 512], BF16)
    pv = k.alloc("pv", PV_N, F32)
    k.dma(pv[:, :], pvd, r=[("dram", "pv")], w=[pv.k()])
    DG = k.alloc("DG5", 60 * 128, BF16)
    for i in range(60):
        eng = "dve" if i % 2 == 0 else "pool"
        k.any(eng, "tensor_scalar", out=DG[:, i * 128:(i + 1) * 128], in0=c["identf"][:, :], scalar1=pv[:, PV_C5W + i:PV_C5W + i + 1], scalar2=None,
              op0=ALU.mult, r=[c["identf"].k(), pv.k()], w=[DG.k(i // 5)])
    onesb = k.alloc("onesb", 128, BF16)
    k.dve("memset", onesb[:, :], 1.0, w=[onesb.k()])
    PRE = [k.alloc(f"PRE5{i}", 516, BF16) for i in range(8)]
    U = [k.alloc(f"U5{i}", 512, F32) for i in range(8)]
    SQ = [k.alloc(f"SQ5{i}", 512, BF16) for i in range(8)]
    RS = [k.alloc(f"RS5{i}", 512, F32) for i in range(8)]
    UN = [k.alloc(f"UN5{i}", 512, BF16) for i in range(8)]
    GKs = [k.alloc(f"GKs{i}", 4 * 512, BF16) for i in range(2)]
    GVs = [k.alloc(f"GVs{i}", 4 * 512, BF16) for i in range(2)]
    ps_cv = list(k.PS)
    ps_ss = ps_cv
    ps_tr = ps_cv
    mts = [("ctx", 0, 256)] + [("lat", m * 512, 512) for m in range(T // 512)]
    cc = 0
    for mi, (kind, t0, W) in enumerate(mts):
        lat = kind == "lat"
        gcol0 = (GP_LAT0 + t0) if lat else (GP_CTX0 + t0)
        tile0 = (t0 // 128) if lat else 32
        col0 = tile0 * 128
        nsub = W // 128
        gks, gvs = GKs[mi % 2], GVs[mi % 2]
        def chunk(j, cc, sl):
            pre = PRE[sl]
            pcv = ps_cv[sl]
            k.dma(pre[:, 0:W + 4], GPRE[j][:, gcol0 - 2:gcol0 + W + 2],
                  r=[("dram", "GPRE", j, x_) for x_ in (["p0", "p1", "p2"] + list(range(max(0, mi - 1), min(9, mi + 2))))], w=[pre.k()])
            for t in range(5):
                k.pe("matmul", pcv[:, 0:W], lhsT=DG[:, (j * 5 + t) * 128:(j * 5 + t + 1) * 128], rhs=pre[:, t:t + W], start=(t == 0), stop=(t == 4),
                     r=[DG.k(j), pre.k()], w=[pcv.k()])
            un = UN[sl]
            if j < 8:
                u, sq, rs, pss = U[sl], SQ[sl], RS[sl], ps_ss[sl]
                k.act("activation", out=u[:, 0:W], in_=pcv[:, 0:W], func=AF.Silu, r=[pcv.k()], w=[u.k()])
                yield
                k.act("activation", out=sq[:, 0:W], in_=u[:, 0:W], func=AF.Square, r=[u.k()], w=[sq.k()])
                yield
                k.pe("matmul", pss[:, 0:W], lhsT=onesb[:, :], rhs=sq[:, 0:W], start=True, stop=True, r=[onesb.k(), sq.k()], w=[pss.k()])
                yield
                k.act("activation", out=rs[:, 0:W], in_=pss[:, 0:W], func=AF.Sqrt, bias=NORM_EPS, scale=1.0, r=[pss.k()], w=[rs.k()])
                yield
                k.dve("reciprocal", out=rs[:, 0:W], in_=rs[:, 0:W], r=[rs.k()], w=[rs.k()])
                yield
                if j < 4:
                    k.dve("scalar_tensor_tensor", out=un[:, 0:W], in0=u[:, 0:W], scalar=128.0 ** -0.5, in1=rs[:, 0:W], op0=ALU.mult, op1=ALU.mult,
                          r=[u.k(), rs.k()], w=[un.k()])
                    k.dma(GQT[j][:, col0:col0 + W], un[:, 0:W], r=[un.k()], w=[("dram", "GQT", j, mi)])
                    yield
                else:
                    k.dve("tensor_tensor", out=un[:, 0:W], in0=u[:, 0:W], in1=rs[:, 0:W], op=ALU.mult, r=[u.k(), rs.k()], w=[un.k()])
                    yield
                    k.dma(GKT[j - 4][:, col0:col0 + W], un[:, 0:W], r=[un.k()], w=[("dram", "GKT", j - 4, mi)])
                    yield
            else:
                k.act("activation", out=un[:, 0:W], in_=pcv[:, 0:W], func=AF.Silu, r=[pcv.k()], w=[un.k()])
                yield
            if j >= 4:
                h = (j - 4) % 4
                ptr = ps_tr[sl]
                ptb = ptr.ap.bitcast(BF16)
                for s in range(nsub):
                    k.pe("transpose", ptb[:, s * 128:(s + 1) * 128], un[:, s * 128:(s + 1) * 128], c["identb"][:, :], r=[un.k(), c["identb"].k()], w=[ptr.k()])
                    yield
                dst = (gks if j < 8 else gvs)
                dview = dst[:, :].rearrange("p (s f) -> p s f", s=4)[:, 0:nsub, h * 128:(h + 1) * 128]
                sview = ptb[:, 0:nsub * 128].rearrange("p (s f) -> p s f", f=128)
                if cc % 2 == 0:
                    k.act("copy", out=dview, in_=sview, r=[ptr.k()], w=[dst.k()])
                    yield
                else:
                    k.dve("tensor_copy", out=dview, in_=sview, r=[ptr.k()], w=[dst.k()])
                    yield
            yield
        run_interleaved([functools.partial(chunk, j, cc + j) for j in range(12)], 8, slotted=True)
        cc += 12
        for s in range(nsub):
            k.dma(GK[tile0 + s], gks[:, s * 512:(s + 1) * 512], r=[gks.k()], w=[("dram", "GK", tile0 + s)])
            k.dma(GV[tile0 + s], gvs[:, s * 512:(s + 1) * 512], r=[gvs.k()], w=[("dram", "GV", tile0 + s)])
    k.release(m0)


GDN_LAG = 45


def p2a_gdn(k, c, nslots=4, as_gens=False):
    nc = k.nc
    m0 = None if as_gens else k.mark()
    GQT = k.dram("GQT", [4, 128, TK], BF16)
    GKT = k.dram("GKT", [4, 128, TK], BF16)
    GK = k.dram("GK", [NTILE, 128, 512], BF16)
    GV = k.dram("GV", [NTILE, 128, 512], BF16)
    ABd = k.dram("AB", [NTILE, 128, 16], F32)
    trid = k.dram("c_tri", [9, 128, 128], F32)
    OD = [k.dram("OF", [32, 128, 512], F32), k.dram("OB", [32, 128, 512], F32)]
    TRI = k.alloc("TRI", 9 * 128, F32)
    for i in range(9):
        k.dma(TRI[:, i * 128:(i + 1) * 128], trid[i], r=[("dram", "c_tri")], w=[TRI.k()])
    tri = lambda i: TRI[:, i * 128:(i + 1) * 128]
    bc4 = lambda ap: ap.unsqueeze(1).to_broadcast([128, 4, 128])
    col4 = lambda ap: ap.unsqueeze(2).to_broadcast([128, 4, 128])
    v3 = lambda t: t[:, :].rearrange("p (h f) -> p h f", h=4)
    ABs = k.alloc("ABs", NTILE * 16, F32)
    for t in range(NTILE):
        k.dma(ABs[:, t * 16:(t + 1) * 16], ABd[t], r=[("dram", "AB", t)], w=[ABs.k()])
    alog = k.alloc("alog", 8, F32)
    dtb = k.alloc("dtb", 8, F32)
    k.dma(alog[:, :], row_bc(k, "a_log"), r=[("dram", "rows")], w=[alog.k()])
    k.dma(dtb[:, :], row_bc(k, "dt_bias"), r=[("dram", "rows")], w=[dtb.k()])
    GALL = k.alloc("GALL", NTILE * 8, F32)
    BALL = k.alloc("BALL", NTILE * 8, F32)
    ab3 = ABs[:, :].rearrange("p (t f) -> p t f", f=16)
    g3 = GALL[:, :].rearrange("p (t f) -> p t f", f=8)
    b3 = BALL[:, :].rearrange("p (t f) -> p t f", f=8)
    bct = lambda ap: ap.unsqueeze(1).to_broadcast([128, NTILE, 8])
    k.dve("tensor_tensor", out=g3, in0=ab3[:, :, 0:8], in1=bct(dtb[:, :]), op=ALU.add, r=[ABs.k(), dtb.k()], w=[GALL.k()])
    k.act("activation", out=GALL[:, :], in_=GALL[:, :], func=AF.Exp, r=[GALL.k()], w=[GALL.k()])
    k.act("activation", out=GALL[:, :], in_=GALL[:, :], func=AF.Ln, bias=1.0, scale=1.0, r=[GALL.k()], w=[GALL.k()])
    k.act("activation", out=alog[:, :], in_=alog[:, :], func=AF.Exp, r=[alog.k()], w=[alog.k()])
    k.dve("scalar_tensor_tensor", out=g3, in0=g3, scalar=-1.0, in1=bct(alog[:, :]), op0=ALU.mult, op1=ALU.mult, r=[GALL.k(), alog.k()], w=[GALL.k()])
    k.act("activation", out=b3, in_=ab3[:, :, 8:16], func=AF.Exp, scale=-1.0, r=[ABs.k()], w=[BALL.k()])
    k.dve("tensor_scalar", out=BALL[:, :], in0=BALL[:, :], scalar1=1.0, scalar2=None, op0=ALU.add, r=[BALL.k()], w=[BALL.k()])
    k.dve("reciprocal", out=BALL[:, :], in_=BALL[:, :], r=[BALL.k()], w=[BALL.k()])
    Sf = [k.alloc(f"Sf{d}", 512, F32) for d in range(2)]
    Sb = [k.alloc(f"Sb{d}", 512, BF16) for d in range(2)]
    for d in range(2):
        k.dve("memset", Sf[d][:, :], 0.0, w=[Sf[d].k()])
        k.pool("memset", Sb[d][:, :], 0.0, w=[Sb[d].k()])
    def bufs(d):
        B = {}
        for n_ in ["qT4", "kT4", "ktok", "vtok", "X", "XT", "PT", "AINC", "AINCT", "KD", "ATn", "QEFF", "N1", "N1T", "N2", "P", "V1", "U1"]:
            B[n_] = k.alloc(f"{n_}{d}", 512, BF16)
        for n_ in ["WUR", "WU"]:
            B[n_] = k.alloc(f"{n_}{d}", 1024, BF16)
        for n_ in ["DIFF", "E", "DMS", "DMI", "T1", "ER", "QD", "OUT"]:
            B[n_] = k.alloc(f"{n_}{d}", 512, F32)
        B["SM"] = k.alloc(f"SM{d}", 32, F32)
        return B
    BUF = [bufs(i) for i in range(nslots)]
    order = [[32, 33] + list(range(32)), [33, 32] + list(range(31, -1, -1))]
    cfg = [dict(tri=2, mi=0, ms=1, jl=127, m1a=5, m1b=6, m2a=7), dict(tri=0, mi=2, ms=3, jl=0, m1a=6, m1b=5, m2a=8)]
    identb = c["identb"]
    sdone = {}

    def unit(n, d, slot):
        Q = k.PS[2 * slot:2 * slot + 2]
        P_GR = P_B = P_W0 = P_Z = Q[0]
        P_SM = P_A = P_T = P_W1 = P_Z2 = Q[1]
        if n == 1 and nslots == 4:
            for _ in range(GDN_LAG):
                yield
        g = order[d][n]
        B = BUF[slot]
        cf = cfg[d]
        lat = g < 32
        sm = B["SM"]
        GC, GL, EG, BE, KDS, GT, TMP = (sm[:, 0:4], sm[:, 4:8], sm[:, 8:12], sm[:, 12:16], sm[:, 16:20], sm[:, 20:24], sm[:, 24:28])
        gcol = GALL[:, g * 8 + d * 4:g * 8 + d * 4 + 4]
        bcol = BALL[:, g * 8 + d * 4:g * 8 + d * 4 + 4]
        for h in range(4):
            k.dma(B["qT4"][:, h * 128:(h + 1) * 128], GQT[h][:, g * 128:(g + 1) * 128], r=[("dram", "GQT", h, mi_) for mi_ in range(9)], w=[B["qT4"].k()])
            k.dma(B["kT4"][:, h * 128:(h + 1) * 128], GKT[h][:, g * 128:(g + 1) * 128], r=[("dram", "GKT", h, mi_) for mi_ in range(9)], w=[B["kT4"].k()])
        k.dma(B["ktok"][:, :], GK[g], r=[("dram", "GK", g)], w=[B["ktok"].k()])
        yield
        k.dma(B["vtok"][:, :], GV[g], r=[("dram", "GV", g)], w=[B["vtok"].k()])
        yield
        for h in range(4):
            k.pe("matmul", P_GR[:, h * 128:(h + 1) * 128], lhsT=gcol[:, h:h + 1].to_broadcast([128, 128]), rhs=tri(cf["tri"]), start=True, stop=True,
                 r=[GALL.k(), TRI.k()], w=[P_GR.k()])
        k.pe("matmul", P_SM[:, 0:4], lhsT=tri(cf["tri"]), rhs=gcol, start=True, stop=True, r=[GALL.k(), TRI.k()], w=[P_SM.k()])
        yield
        k.act("copy", out=GC, in_=P_SM[:, 0:4], r=[P_SM.k()], w=[sm.k()])
        yield
        k.dve("tensor_copy", out=GL, in_=v3(P_GR)[:, :, cf["jl"]], r=[P_GR.k()], w=[sm.k()])
        yield
        k.dve("tensor_tensor", out=v3(B["DIFF"]), in0=col4(GC), in1=v3(P_GR), op=ALU.subtract, r=[sm.k(), P_GR.k()], w=[B["DIFF"].k()])
        yield
        k.act("activation", out=B["ER"][:, :], in_=P_GR[:, :], func=AF.Exp, r=[P_GR.k()], w=[B["ER"].k()])
        yield
        k.pool("tensor_scalar", out=B["DIFF"][:, :], in0=B["DIFF"][:, :], scalar1=0.0, scalar2=None, op0=ALU.min, r=[B["DIFF"].k()], w=[B["DIFF"].k()])
        yield
        k.act("activation", out=B["E"][:, :], in_=B["DIFF"][:, :], func=AF.Exp, r=[B["DIFF"].k()], w=[B["E"].k()])
        yield
        k.pool("tensor_tensor", out=v3(B["DMI"]), in0=v3(B["E"]), in1=bc4(tri(cf["mi"])), op=ALU.mult, r=[B["E"].k(), TRI.k()], w=[B["DMI"].k()])
        yield
        k.act("activation", out=EG, in_=GC, func=AF.Exp, r=[sm.k()], w=[sm.k()])
        yield
        k.dve("tensor_tensor", out=BE, in0=EG, in1=bcol, op=ALU.mult, r=[sm.k(), BALL.k()], w=[sm.k()])
        yield
        k.dve("tensor_tensor", out=TMP, in0=GL, in1=GC, op=ALU.subtract, r=[sm.k()], w=[sm.k()])
        yield
        k.act("activation", out=KDS, in_=TMP, func=AF.Exp, r=[sm.k()], w=[sm.k()])
        yield
        k.act("activation", out=GT, in_=GL, func=AF.Exp, r=[sm.k()], w=[sm.k()])
        yield
        for h in range(4):
            k.pe("matmul", P_A[:, h * 128:(h + 1) * 128], lhsT=B["kT4"][:, h * 128:(h + 1) * 128], rhs=B["kT4"][:, h * 128:(h + 1) * 128], start=True, stop=True,
                 r=[B["kT4"].k()], w=[P_A.k()])
        for h in range(4):
            k.pe("matmul", P_B[:, h * 128:(h + 1) * 128], lhsT=B["qT4"][:, h * 128:(h + 1) * 128], rhs=B["kT4"][:, h * 128:(h + 1) * 128], start=True, stop=True,
                 r=[B["qT4"].k(), B["kT4"].k()], w=[P_B.k()])
        k.dve("tensor_tensor", out=B["T1"][:, :], in0=P_A[:, :], in1=B["DMI"][:, :], op=ALU.mult, r=[P_A.k(), B["DMI"].k()], w=[B["T1"].k()])
        yield
        k.dve("scalar_tensor_tensor", out=v3(B["X"]), in0=v3(B["T1"]), scalar=-1.0, in1=col4(bcol), op0=ALU.mult, op1=ALU.mult,
               r=[B["T1"].k(), BALL.k()], w=[B["X"].k()])
        k.dve("tensor_tensor", out=B["AINC"][:, :], in0=P_B[:, :], in1=B["DMI"][:, :], op=ALU.mult, r=[P_B.k(), B["DMI"].k()], w=[B["AINC"].k()])
        yield
        ptb = P_T.ap.bitcast(BF16)
        for h in range(4):
            k.pe("transpose", ptb[:, h * 128:(h + 1) * 128], B["X"][:, h * 128:(h + 1) * 128], identb[:, :], r=[B["X"].k(), identb.k()], w=[P_T.k()])
        for h in range(4):
            k.pe("transpose", ptb[:, 512 + h * 128:512 + (h + 1) * 128], B["AINC"][:, h * 128:(h + 1) * 128], identb[:, :], r=[B["AINC"].k(), identb.k()], w=[P_T.k()])
        k.act("copy", out=B["XT"][:, :], in_=ptb[:, 0:512], r=[P_T.k()], w=[B["XT"].k()])
        yield
        k.act("copy", out=B["AINCT"][:, :], in_=ptb[:, 512:1024], r=[P_T.k()], w=[B["AINCT"].k()])
        yield
        k.pool("tensor_tensor", out=v3(B["N1"]), in0=v3(B["X"]), in1=bc4(tri(cf["m1a"])), op=ALU.mult, r=[B["X"].k(), TRI.k()], w=[B["N1"].k()])
        yield
        k.pool("tensor_tensor", out=v3(B["N1T"]), in0=v3(B["XT"]), in1=bc4(tri(cf["m1b"])), op=ALU.mult, r=[B["XT"].k(), TRI.k()], w=[B["N1T"].k()])
        yield
        k.pool("tensor_tensor", out=v3(B["N2"]), in0=v3(B["X"]), in1=bc4(tri(cf["m2a"])), op=ALU.mult, r=[B["X"].k(), TRI.k()], w=[B["N2"].k()])
        yield
        k.dve("tensor_tensor", out=v3(B["X"]), in0=v3(B["X"]), in1=bc4(tri(4)), op=ALU.mult, r=[B["X"].k(), TRI.k()], w=[B["X"].k()])
        yield
        k.dve("tensor_tensor", out=v3(B["XT"]), in0=v3(B["XT"]), in1=bc4(tri(4)), op=ALU.mult, r=[B["XT"].k(), TRI.k()], w=[B["XT"].k()])
        yield
        k.pool("tensor_tensor", out=v3(B["P"]), in0=v3(B["X"]), in1=bc4(identb[:, :]), op=ALU.add, r=[B["X"].k(), identb.k()], w=[B["P"].k()])
        yield
        k.dve("tensor_tensor", out=v3(B["PT"]), in0=v3(B["XT"]), in1=bc4(identb[:, :]), op=ALU.add, r=[B["XT"].k(), identb.k()], w=[B["PT"].k()])
        yield

        def mm4(ps, lhs, rhs, acc=None):
            for h in range(4):
                sl = slice(h * 128, (h + 1) * 128)
                k.pe("matmul", ps[:, sl], lhsT=B[lhs][:, sl], rhs=B[rhs][:, sl], start=True, stop=(acc is None), r=[B[lhs].k(), B[rhs].k()], w=[ps.k()])
                if acc is not None:
                    k.pe("matmul", ps[:, sl], lhsT=identb[:, :], rhs=B[acc][:, sl], start=False, stop=True, r=[identb.k(), B[acc].k()], w=[ps.k()])

        for l in range(1, 5):
            mm4(P_A, "XT", "X")
            yield
            mm4(P_B, "X", "XT")
            yield
            k.act("copy", out=B["X"][:, :], in_=P_A[:, :], r=[P_A.k()], w=[B["X"].k()])
            yield
            k.act("copy", out=B["XT"][:, :], in_=P_B[:, :], r=[P_B.k()], w=[B["XT"].k()])
            yield
            mm4(P_T, "XT", "P", acc="P")
            yield
            mm4(P_W0, "X", "PT", acc="PT")
            yield
            k.act("copy", out=B["P"][:, :], in_=P_T[:, :], r=[P_T.k()], w=[B["P"].k()])
            yield
            k.dve("tensor_copy", out=B["PT"][:, :], in_=P_W0[:, :], r=[P_W0.k()], w=[B["PT"].k()])
            yield
        mm4(P_A, "N1T", "P")
        yield
        mm4(P_B, "N1", "PT")
        yield
        k.act("copy", out=B["V1"][:, :], in_=P_A[:, :], r=[P_A.k()], w=[B["V1"].k()])
        yield
        k.act("copy", out=B["U1"][:, :], in_=P_B[:, :], r=[P_B.k()], w=[B["U1"].k()])
        yield
        mm4(P_T, "PT", "V1", acc="P")
        yield
        mm4(P_W0, "P", "U1", acc="PT")
        yield
        k.act("copy", out=B["P"][:, :], in_=P_T[:, :], r=[P_T.k()], w=[B["P"].k()])
        yield
        k.dve("tensor_copy", out=B["PT"][:, :], in_=P_W0[:, :], r=[P_W0.k()], w=[B["PT"].k()])
        yield
        mm4(P_A, "N2", "PT")
        yield
        k.act("copy", out=B["U1"][:, :], in_=P_A[:, :], r=[P_A.k()], w=[B["U1"].k()])
        yield
        mm4(P_T, "P", "U1", acc="PT")
        yield
        k.act("copy", out=B["PT"][:, :], in_=P_T[:, :], r=[P_T.k()], w=[B["PT"].k()])
        yield
        wur = B["WUR"][:, :].rearrange("p (h f) -> p h f", h=4)
        k.pool("tensor_tensor", out=wur[:, :, 0:128], in0=v3(B["ktok"]), in1=col4(BE), op=ALU.mult, r=[B["ktok"].k(), sm.k()], w=[B["WUR"].k()])
        yield
        k.pool("tensor_tensor", out=wur[:, :, 128:256], in0=v3(B["vtok"]), in1=col4(bcol), op=ALU.mult, r=[B["vtok"].k(), BALL.k()], w=[B["WUR"].k()])
        yield
        k.pool("tensor_tensor", out=v3(B["KD"]), in0=v3(B["ktok"]), in1=col4(KDS), op=ALU.mult, r=[B["ktok"].k(), sm.k()], w=[B["KD"].k()])
        yield
        for h in range(4):
            pw = P_W0 if h < 2 else P_W1
            k.pe("matmul", pw[:, (h % 2) * 256:(h % 2 + 1) * 256], lhsT=B["PT"][:, h * 128:(h + 1) * 128], rhs=B["WUR"][:, h * 256:(h + 1) * 256], start=True, stop=True,
                 r=[B["PT"].k(), B["WUR"].k()], w=[pw.k()])
        k.act("copy", out=B["WU"][:, 0:512], in_=P_W0[:, :], r=[P_W0.k()], w=[B["WU"].k()])
        yield
        k.act("copy", out=B["WU"][:, 512:1024], in_=P_W1[:, :], r=[P_W1.k()], w=[B["WU"].k()])
        yield
        wv = lambda h: B["WU"][:, h * 256:h * 256 + 128]
        uv = lambda h: B["WU"][:, h * 256 + 128:h * 256 + 256]
        for h in range(4):
            k.pe("matmul", P_Z[:, h * 128:(h + 1) * 128], lhsT=wv(h), rhs=B["KD"][:, h * 128:(h + 1) * 128], start=True, stop=True,
                 r=[B["WU"].k(), B["KD"].k()], w=[P_Z.k()])
        k.act("activation", out=B["ATn"][:, :], in_=P_Z[:, :], func=AF.Copy, scale=-1.0, r=[P_Z.k()], w=[B["ATn"].k()])
        yield
        while n > 0 and not sdone.get((n - 1, d)):
            yield
        if lat:
            k.pool("tensor_tensor", out=B["QD"][:, :], in0=B["qT4"][:, :], in1=B["ER"][:, :], op=ALU.mult, r=[B["qT4"].k(), B["ER"].k()], w=[B["QD"].k()])
            for h in range(4):
                k.pe("matmul", P_Z2[:, h * 128:(h + 1) * 128], lhsT=wv(h), rhs=B["AINCT"][:, h * 128:(h + 1) * 128], start=True, stop=True,
                     r=[B["WU"].k(), B["AINCT"].k()], w=[P_Z2.k()])
            k.dve("tensor_tensor", out=B["QEFF"][:, :], in0=B["QD"][:, :], in1=P_Z2[:, :], op=ALU.subtract, r=[B["QD"].k(), P_Z2.k()], w=[B["QEFF"].k()])
            assert n == 0 or sdone.get((n - 1, d)), f"GDN interleave order violated (o) at n={n} d={d}"
            for h in range(4):
                sl = slice(h * 128, (h + 1) * 128)
                k.pe("matmul", P_Z[:, sl], lhsT=B["QEFF"][:, sl], rhs=Sb[d][:, sl], start=True, stop=False, r=[B["QEFF"].k(), Sb[d].k()], w=[P_Z.k()])
                k.pe("matmul", P_Z[:, sl], lhsT=B["AINCT"][:, sl], rhs=uv(h), start=False, stop=True, r=[B["AINCT"].k(), B["WU"].k()], w=[P_Z.k()])
            k.act("copy", out=B["OUT"][:, :], in_=P_Z[:, :], r=[P_Z.k()], w=[B["OUT"].k()])
            k.dma(OD[d][g], B["OUT"][:, :], r=[B["OUT"].k()], w=[("dram", "O", d, g)])
        assert n == 0 or sdone.get((n - 1, d)), f"GDN interleave order violated at n={n} d={d}"
        for h in range(4):
            sl = slice(h * 128, (h + 1) * 128)
            k.pe("matmul", P_Z2[:, sl], lhsT=B["ATn"][:, sl], rhs=Sb[d][:, sl], start=True, stop=False, r=[B["ATn"].k(), Sb[d].k()], w=[P_Z2.k()])
            k.pe("matmul", P_Z2[:, sl], lhsT=B["KD"][:, sl], rhs=uv(h), start=False, stop=True, r=[B["KD"].k(), B["WU"].k()], w=[P_Z2.k()])
        k.pool("tensor_tensor", out=v3(Sf[d]), in0=v3(Sf[d]), in1=col4(GT), op=ALU.mult, r=[Sf[d].k(), sm.k()], w=[Sf[d].k()])
        yield
        k.dve("tensor_tensor", out=Sf[d][:, :], in0=Sf[d][:, :], in1=P_Z2[:, :], op=ALU.add, r=[Sf[d].k(), P_Z2.k()], w=[Sf[d].k()])
        yield
        k.act("copy", out=Sb[d][:, :], in_=Sf[d][:, :], r=[Sf[d].k()], w=[Sb[d].k()])
        sdone[(n, d)] = True
        yield

    gens = []
    for n in range(NTILE):
        for d in range(2):
            gens.append(functools.partial(unit, n, d))
    if as_gens:
        return gens
    run_interleaved(gens, nslots, slotted=True)
    k.release(m0)


def gdn_consts():
    idx = np.arange(128)
    ge = (idx[:, None] >= idx[None, :]).astype(np.float32)
    gt = (idx[:, None] > idx[None, :]).astype(np.float32)
    bd32 = ((idx[:, None] // 32 == idx[None, :] // 32) & (idx[:, None] != idx[None, :])).astype(np.float32)
    m1l = ((idx[:, None] // 64 == idx[None, :] // 64) & (idx[:, None] // 32 == idx[None, :] // 32 + 1)).astype(np.float32)
    m2l = ((idx[:, None] >= 64) & (idx[None, :] < 64)).astype(np.float32)
    return np.ascontiguousarray(np.stack([ge, gt, ge.T, gt.T, bd32, m1l, m1l.T, m2l, m2l.T]))


def p2c_gdnout(k, c):
    nc = k.nc
    m0 = k.mark()
    OF = k.dram("OF", [32, 128, 512], F32)
    OB = k.dram("OB", [32, 128, 512], F32)
    ZA = k.dram("ZA", [T, 512], BF16)
    YT = k.dram("YT", [8, 128, T], BF16)
    GG = k.alloc("GGn", 512, F32)
    for h in range(4):
        k.dma(GG[:, h * 128:(h + 1) * 128], row_bc(k, "gdn_g"), r=[("dram", "rows")], w=[GG.k()])
    NS = 8
    of = [k.alloc(f"of{i}", 512, F32) for i in range(NS)]
    ob = [k.alloc(f"ob{i}", 512, F32) for i in range(NS)]
    za = [k.alloc(f"zac{i}", 512, BF16) for i in range(NS)]
    o = [k.alloc(f"oc{i}", 512, F32) for i in range(NS)]
    sq = [k.alloc(f"sqc{i}", 512, F32) for i in range(NS)]
    st = [k.alloc(f"stc{i}", 16, F32) for i in range(NS)]
    yb = [k.alloc(f"yc{i}", 512, BF16) for i in range(NS)]
    yts = [k.alloc(f"ytc{i}", 4 * 512, BF16) for i in range(2)]
    ps_tr = list(k.PS)
    v3 = lambda t: t[:, :].rearrange("p (h f) -> p h f", h=4)

    def tile(g, i):
        m = g // 4
        s = g % 4
        yt = yts[m % 2]
        k.dma(of[i][:, :], OF[g], r=[("dram", "O", 0, g)], w=[of[i].k()])
        k.dma(ob[i][:, :], OB[g], r=[("dram", "O", 1, g)], w=[ob[i].k()])
        k.dma(za[i][:, :], ZA[g * 128:(g + 1) * 128, :], r=[("dram", "ZA", g)], w=[za[i].k()])
        k.dve("tensor_tensor", out=o[i][:, :], in0=of[i][:, :], in1=ob[i][:, :], op=ALU.add, r=[of[i].k(), ob[i].k()], w=[o[i].k()])
        yield
        k.pool("tensor_tensor", out=sq[i][:, :], in0=o[i][:, :], in1=o[i][:, :], op=ALU.mult, r=[o[i].k()], w=[sq[i].k()])
        yield
        k.dve("tensor_reduce", out=st[i][:, 0:4], in_=v3(sq[i]), axis=AX.X, op=ALU.add, r=[sq[i].k()], w=[st[i].k()])
        yield
        k.act("activation", out=st[i][:, 4:8], in_=st[i][:, 0:4], func=AF.Sqrt, scale=1.0 / 128, bias=NORM_EPS, r=[st[i].k()], w=[st[i].k()])
        yield
        k.dve("reciprocal", out=st[i][:, 8:12], in_=st[i][:, 4:8], r=[st[i].k()], w=[st[i].k()])
        yield
        k.dve("tensor_tensor", out=v3(o[i]), in0=v3(o[i]), in1=st[i][:, 8:12].unsqueeze(2).to_broadcast([128, 4, 128]), op=ALU.mult,
              r=[o[i].k(), st[i].k()], w=[o[i].k()])
        yield
        k.pool("tensor_tensor", out=o[i][:, :], in0=o[i][:, :], in1=GG[:, :], op=ALU.mult, r=[o[i].k(), GG.k()], w=[o[i].k()])
        yield
        k.dve("tensor_tensor", out=yb[i][:, :], in0=o[i][:, :], in1=za[i][:, :], op=ALU.mult, r=[o[i].k(), za[i].k()], w=[yb[i].k()])
        yield
        ptr = ps_tr[i]
        ptb = ptr.ap.bitcast(BF16)
        for h in range(4):
            k.pe("transpose", ptb[:, h * 128:(h + 1) * 128], yb[i][:, h * 128:(h + 1) * 128], c["identb"][:, :], r=[yb[i].k(), c["identb"].k()], w=[ptr.k()])
        yield
        dview = yt[:, :].rearrange("p (h t) -> p h t", h=4)[:, :, s * 128:(s + 1) * 128]
        sview = ptb[:, 0:512].rearrange("p (h t) -> p h t", h=4)
        k.act("copy", out=dview, in_=sview, r=[ptr.k()], w=[yt.k()])
        yield

    for mg in range(4):
        run_interleaved([functools.partial(tile, 8 * mg + s_) for s_ in range(8)], NS, slotted=True)
        for m in (2 * mg, 2 * mg + 1):
            yt = yts[m % 2]
            for h in range(4):
                k.dma(YT[h][:, m * 512:(m + 1) * 512], yt[:, h * 512:(h + 1) * 512], r=[yt.k()], w=[("dram", "YT", h, m)])
    k.release(m0)


def p2ab(k, c):
    m0 = k.mark()
    gens = p2a_gdn(k, c, nslots=2, as_gens=True)
    att = p2b_attn(k, c, banks=k.PS[4:8], as_gen=True)
    il = Interleaver(gens, 2, slotted=True)
    g_alive, a_alive, r = True, True, 0
    while g_alive or a_alive:
        if g_alive:
            g_alive = il.step()
        if a_alive and (r % ATT_EVERY == 0 or not g_alive):
            try:
                next(att)
            except StopIteration:
                a_alive = False
        r += 1
    k.release(m0)


ATT_EVERY = 3
ALL_PASSES = None


def all_passes():
    return [p0_mod, p1_inproj, p1b_gdnprep, p2a_gdn, p2c_gdnout, p2b_attn, p3_outproj, l1_pass_a, l1_pass_b]


EXT_IN = ["x", "ctx", "cvec", "c_sel", "c_ident", "c_tri", "ada_w_r", "rows", "ev_w_in_r", "ev_w_out_r", "od_w_in_r", "od_w_out_r", "rope_cs", "rope_sn", "pv"]


def kernel(**inputs):
    maps = host_prep(inputs)
    tri = gdn_consts()
    for m in maps:
        m["c_tri"] = tri
    nc, _ = build_program(all_passes(), ext_in=EXT_IN, ext_out=["y"])
    in_maps = [{k_: m[k_] for k_ in EXT_IN} for m in maps]
    res = run_bass_kernel_spmd(nc, in_maps, core_ids=list(range(8)))
    return np.stack([np.asarray(r["y"], dtype=np.float32) for r in res.results], axis=0)
```
